# Optimizing a Trainium2 kernel written in Bass

```python
import jax
import jax.numpy as jnp
from jax import lax
import numpy as np

D_MODEL = 1024
BATCH = 8
SEQ = 2048
DEPTH = 4

CTX_LEN = 256
GRID_W = 64
ROPE_THETA = 10000.0
NEG_INF = -1e30
EPS = 1e-6

MLA_HEADS = 8
MLA_Q_RANK = 384
MLA_KV_RANK = 256
MLA_NOPE = 64
MLA_ROPE = 32
MLA_V = 64
Q_BLOCK = 128

SWA_HEADS = 8
SWA_KV_HEADS = 2
SWA_HEAD_DIM = 64
WINDOW = 128
WIN_BLOCK = 128

GDN_HEADS = 8
GDN_DK = 64
GDN_DV = 64
GDN_CONV = 3
GDN_CHUNK = 64

FFN_DIM = 2816
FFN_CONV = 3

PROJ_SPLITS = (MLA_Q_RANK, MLA_KV_RANK, MLA_ROPE,
               SWA_HEADS * SWA_HEAD_DIM, SWA_KV_HEADS * SWA_HEAD_DIM, SWA_KV_HEADS * SWA_HEAD_DIM,
               GDN_HEADS * (2 * GDN_DK + GDN_DV), GDN_HEADS * GDN_DV, 2 * GDN_HEADS, 2 * GDN_HEADS,
               3 * D_MODEL)
PROJ_WIDTH = sum(PROJ_SPLITS)

kernel_name = 'hybrid_mla_swa_gdn_diffusion_block'


def rmsnorm(x, g):
    xf = x.astype(jnp.float32)
    y = xf * lax.rsqrt(jnp.mean(xf * xf, axis=-1, keepdims=True) + EPS)
    return (y * g.astype(jnp.float32)).astype(x.dtype)


def l2norm(x):
    xf = x.astype(jnp.float32)
    return xf * lax.rsqrt(jnp.sum(xf * xf, axis=-1, keepdims=True) + EPS)


def modulation(cvec, w_mod, b_mod):
    m = jax.nn.silu(cvec) @ w_mod + b_mod
    return jnp.split(m[..., None, :], 6, axis=-1)


def adaln(x, g, shift, scale):
    return rmsnorm(x, g) * (1 + scale) + shift


def dwconv(x, w):
    k = w.shape[0]
    return lax.conv_general_dilated(x, w[:, None, :].astype(x.dtype), window_strides=(1,),
                                    padding=[(k // 2, k // 2)],
                                    dimension_numbers=('NWC', 'WIO', 'NWC'),
                                    feature_group_count=x.shape[-1])


def axial_rope(x, rows, cols):
    d = x.shape[-1]
    da = d // 2
    half = da // 2
    inv = ROPE_THETA ** (-jnp.arange(half, dtype=jnp.float32) / half)

    def rot(xa, pos):
        ang = pos.astype(jnp.float32)[:, None] * inv[None, :]
        cos = jnp.cos(ang)[None, :, None, :]
        sin = jnp.sin(ang)[None, :, None, :]
        x1 = xa[..., :half].astype(jnp.float32)
        x2 = xa[..., half:].astype(jnp.float32)
        return jnp.concatenate([x1 * cos - x2 * sin, x1 * sin + x2 * cos], axis=-1)

    return jnp.concatenate([rot(x[..., :da], rows), rot(x[..., da:], cols)], axis=-1).astype(x.dtype)


def split_proj(u):
    idx = np.cumsum(PROJ_SPLITS)[:-1].tolist()
    return jnp.split(u, idx, axis=-1)


def flat_heads(t):
    return t.reshape(t.shape[0], t.shape[1], -1)


def attend(q, k, v, scale, sink=None):
    B, Lq, H, d = q.shape
    KV = k.shape[2]
    G = H // KV
    qg = q.reshape(B, Lq, KV, G, d)
    s = jnp.einsum('bqkgd,bjkd->bkgqj', qg, k).astype(jnp.float32) * scale
    if sink is not None:
        sk = jnp.broadcast_to(sink.astype(jnp.float32).reshape(1, KV, G, 1, 1), s.shape[:-1] + (1,))
        p = jax.nn.softmax(jnp.concatenate([sk, s], axis=-1), axis=-1)[..., 1:]
    else:
        p = jax.nn.softmax(s, axis=-1)
    o = jnp.einsum('bkgqj,bjkd->bqkgd', p.astype(v.dtype), v)
    return o.reshape(B, Lq, H, v.shape[-1])


def blockwise_attend(q, k, v, scale):
    B, S, H, d = q.shape
    nb = S // Q_BLOCK
    qb = jnp.moveaxis(q.reshape(B, nb, Q_BLOCK, H, d), 1, 0)
    o = lax.map(lambda qq: attend(qq, k, v, scale), qb)
    return jnp.moveaxis(o, 0, 1).reshape(B, S, H, v.shape[-1])


def window_attend(q, k, v, k_ctx, v_ctx, sink, scale):
    B, S, H, d = q.shape
    KV = k.shape[2]
    G = H // KV
    W = WIN_BLOCK
    nb = S // W
    C = k_ctx.shape[1]
    qb = q.reshape(B, nb, W, KV, G, d)

    def band(t):
        tp = jnp.pad(t, ((0, 0), (W, W), (0, 0), (0, 0))).reshape(B, nb + 2, W, KV, t.shape[-1])
        return jnp.concatenate([tp[:, :-2], tp[:, 1:-1], tp[:, 2:]], axis=2)

    kb, vb = band(k), band(v)
    s_loc = jnp.einsum('bnqkgd,bnjkd->bnkgqj', qb, kb).astype(jnp.float32) * scale
    s_ctx = jnp.einsum('bnqkgd,bckd->bnkgqc', qb, k_ctx).astype(jnp.float32) * scale
    blk = jnp.arange(nb)[:, None, None] * W
    qpos = blk + jnp.arange(W)[None, :, None]
    kpos = blk - W + jnp.arange(3 * W)[None, None, :]
    valid = (jnp.abs(kpos - qpos) <= WINDOW) & (kpos >= 0) & (kpos < S)
    s_loc = jnp.where(valid[None, :, None, None], s_loc, NEG_INF)
    sk = jnp.broadcast_to(sink.astype(jnp.float32).reshape(1, 1, KV, G, 1, 1), s_loc.shape[:-1] + (1,))
    p = jax.nn.softmax(jnp.concatenate([sk, s_ctx, s_loc], axis=-1), axis=-1)
    p_ctx = p[..., 1:1 + C].astype(v.dtype)
    p_loc = p[..., 1 + C:].astype(v.dtype)
    o = (jnp.einsum('bnkgqc,bckd->bnqkgd', p_ctx, v_ctx)
         + jnp.einsum('bnkgqj,bnjkd->bnqkgd', p_loc, vb))
    return o.reshape(B, S, H, d)


def mla_heads(cq, ckv, kr, q_norm, kv_norm, w_uq, w_ukv, rows, cols):
    B, L = cq.shape[:2]
    q = (rmsnorm(cq, q_norm) @ w_uq).reshape(B, L, MLA_HEADS, MLA_NOPE + MLA_ROPE)
    kv = (rmsnorm(ckv, kv_norm) @ w_ukv).reshape(B, L, MLA_HEADS, MLA_NOPE + MLA_V)
    q_nope, q_rope = q[..., :MLA_NOPE], q[..., MLA_NOPE:]
    k_rope = kr[:, :, None, :]
    if rows is not None:
        q_rope = axial_rope(q_rope, rows, cols)
        k_rope = axial_rope(k_rope, rows, cols)
    k_rope = jnp.broadcast_to(k_rope, (B, L, MLA_HEADS, MLA_ROPE))
    q = jnp.concatenate([q_nope, q_rope], axis=-1)
    k = jnp.concatenate([kv[..., :MLA_NOPE], k_rope], axis=-1)
    return q, k, kv[..., MLA_NOPE:]


def swa_heads(q, k, v, rows, cols):
    B, L = q.shape[:2]
    q = q.reshape(B, L, SWA_HEADS, SWA_HEAD_DIM)
    k = k.reshape(B, L, SWA_KV_HEADS, SWA_HEAD_DIM)
    v = v.reshape(B, L, SWA_KV_HEADS, SWA_HEAD_DIM)
    if rows is not None:
        q = axial_rope(q, rows, cols)
        k = axial_rope(k, rows, cols)
    return q, k, v


def gdn_prep(qkv, conv_w):
    B, L, _ = qkv.shape
    y = jax.nn.silu(dwconv(qkv, conv_w))
    q, k, v = jnp.split(y, [GDN_HEADS * GDN_DK, 2 * GDN_HEADS * GDN_DK], axis=-1)
    q = l2norm(q.reshape(B, L, GDN_HEADS, GDN_DK)) * (GDN_DK ** -0.5)
    k = l2norm(k.reshape(B, L, GDN_HEADS, GDN_DK))
    return q, k, v.reshape(B, L, GDN_HEADS, GDN_DV)


def gdn_gates(a, b, a_log, dt_bias):
    B, L = a.shape[:2]
    a = a.astype(jnp.float32).reshape(B, L, 2, GDN_HEADS)
    b = b.astype(jnp.float32).reshape(B, L, 2, GDN_HEADS)
    g = -jnp.exp(a_log.astype(jnp.float32)) * jax.nn.softplus(a + dt_bias.astype(jnp.float32))
    return g, jax.nn.sigmoid(b)


def gated_delta_chunked(q, k, v, g, beta, s0):
    B, L, H, _ = q.shape
    C = GDN_CHUNK
    n = L // C

    def chunks(t):
        return t.astype(jnp.float32).reshape(B, n, C, H, t.shape[-1]).transpose(1, 0, 3, 2, 4)

    q, k, v = chunks(q), chunks(k), chunks(v)
    g = g.reshape(B, n, C, H).transpose(1, 0, 3, 2)
    beta = beta.reshape(B, n, C, H).transpose(1, 0, 3, 2)
    gam = jnp.cumsum(g, axis=-1)
    decay = jnp.exp(jnp.minimum(gam[..., :, None] - gam[..., None, :], 0.0))
    strict = jnp.tril(jnp.ones((C, C), dtype=bool), -1)
    incl = jnp.tril(jnp.ones((C, C), dtype=bool))
    kb = k * beta[..., None]
    a_mat = jnp.where(strict, jnp.einsum('nbhid,nbhjd->nbhij', kb, k) * decay, 0.0)
    t_mat = a_mat + jnp.eye(C, dtype=jnp.float32)
    rhs = jnp.concatenate([v * beta[..., None], kb * jnp.exp(gam)[..., None]], axis=-1)
    wy = lax.linalg.triangular_solve(t_mat, rhs, left_side=True, lower=True, unit_diagonal=True)
    u, w = wy[..., :GDN_DV], wy[..., GDN_DV:]
    qk = jnp.where(incl, jnp.einsum('nbhid,nbhjd->nbhij', q, k) * decay, 0.0)

    def step(S, xs):
        q_c, k_c, u_c, w_c, qk_c, gam_c = xs
        v_new = u_c - jnp.einsum('bhck,bhkv->bhcv', w_c, S)
        o = (jnp.einsum('bhck,bhkv->bhcv', q_c * jnp.exp(gam_c)[..., None], S)
             + jnp.einsum('bhcj,bhjv->bhcv', qk_c, v_new))
        g_last = gam_c[..., -1:]
        S = (S * jnp.exp(g_last)[..., None]
             + jnp.einsum('bhck,bhcv->bhkv', k_c * jnp.exp(g_last - gam_c)[..., None], v_new))
        return S, o

    s_fin, o = lax.scan(step, s0, (q, k, u, w, qk, gam))
    return o.transpose(1, 0, 3, 2, 4).reshape(B, L, H, GDN_DV), s_fin


def gdn_bidir(q, k, v, g, beta, s0_f, s0_b):
    o_f, s_f = gated_delta_chunked(q, k, v, g[..., 0, :], beta[..., 0, :], s0_f)
    rev = lambda t: jnp.flip(t, axis=1)
    o_b, s_b = gated_delta_chunked(rev(q), rev(k), rev(v), rev(g[..., 1, :]), rev(beta[..., 1, :]), s0_b)
    return o_f + rev(o_b), s_f, s_b


def gdn_out(o, z, o_norm):
    B, L = z.shape[:2]
    y = rmsnorm(o, o_norm) * jax.nn.silu(z.reshape(B, L, GDN_HEADS, GDN_DV).astype(jnp.float32))
    return y.reshape(B, L, GDN_HEADS * GDN_DV).astype(z.dtype)


def merge(gates, ya, yb, yc, w_pa, w_pb, w_pc, w_out):
    ga, gb, gc = jnp.split(jax.nn.sigmoid(gates), 3, axis=-1)
    return (ga * (ya @ w_pa) + gb * (yb @ w_pb) + gc * (yc @ w_pc)) @ w_out


def token_mixer(xn_lat, xn_ctx, rows, cols, w_in, q_norm, kv_norm, w_uq, w_ukv, sink,
                conv_w, a_log, dt_bias, o_norm, w_pa, w_pb, w_pc, w_out, with_ctx_out):
    B = xn_lat.shape[0]
    (cq_l, ckv_l, kr_l, sq_l, sk_l, sv_l, qkv_l, z_l, a_l, b_l, gate_l) = split_proj(xn_lat @ w_in)
    (cq_c, ckv_c, kr_c, sq_c, sk_c, sv_c, qkv_c, z_c, a_c, b_c, gate_c) = split_proj(xn_ctx @ w_in)
    scale_a = (MLA_NOPE + MLA_ROPE) ** -0.5
    scale_b = SWA_HEAD_DIM ** -0.5

    qa_l, ka_l, va_l = mla_heads(cq_l, ckv_l, kr_l, q_norm, kv_norm, w_uq, w_ukv, rows, cols)
    qa_c, ka_c, va_c = mla_heads(cq_c, ckv_c, kr_c, q_norm, kv_norm, w_uq, w_ukv, None, None)
    ya_l = blockwise_attend(qa_l, jnp.concatenate([ka_c, ka_l], axis=1),
                            jnp.concatenate([va_c, va_l], axis=1), scale_a)

    qb_l, kb_l, vb_l = swa_heads(sq_l, sk_l, sv_l, rows, cols)
    qb_c, kb_c, vb_c = swa_heads(sq_c, sk_c, sv_c, None, None)
    yb_l = window_attend(qb_l, kb_l, vb_l, kb_c, vb_c, sink, scale_b)

    zeros = jnp.zeros((B, GDN_HEADS, GDN_DK, GDN_DV), jnp.float32)
    q_c, k_c, v_c = gdn_prep(qkv_c, conv_w)
    g_c, beta_c = gdn_gates(a_c, b_c, a_log, dt_bias)
    oc_c, s_f, s_b = gdn_bidir(q_c, k_c, v_c, g_c, beta_c, zeros, zeros)
    q_l, k_l, v_l = gdn_prep(qkv_l, conv_w)
    g_l, beta_l = gdn_gates(a_l, b_l, a_log, dt_bias)
    oc_l, _, _ = gdn_bidir(q_l, k_l, v_l, g_l, beta_l, s_f, s_b)

    y_lat = merge(gate_l, flat_heads(ya_l), flat_heads(yb_l), gdn_out(oc_l, z_l, o_norm),
                  w_pa, w_pb, w_pc, w_out)
    if not with_ctx_out:
        return y_lat, None
    ya_c = attend(qa_c, ka_c, va_c, scale_a)
    yb_c = attend(qb_c, kb_c, vb_c, scale_b, sink)
    y_ctx = merge(gate_c, flat_heads(ya_c), flat_heads(yb_c), gdn_out(oc_c, z_c, o_norm),
                  w_pa, w_pb, w_pc, w_out)
    return y_lat, y_ctx


def conv_ffn(xn, w_up, w_conv, w_down):
    h = dwconv(xn @ w_up, w_conv)
    a, b = jnp.split(h, 2, axis=-1)
    return (jax.nn.silu(a) * b) @ w_down


def setup_inputs(seed: int = 0) -> dict:
    key = jax.random.key(seed)
    ks = jax.random.split(key, 32)
    D = D_MODEL

    def nrm(k, shape, s):
        return jax.random.normal(k, shape, jnp.float32) * s

    dt = jnp.exp(jax.random.uniform(ks[14], (DEPTH, 2, GDN_HEADS), jnp.float32,
                                    float(np.log(1e-3)), float(np.log(1e-1))))
    return {
        'x': nrm(ks[0], (BATCH, SEQ, D), 1.0),
        'c': nrm(ks[1], (BATCH, D), 1.0),
        'ctx': nrm(ks[2], (BATCH, CTX_LEN, D), 1.0),
        'c_ctx': nrm(ks[3], (D,), 1.0),
        'w_mod': nrm(ks[4], (DEPTH, D, 6 * D), 0.5 * D ** -0.5),
        'b_mod': nrm(ks[5], (DEPTH, 6 * D), 0.02),
        'norm1': 1.0 + nrm(ks[6], (DEPTH, D), 0.05),
        'norm2': 1.0 + nrm(ks[7], (DEPTH, D), 0.05),
        'w_in': nrm(ks[8], (DEPTH, D, PROJ_WIDTH), D ** -0.5),
        'mla_q_norm': 1.0 + nrm(ks[9], (DEPTH, MLA_Q_RANK), 0.05),
        'mla_kv_norm': 1.0 + nrm(ks[10], (DEPTH, MLA_KV_RANK), 0.05),
        'w_uq': nrm(ks[11], (DEPTH, MLA_Q_RANK, MLA_HEADS * (MLA_NOPE + MLA_ROPE)), MLA_Q_RANK ** -0.5),
        'w_ukv': nrm(ks[12], (DEPTH, MLA_KV_RANK, MLA_HEADS * (MLA_NOPE + MLA_V)), MLA_KV_RANK ** -0.5),
        'swa_sink': nrm(ks[13], (DEPTH, SWA_HEADS), 0.5),
        'gdn_conv': nrm(ks[15], (DEPTH, GDN_CONV, GDN_HEADS * (2 * GDN_DK + GDN_DV)), GDN_CONV ** -0.5),
        'gdn_a_log': jnp.log(jax.random.uniform(ks[16], (DEPTH, 2, GDN_HEADS), jnp.float32, 1.0, 16.0)),
        'gdn_dt_bias': jnp.log(jnp.expm1(dt)),
        'gdn_norm': 1.0 + nrm(ks[17], (DEPTH, GDN_DV), 0.05),
        'w_branch_a': nrm(ks[18], (DEPTH, MLA_HEADS * MLA_V, D), (MLA_HEADS * MLA_V) ** -0.5),
        'w_branch_b': nrm(ks[19], (DEPTH, SWA_HEADS * SWA_HEAD_DIM, D), (SWA_HEADS * SWA_HEAD_DIM) ** -0.5),
        'w_branch_c': nrm(ks[20], (DEPTH, GDN_HEADS * GDN_DV, D), (GDN_HEADS * GDN_DV) ** -0.5),
        'w_out': nrm(ks[21], (DEPTH, D, D), D ** -0.5),
        'ffn_up': nrm(ks[22], (DEPTH, D, 2 * FFN_DIM), D ** -0.5),
        'ffn_conv': nrm(ks[23], (DEPTH, FFN_CONV, 2 * FFN_DIM), FFN_CONV ** -0.5),
        'ffn_down': nrm(ks[24], (DEPTH, FFN_DIM, D), FFN_DIM ** -0.5),
        'norm_f': 1.0 + nrm(ks[25], (D,), 0.05),
    }


def reference(x, c, ctx, c_ctx, w_mod, b_mod, norm1, norm2, w_in, mla_q_norm, mla_kv_norm,
              w_uq, w_ukv, swa_sink, gdn_conv, gdn_a_log, gdn_dt_bias, gdn_norm,
              w_branch_a, w_branch_b, w_branch_c, w_out, ffn_up, ffn_conv, ffn_down, norm_f):
    S = x.shape[1]
    ROWS = S // GRID_W
    rows = jnp.repeat(jnp.arange(ROWS, dtype=jnp.int32), GRID_W)
    cols = jnp.tile(jnp.arange(GRID_W, dtype=jnp.int32), ROWS)
    h = ctx
    for l in range(DEPTH):
        last = l == DEPTH - 1
        m_lat = modulation(c, w_mod[l], b_mod[l])
        m_ctx = modulation(c_ctx, w_mod[l], b_mod[l])
        xn = adaln(x, norm1[l], m_lat[0], m_lat[1])
        hn = adaln(h, norm1[l], m_ctx[0], m_ctx[1])
        y_lat, y_ctx = token_mixer(xn, hn, rows, cols, w_in[l], mla_q_norm[l], mla_kv_norm[l],
                                   w_uq[l], w_ukv[l], swa_sink[l], gdn_conv[l], gdn_a_log[l],
                                   gdn_dt_bias[l], gdn_norm[l], w_branch_a[l], w_branch_b[l],
                                   w_branch_c[l], w_out[l], not last)
        x = x + m_lat[2] * y_lat
        x = x + m_lat[5] * conv_ffn(adaln(x, norm2[l], m_lat[3], m_lat[4]), ffn_up[l], ffn_conv[l], ffn_down[l])
        if not last:
            h = h + m_ctx[2] * y_ctx
            h = h + m_ctx[5] * conv_ffn(adaln(h, norm2[l], m_ctx[3], m_ctx[4]), ffn_up[l], ffn_conv[l], ffn_down[l])
    return rmsnorm(x, norm_f)
```

```python
import numpy as np
from contextlib import ExitStack
import concourse.bass as bass
import concourse.mybir as mybir
from concourse.bass_utils import run_bass_kernel_spmd

F32 = mybir.dt.float32
BF16 = mybir.dt.bfloat16
AF = mybir.ActivationFunctionType
ALU = mybir.AluOpType
AX = mybir.AxisListType

D = 1024
SEQ = 2048
CTX = 256
T = SEQ + CTX
NT = T // 128
DEPTH = 4
EPS = 1e-6
TCH = [(0, 256), (256, 512), (768, 512), (1280, 512), (1792, 512)]
BIG = 1.0e5

CQ, CKV, KR, KRP, SQ, SQP, SK, SKP, GQ, PAD, GATE, NFM = 0, 384, 640, 672, 704, 1216, 1728, 1856, 1984, 3520, 3584, 6656
SV, ZC, AB, NTM = 0, 128, 640, 672


def tile_chunk(t):
    return 0 if t < 2 else 1 + (t - 2) // 4


class Trk:
    ISSUER = {"pe": "pe", "act": "act", "dve": "dve", "pool": "pool", "sp": "sp", "gq": "pool"}
    NSLOT = {"sp": 24, "gq": 8}

    def __init__(self, nc, es):
        self.nc = nc
        self.eng = {"pe": nc.tensor, "act": nc.scalar, "dve": nc.vector, "pool": nc.gpsimd, "sp": nc.sync}
        self.sem = {e: es.enter_context(nc.semaphore("sem_" + e)) for e in ("pe", "act", "dve", "pool")}
        for q, k in self.NSLOT.items():
            for i in range(k):
                self.sem[(q, i)] = es.enter_context(nc.semaphore(f"sem_{q}{i}"))
        self.cnt = {e: 0 for e in self.sem}
        self.nq = {q: 0 for q in self.NSLOT}
        self.lastw = {}
        self.readers = {}
        self.waited = {}
        self.n_wait = 0
        self.n_op = 0

    def _need(self, issuer, eng, dep):
        de, ds = dep
        if de == "pe" and eng == "pe":
            return
        k = (issuer, de)
        if self.waited.get(k, 0) >= ds:
            return
        self.waited[k] = ds
        self.eng[issuer].wait_ge(self.sem[de], ds)
        self.n_wait += 1

    def op(self, eng, emit, reads=(), writes=()):
        issuer = self.ISSUER[eng]
        for key in reads:
            w = self.lastw.get(key)
            if w is not None:
                self._need(issuer, eng, w)
            if isinstance(key, tuple) and key[0] == "ps":
                for rk, tok in self.readers.get(key, {}).items():
                    if rk != eng:
                        self._need(issuer, eng, tok)
        for key in writes:
            w = self.lastw.get(key)
            if w is not None:
                self._need(issuer, eng, w)
            for tok in self.readers.get(key, {}).values():
                self._need(issuer, eng, tok)
        if eng in self.NSLOT:
            sk = (eng, self.nq[eng] % self.NSLOT[eng])
            self.nq[eng] += 1
            if self.cnt[sk] > 0:
                self._need(issuer, eng, (sk, self.cnt[sk]))
            inc = 16
        else:
            sk = eng
            inc = 1
        inst = emit(self.eng[issuer])
        self.cnt[sk] += inc
        inst.then_inc(self.sem[sk], inc)
        tok = (sk, self.cnt[sk])
        for key in reads:
            self.readers.setdefault(key, {})[sk] = tok
        for key in writes:
            self.lastw[key] = tok
            self.readers[key] = {}
        self.n_op += 1
        return inst

    def barrier(self):
        for issuer in ("pe", "act", "dve", "pool", "sp"):
            for e, c in self.cnt.items():
                if c > 0:
                    self._need(issuer, "x", (e, c))
        self.lastw.clear()
        self.readers.clear()

    def finish(self, eng="sp"):
        for e, c in self.cnt.items():
            if c > 0:
                self._need(eng, "x", (e, c))


class Ring:
    def __init__(self, b, es, name, n, shape, dt=F32):
        self.tiles = [b.sb(es, f"{name}{i}", shape, dt) for i in range(n)]
        self.name = name
        self.n = n
        self.i = 0

    def next(self):
        j = self.i % self.n
        self.i += 1
        return self.tiles[j], (self.name, j)


class Builder:
    def __init__(self, n_layers=DEPTH, stop=None, taps=()):
        self.n_layers = n_layers
        self.stop = stop
        self.taps = set(taps)
        self.nc = bass.Bass("TRN2", target_bir_lowering=False)
        self.inp = {}
        self.scr = {}
        self.ev = 0

    def din(self, name, shape, dt=F32):
        self.inp[name] = self.nc.dram_tensor(name, list(shape), dt, kind="ExternalInput").ap()
        return self.inp[name]

    def dscr(self, name, shape, dt=F32, force=False):
        kind = "ExternalOutput" if (name in self.taps or force) else "Internal"
        self.scr[name] = self.nc.dram_tensor(name, list(shape), dt, kind=kind).ap()
        return self.scr[name]

    def declare(self):
        L = self.n_layers
        d = self.din
        d("xin", [T, D]); d("c_fm", [128, 8, 2])
        d("w_mod", [L, D, 6 * D]); d("b_mod_fm", [128, L, 48])
        d("norm1_fm", [128, L, 8]); d("norm2_fm", [128, L, 8]); d("normf_rep", [128, D])
        d("w_fm", [L, D, NFM]); d("w_tm", [L, D, NTM])
        d("qnorm_fm", [128, L, 3]); d("kvnorm_fm", [128, L, 2])
        d("wuq_n", [L, 384, 512]); d("wuq_r", [L, 384, 256]); d("wuq_rp", [L, 384, 256])
        d("wukv_k", [L, 256, 512]); d("wukv_v", [L, 256, 512])
        d("sink_rep", [64, L, 8])
        d("gconv_fm", [64, L, 24, 3]); d("alog_rep", [128, L, 16]); d("dtb_rep", [128, L, 16]); d("gnorm_rep", [128, L, 64])
        d("w_pa", [L, 512, D]); d("w_pb", [L, 512, D]); d("w_pc", [L, 512, D]); d("w_out", [L, D, D])
        d("ffn_up", [L, D, 5632]); d("fconv_fm", [128, L, 44, 3]); d("ffn_down", [L, 2816, D])
        d("ident", [128, 128]); d("ropeM", [2, 32, T]); d("ropeS", [2, 64, T])
        d("tri", [2, 128, 128]); d("gmask", [4, 128, 128]); d("bmask", [3, 128, 128]); d("swamask", [2, 128, 512]); d("selh", [8, 8, 128])
        self.out = self.nc.dram_tensor("out", [SEQ, D], F32, kind="ExternalOutput").ap()
        s = self.dscr
        s("D_fm", [NFM, T]); s("D_tm", [T, NTM]); s("D_gate", [3072, T], BF16)
        s("D_ya", [512, T], BF16); s("D_yb", [512, T], BF16); s("D_yc", [512, T], BF16)
        s("D_gq", [512, T]); s("D_gk", [512, T]); s("D_ktm", [8, T, 64]); s("D_vtm", [8, T, 64])
        s("D_act", [2816, T], BF16)
        if "D_x" in self.taps:
            s("D_x", [D, T], F32, True); s("D_xn", [D, T], BF16, True); s("D_mod", [128, L * 48 * 2], F32, True); s("D_oc", [T, 512], F32, True)

    def sb(self, es, name, shape, dt=F32):
        self.uid = getattr(self, "uid", 0) + 1
        return es.enter_context(self.nc.sbuf_tensor(f"s{self.uid}_{name}", list(shape), dt))

    def evac_eng(self):
        self.ev += 1
        return "act" if self.ev % 2 else "dve"

    def copy(self, eng, out, in_, reads, writes):
        if eng == "act":
            return self.tr.op("act", lambda e: e.copy(out=out, in_=in_), reads=reads, writes=writes)
        return self.tr.op(eng, lambda e: e.tensor_copy(out=out, in_=in_), reads=reads, writes=writes)

    def dma(self, out, in_, reads=(), writes=(), q="sp"):
        return self.tr.op(q, lambda e: e.dma_start(out=out, in_=in_), reads=reads, writes=writes)

    def mm(self, out, lhsT, rhs, start, stop, reads, writes):
        return self.tr.op("pe", lambda e: e.matmul(out, lhsT, rhs, start=start, stop=stop), reads=reads, writes=writes)

    def rstd_from_ss(self, ss_ap, ss_key, r_ap, r_key, scale):
        tr = self.tr
        tr.op("act", lambda e: e.activation(out=r_ap, in_=ss_ap, func=AF.Sqrt, bias=self.cst[0:r_ap.shape[0], 0:1], scale=scale),
              reads=[ss_key, "cst"], writes=[r_key])
        tr.op("dve", lambda e: e.reciprocal(out=r_ap, in_=r_ap), reads=[r_key], writes=[r_key])

    def build(self):
        nc = self.nc
        self.declare()
        with ExitStack() as es:
            self.tr = tr = Trk(nc, es)
            self.ps = [es.enter_context(nc.psum_tensor(f"ps{i}", [128, 512], F32)) for i in range(8)]
            self.xT = self.sb(es, "xT", [128, 8, T])
            self.ident = self.sb(es, "ident", [128, 128])
            self.ones32 = self.sb(es, "ones32", [128, 128])
            self.ones16 = self.sb(es, "ones16", [128, 64], BF16)
            self.cst = self.sb(es, "cst", [128, 4])
            self.mod = self.sb(es, "mod", [128, self.n_layers, 48, 2])
            self.n1 = self.sb(es, "n1", [128, self.n_layers, 8])
            self.n2 = self.sb(es, "n2", [128, self.n_layers, 8])
            self.gs = self.sb(es, "gs", [128, 2, 8, 2])
            self.dma(self.ident[:], self.inp["ident"], writes=["ident"])
            self.dma(self.n1[:], self.inp["norm1_fm"], writes=["n1"])
            self.dma(self.n2[:], self.inp["norm2_fm"], writes=["n2"])
            tr.op("dve", lambda e: e.memset(self.ones32[:], 1.0), writes=["ones32"])
            tr.op("dve", lambda e: e.memset(self.ones16[:], 1.0), writes=["ones16"])
            tr.op("dve", lambda e: e.memset(self.cst[:, 0:1], EPS), writes=["cst"])
            tr.op("dve", lambda e: e.memset(self.cst[:, 1:2], 1.0), writes=["cst"])
            tr.op("dve", lambda e: e.memset(self.cst[:, 2:3], 0.0), writes=["cst"])
            self.phase_load_x()
            self.phase_mod()
            done = self.stop == "mod"
            for l in range(self.n_layers):
                if done:
                    break
                for name, fn in (("norm1", lambda: self.phase_adaln_proj(l)), ("mla", lambda: self.phase_mla(l)),
                                 ("swa", lambda: self.phase_swa(l)), ("gdnprep", lambda: self.phase_gdn_prep(l)),
                                 ("gdn", lambda: self.phase_gdn(l)), ("merge", lambda: self.phase_merge(l)),
                                 ("ffn", lambda: self.phase_ffn(l))):
                    fn()
                    if self.stop == f"{name}{l}" or (self.stop or "").startswith(name + "_"):
                        done = True
                        break
            if "D_x" in self.taps:
                tr.barrier()
                for ci, (t0, n) in enumerate(TCH):
                    self.dma(self.scr["D_x"].rearrange("(c p) t -> p c t", p=128)[:, :, t0:t0 + n], self.xT[:, :, t0:t0 + n])
                self.dma(self.scr["D_mod"], self.mod[:].rearrange("p l j s -> p (l j s)"))
            if not done:
                self.phase_final()
            tr.barrier()
            tr.finish("sp")
        return nc

    def phase_load_x(self):
        tr = self.tr
        with ExitStack() as es:
            ring = Ring(self, es, "xl", 2, [128, D])
            for t in range(NT):
                xt, xk = ring.next()
                self.dma(xt[:], self.inp["xin"][t * 128:(t + 1) * 128, :], writes=[xk])
                for half in range(2):
                    pk = ("ps", (2 * t + half) % 8)
                    pt = self.ps[(2 * t + half) % 8]
                    for j in range(4):
                        c = half * 4 + j
                        tr.op("pe", lambda e: e.transpose(pt[:, j * 128:(j + 1) * 128], xt[:, c * 128:(c + 1) * 128], self.ident[:]),
                              reads=[xk, "ident"], writes=[pk])
                    self.copy(self.evac_eng(), self.xT[:, half * 4:half * 4 + 4, t * 128:(t + 1) * 128],
                              pt[:].rearrange("p (c t) -> p c t", c=4), reads=[pk], writes=[("xT", tile_chunk(t))])
            tr.barrier()

    def phase_mod(self):
        tr = self.tr
        with ExitStack() as es:
            cs = self.sb(es, "cs", [128, 8, 2])
            bm = self.sb(es, "bm", [128, self.n_layers, 48])
            ring = Ring(self, es, "wm", 2, [128, 8, 512])
            self.dma(cs[:], self.inp["c_fm"], writes=["cs"])
            self.dma(bm[:], self.inp["b_mod_fm"], writes=["bm"])
            tr.op("act", lambda e: e.activation(out=cs[:], in_=cs[:], func=AF.Silu), reads=["cs"], writes=["cs"])
            for l in range(self.n_layers):
                wv = self.inp["w_mod"][l].rearrange("(c p) n -> p c n", p=128)
                pk = ("ps", l % 2)
                pt = self.ps[l % 2][:, 0:96].rearrange("p (j s) -> p j s", s=2)
                for g in range(12):
                    w, wk = ring.next()
                    self.dma(w[:], wv[:, :, g * 512:(g + 1) * 512], writes=[wk])
                    for m in range(4):
                        j = g * 4 + m
                        for c in range(8):
                            self.mm(pt[:, j, :], w[:, c, m * 128:(m + 1) * 128], cs[:, c, :], c == 0, c == 7,
                                    reads=[wk, "cs"], writes=[pk])
                for s in range(2):
                    tr.op("dve", lambda e: e.tensor_tensor(out=self.mod[:, l, :, s], in0=pt[:, :, s], in1=bm[:, l, :], op=ALU.add),
                          reads=[pk, "bm"], writes=["mod"])
            tr.barrier()

    def make_gs(self, l, which):
        nrm = self.n1 if which == 0 else self.n2
        sc0 = 8 + which * 24
        for s in range(2):
            self.tr.op("dve", lambda e: e.scalar_tensor_tensor(out=self.gs[:, which, :, s], in0=self.mod[:, l, sc0:sc0 + 8, s], scalar=1.0,
                                                             in1=nrm[:, l, :], op0=ALU.add, op1=ALU.mult),
                       reads=["mod", "n1", "n2"], writes=["gs"])

    def adaln(self, es, l, which, xn):
        tr = self.tr
        self.make_gs(l, which)
        sh0 = which * 24
        sqr = Ring(self, es, f"sq{which}", 2, [128, 512])
        rr = Ring(self, es, f"rr{which}", 2, [128, 512])
        tmpr = Ring(self, es, f"tm{which}", 2, [128, 512])
        for ci, (t0, n) in enumerate(TCH):
            s = 1 if ci == 0 else 0
            pk = ("ps", ci % 2)
            pt = self.ps[ci % 2]
            for c in range(8):
                sq, sk = sqr.next()
                tr.op("act", lambda e: e.activation(out=sq[:, 0:n], in_=self.xT[:, c, t0:t0 + n], func=AF.Square),
                      reads=[("xT", ci)], writes=[sk])
                self.mm(pt[:, 0:n], self.ones32[:], sq[:, 0:n], c == 0, c == 7, reads=[sk, "ones32"], writes=[pk])
            r, rk = rr.next()
            self.rstd_from_ss(pt[:, 0:n], pk, r[:, 0:n], rk, 1.0 / D)
            for c in range(8):
                tm, tk = tmpr.next()
                tr.op("dve", lambda e: e.scalar_tensor_tensor(out=tm[:, 0:n], in0=self.xT[:, c, t0:t0 + n], scalar=self.gs[:, which, c, s:s + 1],
                                                             in1=r[:, 0:n], op0=ALU.mult, op1=ALU.mult),
                      reads=[("xT", ci), "gs", rk], writes=[tk])
                tr.op("act", lambda e: e.activation(out=xn[:, c, t0:t0 + n], in_=tm[:, 0:n], func=AF.Identity,
                                                    bias=self.mod[:, l, sh0 + c, s:s + 1], scale=1.0),
                      reads=[tk, "mod"], writes=[("xn", ci)])

    def phase_adaln_proj(self, l):
        tr = self.tr
        with ExitStack() as es:
            xn = self.sb(es, "xn", [128, 8, T], BF16)
            self.adaln(es, l, 0, xn)
            if "D_x" in self.taps:
                for ci, (t0, n) in enumerate(TCH):
                    self.dma(self.scr["D_xn"].rearrange("(c p) t -> p c t", p=128)[:, :, t0:t0 + n], xn[:, :, t0:t0 + n], reads=[("xn", ci)])
            wr = Ring(self, es, "wfm", 2, [128, 8, 512], BF16)
            st = Ring(self, es, "stg", 3, [128, 512])
            st16 = Ring(self, es, "stg16", 2, [128, 512], BF16)
            wv = self.inp["w_fm"][l].rearrange("(c p) n -> p c n", p=128)
            pi = 0
            for g in range(NFM // 512):
                w, wk = wr.next()
                self.dma(w[:], wv[:, :, g * 512:(g + 1) * 512], writes=[wk], q="gq")
                col = g * 512
                while col < (g + 1) * 512:
                    if col < KR:
                        mc = 128
                    elif col < SQ:
                        mc = 32
                    elif col < PAD:
                        mc = 64
                    elif col < GATE:
                        col += 64
                        continue
                    else:
                        mc = 128
                    lo = col - g * 512
                    for ci, (t0, n) in enumerate(TCH):
                        pk = ("ps", pi % 4)
                        pt = self.ps[pi % 4]
                        pi += 1
                        for c in range(8):
                            self.mm(pt[0:mc, 0:n], w[:, c, lo:lo + mc], xn[:, c, t0:t0 + n], c == 0, c == 7,
                                    reads=[wk, ("xn", ci)], writes=[pk])
                        if col >= GATE:
                            s16, sk = st16.next()
                            tr.op("act", lambda e: e.activation(out=s16[:, 0:n], in_=pt[:, 0:n], func=AF.Sigmoid), reads=[pk], writes=[sk])
                            self.dma(self.scr["D_gate"][col - GATE:col - GATE + 128, t0:t0 + n], s16[:, 0:n], reads=[sk], writes=[("D_gate", ci)])
                        else:
                            s32, sk = st.next()
                            self.copy(self.evac_eng(), s32[0:mc, 0:n], pt[0:mc, 0:n], reads=[pk], writes=[sk])
                            self.dma(self.scr["D_fm"][col:col + mc, t0:t0 + n], s32[0:mc, 0:n], reads=[sk], writes=["D_fm"])
                    col += mc
            wt = self.sb(es, "wtm", [128, 8, NTM], BF16)
            self.dma(wt[:], self.inp["w_tm"][l].rearrange("(c p) n -> p c n", p=128), writes=["wtm"], q="gq")
            stm = Ring(self, es, "stm", 2, [128, NTM])
            for t in range(NT):
                ci = tile_chunk(t)
                pa, pb = self.ps[4 + (t % 2) * 2], self.ps[5 + (t % 2) * 2]
                ka, kb = ("ps", 4 + (t % 2) * 2), ("ps", 5 + (t % 2) * 2)
                for c in range(8):
                    self.mm(pa[:, 0:512], xn[:, c, t * 128:(t + 1) * 128], wt[:, c, 0:512], c == 0, c == 7, reads=["wtm", ("xn", ci)], writes=[ka])
                for c in range(8):
                    self.mm(pb[:, 0:160], xn[:, c, t * 128:(t + 1) * 128], wt[:, c, 512:672], c == 0, c == 7, reads=["wtm", ("xn", ci)], writes=[kb])
                s, sk = stm.next()
                self.copy("act", s[:, 0:512], pa[:, 0:512], reads=[ka], writes=[sk])
                self.copy("dve", s[:, 512:672], pb[:, 0:160], reads=[kb], writes=[sk])
                self.dma(self.scr["D_tm"][t * 128:(t + 1) * 128, :], s[:], reads=[sk], writes=["D_tm"])
            tr.barrier()

    def rms_fm_from_dram(self, es, row0, nchunk, gain, dst, tag):
        tr = self.tr
        ld = Ring(self, es, f"ld{tag}", 2, [128, nchunk, 512])
        sqr = Ring(self, es, f"sqn{tag}", 2, [128, 512])
        rr = Ring(self, es, f"rn{tag}", 2, [128, 512])
        src = self.scr["D_fm"][row0:row0 + nchunk * 128, :].rearrange("(c p) t -> p c t", p=128)
        for ci, (t0, n) in enumerate(TCH):
            x, xk = ld.next()
            self.dma(x[:, :, 0:n], src[:, :, t0:t0 + n], reads=["D_fm"], writes=[xk])
            pk = ("ps", 6 + ci % 2)
            pt = self.ps[6 + ci % 2]
            for c in range(nchunk):
                sq, sk = sqr.next()
                tr.op("act", lambda e: e.activation(out=sq[:, 0:n], in_=x[:, c, 0:n], func=AF.Square), reads=[xk], writes=[sk])
                self.mm(pt[:, 0:n], self.ones32[:], sq[:, 0:n], c == 0, c == nchunk - 1, reads=[sk, "ones32"], writes=[pk])
            r, rk = rr.next()
            self.rstd_from_ss(pt[:, 0:n], pk, r[:, 0:n], rk, 1.0 / (nchunk * 128))
            for c in range(nchunk):
                tr.op("dve", lambda e: e.scalar_tensor_tensor(out=dst[:, c, t0:t0 + n], in0=x[:, c, 0:n], scalar=gain[:, c:c + 1],
                                                             in1=r[:, 0:n], op0=ALU.mult, op1=ALU.mult),
                      reads=[xk, rk, "gain" + tag], writes=[(tag, ci)])

    def rope_from(self, src_ap, src_key, srcp_ap, srcp_key, cos_ap, sin_ap, dst_ap, dst_key, tmpa, tmpb, ka, kb):
        tr = self.tr
        tr.op("dve", lambda e: e.tensor_tensor(out=tmpa, in0=src_ap, in1=cos_ap, op=ALU.mult), reads=[src_key, "rope"], writes=[ka])
        tr.op("pool" if srcp_key[0] != "ps" else "dve", lambda e: e.tensor_tensor(out=tmpb, in0=srcp_ap, in1=sin_ap, op=ALU.mult),
              reads=[srcp_key, "rope"], writes=[kb])
        tr.op("dve", lambda e: e.tensor_tensor(out=dst_ap, in0=tmpa, in1=tmpb, op=ALU.add), reads=[ka, kb], writes=[dst_key])

    def phase_mla(self, l):
        tr = self.tr
        scale = 96.0 ** -0.5
        with ExitStack() as es:
            qg = self.sb(es, "qg", [128, 3]); kg = self.sb(es, "kg", [128, 2])
            self.dma(qg[:], self.inp["qnorm_fm"][:, l, :], writes=["gaincqn"])
            self.dma(kg[:], self.inp["kvnorm_fm"][:, l, :], writes=["gainckvn"])
            cqn = self.sb(es, "cqn", [128, 3, T], BF16)
            ckvn = self.sb(es, "ckvn", [128, 2, T], BF16)
            cosM = self.sb(es, "cosM", [32, T]); sinM = self.sb(es, "sinM", [32, T])
            self.dma(cosM[:], self.inp["ropeM"][0], writes=["rope"])
            self.dma(sinM[:], self.inp["ropeM"][1], writes=["rope"])
            KrT = self.sb(es, "KrT", [32, T], BF16)
            Vt = self.sb(es, "Vt", [128, NT, 512], BF16)
            wqn = self.sb(es, "wqn", [128, 3, 512], BF16); wqr = self.sb(es, "wqr", [128, 3, 256], BF16)
            wqp = self.sb(es, "wqp", [128, 3, 256], BF16)
            wkn = self.sb(es, "wkn", [128, 2, 512], BF16); wkv = self.sb(es, "wkv", [128, 2, 512], BF16)
            for wt_, nm in ((wqn, "wuq_n"), (wqr, "wuq_r"), (wqp, "wuq_rp"), (wkn, "wukv_k"), (wkv, "wukv_v")):
                self.dma(wt_[:], self.inp[nm][l].rearrange("(c p) n -> p c n", p=128), writes=[nm], q="gq")
            with ExitStack() as es2:
                self.rms_fm_from_dram(es2, CQ, 3, qg, cqn, "cqn")
                self.rms_fm_from_dram(es2, CKV, 2, kg, ckvn, "ckvn")
                kl = Ring(self, es2, "krl", 2, [32, 2, 512])
                ta = Ring(self, es2, "kta", 2, [32, 512]); tb = Ring(self, es2, "ktb", 2, [32, 512])
                for ci, (t0, n) in enumerate(TCH):
                    k2, kk = kl.next()
                    self.dma(k2[:, :, 0:n], self.scr["D_fm"][KR:KR + 64, :].rearrange("(a p) t -> p a t", p=32)[:, :, t0:t0 + n],
                             reads=["D_fm"], writes=[kk])
                    a_, ak = ta.next(); b_, bk = tb.next()
                    self.rope_from(k2[:, 0, 0:n], kk, k2[:, 1, 0:n], kk, cosM[:, t0:t0 + n], sinM[:, t0:t0 + n],
                                   KrT[:, t0:t0 + n], ("KrT", ci), a_[:, 0:n], b_[:, 0:n], ak, bk)
                for t in range(NT):
                    ci = tile_chunk(t)
                    pk = ("ps", 4 + t % 2); pt = self.ps[4 + t % 2]
                    for c in range(2):
                        self.mm(pt[:, :], ckvn[:, c, t * 128:(t + 1) * 128], wkv[:, c, :], c == 0, c == 1, reads=[("ckvn", ci), "wukv_v"], writes=[pk])
                    self.copy(self.evac_eng(), Vt[:, t, :], pt[:, :], reads=[pk], writes=[("Vt", t)])
                tr.barrier()
            if self.stop == "mla_prep":
                return
            with ExitStack() as es2:
                qnr = Ring(self, es2, "QnT", 2, [64, T], BF16); qrr = Ring(self, es2, "QrT", 2, [32, T], BF16)
                knr = Ring(self, es2, "KnT", 2, [64, T], BF16)
                ta = Ring(self, es2, "qta", 2, [32, 512]); tb = Ring(self, es2, "qtb", 2, [32, 512])
                pr = Ring(self, es2, "Pexp", 3, [128, 512], BF16)
                rdr = Ring(self, es2, "rden", 2, [64, 512])
                yar = Ring(self, es2, "yah", 2, [64, T], BF16)
                for h in range(8):
                    Qn, qnk = qnr.next(); Qr, qrk = qrr.next(); Kn, knk = knr.next()
                    for ci, (t0, n) in enumerate(TCH):
                        p0, p1, p2, p3 = self.ps[4], self.ps[5], self.ps[6], self.ps[7]
                        for c in range(3):
                            self.mm(p0[0:64, 0:n], wqn[:, c, h * 64:(h + 1) * 64], cqn[:, c, t0:t0 + n], c == 0, c == 2, reads=["wuq_n", ("cqn", ci)], writes=[("ps", 4)])
                        for c in range(3):
                            self.mm(p1[0:32, 0:n], wqr[:, c, h * 32:(h + 1) * 32], cqn[:, c, t0:t0 + n], c == 0, c == 2, reads=["wuq_r", ("cqn", ci)], writes=[("ps", 5)])
                        for c in range(3):
                            self.mm(p2[0:32, 0:n], wqp[:, c, h * 32:(h + 1) * 32], cqn[:, c, t0:t0 + n], c == 0, c == 2, reads=["wuq_rp", ("cqn", ci)], writes=[("ps", 6)])
                        for c in range(2):
                            self.mm(p3[0:64, 0:n], wkn[:, c, h * 64:(h + 1) * 64], ckvn[:, c, t0:t0 + n], c == 0, c == 1, reads=["wukv_k", ("ckvn", ci)], writes=[("ps", 7)])
                        self.copy("act", Qn[:, t0:t0 + n], p0[0:64, 0:n], reads=[("ps", 4)], writes=[(qnk, ci)])
                        self.copy("act", Kn[:, t0:t0 + n], p3[0:64, 0:n], reads=[("ps", 7)], writes=[(knk, ci)])
                        a_, ak = ta.next(); b_, bk = tb.next()
                        self.rope_from(p1[0:32, 0:n], ("ps", 5), p2[0:32, 0:n], ("ps", 6), cosM[:, t0:t0 + n], sinM[:, t0:t0 + n],
                                       Qr[:, t0:t0 + n], (qrk, ci), a_[:, 0:n], b_[:, 0:n], ak, bk)
                    if self.stop == "mla_proj":
                        break
                    ya, yk = yar.next()
                    for ci, (t0, n) in enumerate(TCH):
                        if self.stop == "mla_ctx" and ci > 0:
                            break
                        kts = [0, 1] if ci == 0 else list(range(NT))
                        po, pd = self.ps[3], self.ps[2]
                        for i, kt in enumerate(kts):
                            kci = tile_chunk(kt)
                            sk_ = ("ps", i % 2); psn = self.ps[i % 2]
                            self.mm(psn[:, 0:n], Kn[:, kt * 128:(kt + 1) * 128], Qn[:, t0:t0 + n], True, False,
                                    reads=[(knk, kci), (qnk, ci)], writes=[sk_])
                            self.mm(psn[:, 0:n], KrT[:, kt * 128:(kt + 1) * 128], Qr[:, t0:t0 + n], False, True,
                                    reads=[("KrT", kci), (qrk, ci)], writes=[sk_])
                            P, Pk = pr.next()
                            tr.op("act", lambda e: e.activation(out=P[:, 0:n], in_=psn[:, 0:n], func=AF.Exp, scale=scale), reads=[sk_], writes=[Pk])
                            self.mm(po[0:64, 0:n], Vt[:, kt, h * 64:(h + 1) * 64], P[:, 0:n], i == 0, i == len(kts) - 1, reads=[("Vt", kt), Pk], writes=[("ps", 3)])
                            self.mm(pd[0:64, 0:n], self.ones16[:, :], P[:, 0:n], i == 0, i == len(kts) - 1, reads=["ones16", Pk], writes=[("ps", 2)])
                        rd, rk = rdr.next()
                        tr.op("dve", lambda e: e.reciprocal(out=rd[:, 0:n], in_=pd[0:64, 0:n]), reads=[("ps", 2)], writes=[rk])
                        tr.op("dve", lambda e: e.tensor_tensor(out=ya[:, t0:t0 + n], in0=po[0:64, 0:n], in1=rd[:, 0:n], op=ALU.mult),
                              reads=[("ps", 3), rk], writes=[yk])
                    self.dma(self.scr["D_ya"][h * 64:(h + 1) * 64, :], ya[:], reads=[yk], writes=["D_ya"])
                tr.barrier()

    def phase_swa(self, l):
        tr = self.tr
        with ExitStack() as es:
            QT = self.sb(es, "sQT", [64, 8, T], BF16)
            KT = self.sb(es, "sKT", [64, 2, T], BF16)
            Vs = self.sb(es, "sVs", [128, NT, 128], BF16)
            cosS = self.sb(es, "cosS", [64, T]); sinS = self.sb(es, "sinS", [64, T])
            msk = self.sb(es, "smask", [128, 2, 512], BF16)
            snk = self.sb(es, "snk", [64, 8])
            self.dma(cosS[:], self.inp["ropeS"][0], writes=["rope"])
            self.dma(sinS[:], self.inp["ropeS"][1], writes=["rope"])
            self.dma(msk[:], self.inp["swamask"].rearrange("a p n -> p a n"), writes=["smask"], q="gq")
            self.dma(snk[:], self.inp["sink_rep"][:, l, :], writes=["snk"])
            tr.op("act", lambda e: e.activation(out=snk[:], in_=snk[:], func=AF.Exp), reads=["snk"], writes=["snk"])
            self.dma(Vs[:], self.scr["D_tm"][:, SV:SV + 128].rearrange("(t p) c -> p t c", p=128), reads=["D_tm"], writes=["sVs"], q="gq")
            with ExitStack() as es2:
                ld = Ring(self, es2, "sld", 1, [64, T]); ldp = Ring(self, es2, "sldp", 1, [64, T])
                ta = Ring(self, es2, "sta", 1, [64, T]); tb = Ring(self, es2, "stb", 1, [64, T])
                for j in range(10):
                    r0, rp = (SQ + j * 64, SQP + j * 64) if j < 8 else (SK + (j - 8) * 64, SKP + (j - 8) * 64)
                    dst = QT[:, j, :] if j < 8 else KT[:, j - 8, :]
                    x, xk = ld.next(); xp, xpk = ldp.next()
                    self.dma(x[:], self.scr["D_fm"][r0:r0 + 64, :], reads=["D_fm"], writes=[xk])
                    self.dma(xp[:], self.scr["D_fm"][rp:rp + 64, :], reads=["D_fm"], writes=[xpk])
                    a_, ak = ta.next(); b_, bk = tb.next()
                    self.rope_from(x[:], xk, xp[:], xpk, cosS[:], sinS[:], dst, ("sQK", j), a_[:], b_[:], ak, bk)
                tr.barrier()
            with ExitStack() as es2:
                pr = Ring(self, es2, "sP", 3, [128, 512], BF16)
                dn = Ring(self, es2, "sdn", 2, [64, 512])
                yr = Ring(self, es2, "syb", 2, [64, 4, 128], BF16)
                ybv = self.scr["D_yb"].rearrange("(h d) t -> d h t", d=64)
                it = 0
                for qt in range(NT):
                    if qt < 2:
                        kts = [(0, None), (1, None)]
                    else:
                        kts = [(0, None), (1, None)]
                        if qt > 2:
                            kts.append((qt - 1, 0))
                        kts.append((qt, None))
                        if qt < NT - 1:
                            kts.append((qt + 1, 1))
                    for g in range(2):
                        po, pd = self.ps[3], self.ps[2]
                        rhs = QT[:, 4 * g:4 * g + 4, qt * 128:(qt + 1) * 128]
                        for i, (kt, mk) in enumerate(kts):
                            sk_ = ("ps", it % 2); psn = self.ps[it % 2]; it += 1
                            self.mm(psn[:, :].rearrange("p (h q) -> p h q", h=4), KT[:, g, kt * 128:(kt + 1) * 128], rhs, True, True,
                                    reads=[("sQK", 8 + g)] + [("sQK", 4 * g + hh) for hh in range(4)], writes=[sk_])
                            P, Pk = pr.next()
                            tr.op("act", lambda e: e.activation(out=P[:], in_=psn[:, :], func=AF.Exp, scale=0.125), reads=[sk_], writes=[Pk])
                            if mk is not None:
                                tr.op("dve", lambda e: e.tensor_tensor(out=P[:], in0=P[:], in1=msk[:, mk, :], op=ALU.mult), reads=[Pk, "smask"], writes=[Pk])
                            self.mm(po[0:64, :], Vs[:, kt, g * 64:(g + 1) * 64], P[:], i == 0, i == len(kts) - 1, reads=["sVs", Pk], writes=[("ps", 3)])
                            self.mm(pd[0:64, :], self.ones16[:, :], P[:], i == 0, i == len(kts) - 1, reads=["ones16", Pk], writes=[("ps", 2)])
                        d_, dk = dn.next()
                        for hh in range(4):
                            tr.op("dve", lambda e: e.tensor_scalar(out=d_[:, hh * 128:(hh + 1) * 128], in0=pd[0:64, hh * 128:(hh + 1) * 128],
                                                                  scalar1=snk[:, 4 * g + hh:4 * g + hh + 1], scalar2=None, op0=ALU.add),
                                  reads=[("ps", 2), "snk"], writes=[dk])
                        tr.op("dve", lambda e: e.reciprocal(out=d_[:], in_=d_[:]), reads=[dk], writes=[dk])
                        y, yk = yr.next()
                        tr.op("dve", lambda e: e.tensor_tensor(out=y[:].rearrange("p h q -> p (h q)"), in0=po[0:64, :], in1=d_[:], op=ALU.mult),
                              reads=[("ps", 3), dk], writes=[yk])
                        self.dma(ybv[:, 4 * g:4 * g + 4, qt * 128:(qt + 1) * 128], y[:], reads=[yk], writes=["D_yb"])
                tr.barrier()

    def phase_gdn_prep(self, l):
        tr = self.tr
        with ExitStack() as es:
            cw = self.sb(es, "gcw", [64, 24, 3])
            self.dma(cw[:], self.inp["gconv_fm"][:, l, :, :], writes=["gcw"])
            ld = Ring(self, es, "gld", 2, [64, T]); yr = Ring(self, es, "gy", 2, [64, T])
            sqr = Ring(self, es, "gsq", 2, [64, 512]); rr = Ring(self, es, "grn", 2, [64, 512])
            tmr = Ring(self, es, "gtm", 2, [128, NT, 64])
            for j in range(24):
                x, xk = ld.next(); y, yk = yr.next()
                self.dma(x[:], self.scr["D_fm"][GQ + j * 64:GQ + (j + 1) * 64, :], reads=["D_fm"], writes=[xk])
                tr.op("dve", lambda e: e.tensor_scalar(out=y[:], in0=x[:], scalar1=cw[:, j, 1:2], scalar2=None, op0=ALU.mult), reads=[xk, "gcw"], writes=[yk])
                for (o0, o1, i0, i1, k) in ((1, CTX, 0, CTX - 1, 0), (CTX + 1, T, CTX, T - 1, 0), (0, CTX - 1, 1, CTX, 2), (CTX, T - 1, CTX + 1, T, 2)):
                    tr.op("dve", lambda e: e.scalar_tensor_tensor(out=y[:, o0:o1], in0=x[:, i0:i1], scalar=cw[:, j, k:k + 1], in1=y[:, o0:o1],
                                                                 op0=ALU.mult, op1=ALU.add), reads=[xk, yk, "gcw"], writes=[yk])
                tr.op("act", lambda e: e.activation(out=y[:], in_=y[:], func=AF.Silu), reads=[yk], writes=[yk])
                if j < 16:
                    for ci, (t0, n) in enumerate(TCH):
                        sq, sk = sqr.next()
                        tr.op("act", lambda e: e.activation(out=sq[:, 0:n], in_=y[:, t0:t0 + n], func=AF.Square), reads=[yk], writes=[sk])
                        pk = ("ps", ci % 2); pt = self.ps[ci % 2]
                        self.mm(pt[0:64, 0:n], self.ones32[0:64, 0:64], sq[:, 0:n], True, True, reads=[sk, "ones32"], writes=[pk])
                        r, rk = rr.next()
                        self.rstd_from_ss(pt[0:64, 0:n], pk, r[:, 0:n], rk, 1.0)
                        tr.op("dve", lambda e: e.scalar_tensor_tensor(out=y[:, t0:t0 + n], in0=y[:, t0:t0 + n], scalar=(0.125 if j < 8 else 1.0),
                                                                     in1=r[:, 0:n], op0=ALU.mult, op1=ALU.mult), reads=[yk, rk], writes=[yk])
                    dst = self.scr["D_gq"] if j < 8 else self.scr["D_gk"]
                    self.dma(dst[(j % 8) * 64:(j % 8 + 1) * 64, :], y[:], reads=[yk], writes=["D_gqk"])
                if j >= 8:
                    tm, tk = tmr.next()
                    for t8 in range(0, NT, 8):
                        nt = min(8, NT - t8)
                        pk = ("ps", 2 + (t8 // 8) % 2); pt = self.ps[2 + (t8 // 8) % 2]
                        for tt in range(nt):
                            t = t8 + tt
                            tr.op("pe", lambda e: e.transpose(pt[:, tt * 64:(tt + 1) * 64], y[:, t * 128:(t + 1) * 128], self.ident[0:64, 0:64]),
                                  reads=[yk, "ident"], writes=[pk])
                        self.copy(self.evac_eng(), tm[:, t8:t8 + nt, :], pt[:, 0:nt * 64].rearrange("p (t d) -> p t d", d=64), reads=[pk], writes=[tk])
                    dst = self.scr["D_ktm"] if j < 16 else self.scr["D_vtm"]
                    self.dma(dst[j % 8].rearrange("(t p) d -> p t d", p=128), tm[:], reads=[tk], writes=["D_kvtm"])
            tr.barrier()

    def phase_gdn(self, l):
        tr = self.tr
        with ExitStack() as es:
            oacc = self.sb(es, "oacc", [128, NT, 512])
            ab = self.sb(es, "gab", [128, NT, 32])
            g = self.sb(es, "gg", [128, NT, 16]); beta = self.sb(es, "gbeta", [128, NT, 16]); nbeta = self.sb(es, "gnbeta", [128, NT, 16])
            gam = self.sb(es, "ggam", [128, NT, 16]); eg = self.sb(es, "geg", [128, NT, 16]); bg = self.sb(es, "gbg", [128, NT, 16])
            edl = self.sb(es, "gedl", [128, NT, 16]); tmp = self.sb(es, "gtmp", [128, NT, 16])
            alog = self.sb(es, "galog", [128, 16]); dtb = self.sb(es, "gdtb", [128, 16])
            gamT = self.sb(es, "gamT", [8, 2, T])
            tri = self.sb(es, "gtri", [128, 2, 128]); gm = self.sb(es, "gmask", [128, 4, 128]); sel = self.sb(es, "gsel", [8, 8, 128])
            S = self.sb(es, "gS", [64, 2, 8, 64])
            self.dma(ab[:], self.scr["D_tm"][:, AB:AB + 32].rearrange("(t p) c -> p t c", p=128), reads=["D_tm"], writes=["gab"])
            self.dma(alog[:], self.inp["alog_rep"][:, l, :], writes=["galog"])
            self.dma(dtb[:], self.inp["dtb_rep"][:, l, :], writes=["gdtb"])
            self.dma(tri[:], self.inp["tri"].rearrange("a p n -> p a n"), writes=["gtri"])
            self.dma(gm[:], self.inp["gmask"].rearrange("a p n -> p a n"), writes=["gmaskc"])
            self.dma(sel[:], self.inp["selh"].rearrange("h u n -> u h n"), writes=["gsel"])
            bmk = self.sb(es, "gbmask", [128, 3, 128])
            self.dma(bmk[:], self.inp["bmask"].rearrange("a p n -> p a n"), writes=["gbmask"])
            tr.op("dve", lambda e: e.memset(S[:], 0.0), writes=[(f"S{d}", h) for d in range(2) for h in range(8)])
            tr.op("act", lambda e: e.activation(out=alog[:], in_=alog[:], func=AF.Exp), reads=["galog"], writes=["galog"])
            for t in range(NT):
                tr.op("dve", lambda e: e.tensor_tensor(out=tmp[:, t, :], in0=ab[:, t, 0:16], in1=dtb[:], op=ALU.add), reads=["gab", "gdtb"], writes=["gtmp"])
            tr.op("act", lambda e: e.activation(out=tmp[:], in_=tmp[:], func=AF.Exp), reads=["gtmp"], writes=["gtmp"])
            tr.op("act", lambda e: e.activation(out=tmp[:], in_=tmp[:], func=AF.Ln, bias=self.cst[:, 1:2], scale=1.0), reads=["gtmp", "cst"], writes=["gtmp"])
            for t in range(NT):
                tr.op("dve", lambda e: e.scalar_tensor_tensor(out=g[:, t, :], in0=tmp[:, t, :], scalar=-1.0, in1=alog[:], op0=ALU.mult, op1=ALU.mult),
                      reads=["gtmp", "galog"], writes=["gg"])
            tr.op("act", lambda e: e.activation(out=beta[:], in_=ab[:, :, 16:32], func=AF.Sigmoid), reads=["gab"], writes=["gbeta"])
            tr.op("dve", lambda e: e.tensor_scalar(out=nbeta[:], in0=beta[:], scalar1=-1.0, scalar2=None, op0=ALU.mult), reads=["gbeta"], writes=["gnbeta"])
            pg = self.ps[0][:, 0:NT * 16].rearrange("p (t u) -> p t u", u=16)
            pt_ = self.ps[1][:, 0:NT * 16].rearrange("p (t u) -> p t u", u=16)
            for t in range(NT):
                for d in range(2):
                    self.mm(pg[:, t, d * 8:(d + 1) * 8], tri[:, d, :], g[:, t, d * 8:(d + 1) * 8], True, True, reads=["gtri", "gg"], writes=[("ps", 0)])
                self.mm(pt_[:, t, :], self.ones32[:], g[:, t, :], True, True, reads=["ones32", "gg"], writes=[("ps", 1)])
            self.copy("dve", gam[:], pg, reads=[("ps", 0)], writes=["ggam"])
            tr.op("dve", lambda e: e.tensor_tensor(out=edl[:], in0=pt_, in1=gam[:], op=ALU.subtract), reads=[("ps", 1), "ggam"], writes=["gedl"])
            tr.op("act", lambda e: e.activation(out=edl[:], in_=edl[:], func=AF.Exp), reads=["gedl"], writes=["gedl"])
            tr.op("act", lambda e: e.activation(out=eg[:], in_=gam[:], func=AF.Exp), reads=["ggam"], writes=["geg"])
            tr.op("dve", lambda e: e.tensor_tensor(out=bg[:], in0=beta[:], in1=eg[:], op=ALU.mult), reads=["gbeta", "geg"], writes=["gbg"])
            for d in range(2):
                for t4 in range(0, NT, 4):
                    nt = min(4, NT - t4)
                    pk = ("ps", 2 + (t4 // 4) % 2); pp = self.ps[2 + (t4 // 4) % 2]
                    for tt in range(nt):
                        t = t4 + tt
                        self.mm(pp[0:8, tt * 128:(tt + 1) * 128], g[:, t, d * 8:(d + 1) * 8], tri[:, d, :], True, True, reads=["gg", "gtri"], writes=[pk])
                    self.copy("act", gamT[:, d, t4 * 128:(t4 + nt) * 128], pp[0:8, 0:nt * 128], reads=[pk], writes=["gamT"])
            if self.stop == "gdn_gates":
                tr.barrier()
                return
            es_scan = ExitStack()
            inr = {nm: Ring(self, es_scan, "gi" + nm, 2, shp) for nm, shp in (("q", [64, 8, 128]), ("k", [64, 8, 128]), ("ktm", [128, 8, 64]), ("vtm", [128, 8, 64]))}
            kdr = Ring(self, es_scan, "gkd", 2, [128, 8, 64])
            t1r = Ring(self, es_scan, "gt1", 2, [128, 128]); t2r = Ring(self, es_scan, "gt2", 2, [128, 128])
            pxr = Ring(self, es_scan, "gpx", 3, [128, 256]); ptr_ = Ring(self, es_scan, "gpt", 3, [128, 128])
            qkr = Ring(self, es_scan, "gqk", 2, [128, 128]); wtr = Ring(self, es_scan, "gwt", 2, [64, 128])
            a0r = Ring(self, es_scan, "ga0", 2, [128, 256]); lr = Ring(self, es_scan, "gl", 2, [128, 2, 128]); ybr = Ring(self, es_scan, "gyb", 6, [128, 128])
            egr = Ring(self, es_scan, "gegb", 2, [64, 128]); qgr = Ring(self, es_scan, "gqg", 2, [64, 128])
            vnr = Ring(self, es_scan, "gvn", 2, [128, 64])
            gqv = self.scr["D_gq"].rearrange("(h d) t -> d h t", d=64)
            gkv = self.scr["D_gk"].rearrange("(h d) t -> d h t", d=64)
            for d in range(2):
                order = list(range(NT)) if d == 0 else [1, 0] + list(range(NT - 1, 1, -1))
                last = 127 if d == 0 else 0
                if (self.stop or "").startswith("gdn_u1") and d == 1:
                    break
                for t in order:
                    if (self.stop or "").startswith("gdn_u1") and t > 0:
                        break
                    qT, qk_ = inr["q"].next(); kT, kk_ = inr["k"].next(); ktm, ktk = inr["ktm"].next(); vtm, vtk = inr["vtm"].next()
                    sl = slice(t * 128, (t + 1) * 128)
                    self.dma(qT[:], gqv[:, :, sl], reads=["D_gqk"], writes=[qk_])
                    self.dma(kT[:], gkv[:, :, sl], reads=["D_gqk"], writes=[kk_])
                    self.dma(ktm[:], self.scr["D_ktm"][:, sl, :].rearrange("h p d -> p h d"), reads=["D_kvtm"], writes=[ktk])
                    self.dma(vtm[:], self.scr["D_vtm"][:, sl, :].rearrange("h p d -> p h d"), reads=["D_kvtm"], writes=[vtk])
                    kd, kdk = kdr.next()
                    u0 = d * 8
                    tr.op("pool", lambda e: e.tensor_tensor(out=kd[:], in0=ktm[:], in1=edl[:, t, u0:u0 + 8].unsqueeze(2).to_broadcast([128, 8, 64]), op=ALU.mult),
                          reads=[ktk, "gedl"], writes=[kdk])
                    for h in range(8):
                        if (self.stop or "").startswith("gdn_u1") and h > 0:
                            break
                        u = u0 + h
                        Sk = f"S{d}"
                        cutk = int(self.stop.split("_")[2]) if (self.stop or "").startswith("gdn_u1_") else None
                        pgb, pgk = self.ps[0], ("ps", 0)
                        self.mm(pgb[:, 0:128], sel[:, h, :], gamT[:, d, sl], True, True, reads=["gsel", "gamT"], writes=[pgk])
                        t1, t1k = t1r.next(); t2, t2k = t2r.next()
                        tr.op("dve", lambda e: e.scalar_tensor_tensor(out=t1[:], in0=pgb[:, 0:128], scalar=gam[:, t, u:u + 1], in1=gm[:, 2 * d, :],
                                                                     op0=ALU.subtract, op1=ALU.add), reads=[pgk, "ggam", "gmaskc"], writes=[t1k])
                        tr.op("act", lambda e: e.activation(out=t1[:], in_=t1[:], func=AF.Exp, scale=-1.0), reads=[t1k], writes=[t1k])
                        tr.op("dve", lambda e: e.scalar_tensor_tensor(out=t2[:], in0=pgb[:, 0:128], scalar=gam[:, t, u:u + 1], in1=gm[:, 2 * d + 1, :],
                                                                     op0=ALU.subtract, op1=ALU.add), reads=[pgk, "ggam", "gmaskc"], writes=[t2k])
                        tr.op("act", lambda e: e.activation(out=t2[:], in_=t2[:], func=AF.Exp), reads=[t2k], writes=[t2k])
                        egb, egk = egr.next()
                        tr.op("act", lambda e: e.activation(out=egb[:], in_=pgb[0:64, 0:128], func=AF.Exp), reads=[pgk], writes=[egk])
                        if cutk is not None and cutk <= 1:
                            break
                        pkk, pkkk = self.ps[1], ("ps", 1)
                        self.mm(pkk[:, 0:128], kT[:, h, :], kT[:, h, :], True, True, reads=[kk_], writes=[pkkk])
                        a0, a0k = a0r.next()
                        tr.op("dve", lambda e: e.scalar_tensor_tensor(out=a0[:, 0:128], in0=pkk[:, 0:128], scalar=nbeta[:, t, u:u + 1], in1=t1[:],
                                                                     op0=ALU.mult, op1=ALU.mult), reads=[pkkk, "gnbeta", t1k], writes=[(a0k, "p")])
                        tr.op("pool", lambda e: e.tensor_scalar(out=a0[:, 128:192], in0=vtm[:, h, :], scalar1=beta[:, t, u:u + 1], scalar2=None, op0=ALU.mult),
                              reads=[vtk, "gbeta"], writes=[(a0k, "x")])
                        tr.op("pool", lambda e: e.tensor_scalar(out=a0[:, 192:256], in0=ktm[:, h, :], scalar1=bg[:, t, u:u + 1], scalar2=None, op0=ALU.mult),
                              reads=[ktk, "gbg"], writes=[(a0k, "x")])
                        ppt, pptk = self.ps[2], ("ps", 2)
                        tr.op("pe", lambda e: e.transpose(ppt[:, 0:128], a0[:, 0:128], self.ident[:]), reads=[(a0k, "p"), "ident"], writes=[pptk])
                        px, pxk = pxr.next(); pT, pTk = ptr_.next(); lt, ltk = lr.next()
                        tr.op("dve", lambda e: e.tensor_tensor(out=px[:, 0:128], in0=ppt[:, 0:128], in1=bmk[:, 0, :], op=ALU.mult), reads=[pptk, "gbmask"], writes=[(pxk, "p")])
                        tr.op("dve", lambda e: e.tensor_tensor(out=lt[:, 0, :], in0=ppt[:, 0:128], in1=bmk[:, 1, :], op=ALU.mult), reads=[pptk, "gbmask"], writes=[(ltk, 0)])
                        tr.op("dve", lambda e: e.tensor_tensor(out=lt[:, 1, :], in0=ppt[:, 0:128], in1=bmk[:, 2, :], op=ALU.mult), reads=[pptk, "gbmask"], writes=[(ltk, 1)])
                        tr.op("pool", lambda e: e.tensor_tensor(out=pT[:], in0=a0[:, 0:128], in1=bmk[:, 0, :], op=ALU.mult), reads=[(a0k, "p"), "gbmask"], writes=[pTk])
                        tr.op("pool", lambda e: e.tensor_copy(out=px[:, 128:256], in_=self.ident[:]), reads=["ident"], writes=[(pxk, "x")])
                        pqk, pqkk = self.ps[3], ("ps", 3)
                        self.mm(pqk[:, 0:128], kT[:, h, :], qT[:, h, :], True, True, reads=[kk_, qk_], writes=[pqkk])
                        qk, qkk = qkr.next()
                        tr.op("dve", lambda e: e.tensor_tensor(out=qk[:], in0=pqk[:, 0:128], in1=t2[:], op=ALU.mult), reads=[pqkk, t2k], writes=[qkk])
                        for lev in range(5):
                            lastlev = lev == 4
                            pc, pck = self.ps[4 + lev % 2], ("ps", 4 + lev % 2)
                            if not lastlev:
                                self.mm(pc[:, 0:256], pT[:], px[:, 0:256], True, True, reads=[pTk, (pxk, "p"), (pxk, "x")], writes=[pck])
                                pc2, pc2k = self.ps[6 + lev % 2], ("ps", 6 + lev % 2)
                                self.mm(pc2[:, 0:128], px[:, 0:128], pT[:], True, True, reads=[pTk, (pxk, "p")], writes=[pc2k])
                                nx, nxk = pxr.next(); nT, nTk = ptr_.next()
                                self.copy("act", nx[:, 0:128], pc[:, 0:128], reads=[pck], writes=[(nxk, "p")])
                                tr.op("dve", lambda e: e.tensor_tensor(out=nx[:, 128:256], in0=pc[:, 128:256], in1=px[:, 128:256], op=ALU.add),
                                      reads=[pck, (pxk, "x")], writes=[(nxk, "x")])
                                self.copy("act", nT[:], pc2[:, 0:128], reads=[pc2k], writes=[nTk])
                                px, pxk, pT, pTk = nx, nxk, nT, nTk
                            else:
                                self.mm(pc[:, 0:128], pT[:], px[:, 128:256], True, True, reads=[pTk, (pxk, "x")], writes=[pck])
                                nx, nxk = pxr.next()
                                tr.op("dve", lambda e: e.tensor_tensor(out=nx[:, 128:256], in0=pc[:, 0:128], in1=px[:, 128:256], op=ALU.add),
                                      reads=[pck, (pxk, "x")], writes=[(nxk, "x")])
                                px, pxk = nx, nxk
                        MT, MTk = px[:, 128:256], (pxk, "x")
                        pa_i = [0]

                        def apply(lhsT_ap, lhs_key, rhs_ap, rhs_key, add_ap=None, add_key=None, eng="act"):
                            pb, pbk = self.ps[4 + pa_i[0] % 4], ("ps", 4 + pa_i[0] % 4)
                            pa_i[0] += 1
                            self.mm(pb[:, 0:128], lhsT_ap, rhs_ap, True, True, reads=[lhs_key, rhs_key], writes=[pbk])
                            yb, ybk = ybr.next()
                            if add_ap is None:
                                self.copy(eng, yb[:], pb[:, 0:128], reads=[pbk], writes=[ybk])
                            else:
                                tr.op("dve", lambda e: e.tensor_tensor(out=yb[:], in0=pb[:, 0:128], in1=add_ap, op=ALU.add), reads=[pbk, add_key], writes=[ybk])
                            return yb, ybk

                        def s64(z_ap, z_key):
                            y1, y1k = apply(MT, MTk, z_ap, z_key, eng="act")
                            z1, z1k = apply(lt[:, 0, :], (ltk, 0), y1[:], y1k, eng="dve")
                            return apply(MT, MTk, z1[:], z1k, add_ap=y1[:], add_key=y1k)

                        y2, y2k = s64(a0[:, 128:256], (a0k, "x"))
                        z2, z2k = apply(lt[:, 1, :], (ltk, 1), y2[:], y2k, eng="act")
                        y4, y4k = s64(z2[:], z2k)
                        px, pxk = pxr.next()
                        tr.op("pool", lambda e: e.tensor_tensor(out=px[:, 128:256], in0=y2[:], in1=y4[:], op=ALU.add), reads=[y2k, y4k], writes=[(pxk, "x")])
                        if cutk is not None and cutk <= 5:
                            break
                        pw, pwk = self.ps[2], ("ps", 2)
                        tr.op("pe", lambda e: e.transpose(pw[0:64, 0:128], px[:, 192:256], self.ident[:]), reads=[(pxk, "x"), "ident"], writes=[pwk])
                        wt_, wtk = wtr.next()
                        self.copy("act", wt_[:], pw[0:64, 0:128], reads=[pwk], writes=[wtk])
                        qg, qgk = qgr.next()
                        tr.op("pool", lambda e: e.tensor_tensor(out=qg[:], in0=qT[:, h, :], in1=egb[:], op=ALU.mult), reads=[qk_, egk], writes=[qgk])
                        if cutk is not None and cutk <= 6:
                            break
                        pv, pvk = self.ps[3], ("ps", 3)
                        self.mm(pv[:, 128:192], wt_[:], S[:, d, h, :], True, True, reads=[wtk, (Sk, h)], writes=[pvk])
                        vn, vnk = vnr.next()
                        tr.op("dve", lambda e: e.tensor_tensor(out=vn[:], in0=px[:, 128:192], in1=pv[:, 128:192], op=ALU.subtract),
                              reads=[(pxk, "x"), pvk], writes=[vnk])
                        if cutk is not None and cutk <= 7:
                            break
                        po, pok = self.ps[1], ("ps", 1)
                        self.mm(po[:, 128:192], qg[:], S[:, d, h, :], True, False, reads=[qgk, (Sk, h)], writes=[pok])
                        self.mm(po[:, 128:192], qk[:], vn[:], False, True, reads=[qkk, vnk], writes=[pok])
                        if d == 0:
                            self.copy("act", oacc[:, t, h * 64:(h + 1) * 64], po[:, 128:192], reads=[pok], writes=[("oacc", t, h)])
                        else:
                            tr.op("dve", lambda e: e.tensor_tensor(out=oacc[:, t, h * 64:(h + 1) * 64], in0=po[:, 128:192], in1=oacc[:, t, h * 64:(h + 1) * 64], op=ALU.add),
                                  reads=[pok, ("oacc", t, h)], writes=[("oacc", t, h)])
                        if cutk is not None and cutk <= 8:
                            break
                        pS, pSk = self.ps[0], ("ps", 0)
                        self.mm(pS[0:64, 128:192], kd[:, h, :], vn[:], True, True, reads=[kdk, vnk], writes=[pSk])
                        tr.op("dve", lambda e: e.scalar_tensor_tensor(out=S[:, d, h, :], in0=S[:, d, h, :], scalar=egb[:, last:last + 1], in1=pS[0:64, 128:192],
                                                                     op0=ALU.mult, op1=ALU.add), reads=[(Sk, h), egk, pSk], writes=[(Sk, h)])
            if (self.stop or "").startswith("gdn_u1"):
                tr.barrier()
                es_scan.close()
                return
            tr.barrier()
            es_scan.close()
            for t in range(NT):
                for h in range(8):
                    tr.lastw[("oacc", t, h)] = None
            tr.lastw = {k: v for k, v in tr.lastw.items() if v is not None}
            if "D_x" in self.taps:
                self.dma(self.scr["D_oc"].rearrange("(t p) c -> p t c", p=128), oacc[:], reads=[("oacc", t, h) for t in range(NT) for h in range(8)])
            gn = self.sb(es, "gn", [128, 64])
            self.dma(gn[:], self.inp["gnorm_rep"][:, l, :], writes=["gn"])
            zr = Ring(self, es, "gz", 2, [128, 512]); sq2 = Ring(self, es, "gsq2", 2, [128, 512]); ssr = Ring(self, es, "gss", 2, [128, 8])
            ycr = Ring(self, es, "gyc", 2, [128, 4, 128], BF16)
            ycv = self.scr["D_yc"].rearrange("(c p) t -> p c t", p=128)
            for t in range(NT):
                z, zk = zr.next()
                self.dma(z[:], self.scr["D_tm"][t * 128:(t + 1) * 128, ZC:ZC + 512], reads=["D_tm"], writes=[zk])
                tr.op("act", lambda e: e.activation(out=z[:], in_=z[:], func=AF.Silu), reads=[zk], writes=[zk])
                ok = [("oacc", t, h) for h in range(8)]
                sq, sk = sq2.next()
                tr.op("pool", lambda e: e.tensor_tensor(out=sq[:], in0=oacc[:, t, :], in1=oacc[:, t, :], op=ALU.mult), reads=ok, writes=[sk])
                ss, ssk = ssr.next()
                tr.op("dve", lambda e: e.tensor_reduce(out=ss[:], in_=sq[:].rearrange("p (h d) -> p h d", d=64), axis=AX.X, op=ALU.add), reads=[sk], writes=[ssk])
                self.rstd_from_ss(ss[:], ssk, ss[:], ssk, 1.0 / 64)
                o3 = oacc[:, t, :].rearrange("p (h d) -> p h d", d=64)
                tr.op("dve", lambda e: e.tensor_tensor(out=sq[:].rearrange("p (h d) -> p h d", d=64), in0=o3, in1=ss[:].unsqueeze(2).to_broadcast([128, 8, 64]), op=ALU.mult),
                      reads=ok + [ssk], writes=[sk])
                tr.op("dve", lambda e: e.tensor_tensor(out=sq[:].rearrange("p (h d) -> p h d", d=64), in0=sq[:].rearrange("p (h d) -> p h d", d=64),
                                                      in1=gn[:].unsqueeze(1).to_broadcast([128, 8, 64]), op=ALU.mult), reads=[sk, "gn"], writes=[sk])
                tr.op("dve", lambda e: e.tensor_tensor(out=sq[:], in0=sq[:], in1=z[:], op=ALU.mult), reads=[sk, zk], writes=[sk])
                pk = ("ps", 4 + t % 2); pt = self.ps[4 + t % 2]
                for c in range(4):
                    tr.op("pe", lambda e: e.transpose(pt[:, c * 128:(c + 1) * 128], sq[:, c * 128:(c + 1) * 128], self.ident[:]), reads=[sk, "ident"], writes=[pk])
                yc, yk = ycr.next()
                self.copy("act", yc[:], pt[:].rearrange("p (c t) -> p c t", c=4), reads=[pk], writes=[yk])
                self.dma(ycv[:, :, t * 128:(t + 1) * 128], yc[:], reads=[yk], writes=["D_yc"])
            tr.barrier()

    def phase_merge(self, l):
        tr = self.tr
        with ExitStack() as es:
            wp = [self.sb(es, f"wp{i}", [128, 4, D], BF16) for i in range(3)]
            wo = self.sb(es, "wo", [128, 8, D], BF16)
            for i, nm in enumerate(("w_pa", "w_pb", "w_pc")):
                self.dma(wp[i][:], self.inp[nm][l].rearrange("(c p) n -> p c n", p=128), writes=[nm], q="gq")
            self.dma(wo[:], self.inp["w_out"][l].rearrange("(c p) n -> p c n", p=128), writes=["w_out"], q="gq")
            gr = Ring(self, es, "mg", 1, [128, 24, 512], BF16)
            yr = [Ring(self, es, f"my{i}", 2, [128, 4, 512], BF16) for i in range(3)]
            mixr = Ring(self, es, "mix", 2, [128, 512]); tmr = Ring(self, es, "mtm", 2, [128, 512])
            mT = Ring(self, es, "mixT", 1, [128, 8, 512], BF16)
            gv = self.scr["D_gate"].rearrange("(c p) t -> p c t", p=128)
            yv = [self.scr[nm].rearrange("(c p) t -> p c t", p=128) for nm in ("D_ya", "D_yb", "D_yc")]
            pi = 0
            for ci, (t0, n) in enumerate(TCH):
                s = 1 if ci == 0 else 0
                gt, gk = gr.next()
                self.dma(gt[:, :, 0:n], gv[:, :, t0:t0 + n], reads=[("D_gate", ci)], writes=[gk])
                ys = []
                for i in range(3):
                    y, yk = yr[i].next()
                    self.dma(y[:, :, 0:n], yv[i][:, :, t0:t0 + n], reads=["D_ya", "D_yb", "D_yc"], writes=[yk])
                    ys.append((y, yk))
                mt, mtk = mT.next()
                for m in range(8):
                    mix, mk = mixr.next()
                    for i in range(3):
                        pk = ("ps", pi % 4); pt = self.ps[pi % 4]; pi += 1
                        y, yk = ys[i]
                        for c in range(4):
                            self.mm(pt[:, 0:n], wp[i][:, c, m * 128:(m + 1) * 128], y[:, c, 0:n], c == 0, c == 3, reads=[("w_pa", "w_pb", "w_pc")[i], yk], writes=[pk])
                        if i == 0:
                            tr.op("dve", lambda e: e.tensor_tensor(out=mix[:, 0:n], in0=pt[:, 0:n], in1=gt[:, m, 0:n], op=ALU.mult), reads=[pk, gk], writes=[mk])
                        else:
                            tm, tk = tmr.next()
                            tr.op("dve", lambda e: e.tensor_tensor(out=tm[:, 0:n], in0=pt[:, 0:n], in1=gt[:, i * 8 + m, 0:n], op=ALU.mult), reads=[pk, gk], writes=[tk])
                            if i == 1:
                                tr.op("pool", lambda e: e.tensor_tensor(out=mix[:, 0:n], in0=mix[:, 0:n], in1=tm[:, 0:n], op=ALU.add), reads=[mk, tk], writes=[mk])
                            else:
                                tr.op("pool", lambda e: e.tensor_tensor(out=mt[:, m, 0:n], in0=mix[:, 0:n], in1=tm[:, 0:n], op=ALU.add), reads=[mk, tk], writes=[(mtk, m)])
                for m in range(8):
                    pk = ("ps", 4 + m % 4); pt = self.ps[4 + m % 4]
                    for c in range(8):
                        self.mm(pt[:, 0:n], wo[:, c, m * 128:(m + 1) * 128], mt[:, c, 0:n], c == 0, c == 7, reads=["w_out", (mtk, c)], writes=[pk])
                    tr.op("dve", lambda e: e.scalar_tensor_tensor(out=self.xT[:, m, t0:t0 + n], in0=pt[:, 0:n], scalar=self.mod[:, l, 16 + m, s:s + 1],
                                                                 in1=self.xT[:, m, t0:t0 + n], op0=ALU.mult, op1=ALU.add),
                          reads=[pk, "mod", ("xT", ci)], writes=[("xT", ci)])
            tr.barrier()

    def phase_ffn(self, l):
        tr = self.tr
        with ExitStack() as es:
            fcw = self.sb(es, "fcw", [128, 44, 3])
            self.dma(fcw[:], self.inp["fconv_fm"][:, l, :, :], writes=["fcw"])
            with ExitStack() as es2:
                xn = self.sb(es2, "xn2", [128, 8, T], BF16)
                self.adaln(es2, l, 1, xn)
                wr = Ring(self, es2, "wup", 2, [128, 8, 2, 128], BF16)
                hr = [Ring(self, es2, f"fh{i}", 1, [128, T]) for i in range(2)]
                cr = [Ring(self, es2, f"fc{i}", 2, [128, T]) for i in range(2)]
                ar = Ring(self, es2, "fact", 2, [128, T], BF16)
                wv = self.inp["ffn_up"][l].rearrange("(c p) n -> p c n", p=128)
                pi = 0
                for cc in range(22):
                    w, wk = wr.next()
                    self.dma(w[:, :, 0, :], wv[:, :, cc * 128:(cc + 1) * 128], writes=[wk], q="gq")
                    self.dma(w[:, :, 1, :], wv[:, :, 2816 + cc * 128:2816 + (cc + 1) * 128], writes=[wk], q="gq")
                    cs_ = []
                    for i in range(2):
                        hb, hk = hr[i].next()
                        for ci, (t0, n) in enumerate(TCH):
                            pk = ("ps", pi % 4); pt = self.ps[pi % 4]; pi += 1
                            for c in range(8):
                                self.mm(pt[:, 0:n], w[:, c, i, :], xn[:, c, t0:t0 + n], c == 0, c == 7, reads=[wk, ("xn", ci)], writes=[pk])
                            self.copy(self.evac_eng(), hb[:, t0:t0 + n], pt[:, 0:n], reads=[pk], writes=[hk])
                        cb, ck = cr[i].next()
                        j = cc + 22 * i
                        eng = "dve" if i == 0 else "pool"
                        tr.op(eng, lambda e: e.tensor_scalar(out=cb[:], in0=hb[:], scalar1=fcw[:, j, 1:2], scalar2=None, op0=ALU.mult), reads=[hk, "fcw"], writes=[ck])
                        for (o0, o1, i0, i1, k) in ((1, CTX, 0, CTX - 1, 0), (CTX + 1, T, CTX, T - 1, 0), (0, CTX - 1, 1, CTX, 2), (CTX, T - 1, CTX + 1, T, 2)):
                            tr.op("dve", lambda e: e.scalar_tensor_tensor(out=cb[:, o0:o1], in0=hb[:, i0:i1], scalar=fcw[:, j, k:k + 1], in1=cb[:, o0:o1],
                                                                       op0=ALU.mult, op1=ALU.add), reads=[hk, ck, "fcw"], writes=[ck])
                        cs_.append((cb, ck))
                    (ca, cak), (cb, cbk) = cs_
                    tr.op("act", lambda e: e.activation(out=ca[:], in_=ca[:], func=AF.Silu), reads=[cak], writes=[cak])
                    a, ak = ar.next()
                    tr.op("dve", lambda e: e.tensor_tensor(out=a[:], in0=ca[:], in1=cb[:], op=ALU.mult), reads=[cak, cbk], writes=[ak])
                    self.dma(self.scr["D_act"][cc * 128:(cc + 1) * 128, :], a[:], reads=[ak], writes=["D_act"])
                tr.barrier()
            with ExitStack() as es2:
                wd = self.sb(es2, "wdn", [128, 22, D], BF16)
                self.dma(wd[:, 0:11, :], self.inp["ffn_down"][l].rearrange("(c p) n -> p c n", p=128)[:, 0:11, :], writes=["wdn"], q="gq")
                self.dma(wd[:, 11:22, :], self.inp["ffn_down"][l].rearrange("(c p) n -> p c n", p=128)[:, 11:22, :], writes=["wdn"], q="gq")
                ar = Ring(self, es2, "fa", 2, [128, 22, 512], BF16)
                av = self.scr["D_act"].rearrange("(c p) t -> p c t", p=128)
                for ci, (t0, n) in enumerate(TCH):
                    s = 1 if ci == 0 else 0
                    a, ak = ar.next()
                    self.dma(a[:, :, 0:n], av[:, :, t0:t0 + n], reads=["D_act"], writes=[ak])
                    for m in range(8):
                        pk = ("ps", m % 4); pt = self.ps[m % 4]
                        for c in range(22):
                            self.mm(pt[:, 0:n], wd[:, c, m * 128:(m + 1) * 128], a[:, c, 0:n], c == 0, c == 21, reads=["wdn", ak], writes=[pk])
                        tr.op("dve", lambda e: e.scalar_tensor_tensor(out=self.xT[:, m, t0:t0 + n], in0=pt[:, 0:n], scalar=self.mod[:, l, 40 + m, s:s + 1],
                                                                     in1=self.xT[:, m, t0:t0 + n], op0=ALU.mult, op1=ALU.add),
                              reads=[pk, "mod", ("xT", ci)], writes=[("xT", ci)])
                tr.barrier()

    def phase_final(self):
        tr = self.tr
        with ExitStack() as es:
            nf = self.sb(es, "nf", [128, D])
            self.dma(nf[:], self.inp["normf_rep"], writes=["nf"])
            xr = Ring(self, es, "fx", 2, [128, D]); sqr = Ring(self, es, "fsq", 2, [128, D]); ssr = Ring(self, es, "fss", 2, [128, 1])
            for t in range(2, NT):
                x, xk = xr.next()
                for half in range(2):
                    pk = ("ps", (2 * t + half) % 8); pt = self.ps[(2 * t + half) % 8]
                    for j in range(4):
                        c = half * 4 + j
                        tr.op("pe", lambda e: e.transpose(pt[:, j * 128:(j + 1) * 128], self.xT[:, c, t * 128:(t + 1) * 128], self.ident[:]),
                              reads=["ident"], writes=[pk])
                    self.copy(self.evac_eng(), x[:, half * 512:(half + 1) * 512], pt[:, :], reads=[pk], writes=[xk])
                sq, sk = sqr.next(); ss, ssk = ssr.next()
                tr.op("pool", lambda e: e.tensor_tensor(out=sq[:], in0=x[:], in1=x[:], op=ALU.mult), reads=[xk], writes=[sk])
                tr.op("dve", lambda e: e.tensor_reduce(out=ss[:], in_=sq[:], axis=AX.X, op=ALU.add), reads=[sk], writes=[ssk])
                self.rstd_from_ss(ss[:], ssk, ss[:], ssk, 1.0 / D)
                tr.op("dve", lambda e: e.scalar_tensor_tensor(out=sq[:], in0=x[:], scalar=ss[:, 0:1], in1=nf[:], op0=ALU.mult, op1=ALU.mult),
                      reads=[xk, ssk, "nf"], writes=[sk])
                self.dma(self.out[(t - 2) * 128:(t - 1) * 128, :], sq[:], reads=[sk])


def _rope_tables(d):
    da = d // 2
    half = da // 2
    inv = (10000.0 ** (-np.arange(half, dtype=np.float32) / half)).astype(np.float32)
    t = np.arange(SEQ)
    rows = (t // 64).astype(np.float32)
    cols = (t % 64).astype(np.float32)
    cos = np.ones((d, T), np.float32)
    sin = np.zeros((d, T), np.float32)
    for i in range(d):
        pos = rows if i < da else cols
        ii = i % da
        ang = pos * inv[ii % half]
        cos[i, CTX:] = np.cos(ang)
        sin[i, CTX:] = np.sin(ang) * (-1.0 if ii < half else 1.0)
    return np.stack([cos, sin])


def _partner(d):
    da = d // 2
    half = da // 2
    idx = np.arange(d)
    ii = idx % da
    return np.where(ii < half, idx + half, idx - half)


def _constants():
    j = np.arange(128)[:, None]
    i = np.arange(128)[None, :]
    tri = np.stack([(j <= i), (j >= i)]).astype(np.float32)
    gmask = np.stack([np.where(i < j, 0.0, BIG),
                      np.where(j <= i, 0.0, -BIG),
                      np.where(i > j, 0.0, BIG),
                      np.where(j >= i, 0.0, -BIG)]).astype(np.float32)
    m_prev = (j >= i).astype(np.float32)
    m_next = (j <= i).astype(np.float32)
    swamask = np.stack([np.tile(m_prev, (1, 4)), np.tile(m_next, (1, 4))]).astype(np.float32)
    b32 = (j // 32 == i // 32)
    b64 = (j // 64 == i // 64)
    bmask = np.stack([b32, b64 & ~b32, ~b64]).astype(np.float32)
    selh = np.zeros((8, 8, 128), np.float32)
    for h in range(8):
        selh[h, h, :] = 1.0
    return dict(ident=np.eye(128, dtype=np.float32), tri=tri, gmask=gmask, bmask=bmask, swamask=swamask, selh=selh,
                ropeM=_rope_tables(32), ropeS=_rope_tables(64))


def _fm(v, p=128):
    v = np.asarray(v, np.float32)
    lead = v.shape[:-1]
    n = v.shape[-1] // p
    return np.ascontiguousarray(np.moveaxis(v.reshape(lead + (n, p)), -1, 0))


def prepare_shared(inp, L=DEPTH):
    f = lambda k: np.asarray(inp[k], np.float32)[:L] if k not in ("norm_f",) else np.asarray(inp[k], np.float32)
    w_in = f("w_in")
    p32 = _partner(32)
    p64 = _partner(64)
    sqp = (np.arange(8)[:, None] * 64 + p64[None, :]).reshape(-1)
    skp = (np.arange(2)[:, None] * 64 + p64[None, :]).reshape(-1)
    w_fm = np.concatenate([w_in[:, :, 0:640], w_in[:, :, 640:672], w_in[:, :, 640:672][:, :, p32],
                           w_in[:, :, 672:1184], w_in[:, :, 672:1184][:, :, sqp],
                           w_in[:, :, 1184:1312], w_in[:, :, 1184:1312][:, :, skp],
                           w_in[:, :, 1440:2976], np.zeros((L, D, 64), np.float32), w_in[:, :, 3520:6592]], axis=2)
    assert w_fm.shape[2] == NFM
    w_tm = np.concatenate([w_in[:, :, 1312:1440], w_in[:, :, 2976:3488], w_in[:, :, 3488:3520]], axis=2)
    wuq = f("w_uq").reshape(L, 384, 8, 96)
    wukv = f("w_ukv").reshape(L, 256, 8, 128)
    sh = dict(
        w_mod=f("w_mod"), b_mod_fm=_fm(f("b_mod")),
        norm1_fm=_fm(f("norm1")), norm2_fm=_fm(f("norm2")), normf_rep=np.ascontiguousarray(np.broadcast_to(f("norm_f"), (128, D))),
        w_fm=np.ascontiguousarray(w_fm), w_tm=np.ascontiguousarray(w_tm),
        qnorm_fm=_fm(f("mla_q_norm")), kvnorm_fm=_fm(f("mla_kv_norm")),
        wuq_n=np.ascontiguousarray(wuq[..., 0:64].reshape(L, 384, 512)),
        wuq_r=np.ascontiguousarray(wuq[..., 64:96].reshape(L, 384, 256)),
        wuq_rp=np.ascontiguousarray(wuq[..., 64:96][..., p32].reshape(L, 384, 256)),
        wukv_k=np.ascontiguousarray(wukv[..., 0:64].reshape(L, 256, 512)),
        wukv_v=np.ascontiguousarray(wukv[..., 64:128].reshape(L, 256, 512)),
        sink_rep=np.ascontiguousarray(np.broadcast_to(f("swa_sink"), (64, L, 8))),
        gconv_fm=np.ascontiguousarray(f("gdn_conv").reshape(L, 3, 24, 64).transpose(3, 0, 2, 1)),
        alog_rep=np.ascontiguousarray(np.broadcast_to(f("gdn_a_log").reshape(L, 16), (128, L, 16))),
        dtb_rep=np.ascontiguousarray(np.broadcast_to(f("gdn_dt_bias").reshape(L, 16), (128, L, 16))),
        gnorm_rep=np.ascontiguousarray(np.broadcast_to(f("gdn_norm"), (128, L, 64))),
        w_pa=f("w_branch_a"), w_pb=f("w_branch_b"), w_pc=f("w_branch_c"), w_out=f("w_out"),
        ffn_up=f("ffn_up"), fconv_fm=np.ascontiguousarray(f("ffn_conv").reshape(L, 3, 44, 128).transpose(3, 0, 2, 1)),
        ffn_down=f("ffn_down"),
    )
    sh.update(_constants())
    return sh


def per_core(inp, b):
    x = np.asarray(inp["x"], np.float32)[b]
    ctx = np.asarray(inp["ctx"], np.float32)[b]
    c = np.asarray(inp["c"], np.float32)[b]
    cc = np.asarray(inp["c_ctx"], np.float32)
    return dict(xin=np.ascontiguousarray(np.concatenate([ctx, x], 0)),
                c_fm=np.ascontiguousarray(np.stack([_fm(c), _fm(cc)], axis=-1)))


_CACHE = {}


def kernel(**inputs):
    if "nc" not in _CACHE:
        _CACHE["nc"] = Builder().build()
    nc = _CACHE["nc"]
    shared = prepare_shared(inputs)
    in_maps = []
    for b in range(8):
        m = dict(shared)
        m.update(per_core(inputs, b))
        in_maps.append(m)
    res = run_bass_kernel_spmd(nc, in_maps, core_ids=list(range(8)))
    return np.stack([np.asarray(r["out"], np.float32) for r in res.results], axis=0)
```

```python
import numpy as np
from contextlib import ExitStack
import concourse.bass as bass
import concourse.mybir as mybir
from concourse.bass_utils import run_bass_kernel_spmd

F32 = mybir.dt.float32
BF16 = mybir.dt.bfloat16
AF = mybir.ActivationFunctionType
ALU = mybir.AluOpType
AX = mybir.AxisListType

D = 1024
SEQ = 2048
CTX = 256
T = SEQ + CTX
NT = T // 128
DEPTH = 4
EPS = 1e-6
TCH = [(0, 256), (256, 512), (768, 512), (1280, 512), (1792, 512)]
BIG = 1.0e5

CQ, CKV, KR, KRP, SQ, SQP, SK, SKP, GQ, PAD, GATE, NFM = 0, 384, 640, 672, 704, 1216, 1728, 1856, 1984, 3520, 3584, 6656
SV, ZC, AB, NTM = 0, 128, 640, 672


def tile_chunk(t):
    return 0 if t < 2 else 1 + (t - 2) // 4


class Trk:
    ISSUER = {"pe": "pe", "act": "act", "dve": "dve", "pool": "pool", "sp": "sp", "gq": "pool"}
    NSLOT = {"sp": 24, "gq": 8}

    def __init__(self, nc, es):
        self.nc = nc
        self.eng = {"pe": nc.tensor, "act": nc.scalar, "dve": nc.vector, "pool": nc.gpsimd, "sp": nc.sync}
        self.sem = {e: es.enter_context(nc.semaphore("sem_" + e)) for e in ("pe", "act", "dve", "pool")}
        for q, k in self.NSLOT.items():
            for i in range(k):
                self.sem[(q, i)] = es.enter_context(nc.semaphore(f"sem_{q}{i}"))
        self.cnt = {e: 0 for e in self.sem}
        self.nq = {q: 0 for q in self.NSLOT}
        self.lastw = {}
        self.readers = {}
        self.waited = {}
        self.n_wait = 0
        self.n_op = 0

    def _need(self, issuer, eng, dep):
        de, ds = dep
        if de == "pe" and eng == "pe":
            return
        k = (issuer, de)
        if self.waited.get(k, 0) >= ds:
            return
        self.waited[k] = ds
        self.eng[issuer].wait_ge(self.sem[de], ds)
        self.n_wait += 1

    def op(self, eng, emit, reads=(), writes=()):
        issuer = self.ISSUER[eng]
        for key in reads:
            w = self.lastw.get(key)
            if w is not None:
                self._need(issuer, eng, w)
            if isinstance(key, tuple) and key[0] == "ps":
                for rk, tok in self.readers.get(key, {}).items():
                    if rk != eng:
                        self._need(issuer, eng, tok)
        for key in writes:
            w = self.lastw.get(key)
            if w is not None:
                self._need(issuer, eng, w)
            for tok in self.readers.get(key, {}).values():
                self._need(issuer, eng, tok)
        if eng in self.NSLOT:
            sk = (eng, self.nq[eng] % self.NSLOT[eng])
            self.nq[eng] += 1
            if self.cnt[sk] > 0:
                self._need(issuer, eng, (sk, self.cnt[sk]))
            inc = 16
        else:
            sk = eng
            inc = 1
        inst = emit(self.eng[issuer])
        self.cnt[sk] += inc
        inst.then_inc(self.sem[sk], inc)
        tok = (sk, self.cnt[sk])
        for key in reads:
            self.readers.setdefault(key, {})[sk] = tok
        for key in writes:
            self.lastw[key] = tok
            self.readers[key] = {}
        self.n_op += 1
        return inst

    def barrier(self):
        for issuer in ("pe", "act", "dve", "pool", "sp"):
            for e, c in self.cnt.items():
                if c > 0:
                    self._need(issuer, "x", (e, c))
        self.lastw.clear()
        self.readers.clear()

    def finish(self, eng="sp"):
        for e, c in self.cnt.items():
            if c > 0:
                self._need(eng, "x", (e, c))


class Ring:
    def __init__(self, b, es, name, n, shape, dt=F32):
        self.tiles = [b.sb(es, f"{name}{i}", shape, dt) for i in range(n)]
        self.name = name
        self.n = n
        self.i = 0

    def next(self):
        j = self.i % self.n
        self.i += 1
        return self.tiles[j], (self.name, j)


class Builder:
    def __init__(self, n_layers=DEPTH, stop=None, taps=()):
        self.n_layers = n_layers
        self.stop = stop
        self.taps = set(taps)
        self.nc = bass.Bass("TRN2", target_bir_lowering=False)
        self.inp = {}
        self.scr = {}
        self.ev = 0

    def din(self, name, shape, dt=F32):
        self.inp[name] = self.nc.dram_tensor(name, list(shape), dt, kind="ExternalInput").ap()
        return self.inp[name]

    def dscr(self, name, shape, dt=F32, force=False):
        kind = "ExternalOutput" if (name in self.taps or force) else "Internal"
        self.scr[name] = self.nc.dram_tensor(name, list(shape), dt, kind=kind).ap()
        return self.scr[name]

    def declare(self):
        L = self.n_layers
        d = self.din
        d("xin", [T, D]); d("c_fm", [128, 8, 2])
        d("w_mod", [L, D, 6 * D]); d("b_mod_fm", [128, L, 48])
        d("norm1_fm", [128, L, 8]); d("norm2_fm", [128, L, 8]); d("normf_rep", [128, D])
        d("w_fm", [L, D, NFM]); d("w_tm", [L, D, NTM])
        d("qnorm_fm", [128, L, 3]); d("kvnorm_fm", [128, L, 2])
        d("wuq_n", [L, 384, 512]); d("wuq_r", [L, 384, 256]); d("wuq_rp", [L, 384, 256])
        d("wukv_k", [L, 256, 512]); d("wukv_v", [L, 256, 512])
        d("sink_rep", [64, L, 8])
        d("gconv_fm", [64, L, 24, 3]); d("alog_rep", [128, L, 16]); d("dtb_rep", [128, L, 16]); d("gnorm_rep", [128, L, 64])
        d("w_pa", [L, 512, D]); d("w_pb", [L, 512, D]); d("w_pc", [L, 512, D]); d("w_out", [L, D, D])
        d("ffn_up", [L, D, 5632]); d("fconv_fm", [128, L, 44, 3]); d("ffn_down", [L, 2816, D])
        d("ident", [128, 128]); d("ropeM", [2, 32, T]); d("ropeS", [2, 64, T])
        d("tri", [2, 128, 128]); d("gmask", [4, 128, 128]); d("bmask", [3, 128, 128]); d("swamask", [2, 128, 512]); d("selh", [8, 8, 128])
        self.out = self.nc.dram_tensor("out", [SEQ, D], F32, kind="ExternalOutput").ap()
        s = self.dscr
        s("D_fm", [NFM, T]); s("D_tm", [T, NTM]); s("D_gate", [3072, T], BF16)
        s("D_ya", [512, T], BF16); s("D_yb", [512, T], BF16); s("D_yc", [512, T], BF16)
        s("D_gq", [512, T]); s("D_gk", [512, T]); s("D_ktm", [8, T, 64]); s("D_vtm", [8, T, 64])
        s("D_act", [2816, T], BF16)
        if "D_x" in self.taps:
            s("D_x", [D, T], F32, True); s("D_xn", [D, T], BF16, True); s("D_mod", [128, L * 48 * 2], F32, True); s("D_oc", [T, 512], F32, True)

    def sb(self, es, name, shape, dt=F32):
        self.uid = getattr(self, "uid", 0) + 1
        return es.enter_context(self.nc.sbuf_tensor(f"s{self.uid}_{name}", list(shape), dt))

    def evac_eng(self):
        self.ev += 1
        return "act" if self.ev % 2 else "dve"

    def copy(self, eng, out, in_, reads, writes):
        if eng == "act":
            return self.tr.op("act", lambda e: e.copy(out=out, in_=in_), reads=reads, writes=writes)
        return self.tr.op(eng, lambda e: e.tensor_copy(out=out, in_=in_), reads=reads, writes=writes)

    def dma(self, out, in_, reads=(), writes=(), q="sp"):
        return self.tr.op(q, lambda e: e.dma_start(out=out, in_=in_), reads=reads, writes=writes)

    def mm(self, out, lhsT, rhs, start, stop, reads, writes):
        return self.tr.op("pe", lambda e: e.matmul(out, lhsT, rhs, start=start, stop=stop), reads=reads, writes=writes)

    def rstd_from_ss(self, ss_ap, ss_key, r_ap, r_key, scale):
        tr = self.tr
        tr.op("act", lambda e: e.activation(out=r_ap, in_=ss_ap, func=AF.Sqrt, bias=self.cst[0:r_ap.shape[0], 0:1], scale=scale),
              reads=[ss_key, "cst"], writes=[r_key])
        tr.op("dve", lambda e: e.reciprocal(out=r_ap, in_=r_ap), reads=[r_key], writes=[r_key])

    def build(self):
        nc = self.nc
        self.declare()
        with ExitStack() as es:
            self.tr = tr = Trk(nc, es)
            self.ps = [es.enter_context(nc.psum_tensor(f"ps{i}", [128, 512], F32)) for i in range(8)]
            self.xT = self.sb(es, "xT", [128, 8, T])
            self.ident = self.sb(es, "ident", [128, 128])
            self.ones32 = self.sb(es, "ones32", [128, 128])
            self.ones16 = self.sb(es, "ones16", [128, 64], BF16)
            self.cst = self.sb(es, "cst", [128, 4])
            self.mod = self.sb(es, "mod", [128, self.n_layers, 48, 2])
            self.n1 = self.sb(es, "n1", [128, self.n_layers, 8])
            self.n2 = self.sb(es, "n2", [128, self.n_layers, 8])
            self.gs = self.sb(es, "gs", [128, 2, 8, 2])
            self.dma(self.ident[:], self.inp["ident"], writes=["ident"])
            self.dma(self.n1[:], self.inp["norm1_fm"], writes=["n1"])
            self.dma(self.n2[:], self.inp["norm2_fm"], writes=["n2"])
            tr.op("dve", lambda e: e.memset(self.ones32[:], 1.0), writes=["ones32"])
            tr.op("dve", lambda e: e.memset(self.ones16[:], 1.0), writes=["ones16"])
            tr.op("dve", lambda e: e.memset(self.cst[:, 0:1], EPS), writes=["cst"])
            tr.op("dve", lambda e: e.memset(self.cst[:, 1:2], 1.0), writes=["cst"])
            tr.op("dve", lambda e: e.memset(self.cst[:, 2:3], 0.0), writes=["cst"])
            self.phase_load_x()
            self.phase_mod()
            done = self.stop == "mod"
            for l in range(self.n_layers):
                if done:
                    break
                for name, fn in (("norm1", lambda: self.phase_adaln_proj(l)), ("mla", lambda: self.phase_mla(l)),
                                 ("swa", lambda: self.phase_swa(l)), ("gdnprep", lambda: self.phase_gdn_prep(l)),
                                 ("gdn", lambda: self.phase_gdn(l)), ("merge", lambda: self.phase_merge(l)),
                                 ("ffn", lambda: self.phase_ffn(l))):
                    fn()
                    if self.stop == f"{name}{l}" or (self.stop or "").startswith(name + "_"):
                        done = True
                        break
            if "D_x" in self.taps:
                tr.barrier()
                for ci, (t0, n) in enumerate(TCH):
                    self.dma(self.scr["D_x"].rearrange("(c p) t -> p c t", p=128)[:, :, t0:t0 + n], self.xT[:, :, t0:t0 + n])
                self.dma(self.scr["D_mod"], self.mod[:].rearrange("p l j s -> p (l j s)"))
            if not done:
                self.phase_final()
            tr.barrier()
            tr.finish("sp")
        return nc

    def phase_load_x(self):
        tr = self.tr
        with ExitStack() as es:
            ring = Ring(self, es, "xl", 2, [128, D])
            for t in range(NT):
                xt, xk = ring.next()
                self.dma(xt[:], self.inp["xin"][t * 128:(t + 1) * 128, :], writes=[xk])
                for half in range(2):
                    pk = ("ps", (2 * t + half) % 8)
                    pt = self.ps[(2 * t + half) % 8]
                    for j in range(4):
                        c = half * 4 + j
                        tr.op("pe", lambda e: e.transpose(pt[:, j * 128:(j + 1) * 128], xt[:, c * 128:(c + 1) * 128], self.ident[:]),
                              reads=[xk, "ident"], writes=[pk])
                    self.copy(self.evac_eng(), self.xT[:, half * 4:half * 4 + 4, t * 128:(t + 1) * 128],
                              pt[:].rearrange("p (c t) -> p c t", c=4), reads=[pk], writes=[("xT", tile_chunk(t))])
            tr.barrier()

    def phase_mod(self):
        tr = self.tr
        with ExitStack() as es:
            cs = self.sb(es, "cs", [128, 8, 2])
            bm = self.sb(es, "bm", [128, self.n_layers, 48])
            ring = Ring(self, es, "wm", 2, [128, 8, 512])
            self.dma(cs[:], self.inp["c_fm"], writes=["cs"])
            self.dma(bm[:], self.inp["b_mod_fm"], writes=["bm"])
            tr.op("act", lambda e: e.activation(out=cs[:], in_=cs[:], func=AF.Silu), reads=["cs"], writes=["cs"])
            for l in range(self.n_layers):
                wv = self.inp["w_mod"][l].rearrange("(c p) n -> p c n", p=128)
                pk = ("ps", l % 2)
                pt = self.ps[l % 2][:, 0:96].rearrange("p (j s) -> p j s", s=2)
                for g in range(12):
                    w, wk = ring.next()
                    self.dma(w[:], wv[:, :, g * 512:(g + 1) * 512], writes=[wk])
                    for m in range(4):
                        j = g * 4 + m
                        for c in range(8):
                            self.mm(pt[:, j, :], w[:, c, m * 128:(m + 1) * 128], cs[:, c, :], c == 0, c == 7,
                                    reads=[wk, "cs"], writes=[pk])
                for s in range(2):
                    tr.op("dve", lambda e: e.tensor_tensor(out=self.mod[:, l, :, s], in0=pt[:, :, s], in1=bm[:, l, :], op=ALU.add),
                          reads=[pk, "bm"], writes=["mod"])
            tr.barrier()

    def make_gs(self, l, which):
        nrm = self.n1 if which == 0 else self.n2
        sc0 = 8 + which * 24
        for s in range(2):
            self.tr.op("dve", lambda e: e.scalar_tensor_tensor(out=self.gs[:, which, :, s], in0=self.mod[:, l, sc0:sc0 + 8, s], scalar=1.0,
                                                             in1=nrm[:, l, :], op0=ALU.add, op1=ALU.mult),
                       reads=["mod", "n1", "n2"], writes=["gs"])

    def adaln(self, es, l, which, xn):
        tr = self.tr
        self.make_gs(l, which)
        sh0 = which * 24
        sqr = Ring(self, es, f"sq{which}", 2, [128, 512])
        rr = Ring(self, es, f"rr{which}", 2, [128, 512])
        tmpr = Ring(self, es, f"tm{which}", 2, [128, 512])
        for ci, (t0, n) in enumerate(TCH):
            s = 1 if ci == 0 else 0
            pk = ("ps", ci % 2)
            pt = self.ps[ci % 2]
            for c in range(8):
                sq, sk = sqr.next()
                tr.op("act", lambda e: e.activation(out=sq[:, 0:n], in_=self.xT[:, c, t0:t0 + n], func=AF.Square),
                      reads=[("xT", ci)], writes=[sk])
                self.mm(pt[:, 0:n], self.ones32[:], sq[:, 0:n], c == 0, c == 7, reads=[sk, "ones32"], writes=[pk])
            r, rk = rr.next()
            self.rstd_from_ss(pt[:, 0:n], pk, r[:, 0:n], rk, 1.0 / D)
            for c in range(8):
                tm, tk = tmpr.next()
                tr.op("dve", lambda e: e.scalar_tensor_tensor(out=tm[:, 0:n], in0=self.xT[:, c, t0:t0 + n], scalar=self.gs[:, which, c, s:s + 1],
                                                             in1=r[:, 0:n], op0=ALU.mult, op1=ALU.mult),
                      reads=[("xT", ci), "gs", rk], writes=[tk])
                tr.op("act", lambda e: e.activation(out=xn[:, c, t0:t0 + n], in_=tm[:, 0:n], func=AF.Identity,
                                                    bias=self.mod[:, l, sh0 + c, s:s + 1], scale=1.0),
                      reads=[tk, "mod"], writes=[("xn", ci)])

    def phase_adaln_proj(self, l):
        tr = self.tr
        with ExitStack() as es:
            xn = self.sb(es, "xn", [128, 8, T], BF16)
            self.adaln(es, l, 0, xn)
            if "D_x" in self.taps:
                for ci, (t0, n) in enumerate(TCH):
                    self.dma(self.scr["D_xn"].rearrange("(c p) t -> p c t", p=128)[:, :, t0:t0 + n], xn[:, :, t0:t0 + n], reads=[("xn", ci)])
            wr = Ring(self, es, "wfm", 2, [128, 8, 512], BF16)
            st = Ring(self, es, "stg", 3, [128, 512])
            st16 = Ring(self, es, "stg16", 2, [128, 512], BF16)
            wv = self.inp["w_fm"][l].rearrange("(c p) n -> p c n", p=128)
            pi = 0
            for g in range(NFM // 512):
                w, wk = wr.next()
                self.dma(w[:], wv[:, :, g * 512:(g + 1) * 512], writes=[wk], q="gq")
                col = g * 512
                while col < (g + 1) * 512:
                    if col < KR:
                        mc = 128
                    elif col < SQ:
                        mc = 32
                    elif col < PAD:
                        mc = 64
                    elif col < GATE:
                        col += 64
                        continue
                    else:
                        mc = 128
                    lo = col - g * 512
                    for ci, (t0, n) in enumerate(TCH):
                        pk = ("ps", pi % 4)
                        pt = self.ps[pi % 4]
                        pi += 1
                        for c in range(8):
                            self.mm(pt[0:mc, 0:n], w[:, c, lo:lo + mc], xn[:, c, t0:t0 + n], c == 0, c == 7,
                                    reads=[wk, ("xn", ci)], writes=[pk])
                        if col >= GATE:
                            s16, sk = st16.next()
                            tr.op("act", lambda e: e.activation(out=s16[:, 0:n], in_=pt[:, 0:n], func=AF.Sigmoid), reads=[pk], writes=[sk])
                            self.dma(self.scr["D_gate"][col - GATE:col - GATE + 128, t0:t0 + n], s16[:, 0:n], reads=[sk], writes=[("D_gate", ci)])
                        else:
                            s32, sk = st.next()
                            self.copy(self.evac_eng(), s32[0:mc, 0:n], pt[0:mc, 0:n], reads=[pk], writes=[sk])
                            self.dma(self.scr["D_fm"][col:col + mc, t0:t0 + n], s32[0:mc, 0:n], reads=[sk], writes=["D_fm"])
                    col += mc
            wt = self.sb(es, "wtm", [128, 8, NTM], BF16)
            self.dma(wt[:], self.inp["w_tm"][l].rearrange("(c p) n -> p c n", p=128), writes=["wtm"], q="gq")
            stm = Ring(self, es, "stm", 2, [128, NTM])
            for t in range(NT):
                ci = tile_chunk(t)
                pa, pb = self.ps[4 + (t % 2) * 2], self.ps[5 + (t % 2) * 2]
                ka, kb = ("ps", 4 + (t % 2) * 2), ("ps", 5 + (t % 2) * 2)
                for c in range(8):
                    self.mm(pa[:, 0:512], xn[:, c, t * 128:(t + 1) * 128], wt[:, c, 0:512], c == 0, c == 7, reads=["wtm", ("xn", ci)], writes=[ka])
                for c in range(8):
                    self.mm(pb[:, 0:160], xn[:, c, t * 128:(t + 1) * 128], wt[:, c, 512:672], c == 0, c == 7, reads=["wtm", ("xn", ci)], writes=[kb])
                s, sk = stm.next()
                self.copy("act", s[:, 0:512], pa[:, 0:512], reads=[ka], writes=[sk])
                self.copy("dve", s[:, 512:672], pb[:, 0:160], reads=[kb], writes=[sk])
                self.dma(self.scr["D_tm"][t * 128:(t + 1) * 128, :], s[:], reads=[sk], writes=["D_tm"])
            tr.barrier()

    def rms_fm_from_dram(self, es, row0, nchunk, gain, dst, tag):
        tr = self.tr
        ld = Ring(self, es, f"ld{tag}", 2, [128, nchunk, 512])
        sqr = Ring(self, es, f"sqn{tag}", 2, [128, 512])
        rr = Ring(self, es, f"rn{tag}", 2, [128, 512])
        src = self.scr["D_fm"][row0:row0 + nchunk * 128, :].rearrange("(c p) t -> p c t", p=128)
        for ci, (t0, n) in enumerate(TCH):
            x, xk = ld.next()
            self.dma(x[:, :, 0:n], src[:, :, t0:t0 + n], reads=["D_fm"], writes=[xk])
            pk = ("ps", 6 + ci % 2)
            pt = self.ps[6 + ci % 2]
            for c in range(nchunk):
                sq, sk = sqr.next()
                tr.op("act", lambda e: e.activation(out=sq[:, 0:n], in_=x[:, c, 0:n], func=AF.Square), reads=[xk], writes=[sk])
                self.mm(pt[:, 0:n], self.ones32[:], sq[:, 0:n], c == 0, c == nchunk - 1, reads=[sk, "ones32"], writes=[pk])
            r, rk = rr.next()
            self.rstd_from_ss(pt[:, 0:n], pk, r[:, 0:n], rk, 1.0 / (nchunk * 128))
            for c in range(nchunk):
                tr.op("dve", lambda e: e.scalar_tensor_tensor(out=dst[:, c, t0:t0 + n], in0=x[:, c, 0:n], scalar=gain[:, c:c + 1],
                                                             in1=r[:, 0:n], op0=ALU.mult, op1=ALU.mult),
                      reads=[xk, rk, "gain" + tag], writes=[(tag, ci)])

    def rope_from(self, src_ap, src_key, srcp_ap, srcp_key, cos_ap, sin_ap, dst_ap, dst_key, tmpa, tmpb, ka, kb):
        tr = self.tr
        tr.op("dve", lambda e: e.tensor_tensor(out=tmpa, in0=src_ap, in1=cos_ap, op=ALU.mult), reads=[src_key, "rope"], writes=[ka])
        tr.op("pool" if srcp_key[0] != "ps" else "dve", lambda e: e.tensor_tensor(out=tmpb, in0=srcp_ap, in1=sin_ap, op=ALU.mult),
              reads=[srcp_key, "rope"], writes=[kb])
        tr.op("dve", lambda e: e.tensor_tensor(out=dst_ap, in0=tmpa, in1=tmpb, op=ALU.add), reads=[ka, kb], writes=[dst_key])

    def phase_mla(self, l):
        tr = self.tr
        scale = 96.0 ** -0.5
        with ExitStack() as es:
            qg = self.sb(es, "qg", [128, 3]); kg = self.sb(es, "kg", [128, 2])
            self.dma(qg[:], self.inp["qnorm_fm"][:, l, :], writes=["gaincqn"])
            self.dma(kg[:], self.inp["kvnorm_fm"][:, l, :], writes=["gainckvn"])
            cqn = self.sb(es, "cqn", [128, 3, T], BF16)
            ckvn = self.sb(es, "ckvn", [128, 2, T], BF16)
            cosM = self.sb(es, "cosM", [32, T]); sinM = self.sb(es, "sinM", [32, T])
            self.dma(cosM[:], self.inp["ropeM"][0], writes=["rope"])
            self.dma(sinM[:], self.inp["ropeM"][1], writes=["rope"])
            KrT = self.sb(es, "KrT", [32, T], BF16)
            Vt = self.sb(es, "Vt", [128, NT, 512], BF16)
            wqn = self.sb(es, "wqn", [128, 3, 512], BF16); wqr = self.sb(es, "wqr", [128, 3, 256], BF16)
            wqp = self.sb(es, "wqp", [128, 3, 256], BF16)
            wkn = self.sb(es, "wkn", [128, 2, 512], BF16); wkv = self.sb(es, "wkv", [128, 2, 512], BF16)
            for wt_, nm in ((wqn, "wuq_n"), (wqr, "wuq_r"), (wqp, "wuq_rp"), (wkn, "wukv_k"), (wkv, "wukv_v")):
                self.dma(wt_[:], self.inp[nm][l].rearrange("(c p) n -> p c n", p=128), writes=[nm], q="gq")
            with ExitStack() as es2:
                self.rms_fm_from_dram(es2, CQ, 3, qg, cqn, "cqn")
                self.rms_fm_from_dram(es2, CKV, 2, kg, ckvn, "ckvn")
                kl = Ring(self, es2, "krl", 2, [32, 2, 512])
                ta = Ring(self, es2, "kta", 2, [32, 512]); tb = Ring(self, es2, "ktb", 2, [32, 512])
                for ci, (t0, n) in enumerate(TCH):
                    k2, kk = kl.next()
                    self.dma(k2[:, :, 0:n], self.scr["D_fm"][KR:KR + 64, :].rearrange("(a p) t -> p a t", p=32)[:, :, t0:t0 + n],
                             reads=["D_fm"], writes=[kk])
                    a_, ak = ta.next(); b_, bk = tb.next()
                    self.rope_from(k2[:, 0, 0:n], kk, k2[:, 1, 0:n], kk, cosM[:, t0:t0 + n], sinM[:, t0:t0 + n],
                                   KrT[:, t0:t0 + n], ("KrT", ci), a_[:, 0:n], b_[:, 0:n], ak, bk)
                for t in range(NT):
                    ci = tile_chunk(t)
                    pk = ("ps", 4 + t % 2); pt = self.ps[4 + t % 2]
                    for c in range(2):
                        self.mm(pt[:, :], ckvn[:, c, t * 128:(t + 1) * 128], wkv[:, c, :], c == 0, c == 1, reads=[("ckvn", ci), "wukv_v"], writes=[pk])
                    self.copy(self.evac_eng(), Vt[:, t, :], pt[:, :], reads=[pk], writes=[("Vt", t)])
                tr.barrier()
            if self.stop == "mla_prep":
                return
            with ExitStack() as es2:
                qnr = Ring(self, es2, "QnT", 2, [64, T], BF16); qrr = Ring(self, es2, "QrT", 2, [32, T], BF16)
                knr = Ring(self, es2, "KnT", 2, [64, T], BF16)
                ta = Ring(self, es2, "qta", 2, [32, 512]); tb = Ring(self, es2, "qtb", 2, [32, 512])
                pr = Ring(self, es2, "Pexp", 3, [128, 512], BF16)
                rdr = Ring(self, es2, "rden", 2, [64, 512])
                yar = Ring(self, es2, "yah", 2, [64, T], BF16)
                for h in range(8):
                    Qn, qnk = qnr.next(); Qr, qrk = qrr.next(); Kn, knk = knr.next()
                    for ci, (t0, n) in enumerate(TCH):
                        p0, p1, p2, p3 = self.ps[4], self.ps[5], self.ps[6], self.ps[7]
                        for c in range(3):
                            self.mm(p0[0:64, 0:n], wqn[:, c, h * 64:(h + 1) * 64], cqn[:, c, t0:t0 + n], c == 0, c == 2, reads=["wuq_n", ("cqn", ci)], writes=[("ps", 4)])
                        for c in range(3):
                            self.mm(p1[0:32, 0:n], wqr[:, c, h * 32:(h + 1) * 32], cqn[:, c, t0:t0 + n], c == 0, c == 2, reads=["wuq_r", ("cqn", ci)], writes=[("ps", 5)])
                        for c in range(3):
                            self.mm(p2[0:32, 0:n], wqp[:, c, h * 32:(h + 1) * 32], cqn[:, c, t0:t0 + n], c == 0, c == 2, reads=["wuq_rp", ("cqn", ci)], writes=[("ps", 6)])
                        for c in range(2):
                            self.mm(p3[0:64, 0:n], wkn[:, c, h * 64:(h + 1) * 64], ckvn[:, c, t0:t0 + n], c == 0, c == 1, reads=["wukv_k", ("ckvn", ci)], writes=[("ps", 7)])
                        self.copy("act", Qn[:, t0:t0 + n], p0[0:64, 0:n], reads=[("ps", 4)], writes=[(qnk, ci)])
                        self.copy("act", Kn[:, t0:t0 + n], p3[0:64, 0:n], reads=[("ps", 7)], writes=[(knk, ci)])
                        a_, ak = ta.next(); b_, bk = tb.next()
                        self.rope_from(p1[0:32, 0:n], ("ps", 5), p2[0:32, 0:n], ("ps", 6), cosM[:, t0:t0 + n], sinM[:, t0:t0 + n],
                                       Qr[:, t0:t0 + n], (qrk, ci), a_[:, 0:n], b_[:, 0:n], ak, bk)
                    if self.stop == "mla_proj":
                        break
                    ya, yk = yar.next()
                    for ci, (t0, n) in enumerate(TCH):
                        if self.stop == "mla_ctx" and ci > 0:
                            break
                        kts = [0, 1] if ci == 0 else list(range(NT))
                        po, pd = self.ps[3], self.ps[2]
                        def s_stage(i, kt):
                            kci = tile_chunk(kt)
                            sk_ = ("ps", i % 2); psn = self.ps[i % 2]
                            self.mm(psn[:, 0:n], Kn[:, kt * 128:(kt + 1) * 128], Qn[:, t0:t0 + n], True, False,
                                    reads=[(knk, kci), (qnk, ci)], writes=[sk_])
                            self.mm(psn[:, 0:n], KrT[:, kt * 128:(kt + 1) * 128], Qr[:, t0:t0 + n], False, True,
                                    reads=[("KrT", kci), (qrk, ci)], writes=[sk_])
                            P, Pk = pr.next()
                            tr.op("act", lambda e: e.activation(out=P[:, 0:n], in_=psn[:, 0:n], func=AF.Exp, scale=scale), reads=[sk_], writes=[Pk])
                            return P, Pk

                        def pv_stage(i, kt, P, Pk):
                            self.mm(po[0:64, 0:n], Vt[:, kt, h * 64:(h + 1) * 64], P[:, 0:n], i == 0, i == len(kts) - 1, reads=[("Vt", kt), Pk], writes=[("ps", 3)])
                            self.mm(pd[0:64, 0:n], self.ones16[:, :], P[:, 0:n], i == 0, i == len(kts) - 1, reads=["ones16", Pk], writes=[("ps", 2)])

                        prev = None
                        for i, kt in enumerate(kts):
                            cur = (i, kt) + s_stage(i, kt)
                            if prev is not None:
                                pv_stage(*prev)
                            prev = cur
                        pv_stage(*prev)
                        rd, rk = rdr.next()
                        tr.op("dve", lambda e: e.reciprocal(out=rd[:, 0:n], in_=pd[0:64, 0:n]), reads=[("ps", 2)], writes=[rk])
                        tr.op("dve", lambda e: e.tensor_tensor(out=ya[:, t0:t0 + n], in0=po[0:64, 0:n], in1=rd[:, 0:n], op=ALU.mult),
                              reads=[("ps", 3), rk], writes=[yk])
                    self.dma(self.scr["D_ya"][h * 64:(h + 1) * 64, :], ya[:], reads=[yk], writes=["D_ya"])
                tr.barrier()

    def phase_swa(self, l):
        tr = self.tr
        with ExitStack() as es:
            QT = self.sb(es, "sQT", [64, 8, T], BF16)
            KT = self.sb(es, "sKT", [64, 2, T], BF16)
            Vs = self.sb(es, "sVs", [128, NT, 128], BF16)
            cosS = self.sb(es, "cosS", [64, T]); sinS = self.sb(es, "sinS", [64, T])
            msk = self.sb(es, "smask", [128, 2, 512], BF16)
            snk = self.sb(es, "snk", [64, 8])
            self.dma(cosS[:], self.inp["ropeS"][0], writes=["rope"])
            self.dma(sinS[:], self.inp["ropeS"][1], writes=["rope"])
            self.dma(msk[:], self.inp["swamask"].rearrange("a p n -> p a n"), writes=["smask"], q="gq")
            self.dma(snk[:], self.inp["sink_rep"][:, l, :], writes=["snk"])
            tr.op("act", lambda e: e.activation(out=snk[:], in_=snk[:], func=AF.Exp), reads=["snk"], writes=["snk"])
            self.dma(Vs[:], self.scr["D_tm"][:, SV:SV + 128].rearrange("(t p) c -> p t c", p=128), reads=["D_tm"], writes=["sVs"], q="gq")
            with ExitStack() as es2:
                ld = Ring(self, es2, "sld", 1, [64, T]); ldp = Ring(self, es2, "sldp", 1, [64, T])
                ta = Ring(self, es2, "sta", 1, [64, T]); tb = Ring(self, es2, "stb", 1, [64, T])
                for j in range(10):
                    r0, rp = (SQ + j * 64, SQP + j * 64) if j < 8 else (SK + (j - 8) * 64, SKP + (j - 8) * 64)
                    dst = QT[:, j, :] if j < 8 else KT[:, j - 8, :]
                    x, xk = ld.next(); xp, xpk = ldp.next()
                    self.dma(x[:], self.scr["D_fm"][r0:r0 + 64, :], reads=["D_fm"], writes=[xk])
                    self.dma(xp[:], self.scr["D_fm"][rp:rp + 64, :], reads=["D_fm"], writes=[xpk])
                    a_, ak = ta.next(); b_, bk = tb.next()
                    self.rope_from(x[:], xk, xp[:], xpk, cosS[:], sinS[:], dst, ("sQK", j), a_[:], b_[:], ak, bk)
                tr.barrier()
            with ExitStack() as es2:
                pr = Ring(self, es2, "sP", 3, [128, 512], BF16)
                dn = Ring(self, es2, "sdn", 2, [64, 512])
                yr = Ring(self, es2, "syb", 2, [64, 4, 128], BF16)
                ybv = self.scr["D_yb"].rearrange("(h d) t -> d h t", d=64)
                it = 0
                for qt in range(NT):
                    if qt < 2:
                        kts = [(0, None), (1, None)]
                    else:
                        kts = [(0, None), (1, None)]
                        if qt > 2:
                            kts.append((qt - 1, 0))
                        kts.append((qt, None))
                        if qt < NT - 1:
                            kts.append((qt + 1, 1))
                    for g in range(2):
                        po, pd = self.ps[3], self.ps[2]
                        rhs = QT[:, 4 * g:4 * g + 4, qt * 128:(qt + 1) * 128]
                        def s_stage(i, kt, mk):
                            nonlocal it
                            sk_ = ("ps", it % 2); psn = self.ps[it % 2]; it += 1
                            self.mm(psn[:, :].rearrange("p (h q) -> p h q", h=4), KT[:, g, kt * 128:(kt + 1) * 128], rhs, True, True,
                                    reads=[("sQK", 8 + g)] + [("sQK", 4 * g + hh) for hh in range(4)], writes=[sk_])
                            P, Pk = pr.next()
                            tr.op("act", lambda e: e.activation(out=P[:], in_=psn[:, :], func=AF.Exp, scale=0.125), reads=[sk_], writes=[Pk])
                            if mk is not None:
                                tr.op("dve", lambda e: e.tensor_tensor(out=P[:], in0=P[:], in1=msk[:, mk, :], op=ALU.mult), reads=[Pk, "smask"], writes=[Pk])
                            return P, Pk

                        def pv_stage(i, kt, P, Pk):
                            self.mm(po[0:64, :], Vs[:, kt, g * 64:(g + 1) * 64], P[:], i == 0, i == len(kts) - 1, reads=["sVs", Pk], writes=[("ps", 3)])
                            self.mm(pd[0:64, :], self.ones16[:, :], P[:], i == 0, i == len(kts) - 1, reads=["ones16", Pk], writes=[("ps", 2)])

                        prev = None
                        for i, (kt, mk) in enumerate(kts):
                            cur = (i, kt) + s_stage(i, kt, mk)
                            if prev is not None:
                                pv_stage(*prev)
                            prev = cur
                        pv_stage(*prev)
                        d_, dk = dn.next()
                        for hh in range(4):
                            tr.op("dve", lambda e: e.tensor_scalar(out=d_[:, hh * 128:(hh + 1) * 128], in0=pd[0:64, hh * 128:(hh + 1) * 128],
                                                                  scalar1=snk[:, 4 * g + hh:4 * g + hh + 1], scalar2=None, op0=ALU.add),
                                  reads=[("ps", 2), "snk"], writes=[dk])
                        tr.op("dve", lambda e: e.reciprocal(out=d_[:], in_=d_[:]), reads=[dk], writes=[dk])
                        y, yk = yr.next()
                        tr.op("dve", lambda e: e.tensor_tensor(out=y[:].rearrange("p h q -> p (h q)"), in0=po[0:64, :], in1=d_[:], op=ALU.mult),
                              reads=[("ps", 3), dk], writes=[yk])
                        self.dma(ybv[:, 4 * g:4 * g + 4, qt * 128:(qt + 1) * 128], y[:], reads=[yk], writes=["D_yb"])
                tr.barrier()

    def phase_gdn_prep(self, l):
        tr = self.tr
        with ExitStack() as es:
            cw = self.sb(es, "gcw", [64, 24, 3])
            self.dma(cw[:], self.inp["gconv_fm"][:, l, :, :], writes=["gcw"])
            ld = Ring(self, es, "gld", 2, [64, T]); yr = Ring(self, es, "gy", 2, [64, T])
            sqr = Ring(self, es, "gsq", 2, [64, 512]); rr = Ring(self, es, "grn", 2, [64, 512])
            tmr = Ring(self, es, "gtm", 2, [128, NT, 64])
            for j in range(24):
                x, xk = ld.next(); y, yk = yr.next()
                self.dma(x[:], self.scr["D_fm"][GQ + j * 64:GQ + (j + 1) * 64, :], reads=["D_fm"], writes=[xk])
                tr.op("dve", lambda e: e.tensor_scalar(out=y[:], in0=x[:], scalar1=cw[:, j, 1:2], scalar2=None, op0=ALU.mult), reads=[xk, "gcw"], writes=[yk])
                for (o0, o1, i0, i1, k) in ((1, CTX, 0, CTX - 1, 0), (CTX + 1, T, CTX, T - 1, 0), (0, CTX - 1, 1, CTX, 2), (CTX, T - 1, CTX + 1, T, 2)):
                    tr.op("dve", lambda e: e.scalar_tensor_tensor(out=y[:, o0:o1], in0=x[:, i0:i1], scalar=cw[:, j, k:k + 1], in1=y[:, o0:o1],
                                                                 op0=ALU.mult, op1=ALU.add), reads=[xk, yk, "gcw"], writes=[yk])
                tr.op("act", lambda e: e.activation(out=y[:], in_=y[:], func=AF.Silu), reads=[yk], writes=[yk])
                if j < 16:
                    for ci, (t0, n) in enumerate(TCH):
                        sq, sk = sqr.next()
                        tr.op("act", lambda e: e.activation(out=sq[:, 0:n], in_=y[:, t0:t0 + n], func=AF.Square), reads=[yk], writes=[sk])
                        pk = ("ps", ci % 2); pt = self.ps[ci % 2]
                        self.mm(pt[0:64, 0:n], self.ones32[0:64, 0:64], sq[:, 0:n], True, True, reads=[sk, "ones32"], writes=[pk])
                        r, rk = rr.next()
                        self.rstd_from_ss(pt[0:64, 0:n], pk, r[:, 0:n], rk, 1.0)
                        tr.op("dve", lambda e: e.scalar_tensor_tensor(out=y[:, t0:t0 + n], in0=y[:, t0:t0 + n], scalar=(0.125 if j < 8 else 1.0),
                                                                     in1=r[:, 0:n], op0=ALU.mult, op1=ALU.mult), reads=[yk, rk], writes=[yk])
                    dst = self.scr["D_gq"] if j < 8 else self.scr["D_gk"]
                    self.dma(dst[(j % 8) * 64:(j % 8 + 1) * 64, :], y[:], reads=[yk], writes=["D_gqk"])
                if j >= 8:
                    tm, tk = tmr.next()
                    for t8 in range(0, NT, 8):
                        nt = min(8, NT - t8)
                        pk = ("ps", 2 + (t8 // 8) % 2); pt = self.ps[2 + (t8 // 8) % 2]
                        for tt in range(nt):
                            t = t8 + tt
                            tr.op("pe", lambda e: e.transpose(pt[:, tt * 64:(tt + 1) * 64], y[:, t * 128:(t + 1) * 128], self.ident[0:64, 0:64]),
                                  reads=[yk, "ident"], writes=[pk])
                        self.copy(self.evac_eng(), tm[:, t8:t8 + nt, :], pt[:, 0:nt * 64].rearrange("p (t d) -> p t d", d=64), reads=[pk], writes=[tk])
                    dst = self.scr["D_ktm"] if j < 16 else self.scr["D_vtm"]
                    self.dma(dst[j % 8].rearrange("(t p) d -> p t d", p=128), tm[:], reads=[tk], writes=["D_kvtm"])
            tr.barrier()

    def phase_gdn(self, l):
        tr = self.tr
        with ExitStack() as es:
            oacc = self.sb(es, "oacc", [128, NT, 512])
            ab = self.sb(es, "gab", [128, NT, 32])
            g = self.sb(es, "gg", [128, NT, 16]); beta = self.sb(es, "gbeta", [128, NT, 16]); nbeta = self.sb(es, "gnbeta", [128, NT, 16])
            gam = self.sb(es, "ggam", [128, NT, 16]); eg = self.sb(es, "geg", [128, NT, 16]); bg = self.sb(es, "gbg", [128, NT, 16])
            edl = self.sb(es, "gedl", [128, NT, 16]); tmp = self.sb(es, "gtmp", [128, NT, 16])
            alog = self.sb(es, "galog", [128, 16]); dtb = self.sb(es, "gdtb", [128, 16])
            tri = self.sb(es, "gtri", [128, 2, 128]); gm = self.sb(es, "gmask", [128, 4, 128]); sel = self.sb(es, "gsel", [8, 8, 128])
            S = self.sb(es, "gS", [64, 2, 8, 64])
            self.dma(ab[:], self.scr["D_tm"][:, AB:AB + 32].rearrange("(t p) c -> p t c", p=128), reads=["D_tm"], writes=["gab"])
            self.dma(alog[:], self.inp["alog_rep"][:, l, :], writes=["galog"])
            self.dma(dtb[:], self.inp["dtb_rep"][:, l, :], writes=["gdtb"])
            self.dma(tri[:], self.inp["tri"].rearrange("a p n -> p a n"), writes=["gtri"])
            self.dma(gm[:], self.inp["gmask"].rearrange("a p n -> p a n"), writes=["gmaskc"])
            self.dma(sel[:], self.inp["selh"].rearrange("h u n -> u h n"), writes=["gsel"])
            bmk = self.sb(es, "gbmask", [128, 3, 128])
            self.dma(bmk[:], self.inp["bmask"].rearrange("a p n -> p a n"), writes=["gbmask"])
            tr.op("dve", lambda e: e.memset(S[:], 0.0), writes=[(f"S{d}", h) for d in range(2) for h in range(8)])
            tr.op("act", lambda e: e.activation(out=alog[:], in_=alog[:], func=AF.Exp), reads=["galog"], writes=["galog"])
            for t in range(NT):
                tr.op("dve", lambda e: e.tensor_tensor(out=tmp[:, t, :], in0=ab[:, t, 0:16], in1=dtb[:], op=ALU.add), reads=["gab", "gdtb"], writes=["gtmp"])
            tr.op("act", lambda e: e.activation(out=tmp[:], in_=tmp[:], func=AF.Exp), reads=["gtmp"], writes=["gtmp"])
            tr.op("act", lambda e: e.activation(out=tmp[:], in_=tmp[:], func=AF.Ln, bias=self.cst[:, 1:2], scale=1.0), reads=["gtmp", "cst"], writes=["gtmp"])
            for t in range(NT):
                tr.op("dve", lambda e: e.scalar_tensor_tensor(out=g[:, t, :], in0=tmp[:, t, :], scalar=-1.0, in1=alog[:], op0=ALU.mult, op1=ALU.mult),
                      reads=["gtmp", "galog"], writes=["gg"])
            tr.op("act", lambda e: e.activation(out=beta[:], in_=ab[:, :, 16:32], func=AF.Sigmoid), reads=["gab"], writes=["gbeta"])
            tr.op("dve", lambda e: e.tensor_scalar(out=nbeta[:], in0=beta[:], scalar1=-1.0, scalar2=None, op0=ALU.mult), reads=["gbeta"], writes=["gnbeta"])
            pg = self.ps[0][:, 0:NT * 16].rearrange("p (t u) -> p t u", u=16)
            pt_ = self.ps[1][:, 0:NT * 16].rearrange("p (t u) -> p t u", u=16)
            for t in range(NT):
                for d in range(2):
                    self.mm(pg[:, t, d * 8:(d + 1) * 8], tri[:, d, :], g[:, t, d * 8:(d + 1) * 8], True, True, reads=["gtri", "gg"], writes=[("ps", 0)])
                self.mm(pt_[:, t, :], self.ones32[:], g[:, t, :], True, True, reads=["ones32", "gg"], writes=[("ps", 1)])
            self.copy("dve", gam[:], pg, reads=[("ps", 0)], writes=["ggam"])
            tr.op("dve", lambda e: e.tensor_tensor(out=edl[:], in0=pt_, in1=gam[:], op=ALU.subtract), reads=[("ps", 1), "ggam"], writes=["gedl"])
            tr.op("act", lambda e: e.activation(out=edl[:], in_=edl[:], func=AF.Exp), reads=["gedl"], writes=["gedl"])
            tr.op("act", lambda e: e.activation(out=eg[:], in_=gam[:], func=AF.Exp), reads=["ggam"], writes=["geg"])
            tr.op("dve", lambda e: e.tensor_tensor(out=bg[:], in0=beta[:], in1=eg[:], op=ALU.mult), reads=["gbeta", "geg"], writes=["gbg"])
            if self.stop == "gdn_gates":
                tr.barrier()
                return
            es_scan = ExitStack()
            NSLOT = 4
            R1 = lambda nm, shp, n=1: [Ring(self, es_scan, f"g{nm}{sl_}", n, shp) for sl_ in range(NSLOT)]
            qTr = R1("iq", [64, 128]); kTr = R1("ik", [64, 128]); ktmr = R1("iktm", [128, 64]); vtmr = R1("ivtm", [128, 64]); kdr = R1("kd", [128, 64])
            t1r = R1("t1", [128, 128]); t2r = R1("t2", [128, 128]); egr = R1("egb", [64, 128])
            a0r = R1("a0", [128, 256]); pxr_ = R1("px", [128, 256], 2); ptr__ = R1("pt", [128, 128], 2); lr = R1("l", [128, 2, 128])
            ybr_ = R1("yb", [128, 128], 5); qkr = R1("qk", [128, 128]); wtr = R1("wt", [64, 128]); qgr = R1("qg", [64, 128]); vnr = R1("vn", [128, 64])
            gTr = Ring(self, es_scan, "ggT", 3, [8, 128])
            gqv = self.scr["D_gq"].rearrange("(h d) t -> h d t", d=64)
            gkv = self.scr["D_gk"].rearrange("(h d) t -> h d t", d=64)

            def unit(d, t, h, s_, gT, gTk):
                u = d * 8 + h
                Sk = f"S{d}"
                last = 127 if d == 0 else 0
                sl = slice(t * 128, (t + 1) * 128)
                pA, pAk, pB, pBk = self.ps[2 * s_], ("ps", 2 * s_), self.ps[2 * s_ + 1], ("ps", 2 * s_ + 1)
                qT, qk_ = qTr[s_].next(); kT, kk_ = kTr[s_].next(); ktm, ktk = ktmr[s_].next(); vtm, vtk = vtmr[s_].next()
                self.dma(qT[:], gqv[h, :, sl], reads=["D_gqk"], writes=[qk_])
                self.dma(kT[:], gkv[h, :, sl], reads=["D_gqk"], writes=[kk_])
                self.dma(ktm[:], self.scr["D_ktm"][h, sl, :], reads=["D_kvtm"], writes=[ktk])
                self.dma(vtm[:], self.scr["D_vtm"][h, sl, :], reads=["D_kvtm"], writes=[vtk])
                yield
                self.mm(pA[:, 0:128], sel[:, h, :], gT[:], True, True, reads=["gsel", gTk], writes=[pAk])
                self.mm(pB[:, 0:128], kT[:], kT[:], True, True, reads=[kk_], writes=[pBk])
                kd, kdk = kdr[s_].next()
                tr.op("pool", lambda e: e.tensor_scalar(out=kd[:], in0=ktm[:], scalar1=edl[:, t, u:u + 1], scalar2=None, op0=ALU.mult), reads=[ktk, "gedl"], writes=[kdk])
                a0, a0k = a0r[s_].next()
                tr.op("pool", lambda e: e.tensor_scalar(out=a0[:, 128:192], in0=vtm[:], scalar1=beta[:, t, u:u + 1], scalar2=None, op0=ALU.mult),
                      reads=[vtk, "gbeta"], writes=[(a0k, "x")])
                tr.op("pool", lambda e: e.tensor_scalar(out=a0[:, 192:256], in0=ktm[:], scalar1=bg[:, t, u:u + 1], scalar2=None, op0=ALU.mult),
                      reads=[ktk, "gbg"], writes=[(a0k, "x")])
                yield
                t1, t1k = t1r[s_].next(); t2, t2k = t2r[s_].next(); egb, egk = egr[s_].next()
                tr.op("dve", lambda e: e.scalar_tensor_tensor(out=t1[:], in0=pA[:, 0:128], scalar=gam[:, t, u:u + 1], in1=gm[:, 2 * d, :],
                                                             op0=ALU.subtract, op1=ALU.add), reads=[pAk, "ggam", "gmaskc"], writes=[t1k])
                tr.op("dve", lambda e: e.scalar_tensor_tensor(out=t2[:], in0=pA[:, 0:128], scalar=gam[:, t, u:u + 1], in1=gm[:, 2 * d + 1, :],
                                                             op0=ALU.subtract, op1=ALU.add), reads=[pAk, "ggam", "gmaskc"], writes=[t2k])
                tr.op("dve", lambda e: e.tensor_copy(out=egb[:], in_=pA[0:64, 0:128]), reads=[pAk], writes=[egk])
                yield
                tr.op("act", lambda e: e.activation(out=t1[:], in_=t1[:], func=AF.Exp, scale=-1.0), reads=[t1k], writes=[t1k])
                tr.op("act", lambda e: e.activation(out=t2[:], in_=t2[:], func=AF.Exp), reads=[t2k], writes=[t2k])
                tr.op("act", lambda e: e.activation(out=egb[:], in_=egb[:], func=AF.Exp), reads=[egk], writes=[egk])
                self.mm(pA[:, 0:128], kT[:], qT[:], True, True, reads=[kk_, qk_], writes=[pAk])
                yield
                tr.op("dve", lambda e: e.scalar_tensor_tensor(out=a0[:, 0:128], in0=pB[:, 0:128], scalar=nbeta[:, t, u:u + 1], in1=t1[:],
                                                             op0=ALU.mult, op1=ALU.mult), reads=[pBk, "gnbeta", t1k], writes=[(a0k, "p")])
                qk, qkk = qkr[s_].next()
                tr.op("dve", lambda e: e.tensor_tensor(out=qk[:], in0=pA[:, 0:128], in1=t2[:], op=ALU.mult), reads=[pAk, t2k], writes=[qkk])
                qg, qgk = qgr[s_].next()
                tr.op("pool", lambda e: e.tensor_tensor(out=qg[:], in0=qT[:], in1=egb[:], op=ALU.mult), reads=[qk_, egk], writes=[qgk])
                yield
                tr.op("pe", lambda e: e.transpose(pB[:, 0:128], a0[:, 0:128], self.ident[:]), reads=[(a0k, "p"), "ident"], writes=[pBk])
                px, pxk = pxr_[s_].next(); pT, pTk = ptr__[s_].next(); lt, ltk = lr[s_].next()
                tr.op("pool", lambda e: e.tensor_tensor(out=pT[:], in0=a0[:, 0:128], in1=bmk[:, 0, :], op=ALU.mult), reads=[(a0k, "p"), "gbmask"], writes=[pTk])
                tr.op("pool", lambda e: e.tensor_copy(out=px[:, 128:256], in_=self.ident[:]), reads=["ident"], writes=[(pxk, "x")])
                yield
                tr.op("dve", lambda e: e.tensor_tensor(out=px[:, 0:128], in0=pB[:, 0:128], in1=bmk[:, 0, :], op=ALU.mult), reads=[pBk, "gbmask"], writes=[(pxk, "p")])
                tr.op("dve", lambda e: e.tensor_tensor(out=lt[:, 0, :], in0=pB[:, 0:128], in1=bmk[:, 1, :], op=ALU.mult), reads=[pBk, "gbmask"], writes=[(ltk, 0)])
                tr.op("dve", lambda e: e.tensor_tensor(out=lt[:, 1, :], in0=pB[:, 0:128], in1=bmk[:, 2, :], op=ALU.mult), reads=[pBk, "gbmask"], writes=[(ltk, 1)])
                yield
                for lev in range(5):
                    if lev < 4:
                        self.mm(pA[:, 0:256], pT[:], px[:, 0:256], True, True, reads=[pTk, (pxk, "p"), (pxk, "x")], writes=[pAk])
                        self.mm(pB[:, 0:128], px[:, 0:128], pT[:], True, True, reads=[pTk, (pxk, "p")], writes=[pBk])
                        yield
                        nx, nxk = pxr_[s_].next(); nT, nTk = ptr__[s_].next()
                        self.copy("act", nT[:], pB[:, 0:128], reads=[pBk], writes=[nTk])
                        tr.op("dve", lambda e: e.tensor_tensor(out=nx[:, 128:256], in0=pA[:, 128:256], in1=px[:, 128:256], op=ALU.add),
                              reads=[pAk, (pxk, "x")], writes=[(nxk, "x")])
                        self.copy("dve", nx[:, 0:128], pA[:, 0:128], reads=[pAk], writes=[(nxk, "p")])
                        px, pxk, pT, pTk = nx, nxk, nT, nTk
                        yield
                    else:
                        self.mm(pA[:, 0:128], pT[:], px[:, 128:256], True, True, reads=[pTk, (pxk, "x")], writes=[pAk])
                        yield
                        nx, nxk = pxr_[s_].next()
                        tr.op("dve", lambda e: e.tensor_tensor(out=nx[:, 128:256], in0=pA[:, 0:128], in1=px[:, 128:256], op=ALU.add),
                              reads=[pAk, (pxk, "x")], writes=[(nxk, "x")])
                        px, pxk = nx, nxk
                        yield
                MT, MTk = px[:, 128:256], (pxk, "x")
                cnt = [0]

                def apply(lhsT_ap, lhs_key, rhs_ap, rhs_key, add_ap=None, add_key=None):
                    pb, pbk = (pA, pAk) if cnt[0] % 2 == 0 else (pB, pBk)
                    eng = "act" if cnt[0] % 2 == 0 else "dve"
                    cnt[0] += 1
                    self.mm(pb[:, 0:128], lhsT_ap, rhs_ap, True, True, reads=[lhs_key, rhs_key], writes=[pbk])
                    yield
                    yb, ybk = ybr_[s_].next()
                    if add_ap is None:
                        self.copy(eng, yb[:], pb[:, 0:128], reads=[pbk], writes=[ybk])
                    else:
                        tr.op("dve", lambda e: e.tensor_tensor(out=yb[:], in0=pb[:, 0:128], in1=add_ap, op=ALU.add), reads=[pbk, add_key], writes=[ybk])
                    yield
                    return yb, ybk

                def s64(z_ap, z_key):
                    y1, y1k = yield from apply(MT, MTk, z_ap, z_key)
                    z1, z1k = yield from apply(lt[:, 0, :], (ltk, 0), y1[:], y1k)
                    r_ = yield from apply(MT, MTk, z1[:], z1k, add_ap=y1[:], add_key=y1k)
                    return r_

                y2, y2k = yield from s64(a0[:, 128:256], (a0k, "x"))
                z2, z2k = yield from apply(lt[:, 1, :], (ltk, 1), y2[:], y2k)
                y4, y4k = yield from s64(z2[:], z2k)
                px, pxk = pxr_[s_].next()
                tr.op("pool", lambda e: e.tensor_tensor(out=px[:, 128:256], in0=y2[:], in1=y4[:], op=ALU.add), reads=[y2k, y4k], writes=[(pxk, "x")])
                yield
                tr.op("pe", lambda e: e.transpose(pA[0:64, 0:128], px[:, 192:256], self.ident[:]), reads=[(pxk, "x"), "ident"], writes=[pAk])
                yield
                wt_, wtk = wtr[s_].next()
                self.copy("act", wt_[:], pA[0:64, 0:128], reads=[pAk], writes=[wtk])
                yield
                self.mm(pB[:, 0:64], wt_[:], S[:, d, h, :], True, True, reads=[wtk, (Sk, h)], writes=[pBk])
                yield
                vn, vnk = vnr[s_].next()
                tr.op("dve", lambda e: e.tensor_tensor(out=vn[:], in0=px[:, 128:192], in1=pB[:, 0:64], op=ALU.subtract),
                      reads=[(pxk, "x"), pBk], writes=[vnk])
                yield
                self.mm(pA[:, 0:64], qg[:], S[:, d, h, :], True, False, reads=[qgk, (Sk, h)], writes=[pAk])
                self.mm(pA[:, 0:64], qk[:], vn[:], False, True, reads=[qkk, vnk], writes=[pAk])
                self.mm(pB[0:64, 0:64], kd[:], vn[:], True, True, reads=[kdk, vnk], writes=[pBk])
                yield
                if d == 0:
                    self.copy("act", oacc[:, t, h * 64:(h + 1) * 64], pA[:, 0:64], reads=[pAk], writes=[("oacc", t, h)])
                else:
                    tr.op("dve", lambda e: e.tensor_tensor(out=oacc[:, t, h * 64:(h + 1) * 64], in0=pA[:, 0:64], in1=oacc[:, t, h * 64:(h + 1) * 64], op=ALU.add),
                          reads=[pAk, ("oacc", t, h)], writes=[("oacc", t, h)])
                tr.op("dve", lambda e: e.scalar_tensor_tensor(out=S[:, d, h, :], in0=S[:, d, h, :], scalar=egb[:, last:last + 1], in1=pB[0:64, 0:64],
                                                             op0=ALU.mult, op1=ALU.add), reads=[(Sk, h), egk, pBk], writes=[(Sk, h)])

            def unit_stream():
                for d in range(2):
                    order = list(range(NT)) if d == 0 else [1, 0] + list(range(NT - 1, 1, -1))
                    for t in order:
                        gT, gTk = gTr.next()
                        pk = ("ps", 7)
                        self.mm(self.ps[7][0:8, 256:384], g[:, t, d * 8:(d + 1) * 8], tri[:, d, :], True, True, reads=["gg", "gtri"], writes=[pk])
                        self.copy("act", gT[:], self.ps[7][0:8, 256:384], reads=[pk], writes=[gTk])
                        for h in range(8):
                            yield (d, t, h, gT, gTk)

            stream = unit_stream()
            active = {}
            free_slots = list(range(NSLOT))
            exhausted = False
            while True:
                while free_slots and not exhausted:
                    try:
                        d_, t_, h_, gT_, gTk_ = next(stream)
                    except StopIteration:
                        exhausted = True
                        break
                    s_ = free_slots.pop(0)
                    active[s_] = unit(d_, t_, h_, s_, gT_, gTk_)
                if not active:
                    break
                for s_ in sorted(active):
                    try:
                        next(active[s_])
                    except StopIteration:
                        del active[s_]
                        free_slots.append(s_)
            if (self.stop or "").startswith("gdn_u1"):
                tr.barrier()
                es_scan.close()
                return
            tr.barrier()
            es_scan.close()
            for t in range(NT):
                for h in range(8):
                    tr.lastw[("oacc", t, h)] = None
            tr.lastw = {k: v for k, v in tr.lastw.items() if v is not None}
            if "D_x" in self.taps:
                self.dma(self.scr["D_oc"].rearrange("(t p) c -> p t c", p=128), oacc[:], reads=[("oacc", t, h) for t in range(NT) for h in range(8)])
            gn = self.sb(es, "gn", [128, 64])
            self.dma(gn[:], self.inp["gnorm_rep"][:, l, :], writes=["gn"])
            zr = Ring(self, es, "gz", 2, [128, 512]); sq2 = Ring(self, es, "gsq2", 2, [128, 512]); ssr = Ring(self, es, "gss", 2, [128, 8])
            ycr = Ring(self, es, "gyc", 2, [128, 4, 128], BF16)
            ycv = self.scr["D_yc"].rearrange("(c p) t -> p c t", p=128)
            for t in range(NT):
                z, zk = zr.next()
                self.dma(z[:], self.scr["D_tm"][t * 128:(t + 1) * 128, ZC:ZC + 512], reads=["D_tm"], writes=[zk])
                tr.op("act", lambda e: e.activation(out=z[:], in_=z[:], func=AF.Silu), reads=[zk], writes=[zk])
                ok = [("oacc", t, h) for h in range(8)]
                sq, sk = sq2.next()
                tr.op("pool", lambda e: e.tensor_tensor(out=sq[:], in0=oacc[:, t, :], in1=oacc[:, t, :], op=ALU.mult), reads=ok, writes=[sk])
                ss, ssk = ssr.next()
                tr.op("dve", lambda e: e.tensor_reduce(out=ss[:], in_=sq[:].rearrange("p (h d) -> p h d", d=64), axis=AX.X, op=ALU.add), reads=[sk], writes=[ssk])
                self.rstd_from_ss(ss[:], ssk, ss[:], ssk, 1.0 / 64)
                o3 = oacc[:, t, :].rearrange("p (h d) -> p h d", d=64)
                tr.op("dve", lambda e: e.tensor_tensor(out=sq[:].rearrange("p (h d) -> p h d", d=64), in0=o3, in1=ss[:].unsqueeze(2).to_broadcast([128, 8, 64]), op=ALU.mult),
                      reads=ok + [ssk], writes=[sk])
                tr.op("dve", lambda e: e.tensor_tensor(out=sq[:].rearrange("p (h d) -> p h d", d=64), in0=sq[:].rearrange("p (h d) -> p h d", d=64),
                                                      in1=gn[:].unsqueeze(1).to_broadcast([128, 8, 64]), op=ALU.mult), reads=[sk, "gn"], writes=[sk])
                tr.op("dve", lambda e: e.tensor_tensor(out=sq[:], in0=sq[:], in1=z[:], op=ALU.mult), reads=[sk, zk], writes=[sk])
                pk = ("ps", 4 + t % 2); pt = self.ps[4 + t % 2]
                for c in range(4):
                    tr.op("pe", lambda e: e.transpose(pt[:, c * 128:(c + 1) * 128], sq[:, c * 128:(c + 1) * 128], self.ident[:]), reads=[sk, "ident"], writes=[pk])
                yc, yk = ycr.next()
                self.copy("act", yc[:], pt[:].rearrange("p (c t) -> p c t", c=4), reads=[pk], writes=[yk])
                self.dma(ycv[:, :, t * 128:(t + 1) * 128], yc[:], reads=[yk], writes=["D_yc"])
            tr.barrier()

    def phase_merge(self, l):
        tr = self.tr
        with ExitStack() as es:
            wp = [self.sb(es, f"wp{i}", [128, 4, D], BF16) for i in range(3)]
            wo = self.sb(es, "wo", [128, 8, D], BF16)
            for i, nm in enumerate(("w_pa", "w_pb", "w_pc")):
                self.dma(wp[i][:], self.inp[nm][l].rearrange("(c p) n -> p c n", p=128), writes=[nm], q="gq")
            self.dma(wo[:], self.inp["w_out"][l].rearrange("(c p) n -> p c n", p=128), writes=["w_out"], q="gq")
            gr = Ring(self, es, "mg", 1, [128, 24, 512], BF16)
            yr = [Ring(self, es, f"my{i}", 2, [128, 4, 512], BF16) for i in range(3)]
            mixr = Ring(self, es, "mix", 2, [128, 512]); tmr = Ring(self, es, "mtm", 2, [128, 512])
            mT = Ring(self, es, "mixT", 1, [128, 8, 512], BF16)
            gv = self.scr["D_gate"].rearrange("(c p) t -> p c t", p=128)
            yv = [self.scr[nm].rearrange("(c p) t -> p c t", p=128) for nm in ("D_ya", "D_yb", "D_yc")]
            pi = 0
            for ci, (t0, n) in enumerate(TCH):
                s = 1 if ci == 0 else 0
                gt, gk = gr.next()
                self.dma(gt[:, :, 0:n], gv[:, :, t0:t0 + n], reads=[("D_gate", ci)], writes=[gk])
                ys = []
                for i in range(3):
                    y, yk = yr[i].next()
                    self.dma(y[:, :, 0:n], yv[i][:, :, t0:t0 + n], reads=["D_ya", "D_yb", "D_yc"], writes=[yk])
                    ys.append((y, yk))
                mt, mtk = mT.next()
                for m in range(8):
                    mix, mk = mixr.next()
                    for i in range(3):
                        pk = ("ps", pi % 4); pt = self.ps[pi % 4]; pi += 1
                        y, yk = ys[i]
                        for c in range(4):
                            self.mm(pt[:, 0:n], wp[i][:, c, m * 128:(m + 1) * 128], y[:, c, 0:n], c == 0, c == 3, reads=[("w_pa", "w_pb", "w_pc")[i], yk], writes=[pk])
                        if i == 0:
                            tr.op("dve", lambda e: e.tensor_tensor(out=mix[:, 0:n], in0=pt[:, 0:n], in1=gt[:, m, 0:n], op=ALU.mult), reads=[pk, gk], writes=[mk])
                        else:
                            tm, tk = tmr.next()
                            tr.op("dve", lambda e: e.tensor_tensor(out=tm[:, 0:n], in0=pt[:, 0:n], in1=gt[:, i * 8 + m, 0:n], op=ALU.mult), reads=[pk, gk], writes=[tk])
                            if i == 1:
                                tr.op("pool", lambda e: e.tensor_tensor(out=mix[:, 0:n], in0=mix[:, 0:n], in1=tm[:, 0:n], op=ALU.add), reads=[mk, tk], writes=[mk])
                            else:
                                tr.op("pool", lambda e: e.tensor_tensor(out=mt[:, m, 0:n], in0=mix[:, 0:n], in1=tm[:, 0:n], op=ALU.add), reads=[mk, tk], writes=[(mtk, m)])
                for m in range(8):
                    pk = ("ps", 4 + m % 4); pt = self.ps[4 + m % 4]
                    for c in range(8):
                        self.mm(pt[:, 0:n], wo[:, c, m * 128:(m + 1) * 128], mt[:, c, 0:n], c == 0, c == 7, reads=["w_out", (mtk, c)], writes=[pk])
                    tr.op("dve", lambda e: e.scalar_tensor_tensor(out=self.xT[:, m, t0:t0 + n], in0=pt[:, 0:n], scalar=self.mod[:, l, 16 + m, s:s + 1],
                                                                 in1=self.xT[:, m, t0:t0 + n], op0=ALU.mult, op1=ALU.add),
                          reads=[pk, "mod", ("xT", ci)], writes=[("xT", ci)])
            tr.barrier()

    def phase_ffn(self, l):
        tr = self.tr
        with ExitStack() as es:
            fcw = self.sb(es, "fcw", [128, 44, 3])
            self.dma(fcw[:], self.inp["fconv_fm"][:, l, :, :], writes=["fcw"])
            with ExitStack() as es2:
                xn = self.sb(es2, "xn2", [128, 8, T], BF16)
                self.adaln(es2, l, 1, xn)
                wr = Ring(self, es2, "wup", 2, [128, 8, 2, 128], BF16)
                hr = [Ring(self, es2, f"fh{i}", 1, [128, T]) for i in range(2)]
                cr = [Ring(self, es2, f"fc{i}", 2, [128, T]) for i in range(2)]
                ar = Ring(self, es2, "fact", 2, [128, T], BF16)
                wv = self.inp["ffn_up"][l].rearrange("(c p) n -> p c n", p=128)
                pi = 0
                for cc in range(22):
                    w, wk = wr.next()
                    self.dma(w[:, :, 0, :], wv[:, :, cc * 128:(cc + 1) * 128], writes=[wk], q="gq")
                    self.dma(w[:, :, 1, :], wv[:, :, 2816 + cc * 128:2816 + (cc + 1) * 128], writes=[wk], q="gq")
                    cs_ = []
                    for i in range(2):
                        hb, hk = hr[i].next()
                        for ci, (t0, n) in enumerate(TCH):
                            pk = ("ps", pi % 4); pt = self.ps[pi % 4]; pi += 1
                            for c in range(8):
                                self.mm(pt[:, 0:n], w[:, c, i, :], xn[:, c, t0:t0 + n], c == 0, c == 7, reads=[wk, ("xn", ci)], writes=[pk])
                            self.copy(self.evac_eng(), hb[:, t0:t0 + n], pt[:, 0:n], reads=[pk], writes=[hk])
                        cb, ck = cr[i].next()
                        j = cc + 22 * i
                        eng = "dve" if i == 0 else "pool"
                        tr.op(eng, lambda e: e.tensor_scalar(out=cb[:], in0=hb[:], scalar1=fcw[:, j, 1:2], scalar2=None, op0=ALU.mult), reads=[hk, "fcw"], writes=[ck])
                        for (o0, o1, i0, i1, k) in ((1, CTX, 0, CTX - 1, 0), (CTX + 1, T, CTX, T - 1, 0), (0, CTX - 1, 1, CTX, 2), (CTX, T - 1, CTX + 1, T, 2)):
                            tr.op("dve", lambda e: e.scalar_tensor_tensor(out=cb[:, o0:o1], in0=hb[:, i0:i1], scalar=fcw[:, j, k:k + 1], in1=cb[:, o0:o1],
                                                                       op0=ALU.mult, op1=ALU.add), reads=[hk, ck, "fcw"], writes=[ck])
                        cs_.append((cb, ck))
                    (ca, cak), (cb, cbk) = cs_
                    tr.op("act", lambda e: e.activation(out=ca[:], in_=ca[:], func=AF.Silu), reads=[cak], writes=[cak])
                    a, ak = ar.next()
                    tr.op("dve", lambda e: e.tensor_tensor(out=a[:], in0=ca[:], in1=cb[:], op=ALU.mult), reads=[cak, cbk], writes=[ak])
                    self.dma(self.scr["D_act"][cc * 128:(cc + 1) * 128, :], a[:], reads=[ak], writes=["D_act"])
                tr.barrier()
            with ExitStack() as es2:
                wd = self.sb(es2, "wdn", [128, 22, D], BF16)
                self.dma(wd[:, 0:11, :], self.inp["ffn_down"][l].rearrange("(c p) n -> p c n", p=128)[:, 0:11, :], writes=["wdn"], q="gq")
                self.dma(wd[:, 11:22, :], self.inp["ffn_down"][l].rearrange("(c p) n -> p c n", p=128)[:, 11:22, :], writes=["wdn"], q="gq")
                ar = Ring(self, es2, "fa", 2, [128, 22, 512], BF16)
                av = self.scr["D_act"].rearrange("(c p) t -> p c t", p=128)
                for ci, (t0, n) in enumerate(TCH):
                    s = 1 if ci == 0 else 0
                    a, ak = ar.next()
                    self.dma(a[:, :, 0:n], av[:, :, t0:t0 + n], reads=["D_act"], writes=[ak])
                    for m in range(8):
                        pk = ("ps", m % 4); pt = self.ps[m % 4]
                        for c in range(22):
                            self.mm(pt[:, 0:n], wd[:, c, m * 128:(m + 1) * 128], a[:, c, 0:n], c == 0, c == 21, reads=["wdn", ak], writes=[pk])
                        tr.op("dve", lambda e: e.scalar_tensor_tensor(out=self.xT[:, m, t0:t0 + n], in0=pt[:, 0:n], scalar=self.mod[:, l, 40 + m, s:s + 1],
                                                                     in1=self.xT[:, m, t0:t0 + n], op0=ALU.mult, op1=ALU.add),
                              reads=[pk, "mod", ("xT", ci)], writes=[("xT", ci)])
                tr.barrier()

    def phase_final(self):
        tr = self.tr
        with ExitStack() as es:
            nf = self.sb(es, "nf", [128, D])
            self.dma(nf[:], self.inp["normf_rep"], writes=["nf"])
            xr = Ring(self, es, "fx", 2, [128, D]); sqr = Ring(self, es, "fsq", 2, [128, D]); ssr = Ring(self, es, "fss", 2, [128, 1])
            for t in range(2, NT):
                x, xk = xr.next()
                for half in range(2):
                    pk = ("ps", (2 * t + half) % 8); pt = self.ps[(2 * t + half) % 8]
                    for j in range(4):
                        c = half * 4 + j
                        tr.op("pe", lambda e: e.transpose(pt[:, j * 128:(j + 1) * 128], self.xT[:, c, t * 128:(t + 1) * 128], self.ident[:]),
                              reads=["ident"], writes=[pk])
                    self.copy(self.evac_eng(), x[:, half * 512:(half + 1) * 512], pt[:, :], reads=[pk], writes=[xk])
                sq, sk = sqr.next(); ss, ssk = ssr.next()
                tr.op("pool", lambda e: e.tensor_tensor(out=sq[:], in0=x[:], in1=x[:], op=ALU.mult), reads=[xk], writes=[sk])
                tr.op("dve", lambda e: e.tensor_reduce(out=ss[:], in_=sq[:], axis=AX.X, op=ALU.add), reads=[sk], writes=[ssk])
                self.rstd_from_ss(ss[:], ssk, ss[:], ssk, 1.0 / D)
                tr.op("dve", lambda e: e.scalar_tensor_tensor(out=sq[:], in0=x[:], scalar=ss[:, 0:1], in1=nf[:], op0=ALU.mult, op1=ALU.mult),
                      reads=[xk, ssk, "nf"], writes=[sk])
                self.dma(self.out[(t - 2) * 128:(t - 1) * 128, :], sq[:], reads=[sk])


def _rope_tables(d):
    da = d // 2
    half = da // 2
    inv = (10000.0 ** (-np.arange(half, dtype=np.float32) / half)).astype(np.float32)
    t = np.arange(SEQ)
    rows = (t // 64).astype(np.float32)
    cols = (t % 64).astype(np.float32)
    cos = np.ones((d, T), np.float32)
    sin = np.zeros((d, T), np.float32)
    for i in range(d):
        pos = rows if i < da else cols
        ii = i % da
        ang = pos * inv[ii % half]
        cos[i, CTX:] = np.cos(ang)
        sin[i, CTX:] = np.sin(ang) * (-1.0 if ii < half else 1.0)
    return np.stack([cos, sin])


def _partner(d):
    da = d // 2
    half = da // 2
    idx = np.arange(d)
    ii = idx % da
    return np.where(ii < half, idx + half, idx - half)


def _constants():
    j = np.arange(128)[:, None]
    i = np.arange(128)[None, :]
    tri = np.stack([(j <= i), (j >= i)]).astype(np.float32)
    gmask = np.stack([np.where(i < j, 0.0, BIG),
                      np.where(j <= i, 0.0, -BIG),
                      np.where(i > j, 0.0, BIG),
                      np.where(j >= i, 0.0, -BIG)]).astype(np.float32)
    m_prev = (j >= i).astype(np.float32)
    m_next = (j <= i).astype(np.float32)
    swamask = np.stack([np.tile(m_prev, (1, 4)), np.tile(m_next, (1, 4))]).astype(np.float32)
    b32 = (j // 32 == i // 32)
    b64 = (j // 64 == i // 64)
    bmask = np.stack([b32, b64 & ~b32, ~b64]).astype(np.float32)
    selh = np.zeros((8, 8, 128), np.float32)
    for h in range(8):
        selh[h, h, :] = 1.0
    return dict(ident=np.eye(128, dtype=np.float32), tri=tri, gmask=gmask, bmask=bmask, swamask=swamask, selh=selh,
                ropeM=_rope_tables(32), ropeS=_rope_tables(64))


def _fm(v, p=128):
    v = np.asarray(v, np.float32)
    lead = v.shape[:-1]
    n = v.shape[-1] // p
    return np.ascontiguousarray(np.moveaxis(v.reshape(lead + (n, p)), -1, 0))


def prepare_shared(inp, L=DEPTH):
    f = lambda k: np.asarray(inp[k], np.float32)[:L] if k not in ("norm_f",) else np.asarray(inp[k], np.float32)
    w_in = f("w_in")
    p32 = _partner(32)
    p64 = _partner(64)
    sqp = (np.arange(8)[:, None] * 64 + p64[None, :]).reshape(-1)
    skp = (np.arange(2)[:, None] * 64 + p64[None, :]).reshape(-1)
    w_fm = np.concatenate([w_in[:, :, 0:640], w_in[:, :, 640:672], w_in[:, :, 640:672][:, :, p32],
                           w_in[:, :, 672:1184], w_in[:, :, 672:1184][:, :, sqp],
                           w_in[:, :, 1184:1312], w_in[:, :, 1184:1312][:, :, skp],
                           w_in[:, :, 1440:2976], np.zeros((L, D, 64), np.float32), w_in[:, :, 3520:6592]], axis=2)
    assert w_fm.shape[2] == NFM
    w_tm = np.concatenate([w_in[:, :, 1312:1440], w_in[:, :, 2976:3488], w_in[:, :, 3488:3520]], axis=2)
    wuq = f("w_uq").reshape(L, 384, 8, 96)
    wukv = f("w_ukv").reshape(L, 256, 8, 128)
    sh = dict(
        w_mod=f("w_mod"), b_mod_fm=_fm(f("b_mod")),
        norm1_fm=_fm(f("norm1")), norm2_fm=_fm(f("norm2")), normf_rep=np.ascontiguousarray(np.broadcast_to(f("norm_f"), (128, D))),
        w_fm=np.ascontiguousarray(w_fm), w_tm=np.ascontiguousarray(w_tm),
        qnorm_fm=_fm(f("mla_q_norm")), kvnorm_fm=_fm(f("mla_kv_norm")),
        wuq_n=np.ascontiguousarray(wuq[..., 0:64].reshape(L, 384, 512)),
        wuq_r=np.ascontiguousarray(wuq[..., 64:96].reshape(L, 384, 256)),
        wuq_rp=np.ascontiguousarray(wuq[..., 64:96][..., p32].reshape(L, 384, 256)),
        wukv_k=np.ascontiguousarray(wukv[..., 0:64].reshape(L, 256, 512)),
        wukv_v=np.ascontiguousarray(wukv[..., 64:128].reshape(L, 256, 512)),
        sink_rep=np.ascontiguousarray(np.broadcast_to(f("swa_sink"), (64, L, 8))),
        gconv_fm=np.ascontiguousarray(f("gdn_conv").reshape(L, 3, 24, 64).transpose(3, 0, 2, 1)),
        alog_rep=np.ascontiguousarray(np.broadcast_to(f("gdn_a_log").reshape(L, 16), (128, L, 16))),
        dtb_rep=np.ascontiguousarray(np.broadcast_to(f("gdn_dt_bias").reshape(L, 16), (128, L, 16))),
        gnorm_rep=np.ascontiguousarray(np.broadcast_to(f("gdn_norm"), (128, L, 64))),
        w_pa=f("w_branch_a"), w_pb=f("w_branch_b"), w_pc=f("w_branch_c"), w_out=f("w_out"),
        ffn_up=f("ffn_up"), fconv_fm=np.ascontiguousarray(f("ffn_conv").reshape(L, 3, 44, 128).transpose(3, 0, 2, 1)),
        ffn_down=f("ffn_down"),
    )
    sh.update(_constants())
    return sh


def per_core(inp, b):
    x = np.asarray(inp["x"], np.float32)[b]
    ctx = np.asarray(inp["ctx"], np.float32)[b]
    c = np.asarray(inp["c"], np.float32)[b]
    cc = np.asarray(inp["c_ctx"], np.float32)
    return dict(xin=np.ascontiguousarray(np.concatenate([ctx, x], 0)),
                c_fm=np.ascontiguousarray(np.stack([_fm(c), _fm(cc)], axis=-1)))


_CACHE = {}


def kernel(**inputs):
    if "nc" not in _CACHE:
        _CACHE["nc"] = Builder().build()
    nc = _CACHE["nc"]
    shared = prepare_shared(inputs)
    in_maps = []
    for b in range(8):
        m = dict(shared)
        m.update(per_core(inputs, b))
        in_maps.append(m)
    res = run_bass_kernel_spmd(nc, in_maps, core_ids=list(range(8)))
    return np.stack([np.asarray(r["out"], np.float32) for r in res.results], axis=0)
```

```python
import numpy as np
from contextlib import ExitStack
import concourse.bass as bass
import concourse.mybir as mybir
from concourse.bass_utils import run_bass_kernel_spmd

F32 = mybir.dt.float32
BF16 = mybir.dt.bfloat16
AF = mybir.ActivationFunctionType
ALU = mybir.AluOpType
AX = mybir.AxisListType

D = 1024
SEQ = 2048
CTX = 256
T = SEQ + CTX
NT = T // 128
DEPTH = 4
EPS = 1e-6
TCH = [(0, 256), (256, 512), (768, 512), (1280, 512), (1792, 512)]
BIG = 1.0e5

CQ, CKV, KR, KRP, SQ, SQP, SK, SKP, GQ, PAD, GATE, NFM = 0, 384, 640, 672, 704, 1216, 1728, 1856, 1984, 3520, 3584, 6656
SV, ZC, AB, NTM = 0, 128, 640, 672


def tile_chunk(t):
    return 0 if t < 2 else 1 + (t - 2) // 4


class Trk:
    ISSUER = {"pe": "pe", "act": "act", "dve": "dve", "pool": "pool", "sp": "sp", "gq": "pool"}
    NSLOT = {"sp": 24, "gq": 8}

    def __init__(self, nc, es):
        self.nc = nc
        self.eng = {"pe": nc.tensor, "act": nc.scalar, "dve": nc.vector, "pool": nc.gpsimd, "sp": nc.sync}
        self.sem = {e: es.enter_context(nc.semaphore("sem_" + e)) for e in ("pe", "act", "dve", "pool")}
        for q, k in self.NSLOT.items():
            for i in range(k):
                self.sem[(q, i)] = es.enter_context(nc.semaphore(f"sem_{q}{i}"))
        self.cnt = {e: 0 for e in self.sem}
        self.nq = {q: 0 for q in self.NSLOT}
        self.lastw = {}
        self.readers = {}
        self.waited = {}
        self.n_wait = 0
        self.n_op = 0

    def _need(self, issuer, eng, dep):
        de, ds = dep
        if de == "pe" and eng == "pe":
            return
        k = (issuer, de)
        if self.waited.get(k, 0) >= ds:
            return
        self.waited[k] = ds
        self.eng[issuer].wait_ge(self.sem[de], ds)
        self.n_wait += 1

    def op(self, eng, emit, reads=(), writes=()):
        issuer = self.ISSUER[eng]
        for key in reads:
            w = self.lastw.get(key)
            if w is not None:
                self._need(issuer, eng, w)
            if isinstance(key, tuple) and key[0] == "ps":
                for rk, tok in self.readers.get(key, {}).items():
                    if rk != eng:
                        self._need(issuer, eng, tok)
        for key in writes:
            w = self.lastw.get(key)
            if w is not None:
                self._need(issuer, eng, w)
            for tok in self.readers.get(key, {}).values():
                self._need(issuer, eng, tok)
        if eng in self.NSLOT:
            sk = (eng, self.nq[eng] % self.NSLOT[eng])
            self.nq[eng] += 1
            if self.cnt[sk] > 0:
                self._need(issuer, eng, (sk, self.cnt[sk]))
            inc = 16
        else:
            sk = eng
            inc = 1
        inst = emit(self.eng[issuer])
        self.cnt[sk] += inc
        inst.then_inc(self.sem[sk], inc)
        tok = (sk, self.cnt[sk])
        for key in reads:
            self.readers.setdefault(key, {})[sk] = tok
        for key in writes:
            self.lastw[key] = tok
            self.readers[key] = {}
        self.n_op += 1
        return inst

    def barrier(self):
        for issuer in ("pe", "act", "dve", "pool", "sp"):
            for e, c in self.cnt.items():
                if c > 0:
                    self._need(issuer, "x", (e, c))
        self.lastw.clear()
        self.readers.clear()

    def finish(self, eng="sp"):
        for e, c in self.cnt.items():
            if c > 0:
                self._need(eng, "x", (e, c))


class Ring:
    def __init__(self, b, es, name, n, shape, dt=F32):
        self.tiles = [b.sb(es, f"{name}{i}", shape, dt) for i in range(n)]
        self.name = name
        self.n = n
        self.i = 0

    def next(self):
        j = self.i % self.n
        self.i += 1
        return self.tiles[j], (self.name, j)


class Builder:
    def __init__(self, n_layers=DEPTH, stop=None, taps=()):
        self.n_layers = n_layers
        self.stop = stop
        self.taps = set(taps)
        self.nc = bass.Bass("TRN2", target_bir_lowering=False)
        self.inp = {}
        self.scr = {}
        self.ev = 0

    def din(self, name, shape, dt=F32):
        self.inp[name] = self.nc.dram_tensor(name, list(shape), dt, kind="ExternalInput").ap()
        return self.inp[name]

    def dscr(self, name, shape, dt=F32, force=False):
        kind = "ExternalOutput" if (name in self.taps or force) else "Internal"
        self.scr[name] = self.nc.dram_tensor(name, list(shape), dt, kind=kind).ap()
        return self.scr[name]

    def declare(self):
        L = self.n_layers
        d = self.din
        d("xin", [T, D]); d("c_fm", [128, 8, 2])
        d("w_mod", [L, D, 6 * D]); d("b_mod_fm", [128, L, 48])
        d("norm1_fm", [128, L, 8]); d("norm2_fm", [128, L, 8]); d("normf_rep", [128, D])
        d("w_fm", [L, D, NFM]); d("w_tm", [L, D, NTM])
        d("qnorm_fm", [128, L, 3]); d("kvnorm_fm", [128, L, 2])
        d("wuq_n", [L, 384, 512]); d("wuq_r", [L, 384, 256]); d("wuq_rp", [L, 384, 256])
        d("wukv_k", [L, 256, 512]); d("wukv_v", [L, 256, 512])
        d("sink_rep", [64, L, 8])
        d("gconv_fm", [128, L, 12, 3]); d("bd64", [128, 128]); d("alog_rep", [128, L, 16]); d("dtb_rep", [128, L, 16]); d("gnorm_rep", [128, L, 64])
        d("w_pa", [L, 512, D]); d("w_pb", [L, 512, D]); d("w_pc", [L, 512, D]); d("w_out", [L, D, D])
        d("ffn_up", [L, D, 5632]); d("fconv_fm", [128, L, 44, 3]); d("ffn_down", [L, 2816, D])
        d("ident", [128, 128]); d("ropeM", [2, 32, T]); d("ropeS", [2, 64, T])
        d("tri", [2, 128, 128]); d("gmask", [4, 128, 128]); d("bmask", [3, 128, 128]); d("swamask", [2, 128, 512]); d("selh", [8, 8, 128])
        self.out = self.nc.dram_tensor("out", [SEQ, D], F32, kind="ExternalOutput").ap()
        s = self.dscr
        s("D_fm", [NFM, T]); s("D_tm", [T, NTM]); s("D_gate", [3072, T], BF16)
        s("D_ya", [512, T], BF16); s("D_yb", [512, T], BF16); s("D_yc", [512, T], BF16)
        s("D_gq", [512, T]); s("D_gk", [512, T]); s("D_ktm", [8, T, 64]); s("D_vtm", [8, T, 64])
        s("D_act", [2816, T], BF16)
        if "D_x" in self.taps:
            s("D_x", [D, T], F32, True); s("D_xn", [D, T], BF16, True); s("D_mod", [128, L * 48 * 2], F32, True); s("D_oc", [T, 512], F32, True)

    def sb(self, es, name, shape, dt=F32):
        self.uid = getattr(self, "uid", 0) + 1
        return es.enter_context(self.nc.sbuf_tensor(f"s{self.uid}_{name}", list(shape), dt))

    def evac_eng(self):
        self.ev += 1
        return "act" if self.ev % 2 else "dve"

    def copy(self, eng, out, in_, reads, writes):
        if eng == "act":
            return self.tr.op("act", lambda e: e.copy(out=out, in_=in_), reads=reads, writes=writes)
        return self.tr.op(eng, lambda e: e.tensor_copy(out=out, in_=in_), reads=reads, writes=writes)

    def dma(self, out, in_, reads=(), writes=(), q="sp"):
        return self.tr.op(q, lambda e: e.dma_start(out=out, in_=in_), reads=reads, writes=writes)

    def mm(self, out, lhsT, rhs, start, stop, reads, writes):
        return self.tr.op("pe", lambda e: e.matmul(out, lhsT, rhs, start=start, stop=stop), reads=reads, writes=writes)

    def rstd_from_ss(self, ss_ap, ss_key, r_ap, r_key, scale):
        tr = self.tr
        tr.op("act", lambda e: e.activation(out=r_ap, in_=ss_ap, func=AF.Sqrt, bias=self.cst[0:r_ap.shape[0], 0:1], scale=scale),
              reads=[ss_key, "cst"], writes=[r_key])
        tr.op("dve", lambda e: e.reciprocal(out=r_ap, in_=r_ap), reads=[r_key], writes=[r_key])

    def build(self):
        nc = self.nc
        self.declare()
        with ExitStack() as es:
            self.tr = tr = Trk(nc, es)
            self.ps = [es.enter_context(nc.psum_tensor(f"ps{i}", [128, 512], F32)) for i in range(8)]
            self.xT = self.sb(es, "xT", [128, 8, T])
            self.ident = self.sb(es, "ident", [128, 128])
            self.ones32 = self.sb(es, "ones32", [128, 128])
            self.ones16 = self.sb(es, "ones16", [128, 64], BF16)
            self.cst = self.sb(es, "cst", [128, 4])
            self.mod = self.sb(es, "mod", [128, self.n_layers, 48, 2])
            self.n1 = self.sb(es, "n1", [128, self.n_layers, 8])
            self.n2 = self.sb(es, "n2", [128, self.n_layers, 8])
            self.gs = self.sb(es, "gs", [128, 2, 8, 2])
            self.dma(self.ident[:], self.inp["ident"], writes=["ident"])
            self.dma(self.n1[:], self.inp["norm1_fm"], writes=["n1"])
            self.dma(self.n2[:], self.inp["norm2_fm"], writes=["n2"])
            tr.op("dve", lambda e: e.memset(self.ones32[:], 1.0), writes=["ones32"])
            tr.op("dve", lambda e: e.memset(self.ones16[:], 1.0), writes=["ones16"])
            tr.op("dve", lambda e: e.memset(self.cst[:, 0:1], EPS), writes=["cst"])
            tr.op("dve", lambda e: e.memset(self.cst[:, 1:2], 1.0), writes=["cst"])
            tr.op("dve", lambda e: e.memset(self.cst[:, 2:3], 0.0), writes=["cst"])
            self.phase_load_x()
            self.phase_mod()
            done = self.stop == "mod"
            for l in range(self.n_layers):
                if done:
                    break
                for name, fn in (("norm1", lambda: self.phase_adaln_proj(l)), ("mla", lambda: self.phase_mla(l)),
                                 ("swa", lambda: self.phase_swa(l)), ("gdnprep", lambda: self.phase_gdn_prep(l)),
                                 ("gdn", lambda: self.phase_gdn(l)), ("merge", lambda: self.phase_merge(l)),
                                 ("ffn", lambda: self.phase_ffn(l))):
                    fn()
                    if self.stop == f"{name}{l}" or (self.stop or "").startswith(name + "_"):
                        done = True
                        break
            if "D_x" in self.taps:
                tr.barrier()
                for ci, (t0, n) in enumerate(TCH):
                    self.dma(self.scr["D_x"].rearrange("(c p) t -> p c t", p=128)[:, :, t0:t0 + n], self.xT[:, :, t0:t0 + n])
                self.dma(self.scr["D_mod"], self.mod[:].rearrange("p l j s -> p (l j s)"))
            if not done:
                self.phase_final()
            tr.barrier()
            tr.finish("sp")
        return nc

    def phase_load_x(self):
        tr = self.tr
        with ExitStack() as es:
            ring = Ring(self, es, "xl", 2, [128, D])
            for t in range(NT):
                xt, xk = ring.next()
                self.dma(xt[:], self.inp["xin"][t * 128:(t + 1) * 128, :], writes=[xk])
                for half in range(2):
                    pk = ("ps", (2 * t + half) % 8)
                    pt = self.ps[(2 * t + half) % 8]
                    for j in range(4):
                        c = half * 4 + j
                        tr.op("pe", lambda e: e.transpose(pt[:, j * 128:(j + 1) * 128], xt[:, c * 128:(c + 1) * 128], self.ident[:]),
                              reads=[xk, "ident"], writes=[pk])
                    self.copy(self.evac_eng(), self.xT[:, half * 4:half * 4 + 4, t * 128:(t + 1) * 128],
                              pt[:].rearrange("p (c t) -> p c t", c=4), reads=[pk], writes=[("xT", tile_chunk(t))])
            tr.barrier()

    def phase_mod(self):
        tr = self.tr
        with ExitStack() as es:
            cs = self.sb(es, "cs", [128, 8, 2])
            bm = self.sb(es, "bm", [128, self.n_layers, 48])
            ring = Ring(self, es, "wm", 2, [128, 8, 512])
            self.dma(cs[:], self.inp["c_fm"], writes=["cs"])
            self.dma(bm[:], self.inp["b_mod_fm"], writes=["bm"])
            tr.op("act", lambda e: e.activation(out=cs[:], in_=cs[:], func=AF.Silu), reads=["cs"], writes=["cs"])
            for l in range(self.n_layers):
                wv = self.inp["w_mod"][l].rearrange("(c p) n -> p c n", p=128)
                pk = ("ps", l % 2)
                pt = self.ps[l % 2][:, 0:96].rearrange("p (j s) -> p j s", s=2)
                for g in range(12):
                    w, wk = ring.next()
                    self.dma(w[:], wv[:, :, g * 512:(g + 1) * 512], writes=[wk])
                    for m in range(4):
                        j = g * 4 + m
                        for c in range(8):
                            self.mm(pt[:, j, :], w[:, c, m * 128:(m + 1) * 128], cs[:, c, :], c == 0, c == 7,
                                    reads=[wk, "cs"], writes=[pk])
                for s in range(2):
                    tr.op("dve", lambda e: e.tensor_tensor(out=self.mod[:, l, :, s], in0=pt[:, :, s], in1=bm[:, l, :], op=ALU.add),
                          reads=[pk, "bm"], writes=["mod"])
            tr.barrier()

    def make_gs(self, l, which):
        nrm = self.n1 if which == 0 else self.n2
        sc0 = 8 + which * 24
        for s in range(2):
            self.tr.op("dve", lambda e: e.scalar_tensor_tensor(out=self.gs[:, which, :, s], in0=self.mod[:, l, sc0:sc0 + 8, s], scalar=1.0,
                                                             in1=nrm[:, l, :], op0=ALU.add, op1=ALU.mult),
                       reads=["mod", "n1", "n2"], writes=["gs"])

    def adaln(self, es, l, which, xn):
        tr = self.tr
        self.make_gs(l, which)
        sh0 = which * 24
        es = ExitStack()
        sqr = Ring(self, es, f"sq{which}", 2, [128, 512])
        rr = Ring(self, es, f"rr{which}", 2, [128, 512])
        tmpr = Ring(self, es, f"tm{which}", 2, [128, 512])
        for ci, (t0, n) in enumerate(TCH):
            s = 1 if ci == 0 else 0
            pk = ("ps", ci % 2)
            pt = self.ps[ci % 2]
            for c in range(8):
                sq, sk = sqr.next()
                tr.op("act", lambda e: e.activation(out=sq[:, 0:n], in_=self.xT[:, c, t0:t0 + n], func=AF.Square),
                      reads=[("xT", ci)], writes=[sk])
                self.mm(pt[:, 0:n], self.ones32[:], sq[:, 0:n], c == 0, c == 7, reads=[sk, "ones32"], writes=[pk])
            r, rk = rr.next()
            self.rstd_from_ss(pt[:, 0:n], pk, r[:, 0:n], rk, 1.0 / D)
            for c in range(8):
                tm, tk = tmpr.next()
                tr.op("dve", lambda e: e.scalar_tensor_tensor(out=tm[:, 0:n], in0=self.xT[:, c, t0:t0 + n], scalar=self.gs[:, which, c, s:s + 1],
                                                             in1=r[:, 0:n], op0=ALU.mult, op1=ALU.mult),
                      reads=[("xT", ci), "gs", rk], writes=[tk])
                tr.op("act", lambda e: e.activation(out=xn[:, c, t0:t0 + n], in_=tm[:, 0:n], func=AF.Identity,
                                                    bias=self.mod[:, l, sh0 + c, s:s + 1], scale=1.0),
                      reads=[tk, "mod"], writes=[("xn", ci)])
        keep = {k: v for k, v in tr.lastw.items() if isinstance(k, tuple) and k[0] == "xn"}
        tr.barrier()
        tr.lastw.update(keep)
        es.close()

    def phase_adaln_proj(self, l):
        tr = self.tr
        with ExitStack() as es:
            xn = self.sb(es, "xn", [128, 8, T], BF16)
            self.adaln(es, l, 0, xn)
            if "D_x" in self.taps:
                for ci, (t0, n) in enumerate(TCH):
                    self.dma(self.scr["D_xn"].rearrange("(c p) t -> p c t", p=128)[:, :, t0:t0 + n], xn[:, :, t0:t0 + n], reads=[("xn", ci)])
            wr = Ring(self, es, "wfm", 2, [128, 8, 512], BF16)
            st = Ring(self, es, "stg", 3, [128, 512])
            st16 = Ring(self, es, "stg16", 2, [128, 512], BF16)
            wv = self.inp["w_fm"][l].rearrange("(c p) n -> p c n", p=128)
            pi = 0
            for g in range(NFM // 512):
                w, wk = wr.next()
                self.dma(w[:], wv[:, :, g * 512:(g + 1) * 512], writes=[wk], q="gq")
                col = g * 512
                while col < (g + 1) * 512:
                    if col < KR:
                        mc = 128
                    elif col < SQ:
                        mc = 32
                    elif col < PAD:
                        mc = 64
                    elif col < GATE:
                        col += 64
                        continue
                    else:
                        mc = 128
                    lo = col - g * 512
                    for ci, (t0, n) in enumerate(TCH):
                        pk = ("ps", pi % 4)
                        pt = self.ps[pi % 4]
                        pi += 1
                        for c in range(8):
                            self.mm(pt[0:mc, 0:n], w[:, c, lo:lo + mc], xn[:, c, t0:t0 + n], c == 0, c == 7,
                                    reads=[wk, ("xn", ci)], writes=[pk])
                        if col >= GATE:
                            s16, sk = st16.next()
                            tr.op("act", lambda e: e.activation(out=s16[:, 0:n], in_=pt[:, 0:n], func=AF.Sigmoid), reads=[pk], writes=[sk])
                            self.dma(self.scr["D_gate"][col - GATE:col - GATE + 128, t0:t0 + n], s16[:, 0:n], reads=[sk], writes=[("D_gate", ci)])
                        else:
                            s32, sk = st.next()
                            self.copy(self.evac_eng(), s32[0:mc, 0:n], pt[0:mc, 0:n], reads=[pk], writes=[sk])
                            self.dma(self.scr["D_fm"][col:col + mc, t0:t0 + n], s32[0:mc, 0:n], reads=[sk], writes=["D_fm"])
                    col += mc
            wt = self.sb(es, "wtm", [128, 8, NTM], BF16)
            self.dma(wt[:], self.inp["w_tm"][l].rearrange("(c p) n -> p c n", p=128), writes=["wtm"], q="gq")
            stm = Ring(self, es, "stm", 2, [128, NTM])
            for t in range(NT):
                ci = tile_chunk(t)
                pa, pb = self.ps[4 + (t % 2) * 2], self.ps[5 + (t % 2) * 2]
                ka, kb = ("ps", 4 + (t % 2) * 2), ("ps", 5 + (t % 2) * 2)
                for c in range(8):
                    self.mm(pa[:, 0:512], xn[:, c, t * 128:(t + 1) * 128], wt[:, c, 0:512], c == 0, c == 7, reads=["wtm", ("xn", ci)], writes=[ka])
                for c in range(8):
                    self.mm(pb[:, 0:160], xn[:, c, t * 128:(t + 1) * 128], wt[:, c, 512:672], c == 0, c == 7, reads=["wtm", ("xn", ci)], writes=[kb])
                s, sk = stm.next()
                self.copy("act", s[:, 0:512], pa[:, 0:512], reads=[ka], writes=[sk])
                self.copy("dve", s[:, 512:672], pb[:, 0:160], reads=[kb], writes=[sk])
                self.dma(self.scr["D_tm"][t * 128:(t + 1) * 128, :], s[:], reads=[sk], writes=["D_tm"])
            tr.barrier()

    def rms_fm_from_dram(self, es, row0, nchunk, gain, dst, tag):
        tr = self.tr
        ld = Ring(self, es, f"ld{tag}", 2, [128, nchunk, 512])
        sqr = Ring(self, es, f"sqn{tag}", 2, [128, 512])
        rr = Ring(self, es, f"rn{tag}", 2, [128, 512])
        src = self.scr["D_fm"][row0:row0 + nchunk * 128, :].rearrange("(c p) t -> p c t", p=128)
        for ci, (t0, n) in enumerate(TCH):
            x, xk = ld.next()
            self.dma(x[:, :, 0:n], src[:, :, t0:t0 + n], reads=["D_fm"], writes=[xk])
            pk = ("ps", 6 + ci % 2)
            pt = self.ps[6 + ci % 2]
            for c in range(nchunk):
                sq, sk = sqr.next()
                tr.op("act", lambda e: e.activation(out=sq[:, 0:n], in_=x[:, c, 0:n], func=AF.Square), reads=[xk], writes=[sk])
                self.mm(pt[:, 0:n], self.ones32[:], sq[:, 0:n], c == 0, c == nchunk - 1, reads=[sk, "ones32"], writes=[pk])
            r, rk = rr.next()
            self.rstd_from_ss(pt[:, 0:n], pk, r[:, 0:n], rk, 1.0 / (nchunk * 128))
            for c in range(nchunk):
                tr.op("dve", lambda e: e.scalar_tensor_tensor(out=dst[:, c, t0:t0 + n], in0=x[:, c, 0:n], scalar=gain[:, c:c + 1],
                                                             in1=r[:, 0:n], op0=ALU.mult, op1=ALU.mult),
                      reads=[xk, rk, "gain" + tag], writes=[(tag, ci)])

    def rope_from(self, src_ap, src_key, srcp_ap, srcp_key, cos_ap, sin_ap, dst_ap, dst_key, tmpa, tmpb, ka, kb):
        tr = self.tr
        tr.op("dve", lambda e: e.tensor_tensor(out=tmpa, in0=src_ap, in1=cos_ap, op=ALU.mult), reads=[src_key, "rope"], writes=[ka])
        tr.op("pool" if srcp_key[0] != "ps" else "dve", lambda e: e.tensor_tensor(out=tmpb, in0=srcp_ap, in1=sin_ap, op=ALU.mult),
              reads=[srcp_key, "rope"], writes=[kb])
        tr.op("dve", lambda e: e.tensor_tensor(out=dst_ap, in0=tmpa, in1=tmpb, op=ALU.add), reads=[ka, kb], writes=[dst_key])

    def phase_mla(self, l):
        tr = self.tr
        scale = 96.0 ** -0.5
        with ExitStack() as es:
            qg = self.sb(es, "qg", [128, 3]); kg = self.sb(es, "kg", [128, 2])
            self.dma(qg[:], self.inp["qnorm_fm"][:, l, :], writes=["gaincqn"])
            self.dma(kg[:], self.inp["kvnorm_fm"][:, l, :], writes=["gainckvn"])
            cqn = self.sb(es, "cqn", [128, 3, T], BF16)
            ckvn = self.sb(es, "ckvn", [128, 2, T], BF16)
            cosM = self.sb(es, "cosM", [32, T]); sinM = self.sb(es, "sinM", [32, T])
            self.dma(cosM[:], self.inp["ropeM"][0], writes=["rope"])
            self.dma(sinM[:], self.inp["ropeM"][1], writes=["rope"])
            KrT = self.sb(es, "KrT", [32, T], BF16)
            Vt = self.sb(es, "Vt", [128, NT, 512], BF16)
            wqn = self.sb(es, "wqn", [128, 3, 512], BF16); wqr = self.sb(es, "wqr", [128, 3, 256], BF16)
            wqp = self.sb(es, "wqp", [128, 3, 256], BF16)
            wkn = self.sb(es, "wkn", [128, 2, 512], BF16); wkv = self.sb(es, "wkv", [128, 2, 512], BF16)
            for wt_, nm in ((wqn, "wuq_n"), (wqr, "wuq_r"), (wqp, "wuq_rp"), (wkn, "wukv_k"), (wkv, "wukv_v")):
                self.dma(wt_[:], self.inp[nm][l].rearrange("(c p) n -> p c n", p=128), writes=[nm], q="gq")
            with ExitStack() as es2:
                self.rms_fm_from_dram(es2, CQ, 3, qg, cqn, "cqn")
                self.rms_fm_from_dram(es2, CKV, 2, kg, ckvn, "ckvn")
                kl = Ring(self, es2, "krl", 2, [32, 2, 512])
                ta = Ring(self, es2, "kta", 2, [32, 512]); tb = Ring(self, es2, "ktb", 2, [32, 512])
                for ci, (t0, n) in enumerate(TCH):
                    k2, kk = kl.next()
                    self.dma(k2[:, :, 0:n], self.scr["D_fm"][KR:KR + 64, :].rearrange("(a p) t -> p a t", p=32)[:, :, t0:t0 + n],
                             reads=["D_fm"], writes=[kk])
                    a_, ak = ta.next(); b_, bk = tb.next()
                    self.rope_from(k2[:, 0, 0:n], kk, k2[:, 1, 0:n], kk, cosM[:, t0:t0 + n], sinM[:, t0:t0 + n],
                                   KrT[:, t0:t0 + n], ("KrT", ci), a_[:, 0:n], b_[:, 0:n], ak, bk)
                for t in range(NT):
                    ci = tile_chunk(t)
                    pk = ("ps", 4 + t % 2); pt = self.ps[4 + t % 2]
                    for c in range(2):
                        self.mm(pt[:, :], ckvn[:, c, t * 128:(t + 1) * 128], wkv[:, c, :], c == 0, c == 1, reads=[("ckvn", ci), "wukv_v"], writes=[pk])
                    self.copy(self.evac_eng(), Vt[:, t, :], pt[:, :], reads=[pk], writes=[("Vt", t)])
                tr.barrier()
            if self.stop == "mla_prep":
                return
            with ExitStack() as es2:
                qnr = Ring(self, es2, "QnT", 2, [64, T], BF16); qrr = Ring(self, es2, "QrT", 2, [32, T], BF16)
                knr = Ring(self, es2, "KnT", 2, [64, T], BF16)
                ta = Ring(self, es2, "qta", 2, [32, 512]); tb = Ring(self, es2, "qtb", 2, [32, 512])
                pr = Ring(self, es2, "Pexp", 4, [128, 512], BF16)
                rdr = Ring(self, es2, "rden", 2, [64, 512])
                yar = Ring(self, es2, "yah", 2, [64, T], BF16)
                for h in range(8):
                    Qn, qnk = qnr.next(); Qr, qrk = qrr.next(); Kn, knk = knr.next()
                    for ci, (t0, n) in enumerate(TCH):
                        p0, p1, p2, p3 = self.ps[4], self.ps[5], self.ps[6], self.ps[7]
                        for c in range(3):
                            self.mm(p0[0:64, 0:n], wqn[:, c, h * 64:(h + 1) * 64], cqn[:, c, t0:t0 + n], c == 0, c == 2, reads=["wuq_n", ("cqn", ci)], writes=[("ps", 4)])
                        for c in range(3):
                            self.mm(p1[0:32, 0:n], wqr[:, c, h * 32:(h + 1) * 32], cqn[:, c, t0:t0 + n], c == 0, c == 2, reads=["wuq_r", ("cqn", ci)], writes=[("ps", 5)])
                        for c in range(3):
                            self.mm(p2[0:32, 0:n], wqp[:, c, h * 32:(h + 1) * 32], cqn[:, c, t0:t0 + n], c == 0, c == 2, reads=["wuq_rp", ("cqn", ci)], writes=[("ps", 6)])
                        for c in range(2):
                            self.mm(p3[0:64, 0:n], wkn[:, c, h * 64:(h + 1) * 64], ckvn[:, c, t0:t0 + n], c == 0, c == 1, reads=["wukv_k", ("ckvn", ci)], writes=[("ps", 7)])
                        self.copy("act", Qn[:, t0:t0 + n], p0[0:64, 0:n], reads=[("ps", 4)], writes=[(qnk, ci)])
                        self.copy("act", Kn[:, t0:t0 + n], p3[0:64, 0:n], reads=[("ps", 7)], writes=[(knk, ci)])
                        a_, ak = ta.next(); b_, bk = tb.next()
                        self.rope_from(p1[0:32, 0:n], ("ps", 5), p2[0:32, 0:n], ("ps", 6), cosM[:, t0:t0 + n], sinM[:, t0:t0 + n],
                                       Qr[:, t0:t0 + n], (qrk, ci), a_[:, 0:n], b_[:, 0:n], ak, bk)
                    if self.stop == "mla_proj":
                        break
                    ya, yk = yar.next()
                    for ci, (t0, n) in enumerate(TCH):
                        if self.stop == "mla_ctx" and ci > 0:
                            break
                        kts = [0, 1] if ci == 0 else list(range(NT))
                        po, pd = self.ps[3], self.ps[2]
                        def s_stage(i, kt):
                            kci = tile_chunk(kt)
                            bank = (0, 1, 4)[i % 3]
                            sk_ = ("ps", bank); psn = self.ps[bank]
                            self.mm(psn[:, 0:n], Kn[:, kt * 128:(kt + 1) * 128], Qn[:, t0:t0 + n], True, False,
                                    reads=[(knk, kci), (qnk, ci)], writes=[sk_])
                            self.mm(psn[:, 0:n], KrT[:, kt * 128:(kt + 1) * 128], Qr[:, t0:t0 + n], False, True,
                                    reads=[("KrT", kci), (qrk, ci)], writes=[sk_])
                            P, Pk = pr.next()
                            tr.op("act", lambda e: e.activation(out=P[:, 0:n], in_=psn[:, 0:n], func=AF.Exp, scale=scale), reads=[sk_], writes=[Pk])
                            return P, Pk

                        def pv_stage(i, kt, P, Pk):
                            self.mm(po[0:64, 0:n], Vt[:, kt, h * 64:(h + 1) * 64], P[:, 0:n], i == 0, i == len(kts) - 1, reads=[("Vt", kt), Pk], writes=[("ps", 3)])
                            self.mm(pd[0:64, 0:n], self.ones16[:, :], P[:, 0:n], i == 0, i == len(kts) - 1, reads=["ones16", Pk], writes=[("ps", 2)])

                        pend = []
                        for i, kt in enumerate(kts):
                            pend.append((i, kt) + s_stage(i, kt))
                            if len(pend) > 2:
                                pv_stage(*pend.pop(0))
                        while pend:
                            pv_stage(*pend.pop(0))
                        rd, rk = rdr.next()
                        tr.op("dve", lambda e: e.reciprocal(out=rd[:, 0:n], in_=pd[0:64, 0:n]), reads=[("ps", 2)], writes=[rk])
                        tr.op("dve", lambda e: e.tensor_tensor(out=ya[:, t0:t0 + n], in0=po[0:64, 0:n], in1=rd[:, 0:n], op=ALU.mult),
                              reads=[("ps", 3), rk], writes=[yk])
                    self.dma(self.scr["D_ya"][h * 64:(h + 1) * 64, :], ya[:], reads=[yk], writes=["D_ya"])
                tr.barrier()

    def phase_swa(self, l):
        tr = self.tr
        with ExitStack() as es:
            QT = self.sb(es, "sQT", [64, 8, T], BF16)
            KT = self.sb(es, "sKT", [64, 2, T], BF16)
            Vs = self.sb(es, "sVs", [128, NT, 128], BF16)
            cosS = self.sb(es, "cosS", [64, T]); sinS = self.sb(es, "sinS", [64, T])
            msk = self.sb(es, "smask", [128, 2, 512], BF16)
            snk = self.sb(es, "snk", [64, 8])
            self.dma(cosS[:], self.inp["ropeS"][0], writes=["rope"])
            self.dma(sinS[:], self.inp["ropeS"][1], writes=["rope"])
            self.dma(msk[:], self.inp["swamask"].rearrange("a p n -> p a n"), writes=["smask"], q="gq")
            self.dma(snk[:], self.inp["sink_rep"][:, l, :], writes=["snk"])
            tr.op("act", lambda e: e.activation(out=snk[:], in_=snk[:], func=AF.Exp), reads=["snk"], writes=["snk"])
            self.dma(Vs[:], self.scr["D_tm"][:, SV:SV + 128].rearrange("(t p) c -> p t c", p=128), reads=["D_tm"], writes=["sVs"], q="gq")
            with ExitStack() as es2:
                ld = Ring(self, es2, "sld", 1, [64, T]); ldp = Ring(self, es2, "sldp", 1, [64, T])
                ta = Ring(self, es2, "sta", 1, [64, T]); tb = Ring(self, es2, "stb", 1, [64, T])
                for j in range(10):
                    r0, rp = (SQ + j * 64, SQP + j * 64) if j < 8 else (SK + (j - 8) * 64, SKP + (j - 8) * 64)
                    dst = QT[:, j, :] if j < 8 else KT[:, j - 8, :]
                    x, xk = ld.next(); xp, xpk = ldp.next()
                    self.dma(x[:], self.scr["D_fm"][r0:r0 + 64, :], reads=["D_fm"], writes=[xk])
                    self.dma(xp[:], self.scr["D_fm"][rp:rp + 64, :], reads=["D_fm"], writes=[xpk])
                    a_, ak = ta.next(); b_, bk = tb.next()
                    self.rope_from(x[:], xk, xp[:], xpk, cosS[:], sinS[:], dst, ("sQK", j), a_[:], b_[:], ak, bk)
                tr.barrier()
            with ExitStack() as es2:
                pr = Ring(self, es2, "sP", 3, [128, 512], BF16)
                dn = Ring(self, es2, "sdn", 2, [64, 512])
                yr = Ring(self, es2, "syb", 2, [64, 4, 128], BF16)
                ybv = self.scr["D_yb"].rearrange("(h d) t -> d h t", d=64)
                it = 0
                for qt in range(NT):
                    if qt < 2:
                        kts = [(0, None), (1, None)]
                    else:
                        kts = [(0, None), (1, None)]
                        if qt > 2:
                            kts.append((qt - 1, 0))
                        kts.append((qt, None))
                        if qt < NT - 1:
                            kts.append((qt + 1, 1))
                    for g in range(2):
                        po, pd = self.ps[3], self.ps[2]
                        rhs = QT[:, 4 * g:4 * g + 4, qt * 128:(qt + 1) * 128]
                        def s_stage(i, kt, mk):
                            nonlocal it
                            sk_ = ("ps", it % 2); psn = self.ps[it % 2]; it += 1
                            self.mm(psn[:, :].rearrange("p (h q) -> p h q", h=4), KT[:, g, kt * 128:(kt + 1) * 128], rhs, True, True,
                                    reads=[("sQK", 8 + g)] + [("sQK", 4 * g + hh) for hh in range(4)], writes=[sk_])
                            P, Pk = pr.next()
                            tr.op("act", lambda e: e.activation(out=P[:], in_=psn[:, :], func=AF.Exp, scale=0.125), reads=[sk_], writes=[Pk])
                            if mk is not None:
                                tr.op("dve", lambda e: e.tensor_tensor(out=P[:], in0=P[:], in1=msk[:, mk, :], op=ALU.mult), reads=[Pk, "smask"], writes=[Pk])
                            return P, Pk

                        def pv_stage(i, kt, P, Pk):
                            self.mm(po[0:64, :], Vs[:, kt, g * 64:(g + 1) * 64], P[:], i == 0, i == len(kts) - 1, reads=["sVs", Pk], writes=[("ps", 3)])
                            self.mm(pd[0:64, :], self.ones16[:, :], P[:], i == 0, i == len(kts) - 1, reads=["ones16", Pk], writes=[("ps", 2)])

                        prev = None
                        for i, (kt, mk) in enumerate(kts):
                            cur = (i, kt) + s_stage(i, kt, mk)
                            if prev is not None:
                                pv_stage(*prev)
                            prev = cur
                        pv_stage(*prev)
                        d_, dk = dn.next()
                        for hh in range(4):
                            tr.op("dve", lambda e: e.tensor_scalar(out=d_[:, hh * 128:(hh + 1) * 128], in0=pd[0:64, hh * 128:(hh + 1) * 128],
                                                                  scalar1=snk[:, 4 * g + hh:4 * g + hh + 1], scalar2=None, op0=ALU.add),
                                  reads=[("ps", 2), "snk"], writes=[dk])
                        tr.op("dve", lambda e: e.reciprocal(out=d_[:], in_=d_[:]), reads=[dk], writes=[dk])
                        y, yk = yr.next()
                        tr.op("dve", lambda e: e.tensor_tensor(out=y[:].rearrange("p h q -> p (h q)"), in0=po[0:64, :], in1=d_[:], op=ALU.mult),
                              reads=[("ps", 3), dk], writes=[yk])
                        self.dma(ybv[:, 4 * g:4 * g + 4, qt * 128:(qt + 1) * 128], y[:], reads=[yk], writes=["D_yb"])
                tr.barrier()

    def phase_gdn_prep(self, l):
        tr = self.tr
        with ExitStack() as es:
            cw = self.sb(es, "gcw", [128, 12, 3])
            bd = self.sb(es, "gbd64", [128, 128])
            self.dma(cw[:], self.inp["gconv_fm"][:, l, :, :], writes=["gcw"])
            self.dma(bd[:], self.inp["bd64"], writes=["gbd64"])
            ld = Ring(self, es, "gld", 2, [128, T]); yr = Ring(self, es, "gy", 2, [128, T])
            sqr = Ring(self, es, "gsq", 2, [128, 512]); rr = Ring(self, es, "grn", 2, [128, 512])
            tmr = Ring(self, es, "gtm", 2, [128, NT, 128])
            for j in range(12):
                x, xk = ld.next(); y, yk = yr.next()
                self.dma(x[:], self.scr["D_fm"][GQ + j * 128:GQ + (j + 1) * 128, :], reads=["D_fm"], writes=[xk])
                tr.op("pool", lambda e: e.tensor_scalar(out=y[:], in0=x[:], scalar1=cw[:, j, 1:2], scalar2=None, op0=ALU.mult), reads=[xk, "gcw"], writes=[yk])
                for (o0, o1, i0, i1, k) in ((1, CTX, 0, CTX - 1, 0), (CTX + 1, T, CTX, T - 1, 0), (0, CTX - 1, 1, CTX, 2), (CTX, T - 1, CTX + 1, T, 2)):
                    tr.op("dve", lambda e: e.scalar_tensor_tensor(out=y[:, o0:o1], in0=x[:, i0:i1], scalar=cw[:, j, k:k + 1], in1=y[:, o0:o1],
                                                                 op0=ALU.mult, op1=ALU.add), reads=[xk, yk, "gcw"], writes=[yk])
                tr.op("act", lambda e: e.activation(out=y[:], in_=y[:], func=AF.Silu), reads=[yk], writes=[yk])
                if j < 8:
                    for ci, (t0, n) in enumerate(TCH):
                        sq, sk = sqr.next()
                        tr.op("act", lambda e: e.activation(out=sq[:, 0:n], in_=y[:, t0:t0 + n], func=AF.Square), reads=[yk], writes=[sk])
                        pk = ("ps", ci % 2); pt = self.ps[ci % 2]
                        self.mm(pt[:, 0:n], bd[:], sq[:, 0:n], True, True, reads=[sk, "gbd64"], writes=[pk])
                        r, rk = rr.next()
                        self.rstd_from_ss(pt[:, 0:n], pk, r[:, 0:n], rk, 1.0)
                        tr.op("dve", lambda e: e.scalar_tensor_tensor(out=y[:, t0:t0 + n], in0=y[:, t0:t0 + n], scalar=(0.125 if j < 4 else 1.0),
                                                                     in1=r[:, 0:n], op0=ALU.mult, op1=ALU.mult), reads=[yk, rk], writes=[yk])
                    dst = self.scr["D_gq"] if j < 4 else self.scr["D_gk"]
                    self.dma(dst[(j % 4) * 128:(j % 4 + 1) * 128, :], y[:], reads=[yk], writes=["D_gqk"])
                if j >= 4:
                    tm, tk = tmr.next()
                    for t4 in range(0, NT, 4):
                        nt = min(4, NT - t4)
                        pk = ("ps", 2 + (t4 // 4) % 2); pt = self.ps[2 + (t4 // 4) % 2]
                        for tt in range(nt):
                            t = t4 + tt
                            tr.op("pe", lambda e: e.transpose(pt[:, tt * 128:(tt + 1) * 128], y[:, t * 128:(t + 1) * 128], self.ident[:]),
                                  reads=[yk, "ident"], writes=[pk])
                        self.copy(self.evac_eng(), tm[:, t4:t4 + nt, :], pt[:, 0:nt * 128].rearrange("p (t d) -> p t d", d=128), reads=[pk], writes=[tk])
                    dst = self.scr["D_ktm"] if j < 8 else self.scr["D_vtm"]
                    c = (j % 4) * 2
                    for hh in range(2):
                        self.dma(dst[c + hh].rearrange("(t p) d -> p t d", p=128), tm[:, :, hh * 64:(hh + 1) * 64], reads=[tk], writes=["D_kvtm"])
            tr.barrier()

    def phase_gdn(self, l):
        tr = self.tr
        with ExitStack() as es:
            oacc = self.sb(es, "oacc", [128, NT, 512])
            ab = self.sb(es, "gab", [128, NT, 32])
            g = self.sb(es, "gg", [128, NT, 16]); beta = self.sb(es, "gbeta", [128, NT, 16]); nbeta = self.sb(es, "gnbeta", [128, NT, 16])
            gam = self.sb(es, "ggam", [128, NT, 16]); eg = self.sb(es, "geg", [128, NT, 16]); bg = self.sb(es, "gbg", [128, NT, 16])
            edl = self.sb(es, "gedl", [128, NT, 16]); tmp = self.sb(es, "gtmp", [128, NT, 16])
            alog = self.sb(es, "galog", [128, 16]); dtb = self.sb(es, "gdtb", [128, 16])
            tri = self.sb(es, "gtri", [128, 2, 128]); gm = self.sb(es, "gmask", [128, 4, 128]); sel = self.sb(es, "gsel", [8, 8, 128])
            S = self.sb(es, "gS", [64, 2, 8, 64])
            self.dma(ab[:], self.scr["D_tm"][:, AB:AB + 32].rearrange("(t p) c -> p t c", p=128), reads=["D_tm"], writes=["gab"])
            self.dma(alog[:], self.inp["alog_rep"][:, l, :], writes=["galog"])
            self.dma(dtb[:], self.inp["dtb_rep"][:, l, :], writes=["gdtb"])
            self.dma(tri[:], self.inp["tri"].rearrange("a p n -> p a n"), writes=["gtri"])
            self.dma(gm[:], self.inp["gmask"].rearrange("a p n -> p a n"), writes=["gmaskc"])
            self.dma(sel[:], self.inp["selh"].rearrange("h u n -> u h n"), writes=["gsel"])
            bmk = self.sb(es, "gbmask", [128, 3, 128])
            self.dma(bmk[:], self.inp["bmask"].rearrange("a p n -> p a n"), writes=["gbmask"])
            tr.op("dve", lambda e: e.memset(S[:], 0.0), writes=[(f"S{d}", h) for d in range(2) for h in range(8)])
            tr.op("act", lambda e: e.activation(out=alog[:], in_=alog[:], func=AF.Exp), reads=["galog"], writes=["galog"])
            for t in range(NT):
                tr.op("dve", lambda e: e.tensor_tensor(out=tmp[:, t, :], in0=ab[:, t, 0:16], in1=dtb[:], op=ALU.add), reads=["gab", "gdtb"], writes=["gtmp"])
            tr.op("act", lambda e: e.activation(out=tmp[:], in_=tmp[:], func=AF.Exp), reads=["gtmp"], writes=["gtmp"])
            tr.op("act", lambda e: e.activation(out=tmp[:], in_=tmp[:], func=AF.Ln, bias=self.cst[:, 1:2], scale=1.0), reads=["gtmp", "cst"], writes=["gtmp"])
            for t in range(NT):
                tr.op("dve", lambda e: e.scalar_tensor_tensor(out=g[:, t, :], in0=tmp[:, t, :], scalar=-1.0, in1=alog[:], op0=ALU.mult, op1=ALU.mult),
                      reads=["gtmp", "galog"], writes=["gg"])
            tr.op("act", lambda e: e.activation(out=beta[:], in_=ab[:, :, 16:32], func=AF.Sigmoid), reads=["gab"], writes=["gbeta"])
            tr.op("dve", lambda e: e.tensor_scalar(out=nbeta[:], in0=beta[:], scalar1=-1.0, scalar2=None, op0=ALU.mult), reads=["gbeta"], writes=["gnbeta"])
            pg = self.ps[0][:, 0:NT * 16].rearrange("p (t u) -> p t u", u=16)
            pt_ = self.ps[1][:, 0:NT * 16].rearrange("p (t u) -> p t u", u=16)
            for t in range(NT):
                for d in range(2):
                    self.mm(pg[:, t, d * 8:(d + 1) * 8], tri[:, d, :], g[:, t, d * 8:(d + 1) * 8], True, True, reads=["gtri", "gg"], writes=[("ps", 0)])
                self.mm(pt_[:, t, :], self.ones32[:], g[:, t, :], True, True, reads=["ones32", "gg"], writes=[("ps", 1)])
            self.copy("dve", gam[:], pg, reads=[("ps", 0)], writes=["ggam"])
            tr.op("dve", lambda e: e.tensor_tensor(out=edl[:], in0=pt_, in1=gam[:], op=ALU.subtract), reads=[("ps", 1), "ggam"], writes=["gedl"])
            tr.op("act", lambda e: e.activation(out=edl[:], in_=edl[:], func=AF.Exp), reads=["gedl"], writes=["gedl"])
            tr.op("act", lambda e: e.activation(out=eg[:], in_=gam[:], func=AF.Exp), reads=["ggam"], writes=["geg"])
            tr.op("dve", lambda e: e.tensor_tensor(out=bg[:], in0=beta[:], in1=eg[:], op=ALU.mult), reads=["gbeta", "geg"], writes=["gbg"])
            if self.stop == "gdn_gates":
                tr.barrier()
                return
            es_scan = ExitStack()
            NSLOT = 4
            R1 = lambda nm, shp, n=1: [Ring(self, es_scan, f"g{nm}{sl_}", n, shp) for sl_ in range(NSLOT)]
            qTr = R1("iq", [64, 128]); kTr = R1("ik", [64, 128]); ktmr = R1("iktm", [128, 64]); vtmr = R1("ivtm", [128, 64]); kdr = R1("kd", [128, 64])
            t1r = R1("t1", [128, 128]); t2r = R1("t2", [128, 128]); egr = R1("egb", [64, 128])
            a0r = R1("a0", [128, 256]); pxr_ = R1("px", [128, 256], 2); ptr__ = R1("pt", [128, 128], 2); lr = R1("l", [128, 2, 128])
            ybr_ = R1("yb", [128, 128], 5); qkr = R1("qk", [128, 128]); wtr = R1("wt", [64, 128]); qgr = R1("qg", [64, 128]); vnr = R1("vn", [128, 64])
            gTr = Ring(self, es_scan, "ggT", 3, [8, 128])
            gqv = self.scr["D_gq"].rearrange("(h d) t -> h d t", d=64)
            gkv = self.scr["D_gk"].rearrange("(h d) t -> h d t", d=64)

            def unit(d, t, h, s_, gT, gTk):
                u = d * 8 + h
                Sk = f"S{d}"
                last = 127 if d == 0 else 0
                sl = slice(t * 128, (t + 1) * 128)
                pA, pAk, pB, pBk = self.ps[2 * s_], ("ps", 2 * s_), self.ps[2 * s_ + 1], ("ps", 2 * s_ + 1)
                qT, qk_ = qTr[s_].next(); kT, kk_ = kTr[s_].next(); ktm, ktk = ktmr[s_].next(); vtm, vtk = vtmr[s_].next()
                self.dma(qT[:], gqv[h, :, sl], reads=["D_gqk"], writes=[qk_])
                self.dma(kT[:], gkv[h, :, sl], reads=["D_gqk"], writes=[kk_])
                self.dma(ktm[:], self.scr["D_ktm"][h, sl, :], reads=["D_kvtm"], writes=[ktk])
                self.dma(vtm[:], self.scr["D_vtm"][h, sl, :], reads=["D_kvtm"], writes=[vtk])
                yield
                self.mm(pA[:, 0:128], sel[:, h, :], gT[:], True, True, reads=["gsel", gTk], writes=[pAk])
                self.mm(pB[:, 0:128], kT[:], kT[:], True, True, reads=[kk_], writes=[pBk])
                kd, kdk = kdr[s_].next()
                tr.op("pool", lambda e: e.tensor_scalar(out=kd[:], in0=ktm[:], scalar1=edl[:, t, u:u + 1], scalar2=None, op0=ALU.mult), reads=[ktk, "gedl"], writes=[kdk])
                a0, a0k = a0r[s_].next()
                tr.op("pool", lambda e: e.tensor_scalar(out=a0[:, 128:192], in0=vtm[:], scalar1=beta[:, t, u:u + 1], scalar2=None, op0=ALU.mult),
                      reads=[vtk, "gbeta"], writes=[(a0k, "x")])
                tr.op("pool", lambda e: e.tensor_scalar(out=a0[:, 192:256], in0=ktm[:], scalar1=bg[:, t, u:u + 1], scalar2=None, op0=ALU.mult),
                      reads=[ktk, "gbg"], writes=[(a0k, "x")])
                yield
                t1, t1k = t1r[s_].next(); t2, t2k = t2r[s_].next(); egb, egk = egr[s_].next()
                tr.op("dve", lambda e: e.scalar_tensor_tensor(out=t1[:], in0=pA[:, 0:128], scalar=gam[:, t, u:u + 1], in1=gm[:, 2 * d, :],
                                                             op0=ALU.subtract, op1=ALU.add), reads=[pAk, "ggam", "gmaskc"], writes=[t1k])
                tr.op("dve", lambda e: e.scalar_tensor_tensor(out=t2[:], in0=pA[:, 0:128], scalar=gam[:, t, u:u + 1], in1=gm[:, 2 * d + 1, :],
                                                             op0=ALU.subtract, op1=ALU.add), reads=[pAk, "ggam", "gmaskc"], writes=[t2k])
                tr.op("dve", lambda e: e.tensor_copy(out=egb[:], in_=pA[0:64, 0:128]), reads=[pAk], writes=[egk])
                yield
                tr.op("act", lambda e: e.activation(out=t1[:], in_=t1[:], func=AF.Exp, scale=-1.0), reads=[t1k], writes=[t1k])
                tr.op("act", lambda e: e.activation(out=t2[:], in_=t2[:], func=AF.Exp), reads=[t2k], writes=[t2k])
                tr.op("act", lambda e: e.activation(out=egb[:], in_=egb[:], func=AF.Exp), reads=[egk], writes=[egk])
                self.mm(pA[:, 0:128], kT[:], qT[:], True, True, reads=[kk_, qk_], writes=[pAk])
                yield
                tr.op("dve", lambda e: e.scalar_tensor_tensor(out=a0[:, 0:128], in0=pB[:, 0:128], scalar=nbeta[:, t, u:u + 1], in1=t1[:],
                                                             op0=ALU.mult, op1=ALU.mult), reads=[pBk, "gnbeta", t1k], writes=[(a0k, "p")])
                qk, qkk = qkr[s_].next()
                tr.op("dve", lambda e: e.tensor_tensor(out=qk[:], in0=pA[:, 0:128], in1=t2[:], op=ALU.mult), reads=[pAk, t2k], writes=[qkk])
                qg, qgk = qgr[s_].next()
                tr.op("pool", lambda e: e.tensor_tensor(out=qg[:], in0=qT[:], in1=egb[:], op=ALU.mult), reads=[qk_, egk], writes=[qgk])
                yield
                tr.op("pe", lambda e: e.transpose(pB[:, 0:128], a0[:, 0:128], self.ident[:]), reads=[(a0k, "p"), "ident"], writes=[pBk])
                px, pxk = pxr_[s_].next(); pT, pTk = ptr__[s_].next(); lt, ltk = lr[s_].next()
                tr.op("pool", lambda e: e.tensor_tensor(out=pT[:], in0=a0[:, 0:128], in1=bmk[:, 0, :], op=ALU.mult), reads=[(a0k, "p"), "gbmask"], writes=[pTk])
                tr.op("pool", lambda e: e.tensor_copy(out=px[:, 128:256], in_=self.ident[:]), reads=["ident"], writes=[(pxk, "x")])
                yield
                tr.op("dve", lambda e: e.tensor_tensor(out=px[:, 0:128], in0=pB[:, 0:128], in1=bmk[:, 0, :], op=ALU.mult), reads=[pBk, "gbmask"], writes=[(pxk, "p")])
                tr.op("dve", lambda e: e.tensor_tensor(out=lt[:, 0, :], in0=pB[:, 0:128], in1=bmk[:, 1, :], op=ALU.mult), reads=[pBk, "gbmask"], writes=[(ltk, 0)])
                tr.op("dve", lambda e: e.tensor_tensor(out=lt[:, 1, :], in0=pB[:, 0:128], in1=bmk[:, 2, :], op=ALU.mult), reads=[pBk, "gbmask"], writes=[(ltk, 1)])
                yield
                for lev in range(5):
                    if lev < 4:
                        self.mm(pA[:, 0:256], pT[:], px[:, 0:256], True, True, reads=[pTk, (pxk, "p"), (pxk, "x")], writes=[pAk])
                        self.mm(pB[:, 0:128], px[:, 0:128], pT[:], True, True, reads=[pTk, (pxk, "p")], writes=[pBk])
                        yield
                        nx, nxk = pxr_[s_].next(); nT, nTk = ptr__[s_].next()
                        self.copy("act", nT[:], pB[:, 0:128], reads=[pBk], writes=[nTk])
                        tr.op("dve", lambda e: e.tensor_tensor(out=nx[:, 128:256], in0=pA[:, 128:256], in1=px[:, 128:256], op=ALU.add),
                              reads=[pAk, (pxk, "x")], writes=[(nxk, "x")])
                        self.copy("dve", nx[:, 0:128], pA[:, 0:128], reads=[pAk], writes=[(nxk, "p")])
                        px, pxk, pT, pTk = nx, nxk, nT, nTk
                        yield
                    else:
                        self.mm(pA[:, 0:128], pT[:], px[:, 128:256], True, True, reads=[pTk, (pxk, "x")], writes=[pAk])
                        yield
                        nx, nxk = pxr_[s_].next()
                        tr.op("dve", lambda e: e.tensor_tensor(out=nx[:, 128:256], in0=pA[:, 0:128], in1=px[:, 128:256], op=ALU.add),
                              reads=[pAk, (pxk, "x")], writes=[(nxk, "x")])
                        px, pxk = nx, nxk
                        yield
                MT, MTk = px[:, 128:256], (pxk, "x")
                cnt = [0]

                def apply(lhsT_ap, lhs_key, rhs_ap, rhs_key, add_ap=None, add_key=None):
                    pb, pbk = (pA, pAk) if cnt[0] % 2 == 0 else (pB, pBk)
                    eng = "act" if cnt[0] % 2 == 0 else "dve"
                    cnt[0] += 1
                    self.mm(pb[:, 0:128], lhsT_ap, rhs_ap, True, True, reads=[lhs_key, rhs_key], writes=[pbk])
                    yield
                    yb, ybk = ybr_[s_].next()
                    if add_ap is None:
                        self.copy(eng, yb[:], pb[:, 0:128], reads=[pbk], writes=[ybk])
                    else:
                        tr.op("dve", lambda e: e.tensor_tensor(out=yb[:], in0=pb[:, 0:128], in1=add_ap, op=ALU.add), reads=[pbk, add_key], writes=[ybk])
                    yield
                    return yb, ybk

                def s64(z_ap, z_key):
                    y1, y1k = yield from apply(MT, MTk, z_ap, z_key)
                    z1, z1k = yield from apply(lt[:, 0, :], (ltk, 0), y1[:], y1k)
                    r_ = yield from apply(MT, MTk, z1[:], z1k, add_ap=y1[:], add_key=y1k)
                    return r_

                y2, y2k = yield from s64(a0[:, 128:256], (a0k, "x"))
                z2, z2k = yield from apply(lt[:, 1, :], (ltk, 1), y2[:], y2k)
                y4, y4k = yield from s64(z2[:], z2k)
                px, pxk = pxr_[s_].next()
                tr.op("pool", lambda e: e.tensor_tensor(out=px[:, 128:256], in0=y2[:], in1=y4[:], op=ALU.add), reads=[y2k, y4k], writes=[(pxk, "x")])
                yield
                tr.op("pe", lambda e: e.transpose(pA[0:64, 0:128], px[:, 192:256], self.ident[:]), reads=[(pxk, "x"), "ident"], writes=[pAk])
                yield
                wt_, wtk = wtr[s_].next()
                self.copy("act", wt_[:], pA[0:64, 0:128], reads=[pAk], writes=[wtk])
                yield
                self.mm(pB[:, 0:64], wt_[:], S[:, d, h, :], True, True, reads=[wtk, (Sk, h)], writes=[pBk])
                yield
                vn, vnk = vnr[s_].next()
                tr.op("dve", lambda e: e.tensor_tensor(out=vn[:], in0=px[:, 128:192], in1=pB[:, 0:64], op=ALU.subtract),
                      reads=[(pxk, "x"), pBk], writes=[vnk])
                yield
                self.mm(pA[:, 0:64], qg[:], S[:, d, h, :], True, False, reads=[qgk, (Sk, h)], writes=[pAk])
                self.mm(pA[:, 0:64], qk[:], vn[:], False, True, reads=[qkk, vnk], writes=[pAk])
                self.mm(pB[0:64, 0:64], kd[:], vn[:], True, True, reads=[kdk, vnk], writes=[pBk])
                yield
                if d == 0:
                    self.copy("act", oacc[:, t, h * 64:(h + 1) * 64], pA[:, 0:64], reads=[pAk], writes=[("oacc", t, h)])
                else:
                    tr.op("dve", lambda e: e.tensor_tensor(out=oacc[:, t, h * 64:(h + 1) * 64], in0=pA[:, 0:64], in1=oacc[:, t, h * 64:(h + 1) * 64], op=ALU.add),
                          reads=[pAk, ("oacc", t, h)], writes=[("oacc", t, h)])
                tr.op("dve", lambda e: e.scalar_tensor_tensor(out=S[:, d, h, :], in0=S[:, d, h, :], scalar=egb[:, last:last + 1], in1=pB[0:64, 0:64],
                                                             op0=ALU.mult, op1=ALU.add), reads=[(Sk, h), egk, pBk], writes=[(Sk, h)])

            def unit_stream():
                for d in range(2):
                    order = list(range(NT)) if d == 0 else [1, 0] + list(range(NT - 1, 1, -1))
                    for t in order:
                        gT, gTk = gTr.next()
                        pk = ("ps", 7)
                        self.mm(self.ps[7][0:8, 256:384], g[:, t, d * 8:(d + 1) * 8], tri[:, d, :], True, True, reads=["gg", "gtri"], writes=[pk])
                        self.copy("act", gT[:], self.ps[7][0:8, 256:384], reads=[pk], writes=[gTk])
                        for h in range(8):
                            yield (d, t, h, gT, gTk)

            stream = unit_stream()
            active = {}
            free_slots = list(range(NSLOT))
            exhausted = False
            while True:
                while free_slots and not exhausted:
                    try:
                        d_, t_, h_, gT_, gTk_ = next(stream)
                    except StopIteration:
                        exhausted = True
                        break
                    s_ = free_slots.pop(0)
                    active[s_] = unit(d_, t_, h_, s_, gT_, gTk_)
                if not active:
                    break
                for s_ in sorted(active):
                    try:
                        next(active[s_])
                    except StopIteration:
                        del active[s_]
                        free_slots.append(s_)
            if (self.stop or "").startswith("gdn_u1"):
                tr.barrier()
                es_scan.close()
                return
            tr.barrier()
            es_scan.close()
            for t in range(NT):
                for h in range(8):
                    tr.lastw[("oacc", t, h)] = None
            tr.lastw = {k: v for k, v in tr.lastw.items() if v is not None}
            if "D_x" in self.taps:
                self.dma(self.scr["D_oc"].rearrange("(t p) c -> p t c", p=128), oacc[:], reads=[("oacc", t, h) for t in range(NT) for h in range(8)])
            gn = self.sb(es, "gn", [128, 64])
            self.dma(gn[:], self.inp["gnorm_rep"][:, l, :], writes=["gn"])
            zr = Ring(self, es, "gz", 2, [128, 512]); sq2 = Ring(self, es, "gsq2", 2, [128, 512]); ssr = Ring(self, es, "gss", 2, [128, 8])
            ycr = Ring(self, es, "gyc", 2, [128, 4, 128], BF16)
            ycv = self.scr["D_yc"].rearrange("(c p) t -> p c t", p=128)
            for t in range(NT):
                z, zk = zr.next()
                self.dma(z[:], self.scr["D_tm"][t * 128:(t + 1) * 128, ZC:ZC + 512], reads=["D_tm"], writes=[zk])
                tr.op("act", lambda e: e.activation(out=z[:], in_=z[:], func=AF.Silu), reads=[zk], writes=[zk])
                ok = [("oacc", t, h) for h in range(8)]
                sq, sk = sq2.next()
                tr.op("pool", lambda e: e.tensor_tensor(out=sq[:], in0=oacc[:, t, :], in1=oacc[:, t, :], op=ALU.mult), reads=ok, writes=[sk])
                ss, ssk = ssr.next()
                tr.op("dve", lambda e: e.tensor_reduce(out=ss[:], in_=sq[:].rearrange("p (h d) -> p h d", d=64), axis=AX.X, op=ALU.add), reads=[sk], writes=[ssk])
                self.rstd_from_ss(ss[:], ssk, ss[:], ssk, 1.0 / 64)
                o3 = oacc[:, t, :].rearrange("p (h d) -> p h d", d=64)
                tr.op("dve", lambda e: e.tensor_tensor(out=sq[:].rearrange("p (h d) -> p h d", d=64), in0=o3, in1=ss[:].unsqueeze(2).to_broadcast([128, 8, 64]), op=ALU.mult),
                      reads=ok + [ssk], writes=[sk])
                tr.op("dve", lambda e: e.tensor_tensor(out=sq[:].rearrange("p (h d) -> p h d", d=64), in0=sq[:].rearrange("p (h d) -> p h d", d=64),
                                                      in1=gn[:].unsqueeze(1).to_broadcast([128, 8, 64]), op=ALU.mult), reads=[sk, "gn"], writes=[sk])
                tr.op("dve", lambda e: e.tensor_tensor(out=sq[:], in0=sq[:], in1=z[:], op=ALU.mult), reads=[sk, zk], writes=[sk])
                pk = ("ps", 4 + t % 2); pt = self.ps[4 + t % 2]
                for c in range(4):
                    tr.op("pe", lambda e: e.transpose(pt[:, c * 128:(c + 1) * 128], sq[:, c * 128:(c + 1) * 128], self.ident[:]), reads=[sk, "ident"], writes=[pk])
                yc, yk = ycr.next()
                self.copy("act", yc[:], pt[:].rearrange("p (c t) -> p c t", c=4), reads=[pk], writes=[yk])
                self.dma(ycv[:, :, t * 128:(t + 1) * 128], yc[:], reads=[yk], writes=["D_yc"])
            tr.barrier()

    def phase_merge(self, l):
        tr = self.tr
        with ExitStack() as es:
            wp = [self.sb(es, f"wp{i}", [128, 4, D], BF16) for i in range(3)]
            wo = self.sb(es, "wo", [128, 8, D], BF16)
            for i, nm in enumerate(("w_pa", "w_pb", "w_pc")):
                self.dma(wp[i][:], self.inp[nm][l].rearrange("(c p) n -> p c n", p=128), writes=[nm], q="gq")
            self.dma(wo[:], self.inp["w_out"][l].rearrange("(c p) n -> p c n", p=128), writes=["w_out"], q="gq")
            gr = Ring(self, es, "mg", 1, [128, 24, 512], BF16)
            yr = [Ring(self, es, f"my{i}", 2, [128, 4, 512], BF16) for i in range(3)]
            mixr = Ring(self, es, "mix", 2, [128, 512]); tmr = Ring(self, es, "mtm", 2, [128, 512])
            mT = Ring(self, es, "mixT", 1, [128, 8, 512], BF16)
            gv = self.scr["D_gate"].rearrange("(c p) t -> p c t", p=128)
            yv = [self.scr[nm].rearrange("(c p) t -> p c t", p=128) for nm in ("D_ya", "D_yb", "D_yc")]
            pi = 0
            for ci, (t0, n) in enumerate(TCH):
                s = 1 if ci == 0 else 0
                gt, gk = gr.next()
                self.dma(gt[:, :, 0:n], gv[:, :, t0:t0 + n], reads=[("D_gate", ci)], writes=[gk])
                ys = []
                for i in range(3):
                    y, yk = yr[i].next()
                    self.dma(y[:, :, 0:n], yv[i][:, :, t0:t0 + n], reads=["D_ya", "D_yb", "D_yc"], writes=[yk])
                    ys.append((y, yk))
                mt, mtk = mT.next()
                for m in range(8):
                    mix, mk = mixr.next()
                    for i in range(3):
                        pk = ("ps", pi % 4); pt = self.ps[pi % 4]; pi += 1
                        y, yk = ys[i]
                        for c in range(4):
                            self.mm(pt[:, 0:n], wp[i][:, c, m * 128:(m + 1) * 128], y[:, c, 0:n], c == 0, c == 3, reads=[("w_pa", "w_pb", "w_pc")[i], yk], writes=[pk])
                        if i == 0:
                            tr.op("dve", lambda e: e.tensor_tensor(out=mix[:, 0:n], in0=pt[:, 0:n], in1=gt[:, m, 0:n], op=ALU.mult), reads=[pk, gk], writes=[mk])
                        else:
                            tm, tk = tmr.next()
                            tr.op("dve", lambda e: e.tensor_tensor(out=tm[:, 0:n], in0=pt[:, 0:n], in1=gt[:, i * 8 + m, 0:n], op=ALU.mult), reads=[pk, gk], writes=[tk])
                            if i == 1:
                                tr.op("pool", lambda e: e.tensor_tensor(out=mix[:, 0:n], in0=mix[:, 0:n], in1=tm[:, 0:n], op=ALU.add), reads=[mk, tk], writes=[mk])
                            else:
                                tr.op("pool", lambda e: e.tensor_tensor(out=mt[:, m, 0:n], in0=mix[:, 0:n], in1=tm[:, 0:n], op=ALU.add), reads=[mk, tk], writes=[(mtk, m)])
                for m in range(8):
                    pk = ("ps", 4 + m % 4); pt = self.ps[4 + m % 4]
                    for c in range(8):
                        self.mm(pt[:, 0:n], wo[:, c, m * 128:(m + 1) * 128], mt[:, c, 0:n], c == 0, c == 7, reads=["w_out", (mtk, c)], writes=[pk])
                    tr.op("dve", lambda e: e.scalar_tensor_tensor(out=self.xT[:, m, t0:t0 + n], in0=pt[:, 0:n], scalar=self.mod[:, l, 16 + m, s:s + 1],
                                                                 in1=self.xT[:, m, t0:t0 + n], op0=ALU.mult, op1=ALU.add),
                          reads=[pk, "mod", ("xT", ci)], writes=[("xT", ci)])
            tr.barrier()

    def phase_ffn(self, l):
        tr = self.tr
        with ExitStack() as es:
            fcw = self.sb(es, "fcw", [128, 44, 3])
            self.dma(fcw[:], self.inp["fconv_fm"][:, l, :, :], writes=["fcw"])
            with ExitStack() as es2:
                xn = self.sb(es2, "xn2", [128, 8, T], BF16)
                self.adaln(es2, l, 1, xn)
                wr = Ring(self, es2, "wup", 2, [128, 8, 2, 128], BF16)
                hr = [Ring(self, es2, f"fh{i}", 2, [128, T]) for i in range(2)]
                cr = [Ring(self, es2, f"fc{i}", 2, [128, T]) for i in range(2)]
                ar = Ring(self, es2, "fact", 2, [128, T], BF16)
                wv = self.inp["ffn_up"][l].rearrange("(c p) n -> p c n", p=128)
                pi = 0
                for cc in range(22):
                    w, wk = wr.next()
                    self.dma(w[:, :, 0, :], wv[:, :, cc * 128:(cc + 1) * 128], writes=[wk], q="gq")
                    self.dma(w[:, :, 1, :], wv[:, :, 2816 + cc * 128:2816 + (cc + 1) * 128], writes=[wk], q="gq")
                    cs_ = []
                    for i in range(2):
                        hb, hk = hr[i].next()
                        for ci, (t0, n) in enumerate(TCH):
                            pk = ("ps", pi % 4); pt = self.ps[pi % 4]; pi += 1
                            for c in range(8):
                                self.mm(pt[:, 0:n], w[:, c, i, :], xn[:, c, t0:t0 + n], c == 0, c == 7, reads=[wk, ("xn", ci)], writes=[pk])
                            self.copy(self.evac_eng(), hb[:, t0:t0 + n], pt[:, 0:n], reads=[pk], writes=[hk])
                        cb, ck = cr[i].next()
                        j = cc + 22 * i
                        eng = "dve" if i == 0 else "pool"
                        tr.op(eng, lambda e: e.tensor_scalar(out=cb[:], in0=hb[:], scalar1=fcw[:, j, 1:2], scalar2=None, op0=ALU.mult), reads=[hk, "fcw"], writes=[ck])
                        for (o0, o1, i0, i1, k) in ((1, CTX, 0, CTX - 1, 0), (CTX + 1, T, CTX, T - 1, 0), (0, CTX - 1, 1, CTX, 2), (CTX, T - 1, CTX + 1, T, 2)):
                            tr.op("dve", lambda e: e.scalar_tensor_tensor(out=cb[:, o0:o1], in0=hb[:, i0:i1], scalar=fcw[:, j, k:k + 1], in1=cb[:, o0:o1],
                                                                       op0=ALU.mult, op1=ALU.add), reads=[hk, ck, "fcw"], writes=[ck])
                        cs_.append((cb, ck))
                    (ca, cak), (cb, cbk) = cs_
                    tr.op("act", lambda e: e.activation(out=ca[:], in_=ca[:], func=AF.Silu), reads=[cak], writes=[cak])
                    a, ak = ar.next()
                    tr.op("dve", lambda e: e.tensor_tensor(out=a[:], in0=ca[:], in1=cb[:], op=ALU.mult), reads=[cak, cbk], writes=[ak])
                    self.dma(self.scr["D_act"][cc * 128:(cc + 1) * 128, :], a[:], reads=[ak], writes=["D_act"])
                tr.barrier()
            with ExitStack() as es2:
                wd = self.sb(es2, "wdn", [128, 22, D], BF16)
                self.dma(wd[:, 0:11, :], self.inp["ffn_down"][l].rearrange("(c p) n -> p c n", p=128)[:, 0:11, :], writes=["wdn"], q="gq")
                self.dma(wd[:, 11:22, :], self.inp["ffn_down"][l].rearrange("(c p) n -> p c n", p=128)[:, 11:22, :], writes=["wdn"], q="gq")
                ar = Ring(self, es2, "fa", 2, [128, 22, 512], BF16)
                av = self.scr["D_act"].rearrange("(c p) t -> p c t", p=128)
                for ci, (t0, n) in enumerate(TCH):
                    s = 1 if ci == 0 else 0
                    a, ak = ar.next()
                    self.dma(a[:, :, 0:n], av[:, :, t0:t0 + n], reads=["D_act"], writes=[ak])
                    for m in range(8):
                        pk = ("ps", m % 4); pt = self.ps[m % 4]
                        for c in range(22):
                            self.mm(pt[:, 0:n], wd[:, c, m * 128:(m + 1) * 128], a[:, c, 0:n], c == 0, c == 21, reads=["wdn", ak], writes=[pk])
                        tr.op("dve", lambda e: e.scalar_tensor_tensor(out=self.xT[:, m, t0:t0 + n], in0=pt[:, 0:n], scalar=self.mod[:, l, 40 + m, s:s + 1],
                                                                     in1=self.xT[:, m, t0:t0 + n], op0=ALU.mult, op1=ALU.add),
                              reads=[pk, "mod", ("xT", ci)], writes=[("xT", ci)])
                tr.barrier()

    def phase_final(self):
        tr = self.tr
        with ExitStack() as es:
            nf = self.sb(es, "nf", [128, D])
            self.dma(nf[:], self.inp["normf_rep"], writes=["nf"])
            xr = Ring(self, es, "fx", 2, [128, D]); sqr = Ring(self, es, "fsq", 2, [128, D]); ssr = Ring(self, es, "fss", 2, [128, 1])
            for t in range(2, NT):
                x, xk = xr.next()
                for half in range(2):
                    pk = ("ps", (2 * t + half) % 8); pt = self.ps[(2 * t + half) % 8]
                    for j in range(4):
                        c = half * 4 + j
                        tr.op("pe", lambda e: e.transpose(pt[:, j * 128:(j + 1) * 128], self.xT[:, c, t * 128:(t + 1) * 128], self.ident[:]),
                              reads=["ident"], writes=[pk])
                    self.copy(self.evac_eng(), x[:, half * 512:(half + 1) * 512], pt[:, :], reads=[pk], writes=[xk])
                sq, sk = sqr.next(); ss, ssk = ssr.next()
                tr.op("pool", lambda e: e.tensor_tensor(out=sq[:], in0=x[:], in1=x[:], op=ALU.mult), reads=[xk], writes=[sk])
                tr.op("dve", lambda e: e.tensor_reduce(out=ss[:], in_=sq[:], axis=AX.X, op=ALU.add), reads=[sk], writes=[ssk])
                self.rstd_from_ss(ss[:], ssk, ss[:], ssk, 1.0 / D)
                tr.op("dve", lambda e: e.scalar_tensor_tensor(out=sq[:], in0=x[:], scalar=ss[:, 0:1], in1=nf[:], op0=ALU.mult, op1=ALU.mult),
                      reads=[xk, ssk, "nf"], writes=[sk])
                self.dma(self.out[(t - 2) * 128:(t - 1) * 128, :], sq[:], reads=[sk])


def _rope_tables(d):
    da = d // 2
    half = da // 2
    inv = (10000.0 ** (-np.arange(half, dtype=np.float32) / half)).astype(np.float32)
    t = np.arange(SEQ)
    rows = (t // 64).astype(np.float32)
    cols = (t % 64).astype(np.float32)
    cos = np.ones((d, T), np.float32)
    sin = np.zeros((d, T), np.float32)
    for i in range(d):
        pos = rows if i < da else cols
        ii = i % da
        ang = pos * inv[ii % half]
        cos[i, CTX:] = np.cos(ang)
        sin[i, CTX:] = np.sin(ang) * (-1.0 if ii < half else 1.0)
    return np.stack([cos, sin])


def _partner(d):
    da = d // 2
    half = da // 2
    idx = np.arange(d)
    ii = idx % da
    return np.where(ii < half, idx + half, idx - half)


def _constants():
    j = np.arange(128)[:, None]
    i = np.arange(128)[None, :]
    tri = np.stack([(j <= i), (j >= i)]).astype(np.float32)
    gmask = np.stack([np.where(i < j, 0.0, BIG),
                      np.where(j <= i, 0.0, -BIG),
                      np.where(i > j, 0.0, BIG),
                      np.where(j >= i, 0.0, -BIG)]).astype(np.float32)
    m_prev = (j >= i).astype(np.float32)
    m_next = (j <= i).astype(np.float32)
    swamask = np.stack([np.tile(m_prev, (1, 4)), np.tile(m_next, (1, 4))]).astype(np.float32)
    b32 = (j // 32 == i // 32)
    b64 = (j // 64 == i // 64)
    bmask = np.stack([b32, b64 & ~b32, ~b64]).astype(np.float32)
    selh = np.zeros((8, 8, 128), np.float32)
    for h in range(8):
        selh[h, h, :] = 1.0
    return dict(ident=np.eye(128, dtype=np.float32), tri=tri, gmask=gmask, bmask=bmask, bd64=b64.astype(np.float32), swamask=swamask, selh=selh,
                ropeM=_rope_tables(32), ropeS=_rope_tables(64))


def _fm(v, p=128):
    v = np.asarray(v, np.float32)
    lead = v.shape[:-1]
    n = v.shape[-1] // p
    return np.ascontiguousarray(np.moveaxis(v.reshape(lead + (n, p)), -1, 0))


def prepare_shared(inp, L=DEPTH):
    f = lambda k: np.asarray(inp[k], np.float32)[:L] if k not in ("norm_f",) else np.asarray(inp[k], np.float32)
    w_in = f("w_in")
    p32 = _partner(32)
    p64 = _partner(64)
    sqp = (np.arange(8)[:, None] * 64 + p64[None, :]).reshape(-1)
    skp = (np.arange(2)[:, None] * 64 + p64[None, :]).reshape(-1)
    w_fm = np.concatenate([w_in[:, :, 0:640], w_in[:, :, 640:672], w_in[:, :, 640:672][:, :, p32],
                           w_in[:, :, 672:1184], w_in[:, :, 672:1184][:, :, sqp],
                           w_in[:, :, 1184:1312], w_in[:, :, 1184:1312][:, :, skp],
                           w_in[:, :, 1440:2976], np.zeros((L, D, 64), np.float32), w_in[:, :, 3520:6592]], axis=2)
    assert w_fm.shape[2] == NFM
    w_tm = np.concatenate([w_in[:, :, 1312:1440], w_in[:, :, 2976:3488], w_in[:, :, 3488:3520]], axis=2)
    wuq = f("w_uq").reshape(L, 384, 8, 96)
    wukv = f("w_ukv").reshape(L, 256, 8, 128)
    sh = dict(
        w_mod=f("w_mod"), b_mod_fm=_fm(f("b_mod")),
        norm1_fm=_fm(f("norm1")), norm2_fm=_fm(f("norm2")), normf_rep=np.ascontiguousarray(np.broadcast_to(f("norm_f"), (128, D))),
        w_fm=np.ascontiguousarray(w_fm), w_tm=np.ascontiguousarray(w_tm),
        qnorm_fm=_fm(f("mla_q_norm")), kvnorm_fm=_fm(f("mla_kv_norm")),
        wuq_n=np.ascontiguousarray(wuq[..., 0:64].reshape(L, 384, 512)),
        wuq_r=np.ascontiguousarray(wuq[..., 64:96].reshape(L, 384, 256)),
        wuq_rp=np.ascontiguousarray(wuq[..., 64:96][..., p32].reshape(L, 384, 256)),
        wukv_k=np.ascontiguousarray(wukv[..., 0:64].reshape(L, 256, 512)),
        wukv_v=np.ascontiguousarray(wukv[..., 64:128].reshape(L, 256, 512)),
        sink_rep=np.ascontiguousarray(np.broadcast_to(f("swa_sink"), (64, L, 8))),
        gconv_fm=np.ascontiguousarray(f("gdn_conv").reshape(L, 3, 12, 128).transpose(3, 0, 2, 1)),
        alog_rep=np.ascontiguousarray(np.broadcast_to(f("gdn_a_log").reshape(L, 16), (128, L, 16))),
        dtb_rep=np.ascontiguousarray(np.broadcast_to(f("gdn_dt_bias").reshape(L, 16), (128, L, 16))),
        gnorm_rep=np.ascontiguousarray(np.broadcast_to(f("gdn_norm"), (128, L, 64))),
        w_pa=f("w_branch_a"), w_pb=f("w_branch_b"), w_pc=f("w_branch_c"), w_out=f("w_out"),
        ffn_up=f("ffn_up"), fconv_fm=np.ascontiguousarray(f("ffn_conv").reshape(L, 3, 44, 128).transpose(3, 0, 2, 1)),
        ffn_down=f("ffn_down"),
    )
    sh.update(_constants())
    return sh


def per_core(inp, b):
    x = np.asarray(inp["x"], np.float32)[b]
    ctx = np.asarray(inp["ctx"], np.float32)[b]
    c = np.asarray(inp["c"], np.float32)[b]
    cc = np.asarray(inp["c_ctx"], np.float32)
    return dict(xin=np.ascontiguousarray(np.concatenate([ctx, x], 0)),
                c_fm=np.ascontiguousarray(np.stack([_fm(c), _fm(cc)], axis=-1)))


_CACHE = {}


def kernel(**inputs):
    if "nc" not in _CACHE:
        _CACHE["nc"] = Builder().build()
    nc = _CACHE["nc"]
    shared = prepare_shared(inputs)
    in_maps = []
    for b in range(8):
        m = dict(shared)
        m.update(per_core(inputs, b))
        in_maps.append(m)
    res = run_bass_kernel_spmd(nc, in_maps, core_ids=list(range(8)))
    return np.stack([np.asarray(r["out"], np.float32) for r in res.results], axis=0)
```

```python
import numpy as np
from contextlib import ExitStack
import concourse.bass as bass
import concourse.mybir as mybir
from concourse.bass_utils import run_bass_kernel_spmd

F32 = mybir.dt.float32
BF16 = mybir.dt.bfloat16
AF = mybir.ActivationFunctionType
ALU = mybir.AluOpType
AX = mybir.AxisListType

D = 1024
SEQ = 2048
CTX = 256
T = SEQ + CTX
NT = T // 128
DEPTH = 4
EPS = 1e-6
TCH = [(0, 256), (256, 512), (768, 512), (1280, 512), (1792, 512)]
BIG = 1.0e5

CQ, CKV, KR, KRP, SQ, SQP, SK, SKP, GQ, PAD, GATE, NFM = 0, 384, 640, 672, 704, 1216, 1728, 1856, 1984, 3520, 3584, 6656
SV, ZC, AB, NTM = 0, 128, 640, 672


def tile_chunk(t):
    return 0 if t < 2 else 1 + (t - 2) // 4


class Trk:
    ISSUER = {"pe": "pe", "act": "act", "dve": "dve", "pool": "pool", "sp": "sp", "gq": "pool"}
    NSLOT = {"sp": 24, "gq": 8}

    def __init__(self, nc, es):
        self.nc = nc
        self.eng = {"pe": nc.tensor, "act": nc.scalar, "dve": nc.vector, "pool": nc.gpsimd, "sp": nc.sync}
        self.sem = {e: es.enter_context(nc.semaphore("sem_" + e)) for e in ("pe", "act", "dve", "pool")}
        for q, k in self.NSLOT.items():
            for i in range(k):
                self.sem[(q, i)] = es.enter_context(nc.semaphore(f"sem_{q}{i}"))
        self.cnt = {e: 0 for e in self.sem}
        self.nq = {q: 0 for q in self.NSLOT}
        self.lastw = {}
        self.readers = {}
        self.waited = {}
        self.n_wait = 0
        self.n_op = 0

    def _need(self, issuer, eng, dep):
        de, ds = dep
        if de == "pe" and eng == "pe":
            return
        k = (issuer, de)
        if self.waited.get(k, 0) >= ds:
            return
        self.waited[k] = ds
        self.eng[issuer].wait_ge(self.sem[de], ds)
        self.n_wait += 1

    def op(self, eng, emit, reads=(), writes=()):
        issuer = self.ISSUER[eng]
        for key in reads:
            w = self.lastw.get(key)
            if w is not None:
                self._need(issuer, eng, w)
            if isinstance(key, tuple) and key[0] == "ps":
                for rk, tok in self.readers.get(key, {}).items():
                    if rk != eng:
                        self._need(issuer, eng, tok)
        for key in writes:
            w = self.lastw.get(key)
            if w is not None:
                self._need(issuer, eng, w)
            for tok in self.readers.get(key, {}).values():
                self._need(issuer, eng, tok)
        if eng in self.NSLOT:
            sk = (eng, self.nq[eng] % self.NSLOT[eng])
            self.nq[eng] += 1
            if self.cnt[sk] > 0:
                self._need(issuer, eng, (sk, self.cnt[sk]))
            inc = 16
        else:
            sk = eng
            inc = 1
        inst = emit(self.eng[issuer])
        self.cnt[sk] += inc
        inst.then_inc(self.sem[sk], inc)
        tok = (sk, self.cnt[sk])
        for key in reads:
            self.readers.setdefault(key, {})[sk] = tok
        for key in writes:
            self.lastw[key] = tok
            self.readers[key] = {}
        self.n_op += 1
        return inst

    def barrier(self):
        for issuer in ("pe", "act", "dve", "pool", "sp"):
            for e, c in self.cnt.items():
                if c > 0:
                    self._need(issuer, "x", (e, c))
        self.lastw.clear()
        self.readers.clear()

    def finish(self, eng="sp"):
        for e, c in self.cnt.items():
            if c > 0:
                self._need(eng, "x", (e, c))


class Ring:
    def __init__(self, b, es, name, n, shape, dt=F32):
        self.tiles = [b.sb(es, f"{name}{i}", shape, dt) for i in range(n)]
        self.name = name
        self.n = n
        self.i = 0

    def next(self):
        j = self.i % self.n
        self.i += 1
        return self.tiles[j], (self.name, j)


class Builder:
    def __init__(self, n_layers=DEPTH, stop=None, taps=()):
        self.n_layers = n_layers
        self.stop = stop
        self.taps = set(taps)
        self.nc = bass.Bass("TRN2", target_bir_lowering=False)
        self.inp = {}
        self.scr = {}
        self.ev = 0

    def din(self, name, shape, dt=F32):
        self.inp[name] = self.nc.dram_tensor(name, list(shape), dt, kind="ExternalInput").ap()
        return self.inp[name]

    def dscr(self, name, shape, dt=F32, force=False):
        kind = "ExternalOutput" if (name in self.taps or force) else "Internal"
        self.scr[name] = self.nc.dram_tensor(name, list(shape), dt, kind=kind).ap()
        return self.scr[name]

    def declare(self):
        L = self.n_layers
        d = self.din
        d("xin", [T, D]); d("c_fm", [128, 8, 2])
        d("w_mod", [L, D, 6 * D]); d("b_mod_fm", [128, L, 48])
        d("norm1_fm", [128, L, 8]); d("norm2_fm", [128, L, 8]); d("normf_rep", [128, D])
        d("w_fm", [L, D, NFM]); d("w_tm", [L, D, NTM])
        d("qnorm_fm", [128, L, 3]); d("kvnorm_fm", [128, L, 2])
        d("wuq_n", [L, 384, 512]); d("wuq_r", [L, 384, 256]); d("wuq_rp", [L, 384, 256])
        d("wukv_k", [L, 256, 512]); d("wukv_v", [L, 256, 512])
        d("sink_rep", [64, L, 8])
        d("gconv_fm", [128, L, 12, 3]); d("bd64", [128, 128]); d("alog_rep", [128, L, 16]); d("dtb_rep", [128, L, 16]); d("gnorm_rep", [128, L, 64])
        d("w_pa", [L, 512, D]); d("w_pb", [L, 512, D]); d("w_pc", [L, 512, D]); d("w_out", [L, D, D])
        d("ffn_up", [L, D, 5632]); d("fconv_fm", [128, L, 44, 3]); d("ffn_down", [L, 2816, D])
        d("ident", [128, 128]); d("ropeM", [2, 32, T]); d("ropeS", [2, 64, T])
        d("tri", [2, 128, 128]); d("gmask", [4, 128, 128]); d("bmask", [3, 128, 128]); d("swamask", [2, 128, 512]); d("selh", [8, 8, 128])
        self.out = self.nc.dram_tensor("out", [SEQ, D], F32, kind="ExternalOutput").ap()
        s = self.dscr
        s("D_fm", [NFM, T]); s("D_tm", [T, NTM]); s("D_gate", [3072, T], BF16)
        s("D_ya", [512, T], BF16); s("D_yb", [512, T], BF16); s("D_yc", [512, T], BF16)
        s("D_gq", [512, T]); s("D_gk", [512, T]); s("D_ktm", [8, T, 64]); s("D_vtm", [8, T, 64])
        s("D_act", [2816, T], BF16)
        if "D_x" in self.taps:
            s("D_x", [D, T], F32, True); s("D_xn", [D, T], BF16, True); s("D_mod", [128, L * 48 * 2], F32, True); s("D_oc", [T, 512], F32, True)

    def sb(self, es, name, shape, dt=F32):
        self.uid = getattr(self, "uid", 0) + 1
        return es.enter_context(self.nc.sbuf_tensor(f"s{self.uid}_{name}", list(shape), dt))

    def evac_eng(self):
        self.ev += 1
        return "act" if self.ev % 2 else "dve"

    def copy(self, eng, out, in_, reads, writes):
        if eng == "act":
            return self.tr.op("act", lambda e: e.copy(out=out, in_=in_), reads=reads, writes=writes)
        return self.tr.op(eng, lambda e: e.tensor_copy(out=out, in_=in_), reads=reads, writes=writes)

    def dma(self, out, in_, reads=(), writes=(), q="sp"):
        return self.tr.op(q, lambda e: e.dma_start(out=out, in_=in_), reads=reads, writes=writes)

    def mm(self, out, lhsT, rhs, start, stop, reads, writes):
        return self.tr.op("pe", lambda e: e.matmul(out, lhsT, rhs, start=start, stop=stop), reads=reads, writes=writes)

    def rstd_from_ss(self, ss_ap, ss_key, r_ap, r_key, scale):
        tr = self.tr
        tr.op("act", lambda e: e.activation(out=r_ap, in_=ss_ap, func=AF.Sqrt, bias=self.cst[0:r_ap.shape[0], 0:1], scale=scale),
              reads=[ss_key, "cst"], writes=[r_key])
        tr.op("dve", lambda e: e.reciprocal(out=r_ap, in_=r_ap), reads=[r_key], writes=[r_key])

    def build(self):
        nc = self.nc
        self.declare()
        with ExitStack() as es:
            self.tr = tr = Trk(nc, es)
            self.ps = [es.enter_context(nc.psum_tensor(f"ps{i}", [128, 512], F32)) for i in range(8)]
            self.xT = self.sb(es, "xT", [128, 8, T])
            self.ident = self.sb(es, "ident", [128, 128])
            self.ones32 = self.sb(es, "ones32", [128, 128])
            self.ones16 = self.sb(es, "ones16", [128, 128], BF16)
            self.cst = self.sb(es, "cst", [128, 4])
            self.mod = self.sb(es, "mod", [128, self.n_layers, 48, 2])
            self.n1 = self.sb(es, "n1", [128, self.n_layers, 8])
            self.n2 = self.sb(es, "n2", [128, self.n_layers, 8])
            self.gs = self.sb(es, "gs", [128, 2, 8, 2])
            self.dma(self.ident[:], self.inp["ident"], writes=["ident"])
            self.dma(self.n1[:], self.inp["norm1_fm"], writes=["n1"])
            self.dma(self.n2[:], self.inp["norm2_fm"], writes=["n2"])
            tr.op("dve", lambda e: e.memset(self.ones32[:], 1.0), writes=["ones32"])
            tr.op("dve", lambda e: e.memset(self.ones16[:], 1.0), writes=["ones16"])
            tr.op("dve", lambda e: e.memset(self.cst[:, 0:1], EPS), writes=["cst"])
            tr.op("dve", lambda e: e.memset(self.cst[:, 1:2], 1.0), writes=["cst"])
            tr.op("dve", lambda e: e.memset(self.cst[:, 2:3], 0.0), writes=["cst"])
            self.phase_load_x()
            self.phase_mod()
            done = self.stop == "mod"
            for l in range(self.n_layers):
                if done:
                    break
                for name, fn in (("norm1", lambda: self.phase_adaln_proj(l)), ("mla", lambda: self.phase_mla(l)),
                                 ("swa", lambda: self.phase_swa(l)), ("gdnprep", lambda: self.phase_gdn_prep(l)),
                                 ("gdn", lambda: self.phase_gdn(l)), ("merge", lambda: self.phase_merge(l)),
                                 ("ffn", lambda: self.phase_ffn(l))):
                    fn()
                    if self.stop == f"{name}{l}" or (self.stop or "").startswith(name + "_"):
                        done = True
                        break
            if "D_x" in self.taps:
                tr.barrier()
                for ci, (t0, n) in enumerate(TCH):
                    self.dma(self.scr["D_x"].rearrange("(c p) t -> p c t", p=128)[:, :, t0:t0 + n], self.xT[:, :, t0:t0 + n])
                self.dma(self.scr["D_mod"], self.mod[:].rearrange("p l j s -> p (l j s)"))
            if not done:
                self.phase_final()
            tr.barrier()
            tr.finish("sp")
        return nc

    def phase_load_x(self):
        tr = self.tr
        with ExitStack() as es:
            ring = Ring(self, es, "xl", 2, [128, D])
            for t in range(NT):
                xt, xk = ring.next()
                self.dma(xt[:], self.inp["xin"][t * 128:(t + 1) * 128, :], writes=[xk])
                for half in range(2):
                    pk = ("ps", (2 * t + half) % 8)
                    pt = self.ps[(2 * t + half) % 8]
                    for j in range(4):
                        c = half * 4 + j
                        tr.op("pe", lambda e: e.transpose(pt[:, j * 128:(j + 1) * 128], xt[:, c * 128:(c + 1) * 128], self.ident[:]),
                              reads=[xk, "ident"], writes=[pk])
                    self.copy(self.evac_eng(), self.xT[:, half * 4:half * 4 + 4, t * 128:(t + 1) * 128],
                              pt[:].rearrange("p (c t) -> p c t", c=4), reads=[pk], writes=[("xT", tile_chunk(t))])
            tr.barrier()

    def phase_mod(self):
        tr = self.tr
        with ExitStack() as es:
            cs = self.sb(es, "cs", [128, 8, 2])
            bm = self.sb(es, "bm", [128, self.n_layers, 48])
            ring = Ring(self, es, "wm", 2, [128, 8, 512])
            self.dma(cs[:], self.inp["c_fm"], writes=["cs"])
            self.dma(bm[:], self.inp["b_mod_fm"], writes=["bm"])
            tr.op("act", lambda e: e.activation(out=cs[:], in_=cs[:], func=AF.Silu), reads=["cs"], writes=["cs"])
            for l in range(self.n_layers):
                wv = self.inp["w_mod"][l].rearrange("(c p) n -> p c n", p=128)
                pk = ("ps", l % 2)
                pt = self.ps[l % 2][:, 0:96].rearrange("p (j s) -> p j s", s=2)
                for g in range(12):
                    w, wk = ring.next()
                    self.dma(w[:], wv[:, :, g * 512:(g + 1) * 512], writes=[wk])
                    for m in range(4):
                        j = g * 4 + m
                        for c in range(8):
                            self.mm(pt[:, j, :], w[:, c, m * 128:(m + 1) * 128], cs[:, c, :], c == 0, c == 7,
                                    reads=[wk, "cs"], writes=[pk])
                for s in range(2):
                    tr.op("dve", lambda e: e.tensor_tensor(out=self.mod[:, l, :, s], in0=pt[:, :, s], in1=bm[:, l, :], op=ALU.add),
                          reads=[pk, "bm"], writes=["mod"])
            tr.barrier()

    def make_gs(self, l, which):
        nrm = self.n1 if which == 0 else self.n2
        sc0 = 8 + which * 24
        for s in range(2):
            self.tr.op("dve", lambda e: e.scalar_tensor_tensor(out=self.gs[:, which, :, s], in0=self.mod[:, l, sc0:sc0 + 8, s], scalar=1.0,
                                                             in1=nrm[:, l, :], op0=ALU.add, op1=ALU.mult),
                       reads=["mod", "n1", "n2"], writes=["gs"])

    def adaln(self, es, l, which, xn):
        tr = self.tr
        self.make_gs(l, which)
        sh0 = which * 24
        es = ExitStack()
        sqr = Ring(self, es, f"sq{which}", 2, [128, 512])
        rr = Ring(self, es, f"rr{which}", 2, [128, 512])
        tmpr = Ring(self, es, f"tm{which}", 2, [128, 512])
        for ci, (t0, n) in enumerate(TCH):
            s = 1 if ci == 0 else 0
            pk = ("ps", ci % 2)
            pt = self.ps[ci % 2]
            for c in range(8):
                sq, sk = sqr.next()
                tr.op("act", lambda e: e.activation(out=sq[:, 0:n], in_=self.xT[:, c, t0:t0 + n], func=AF.Square),
                      reads=[("xT", ci)], writes=[sk])
                self.mm(pt[:, 0:n], self.ones32[:], sq[:, 0:n], c == 0, c == 7, reads=[sk, "ones32"], writes=[pk])
            r, rk = rr.next()
            self.rstd_from_ss(pt[:, 0:n], pk, r[:, 0:n], rk, 1.0 / D)
            for c in range(8):
                tm, tk = tmpr.next()
                tr.op("dve", lambda e: e.scalar_tensor_tensor(out=tm[:, 0:n], in0=self.xT[:, c, t0:t0 + n], scalar=self.gs[:, which, c, s:s + 1],
                                                             in1=r[:, 0:n], op0=ALU.mult, op1=ALU.mult),
                      reads=[("xT", ci), "gs", rk], writes=[tk])
                tr.op("act", lambda e: e.activation(out=xn[:, c, t0:t0 + n], in_=tm[:, 0:n], func=AF.Identity,
                                                    bias=self.mod[:, l, sh0 + c, s:s + 1], scale=1.0),
                      reads=[tk, "mod"], writes=[("xn", ci)])
        keep = {k: v for k, v in tr.lastw.items() if isinstance(k, tuple) and k[0] == "xn"}
        tr.barrier()
        tr.lastw.update(keep)
        es.close()

    def phase_adaln_proj(self, l):
        tr = self.tr
        with ExitStack() as es:
            xn = self.sb(es, "xn", [128, 8, T], BF16)
            self.adaln(es, l, 0, xn)
            if "D_x" in self.taps:
                for ci, (t0, n) in enumerate(TCH):
                    self.dma(self.scr["D_xn"].rearrange("(c p) t -> p c t", p=128)[:, :, t0:t0 + n], xn[:, :, t0:t0 + n], reads=[("xn", ci)])
            wr = Ring(self, es, "wfm", 2, [128, 8, 512], BF16)
            st = Ring(self, es, "stg", 3, [128, 512])
            st16 = Ring(self, es, "stg16", 2, [128, 512], BF16)
            wv = self.inp["w_fm"][l].rearrange("(c p) n -> p c n", p=128)
            pi = 0
            for g in range(NFM // 512):
                w, wk = wr.next()
                self.dma(w[:], wv[:, :, g * 512:(g + 1) * 512], writes=[wk], q="gq")
                col = g * 512
                while col < (g + 1) * 512:
                    if col < KR:
                        mc = 128
                    elif col < SQ:
                        mc = 32
                    elif col < PAD:
                        mc = 64
                    elif col < GATE:
                        col += 64
                        continue
                    else:
                        mc = 128
                    lo = col - g * 512
                    for ci, (t0, n) in enumerate(TCH):
                        pk = ("ps", pi % 4)
                        pt = self.ps[pi % 4]
                        pi += 1
                        for c in range(8):
                            self.mm(pt[0:mc, 0:n], w[:, c, lo:lo + mc], xn[:, c, t0:t0 + n], c == 0, c == 7,
                                    reads=[wk, ("xn", ci)], writes=[pk])
                        if col >= GATE:
                            s16, sk = st16.next()
                            tr.op("act", lambda e: e.activation(out=s16[:, 0:n], in_=pt[:, 0:n], func=AF.Sigmoid), reads=[pk], writes=[sk])
                            self.dma(self.scr["D_gate"][col - GATE:col - GATE + 128, t0:t0 + n], s16[:, 0:n], reads=[sk], writes=[("D_gate", ci)])
                        else:
                            s32, sk = st.next()
                            self.copy(self.evac_eng(), s32[0:mc, 0:n], pt[0:mc, 0:n], reads=[pk], writes=[sk])
                            self.dma(self.scr["D_fm"][col:col + mc, t0:t0 + n], s32[0:mc, 0:n], reads=[sk], writes=["D_fm"])
                    col += mc
            wt = self.sb(es, "wtm", [128, 8, NTM], BF16)
            self.dma(wt[:], self.inp["w_tm"][l].rearrange("(c p) n -> p c n", p=128), writes=["wtm"], q="gq")
            stm = Ring(self, es, "stm", 2, [128, NTM])
            for t in range(NT):
                ci = tile_chunk(t)
                pa, pb = self.ps[4 + (t % 2) * 2], self.ps[5 + (t % 2) * 2]
                ka, kb = ("ps", 4 + (t % 2) * 2), ("ps", 5 + (t % 2) * 2)
                for c in range(8):
                    self.mm(pa[:, 0:512], xn[:, c, t * 128:(t + 1) * 128], wt[:, c, 0:512], c == 0, c == 7, reads=["wtm", ("xn", ci)], writes=[ka])
                for c in range(8):
                    self.mm(pb[:, 0:160], xn[:, c, t * 128:(t + 1) * 128], wt[:, c, 512:672], c == 0, c == 7, reads=["wtm", ("xn", ci)], writes=[kb])
                s, sk = stm.next()
                self.copy("act", s[:, 0:512], pa[:, 0:512], reads=[ka], writes=[sk])
                self.copy("dve", s[:, 512:672], pb[:, 0:160], reads=[kb], writes=[sk])
                self.dma(self.scr["D_tm"][t * 128:(t + 1) * 128, :], s[:], reads=[sk], writes=["D_tm"])
            tr.barrier()

    def rms_fm_from_dram(self, es, row0, nchunk, gain, dst, tag):
        tr = self.tr
        ld = Ring(self, es, f"ld{tag}", 2, [128, nchunk, 512])
        sqr = Ring(self, es, f"sqn{tag}", 2, [128, 512])
        rr = Ring(self, es, f"rn{tag}", 2, [128, 512])
        src = self.scr["D_fm"][row0:row0 + nchunk * 128, :].rearrange("(c p) t -> p c t", p=128)
        for ci, (t0, n) in enumerate(TCH):
            x, xk = ld.next()
            self.dma(x[:, :, 0:n], src[:, :, t0:t0 + n], reads=["D_fm"], writes=[xk])
            pk = ("ps", 6 + ci % 2)
            pt = self.ps[6 + ci % 2]
            for c in range(nchunk):
                sq, sk = sqr.next()
                tr.op("act", lambda e: e.activation(out=sq[:, 0:n], in_=x[:, c, 0:n], func=AF.Square), reads=[xk], writes=[sk])
                self.mm(pt[:, 0:n], self.ones32[:], sq[:, 0:n], c == 0, c == nchunk - 1, reads=[sk, "ones32"], writes=[pk])
            r, rk = rr.next()
            self.rstd_from_ss(pt[:, 0:n], pk, r[:, 0:n], rk, 1.0 / (nchunk * 128))
            for c in range(nchunk):
                tr.op("dve", lambda e: e.scalar_tensor_tensor(out=dst[:, c, t0:t0 + n], in0=x[:, c, 0:n], scalar=gain[:, c:c + 1],
                                                             in1=r[:, 0:n], op0=ALU.mult, op1=ALU.mult),
                      reads=[xk, rk, "gain" + tag], writes=[(tag, ci)])

    def rope_from(self, src_ap, src_key, srcp_ap, srcp_key, cos_ap, sin_ap, dst_ap, dst_key, tmpa, tmpb, ka, kb):
        tr = self.tr
        tr.op("dve", lambda e: e.tensor_tensor(out=tmpa, in0=src_ap, in1=cos_ap, op=ALU.mult), reads=[src_key, "rope"], writes=[ka])
        tr.op("pool" if srcp_key[0] != "ps" else "dve", lambda e: e.tensor_tensor(out=tmpb, in0=srcp_ap, in1=sin_ap, op=ALU.mult),
              reads=[srcp_key, "rope"], writes=[kb])
        tr.op("dve", lambda e: e.tensor_tensor(out=dst_ap, in0=tmpa, in1=tmpb, op=ALU.add), reads=[ka, kb], writes=[dst_key])

    def phase_mla(self, l):
        tr = self.tr
        scale = 96.0 ** -0.5
        with ExitStack() as es:
            qg = self.sb(es, "qg", [128, 3]); kg = self.sb(es, "kg", [128, 2])
            self.dma(qg[:], self.inp["qnorm_fm"][:, l, :], writes=["gaincqn"])
            self.dma(kg[:], self.inp["kvnorm_fm"][:, l, :], writes=["gainckvn"])
            cqn = self.sb(es, "cqn", [128, 3, T], BF16)
            ckvn = self.sb(es, "ckvn", [128, 2, T], BF16)
            cosM = self.sb(es, "cosM", [32, T]); sinM = self.sb(es, "sinM", [32, T])
            self.dma(cosM[:], self.inp["ropeM"][0], writes=["rope"])
            self.dma(sinM[:], self.inp["ropeM"][1], writes=["rope"])
            KrT = self.sb(es, "KrT", [128, T], BF16)
            tr.op("pool", lambda e: e.memset(KrT[32:64, :], 0.0), writes=[("KrT", ci) for ci in range(5)])
            tr.op("pool", lambda e: e.memset(KrT[64:128, :], 0.0), writes=[("KrT", ci) for ci in range(5)])
            Vt = self.sb(es, "Vt", [128, NT, 576], BF16)
            wqn = self.sb(es, "wqn", [128, 3, 512], BF16); wqr = self.sb(es, "wqr", [128, 3, 256], BF16)
            wqp = self.sb(es, "wqp", [128, 3, 256], BF16)
            wkn = self.sb(es, "wkn", [128, 2, 512], BF16); wkv = self.sb(es, "wkv", [128, 2, 512], BF16)
            for wt_, nm in ((wqn, "wuq_n"), (wqr, "wuq_r"), (wqp, "wuq_rp"), (wkn, "wukv_k"), (wkv, "wukv_v")):
                self.dma(wt_[:], self.inp[nm][l].rearrange("(c p) n -> p c n", p=128), writes=[nm], q="gq")
            with ExitStack() as es2:
                self.rms_fm_from_dram(es2, CQ, 3, qg, cqn, "cqn")
                self.rms_fm_from_dram(es2, CKV, 2, kg, ckvn, "ckvn")
                kl = Ring(self, es2, "krl", 2, [32, 2, 512])
                ta = Ring(self, es2, "kta", 2, [32, 512]); tb = Ring(self, es2, "ktb", 2, [32, 512])
                for ci, (t0, n) in enumerate(TCH):
                    k2, kk = kl.next()
                    self.dma(k2[:, :, 0:n], self.scr["D_fm"][KR:KR + 64, :].rearrange("(a p) t -> p a t", p=32)[:, :, t0:t0 + n],
                             reads=["D_fm"], writes=[kk])
                    a_, ak = ta.next(); b_, bk = tb.next()
                    self.rope_from(k2[:, 0, 0:n], kk, k2[:, 1, 0:n], kk, cosM[:, t0:t0 + n], sinM[:, t0:t0 + n],
                                   KrT[0:32, t0:t0 + n], ("KrT", ci), a_[:, 0:n], b_[:, 0:n], ak, bk)
                for t in range(NT):
                    tr.op("pool", lambda e: e.memset(Vt[:, t, 512:576], 0.0), writes=[("Vt", t)])
                for t in range(NT):
                    ci = tile_chunk(t)
                    pk = ("ps", 4 + t % 2); pt = self.ps[4 + t % 2]
                    for c in range(2):
                        self.mm(pt[:, :], ckvn[:, c, t * 128:(t + 1) * 128], wkv[:, c, :], c == 0, c == 1, reads=[("ckvn", ci), "wukv_v"], writes=[pk])
                    self.copy(self.evac_eng(), Vt[:, t, 0:512], pt[:, :], reads=[pk], writes=[("Vt", t)])
                tr.barrier()
            if self.stop == "mla_prep":
                return
            with ExitStack() as es2:
                qnr = Ring(self, es2, "QnT", 2, [128, T], BF16); qrr = Ring(self, es2, "QrT", 2, [128, T], BF16)
                knr = Ring(self, es2, "KnT", 2, [128, T], BF16)
                for rg, lo in ((qnr, 64), (qrr, 32), (knr, 64)):
                    for ti, tl in enumerate(rg.tiles):
                        for p0_, p1_ in (((32, 64), (64, 128)) if lo == 32 else ((64, 128),)):
                            tr.op("pool", lambda e: e.memset(tl[p0_:p1_, :], 0.0), writes=[((rg.name, ti), ci) for ci in range(5)])
                ta = Ring(self, es2, "qta", 2, [32, 512]); tb = Ring(self, es2, "qtb", 2, [32, 512])
                pr = Ring(self, es2, "Pexp", 4, [128, 512], BF16)
                rdr = Ring(self, es2, "rden", 2, [64, 512])
                yar = Ring(self, es2, "yah", 2, [64, T], BF16)
                for h in range(8):
                    Qn, qnk = qnr.next(); Qr, qrk = qrr.next(); Kn, knk = knr.next()
                    for ci, (t0, n) in enumerate(TCH):
                        p0, p1, p2, p3 = self.ps[4], self.ps[5], self.ps[6], self.ps[7]
                        for c in range(3):
                            self.mm(p0[0:64, 0:n], wqn[:, c, h * 64:(h + 1) * 64], cqn[:, c, t0:t0 + n], c == 0, c == 2, reads=["wuq_n", ("cqn", ci)], writes=[("ps", 4)])
                        for c in range(3):
                            self.mm(p1[0:32, 0:n], wqr[:, c, h * 32:(h + 1) * 32], cqn[:, c, t0:t0 + n], c == 0, c == 2, reads=["wuq_r", ("cqn", ci)], writes=[("ps", 5)])
                        for c in range(3):
                            self.mm(p2[0:32, 0:n], wqp[:, c, h * 32:(h + 1) * 32], cqn[:, c, t0:t0 + n], c == 0, c == 2, reads=["wuq_rp", ("cqn", ci)], writes=[("ps", 6)])
                        for c in range(2):
                            self.mm(p3[0:64, 0:n], wkn[:, c, h * 64:(h + 1) * 64], ckvn[:, c, t0:t0 + n], c == 0, c == 1, reads=["wukv_k", ("ckvn", ci)], writes=[("ps", 7)])
                        self.copy("act", Qn[0:64, t0:t0 + n], p0[0:64, 0:n], reads=[("ps", 4)], writes=[(qnk, ci)])
                        self.copy("act", Kn[0:64, t0:t0 + n], p3[0:64, 0:n], reads=[("ps", 7)], writes=[(knk, ci)])
                        a_, ak = ta.next(); b_, bk = tb.next()
                        self.rope_from(p1[0:32, 0:n], ("ps", 5), p2[0:32, 0:n], ("ps", 6), cosM[:, t0:t0 + n], sinM[:, t0:t0 + n],
                                       Qr[0:32, t0:t0 + n], (qrk, ci), a_[:, 0:n], b_[:, 0:n], ak, bk)
                    if self.stop == "mla_proj":
                        break
                    ya, yk = yar.next()
                    for ci, (t0, n) in enumerate(TCH):
                        if self.stop == "mla_ctx" and ci > 0:
                            break
                        kts = [0, 1] if ci == 0 else list(range(NT))
                        po, pd = self.ps[3], self.ps[2]
                        def s_stage(i, kt):
                            kci = tile_chunk(kt)
                            bank = (0, 1, 4)[i % 3]
                            sk_ = ("ps", bank); psn = self.ps[bank]
                            self.mm(psn[:, 0:n], Kn[:, kt * 128:(kt + 1) * 128], Qn[:, t0:t0 + n], True, False,
                                    reads=[(knk, kci), (qnk, ci)], writes=[sk_])
                            self.mm(psn[:, 0:n], KrT[:, kt * 128:(kt + 1) * 128], Qr[:, t0:t0 + n], False, True,
                                    reads=[("KrT", kci), (qrk, ci)], writes=[sk_])
                            P, Pk = pr.next()
                            tr.op("act", lambda e: e.activation(out=P[:, 0:n], in_=psn[:, 0:n], func=AF.Exp, scale=scale), reads=[sk_], writes=[Pk])
                            return P, Pk

                        def pv_stage(i, kt, P, Pk):
                            self.mm(po[:, 0:n], Vt[:, kt, h * 64:h * 64 + 128], P[:, 0:n], i == 0, i == len(kts) - 1, reads=[("Vt", kt), Pk], writes=[("ps", 3)])
                            self.mm(pd[:, 0:n], self.ones16[:, :], P[:, 0:n], i == 0, i == len(kts) - 1, reads=["ones16", Pk], writes=[("ps", 2)])

                        pend = []
                        for i, kt in enumerate(kts):
                            pend.append((i, kt) + s_stage(i, kt))
                            if len(pend) > 2:
                                pv_stage(*pend.pop(0))
                        while pend:
                            pv_stage(*pend.pop(0))
                        rd, rk = rdr.next()
                        tr.op("dve", lambda e: e.reciprocal(out=rd[:, 0:n], in_=pd[0:64, 0:n]), reads=[("ps", 2)], writes=[rk])
                        tr.op("dve", lambda e: e.tensor_tensor(out=ya[:, t0:t0 + n], in0=po[0:64, 0:n], in1=rd[:, 0:n], op=ALU.mult),
                              reads=[("ps", 3), rk], writes=[yk])
                    self.dma(self.scr["D_ya"][h * 64:(h + 1) * 64, :], ya[:], reads=[yk], writes=["D_ya"])
                tr.barrier()

    def phase_swa(self, l):
        tr = self.tr
        with ExitStack() as es:
            QT = self.sb(es, "sQT", [128, 8, T], BF16)
            KT = self.sb(es, "sKT", [128, 2, T], BF16)
            tr.op("pool", lambda e: e.memset(QT[64:128, :, :], 0.0), writes=[("sQK", j) for j in range(8)])
            tr.op("pool", lambda e: e.memset(KT[64:128, :, :], 0.0), writes=[("sQK", 8), ("sQK", 9)])
            Vs = self.sb(es, "sVs", [128, NT, 128], BF16)
            cosS = self.sb(es, "cosS", [64, T]); sinS = self.sb(es, "sinS", [64, T])
            msk = self.sb(es, "smask", [128, 2, 512], BF16)
            snk = self.sb(es, "snk", [64, 8])
            self.dma(cosS[:], self.inp["ropeS"][0], writes=["rope"])
            self.dma(sinS[:], self.inp["ropeS"][1], writes=["rope"])
            self.dma(msk[:], self.inp["swamask"].rearrange("a p n -> p a n"), writes=["smask"], q="gq")
            self.dma(snk[:], self.inp["sink_rep"][:, l, :], writes=["snk"])
            tr.op("act", lambda e: e.activation(out=snk[:], in_=snk[:], func=AF.Exp), reads=["snk"], writes=["snk"])
            self.dma(Vs[:], self.scr["D_tm"][:, SV:SV + 128].rearrange("(t p) c -> p t c", p=128), reads=["D_tm"], writes=["sVs"], q="gq")
            with ExitStack() as es2:
                ld = Ring(self, es2, "sld", 1, [64, T]); ldp = Ring(self, es2, "sldp", 1, [64, T])
                ta = Ring(self, es2, "sta", 1, [64, T]); tb = Ring(self, es2, "stb", 1, [64, T])
                for j in range(10):
                    r0, rp = (SQ + j * 64, SQP + j * 64) if j < 8 else (SK + (j - 8) * 64, SKP + (j - 8) * 64)
                    dst = QT[0:64, j, :] if j < 8 else KT[0:64, j - 8, :]
                    x, xk = ld.next(); xp, xpk = ldp.next()
                    self.dma(x[:], self.scr["D_fm"][r0:r0 + 64, :], reads=["D_fm"], writes=[xk])
                    self.dma(xp[:], self.scr["D_fm"][rp:rp + 64, :], reads=["D_fm"], writes=[xpk])
                    a_, ak = ta.next(); b_, bk = tb.next()
                    self.rope_from(x[:], xk, xp[:], xpk, cosS[:], sinS[:], dst, ("sQK", j), a_[:], b_[:], ak, bk)
                tr.barrier()
            with ExitStack() as es2:
                pr = Ring(self, es2, "sP", 3, [128, 512], BF16)
                dn = Ring(self, es2, "sdn", 2, [64, 512])
                yr = Ring(self, es2, "syb", 2, [64, 4, 128], BF16)
                ybv = self.scr["D_yb"].rearrange("(h d) t -> d h t", d=64)
                it = 0
                for qt in range(NT):
                    if qt < 2:
                        kts = [(0, None), (1, None)]
                    else:
                        kts = [(0, None), (1, None)]
                        if qt > 2:
                            kts.append((qt - 1, 0))
                        kts.append((qt, None))
                        if qt < NT - 1:
                            kts.append((qt + 1, 1))
                    for g in range(2):
                        po, pd = self.ps[3], self.ps[2]
                        rhs = QT[:, 4 * g:4 * g + 4, qt * 128:(qt + 1) * 128]
                        def s_stage(i, kt, mk):
                            nonlocal it
                            sk_ = ("ps", it % 2); psn = self.ps[it % 2]; it += 1
                            self.mm(psn[:, :].rearrange("p (h q) -> p h q", h=4), KT[:, g, kt * 128:(kt + 1) * 128], rhs, True, True,
                                    reads=[("sQK", 8 + g)] + [("sQK", 4 * g + hh) for hh in range(4)], writes=[sk_])
                            P, Pk = pr.next()
                            tr.op("act", lambda e: e.activation(out=P[:], in_=psn[:, :], func=AF.Exp, scale=0.125), reads=[sk_], writes=[Pk])
                            if mk is not None:
                                tr.op("dve", lambda e: e.tensor_tensor(out=P[:], in0=P[:], in1=msk[:, mk, :], op=ALU.mult), reads=[Pk, "smask"], writes=[Pk])
                            return P, Pk

                        def pv_stage(i, kt, P, Pk):
                            self.mm(po[0:64, :], Vs[:, kt, g * 64:(g + 1) * 64], P[:], i == 0, i == len(kts) - 1, reads=["sVs", Pk], writes=[("ps", 3)])
                            self.mm(pd[:, :], self.ones16[:, :], P[:], i == 0, i == len(kts) - 1, reads=["ones16", Pk], writes=[("ps", 2)])

                        prev = None
                        for i, (kt, mk) in enumerate(kts):
                            cur = (i, kt) + s_stage(i, kt, mk)
                            if prev is not None:
                                pv_stage(*prev)
                            prev = cur
                        pv_stage(*prev)
                        d_, dk = dn.next()
                        for hh in range(4):
                            tr.op("dve", lambda e: e.tensor_scalar(out=d_[:, hh * 128:(hh + 1) * 128], in0=pd[0:64, hh * 128:(hh + 1) * 128],
                                                                  scalar1=snk[:, 4 * g + hh:4 * g + hh + 1], scalar2=None, op0=ALU.add),
                                  reads=[("ps", 2), "snk"], writes=[dk])
                        tr.op("dve", lambda e: e.reciprocal(out=d_[:], in_=d_[:]), reads=[dk], writes=[dk])
                        y, yk = yr.next()
                        tr.op("dve", lambda e: e.tensor_tensor(out=y[:].rearrange("p h q -> p (h q)"), in0=po[0:64, :], in1=d_[:], op=ALU.mult),
                              reads=[("ps", 3), dk], writes=[yk])
                        self.dma(ybv[:, 4 * g:4 * g + 4, qt * 128:(qt + 1) * 128], y[:], reads=[yk], writes=["D_yb"])
                tr.barrier()

    def phase_gdn_prep(self, l):
        tr = self.tr
        with ExitStack() as es:
            cw = self.sb(es, "gcw", [128, 12, 3])
            bd = self.sb(es, "gbd64", [128, 128])
            self.dma(cw[:], self.inp["gconv_fm"][:, l, :, :], writes=["gcw"])
            self.dma(bd[:], self.inp["bd64"], writes=["gbd64"])
            ld = Ring(self, es, "gld", 2, [128, T]); yr = Ring(self, es, "gy", 2, [128, T])
            sqr = Ring(self, es, "gsq", 2, [128, 512]); rr = Ring(self, es, "grn", 2, [128, 512])
            tmr = Ring(self, es, "gtm", 2, [128, NT, 128])
            for j in range(12):
                x, xk = ld.next(); y, yk = yr.next()
                self.dma(x[:], self.scr["D_fm"][GQ + j * 128:GQ + (j + 1) * 128, :], reads=["D_fm"], writes=[xk])
                tr.op("pool", lambda e: e.tensor_scalar(out=y[:], in0=x[:], scalar1=cw[:, j, 1:2], scalar2=None, op0=ALU.mult), reads=[xk, "gcw"], writes=[yk])
                for (o0, o1, i0, i1, k) in ((1, CTX, 0, CTX - 1, 0), (CTX + 1, T, CTX, T - 1, 0), (0, CTX - 1, 1, CTX, 2), (CTX, T - 1, CTX + 1, T, 2)):
                    tr.op("dve", lambda e: e.scalar_tensor_tensor(out=y[:, o0:o1], in0=x[:, i0:i1], scalar=cw[:, j, k:k + 1], in1=y[:, o0:o1],
                                                                 op0=ALU.mult, op1=ALU.add), reads=[xk, yk, "gcw"], writes=[yk])
                tr.op("act", lambda e: e.activation(out=y[:], in_=y[:], func=AF.Silu), reads=[yk], writes=[yk])
                if j < 8:
                    for ci, (t0, n) in enumerate(TCH):
                        sq, sk = sqr.next()
                        tr.op("act", lambda e: e.activation(out=sq[:, 0:n], in_=y[:, t0:t0 + n], func=AF.Square), reads=[yk], writes=[sk])
                        pk = ("ps", ci % 2); pt = self.ps[ci % 2]
                        self.mm(pt[:, 0:n], bd[:], sq[:, 0:n], True, True, reads=[sk, "gbd64"], writes=[pk])
                        r, rk = rr.next()
                        self.rstd_from_ss(pt[:, 0:n], pk, r[:, 0:n], rk, 1.0)
                        tr.op("dve", lambda e: e.scalar_tensor_tensor(out=y[:, t0:t0 + n], in0=y[:, t0:t0 + n], scalar=(0.125 if j < 4 else 1.0),
                                                                     in1=r[:, 0:n], op0=ALU.mult, op1=ALU.mult), reads=[yk, rk], writes=[yk])
                    dst = self.scr["D_gq"] if j < 4 else self.scr["D_gk"]
                    self.dma(dst[(j % 4) * 128:(j % 4 + 1) * 128, :], y[:], reads=[yk], writes=["D_gqk"])
                if j >= 4:
                    tm, tk = tmr.next()
                    for t4 in range(0, NT, 4):
                        nt = min(4, NT - t4)
                        pk = ("ps", 2 + (t4 // 4) % 2); pt = self.ps[2 + (t4 // 4) % 2]
                        for tt in range(nt):
                            t = t4 + tt
                            tr.op("pe", lambda e: e.transpose(pt[:, tt * 128:(tt + 1) * 128], y[:, t * 128:(t + 1) * 128], self.ident[:]),
                                  reads=[yk, "ident"], writes=[pk])
                        self.copy(self.evac_eng(), tm[:, t4:t4 + nt, :], pt[:, 0:nt * 128].rearrange("p (t d) -> p t d", d=128), reads=[pk], writes=[tk])
                    dst = self.scr["D_ktm"] if j < 8 else self.scr["D_vtm"]
                    c = (j % 4) * 2
                    for hh in range(2):
                        self.dma(dst[c + hh].rearrange("(t p) d -> p t d", p=128), tm[:, :, hh * 64:(hh + 1) * 64], reads=[tk], writes=["D_kvtm"])
            tr.barrier()

    def phase_gdn(self, l):
        tr = self.tr
        with ExitStack() as es:
            oacc = self.sb(es, "oacc", [128, NT, 512])
            ab = self.sb(es, "gab", [128, NT, 32])
            g = self.sb(es, "gg", [128, NT, 16]); beta = self.sb(es, "gbeta", [128, NT, 16]); nbeta = self.sb(es, "gnbeta", [128, NT, 16])
            gam = self.sb(es, "ggam", [128, NT, 16]); eg = self.sb(es, "geg", [128, NT, 16]); bg = self.sb(es, "gbg", [128, NT, 16])
            edl = self.sb(es, "gedl", [128, NT, 16]); tmp = self.sb(es, "gtmp", [128, NT, 16])
            alog = self.sb(es, "galog", [128, 16]); dtb = self.sb(es, "gdtb", [128, 16])
            tri = self.sb(es, "gtri", [128, 2, 128]); gm = self.sb(es, "gmask", [128, 4, 128]); sel = self.sb(es, "gsel", [128, 8, 128])
            S = self.sb(es, "gS", [128, 2, 8, 64])
            self.dma(ab[:], self.scr["D_tm"][:, AB:AB + 32].rearrange("(t p) c -> p t c", p=128), reads=["D_tm"], writes=["gab"])
            self.dma(alog[:], self.inp["alog_rep"][:, l, :], writes=["galog"])
            self.dma(dtb[:], self.inp["dtb_rep"][:, l, :], writes=["gdtb"])
            self.dma(tri[:], self.inp["tri"].rearrange("a p n -> p a n"), writes=["gtri"])
            self.dma(gm[:], self.inp["gmask"].rearrange("a p n -> p a n"), writes=["gmaskc"])
            tr.op("pool", lambda e: e.memset(sel[:], 0.0), writes=["gsel"])
            self.dma(sel[0:8, :, :], self.inp["selh"].rearrange("h u n -> u h n"), writes=["gsel"])
            bmk = self.sb(es, "gbmask", [128, 3, 128])
            self.dma(bmk[:], self.inp["bmask"].rearrange("a p n -> p a n"), writes=["gbmask"])
            tr.op("dve", lambda e: e.memset(S[:], 0.0), writes=[(f"S{d}", h) for d in range(2) for h in range(8)])
            tr.op("act", lambda e: e.activation(out=alog[:], in_=alog[:], func=AF.Exp), reads=["galog"], writes=["galog"])
            for t in range(NT):
                tr.op("dve", lambda e: e.tensor_tensor(out=tmp[:, t, :], in0=ab[:, t, 0:16], in1=dtb[:], op=ALU.add), reads=["gab", "gdtb"], writes=["gtmp"])
            tr.op("act", lambda e: e.activation(out=tmp[:], in_=tmp[:], func=AF.Exp), reads=["gtmp"], writes=["gtmp"])
            tr.op("act", lambda e: e.activation(out=tmp[:], in_=tmp[:], func=AF.Ln, bias=self.cst[:, 1:2], scale=1.0), reads=["gtmp", "cst"], writes=["gtmp"])
            for t in range(NT):
                tr.op("dve", lambda e: e.scalar_tensor_tensor(out=g[:, t, :], in0=tmp[:, t, :], scalar=-1.0, in1=alog[:], op0=ALU.mult, op1=ALU.mult),
                      reads=["gtmp", "galog"], writes=["gg"])
            tr.op("act", lambda e: e.activation(out=beta[:], in_=ab[:, :, 16:32], func=AF.Sigmoid), reads=["gab"], writes=["gbeta"])
            tr.op("dve", lambda e: e.tensor_scalar(out=nbeta[:], in0=beta[:], scalar1=-1.0, scalar2=None, op0=ALU.mult), reads=["gbeta"], writes=["gnbeta"])
            pg = self.ps[0][:, 0:NT * 16].rearrange("p (t u) -> p t u", u=16)
            pt_ = self.ps[1][:, 0:NT * 16].rearrange("p (t u) -> p t u", u=16)
            for t in range(NT):
                for d in range(2):
                    self.mm(pg[:, t, d * 8:(d + 1) * 8], tri[:, d, :], g[:, t, d * 8:(d + 1) * 8], True, True, reads=["gtri", "gg"], writes=[("ps", 0)])
                self.mm(pt_[:, t, :], self.ones32[:], g[:, t, :], True, True, reads=["ones32", "gg"], writes=[("ps", 1)])
            self.copy("dve", gam[:], pg, reads=[("ps", 0)], writes=["ggam"])
            tr.op("dve", lambda e: e.tensor_tensor(out=edl[:], in0=pt_, in1=gam[:], op=ALU.subtract), reads=[("ps", 1), "ggam"], writes=["gedl"])
            tr.op("act", lambda e: e.activation(out=edl[:], in_=edl[:], func=AF.Exp), reads=["gedl"], writes=["gedl"])
            tr.op("act", lambda e: e.activation(out=eg[:], in_=gam[:], func=AF.Exp), reads=["ggam"], writes=["geg"])
            tr.op("dve", lambda e: e.tensor_tensor(out=bg[:], in0=beta[:], in1=eg[:], op=ALU.mult), reads=["gbeta", "geg"], writes=["gbg"])
            if self.stop == "gdn_gates":
                tr.barrier()
                return
            es_scan = ExitStack()
            NSLOT = 4
            R1 = lambda nm, shp, n=1: [Ring(self, es_scan, f"g{nm}{sl_}", n, shp) for sl_ in range(NSLOT)]
            qTr = R1("iq", [128, 128]); kTr = R1("ik", [128, 128]); ktmr = R1("iktm", [128, 64]); vtmr = R1("ivtm", [128, 64]); kdr = R1("kd", [128, 64])
            t1r = R1("t1", [128, 128]); t2r = R1("t2", [128, 128]); egr = R1("egb", [64, 128])
            a0r = R1("a0", [128, 256]); pxr_ = R1("px", [128, 256], 2); ptr__ = R1("pt", [128, 128], 2); lr = R1("l", [128, 2, 128])
            ybr_ = R1("yb", [128, 128], 5); qkr = R1("qk", [128, 128]); wtr = R1("wt", [128, 128]); qgr = R1("qg", [128, 128]); vnr = R1("vn", [128, 64])
            gTr = Ring(self, es_scan, "ggT", 3, [128, 128])
            for rg in qTr + kTr + wtr + qgr + [gTr]:
                for ti, tl in enumerate(rg.tiles):
                    tr.op("pool", lambda e: e.memset(tl[:], 0.0), writes=[(rg.name, ti)])
            gqv = self.scr["D_gq"].rearrange("(h d) t -> h d t", d=64)
            gkv = self.scr["D_gk"].rearrange("(h d) t -> h d t", d=64)

            def unit(d, t, h, s_, gT, gTk):
                u = d * 8 + h
                Sk = f"S{d}"
                last = 127 if d == 0 else 0
                sl = slice(t * 128, (t + 1) * 128)
                pA, pAk, pB, pBk = self.ps[2 * s_], ("ps", 2 * s_), self.ps[2 * s_ + 1], ("ps", 2 * s_ + 1)
                qT, qk_ = qTr[s_].next(); kT, kk_ = kTr[s_].next(); ktm, ktk = ktmr[s_].next(); vtm, vtk = vtmr[s_].next()
                self.dma(qT[0:64, :], gqv[h, :, sl], reads=["D_gqk"], writes=[qk_])
                self.dma(kT[0:64, :], gkv[h, :, sl], reads=["D_gqk"], writes=[kk_])
                self.dma(ktm[:], self.scr["D_ktm"][h, sl, :], reads=["D_kvtm"], writes=[ktk])
                self.dma(vtm[:], self.scr["D_vtm"][h, sl, :], reads=["D_kvtm"], writes=[vtk])
                yield
                self.mm(pA[:, 0:128], sel[:, h, :], gT[:], True, True, reads=["gsel", gTk], writes=[pAk])
                self.mm(pB[:, 0:128], kT[:], kT[:], True, True, reads=[kk_], writes=[pBk])
                kd, kdk = kdr[s_].next()
                tr.op("pool", lambda e: e.tensor_scalar(out=kd[:], in0=ktm[:], scalar1=edl[:, t, u:u + 1], scalar2=None, op0=ALU.mult), reads=[ktk, "gedl"], writes=[kdk])
                a0, a0k = a0r[s_].next()
                tr.op("pool", lambda e: e.tensor_scalar(out=a0[:, 128:192], in0=vtm[:], scalar1=beta[:, t, u:u + 1], scalar2=None, op0=ALU.mult),
                      reads=[vtk, "gbeta"], writes=[(a0k, "x")])
                tr.op("pool", lambda e: e.tensor_scalar(out=a0[:, 192:256], in0=ktm[:], scalar1=bg[:, t, u:u + 1], scalar2=None, op0=ALU.mult),
                      reads=[ktk, "gbg"], writes=[(a0k, "x")])
                yield
                t1, t1k = t1r[s_].next(); t2, t2k = t2r[s_].next(); egb, egk = egr[s_].next()
                tr.op("dve", lambda e: e.scalar_tensor_tensor(out=t1[:], in0=pA[:, 0:128], scalar=gam[:, t, u:u + 1], in1=gm[:, 2 * d, :],
                                                             op0=ALU.subtract, op1=ALU.add), reads=[pAk, "ggam", "gmaskc"], writes=[t1k])
                tr.op("dve", lambda e: e.scalar_tensor_tensor(out=t2[:], in0=pA[:, 0:128], scalar=gam[:, t, u:u + 1], in1=gm[:, 2 * d + 1, :],
                                                             op0=ALU.subtract, op1=ALU.add), reads=[pAk, "ggam", "gmaskc"], writes=[t2k])
                tr.op("dve", lambda e: e.tensor_copy(out=egb[:], in_=pA[0:64, 0:128]), reads=[pAk], writes=[egk])
                yield
                tr.op("act", lambda e: e.activation(out=t1[:], in_=t1[:], func=AF.Exp, scale=-1.0), reads=[t1k], writes=[t1k])
                tr.op("act", lambda e: e.activation(out=t2[:], in_=t2[:], func=AF.Exp), reads=[t2k], writes=[t2k])
                tr.op("act", lambda e: e.activation(out=egb[:], in_=egb[:], func=AF.Exp), reads=[egk], writes=[egk])
                self.mm(pA[:, 0:128], kT[:], qT[:], True, True, reads=[kk_, qk_], writes=[pAk])
                yield
                tr.op("dve", lambda e: e.scalar_tensor_tensor(out=a0[:, 0:128], in0=pB[:, 0:128], scalar=nbeta[:, t, u:u + 1], in1=t1[:],
                                                             op0=ALU.mult, op1=ALU.mult), reads=[pBk, "gnbeta", t1k], writes=[(a0k, "p")])
                qk, qkk = qkr[s_].next()
                tr.op("dve", lambda e: e.tensor_tensor(out=qk[:], in0=pA[:, 0:128], in1=t2[:], op=ALU.mult), reads=[pAk, t2k], writes=[qkk])
                qg, qgk = qgr[s_].next()
                tr.op("pool", lambda e: e.tensor_tensor(out=qg[0:64, :], in0=qT[0:64, :], in1=egb[:], op=ALU.mult), reads=[qk_, egk], writes=[qgk])
                yield
                tr.op("pe", lambda e: e.transpose(pB[:, 0:128], a0[:, 0:128], self.ident[:]), reads=[(a0k, "p"), "ident"], writes=[pBk])
                px, pxk = pxr_[s_].next(); pT, pTk = ptr__[s_].next(); lt, ltk = lr[s_].next()
                tr.op("pool", lambda e: e.tensor_tensor(out=pT[:], in0=a0[:, 0:128], in1=bmk[:, 0, :], op=ALU.mult), reads=[(a0k, "p"), "gbmask"], writes=[pTk])
                tr.op("pool", lambda e: e.tensor_copy(out=px[:, 128:256], in_=self.ident[:]), reads=["ident"], writes=[(pxk, "x")])
                yield
                tr.op("dve", lambda e: e.tensor_tensor(out=px[:, 0:128], in0=pB[:, 0:128], in1=bmk[:, 0, :], op=ALU.mult), reads=[pBk, "gbmask"], writes=[(pxk, "p")])
                tr.op("dve", lambda e: e.tensor_tensor(out=lt[:, 0, :], in0=pB[:, 0:128], in1=bmk[:, 1, :], op=ALU.mult), reads=[pBk, "gbmask"], writes=[(ltk, 0)])
                tr.op("dve", lambda e: e.tensor_tensor(out=lt[:, 1, :], in0=pB[:, 0:128], in1=bmk[:, 2, :], op=ALU.mult), reads=[pBk, "gbmask"], writes=[(ltk, 1)])
                yield
                for lev in range(5):
                    if lev < 4:
                        self.mm(pA[:, 0:256], pT[:], px[:, 0:256], True, True, reads=[pTk, (pxk, "p"), (pxk, "x")], writes=[pAk])
                        self.mm(pB[:, 0:128], px[:, 0:128], pT[:], True, True, reads=[pTk, (pxk, "p")], writes=[pBk])
                        yield
                        nx, nxk = pxr_[s_].next(); nT, nTk = ptr__[s_].next()
                        self.copy("act", nT[:], pB[:, 0:128], reads=[pBk], writes=[nTk])
                        tr.op("dve", lambda e: e.tensor_tensor(out=nx[:, 128:256], in0=pA[:, 128:256], in1=px[:, 128:256], op=ALU.add),
                              reads=[pAk, (pxk, "x")], writes=[(nxk, "x")])
                        self.copy("dve", nx[:, 0:128], pA[:, 0:128], reads=[pAk], writes=[(nxk, "p")])
                        px, pxk, pT, pTk = nx, nxk, nT, nTk
                        yield
                    else:
                        self.mm(pA[:, 0:128], pT[:], px[:, 128:256], True, True, reads=[pTk, (pxk, "x")], writes=[pAk])
                        yield
                        nx, nxk = pxr_[s_].next()
                        tr.op("dve", lambda e: e.tensor_tensor(out=nx[:, 128:256], in0=pA[:, 0:128], in1=px[:, 128:256], op=ALU.add),
                              reads=[pAk, (pxk, "x")], writes=[(nxk, "x")])
                        px, pxk = nx, nxk
                        yield
                MT, MTk = px[:, 128:256], (pxk, "x")
                cnt = [0]

                def apply(lhsT_ap, lhs_key, rhs_ap, rhs_key, add_ap=None, add_key=None):
                    pb, pbk = (pA, pAk) if cnt[0] % 2 == 0 else (pB, pBk)
                    eng = "act" if cnt[0] % 2 == 0 else "dve"
                    cnt[0] += 1
                    self.mm(pb[:, 0:128], lhsT_ap, rhs_ap, True, True, reads=[lhs_key, rhs_key], writes=[pbk])
                    yield
                    yb, ybk = ybr_[s_].next()
                    if add_ap is None:
                        self.copy(eng, yb[:], pb[:, 0:128], reads=[pbk], writes=[ybk])
                    else:
                        tr.op("dve", lambda e: e.tensor_tensor(out=yb[:], in0=pb[:, 0:128], in1=add_ap, op=ALU.add), reads=[pbk, add_key], writes=[ybk])
                    yield
                    return yb, ybk

                def s64(z_ap, z_key):
                    y1, y1k = yield from apply(MT, MTk, z_ap, z_key)
                    z1, z1k = yield from apply(lt[:, 0, :], (ltk, 0), y1[:], y1k)
                    r_ = yield from apply(MT, MTk, z1[:], z1k, add_ap=y1[:], add_key=y1k)
                    return r_

                y2, y2k = yield from s64(a0[:, 128:256], (a0k, "x"))
                z2, z2k = yield from apply(lt[:, 1, :], (ltk, 1), y2[:], y2k)
                y4, y4k = yield from s64(z2[:], z2k)
                px, pxk = pxr_[s_].next()
                tr.op("pool", lambda e: e.tensor_tensor(out=px[:, 128:256], in0=y2[:], in1=y4[:], op=ALU.add), reads=[y2k, y4k], writes=[(pxk, "x")])
                yield
                tr.op("pe", lambda e: e.transpose(pA[0:64, 0:128], px[:, 192:256], self.ident[:]), reads=[(pxk, "x"), "ident"], writes=[pAk])
                yield
                wt_, wtk = wtr[s_].next()
                self.copy("act", wt_[0:64, :], pA[0:64, 0:128], reads=[pAk], writes=[wtk])
                yield
                self.mm(pB[:, 0:64], wt_[:], S[:, d, h, :], True, True, reads=[wtk, (Sk, h)], writes=[pBk])
                yield
                vn, vnk = vnr[s_].next()
                tr.op("dve", lambda e: e.tensor_tensor(out=vn[:], in0=px[:, 128:192], in1=pB[:, 0:64], op=ALU.subtract),
                      reads=[(pxk, "x"), pBk], writes=[vnk])
                yield
                self.mm(pA[:, 0:64], qg[:], S[:, d, h, :], True, False, reads=[qgk, (Sk, h)], writes=[pAk])
                self.mm(pA[:, 0:64], qk[:], vn[:], False, True, reads=[qkk, vnk], writes=[pAk])
                self.mm(pB[0:64, 0:64], kd[:], vn[:], True, True, reads=[kdk, vnk], writes=[pBk])
                yield
                if d == 0:
                    self.copy("act", oacc[:, t, h * 64:(h + 1) * 64], pA[:, 0:64], reads=[pAk], writes=[("oacc", t, h)])
                else:
                    tr.op("dve", lambda e: e.tensor_tensor(out=oacc[:, t, h * 64:(h + 1) * 64], in0=pA[:, 0:64], in1=oacc[:, t, h * 64:(h + 1) * 64], op=ALU.add),
                          reads=[pAk, ("oacc", t, h)], writes=[("oacc", t, h)])
                tr.op("dve", lambda e: e.scalar_tensor_tensor(out=S[0:64, d, h, :], in0=S[0:64, d, h, :], scalar=egb[:, last:last + 1], in1=pB[0:64, 0:64],
                                                             op0=ALU.mult, op1=ALU.add), reads=[(Sk, h), egk, pBk], writes=[(Sk, h)])

            def unit_stream():
                for d in range(2):
                    order = list(range(NT)) if d == 0 else [1, 0] + list(range(NT - 1, 1, -1))
                    for t in order:
                        gT, gTk = gTr.next()
                        pk = ("ps", 7)
                        self.mm(self.ps[7][0:8, 256:384], g[:, t, d * 8:(d + 1) * 8], tri[:, d, :], True, True, reads=["gg", "gtri"], writes=[pk])
                        self.copy("act", gT[0:8, :], self.ps[7][0:8, 256:384], reads=[pk], writes=[gTk])
                        for h in range(8):
                            yield (d, t, h, gT, gTk)

            stream = unit_stream()
            active = {}
            free_slots = list(range(NSLOT))
            exhausted = False
            while True:
                while free_slots and not exhausted:
                    try:
                        d_, t_, h_, gT_, gTk_ = next(stream)
                    except StopIteration:
                        exhausted = True
                        break
                    s_ = free_slots.pop(0)
                    active[s_] = unit(d_, t_, h_, s_, gT_, gTk_)
                if not active:
                    break
                for s_ in sorted(active):
                    try:
                        next(active[s_])
                    except StopIteration:
                        del active[s_]
                        free_slots.append(s_)
            if (self.stop or "").startswith("gdn_u1"):
                tr.barrier()
                es_scan.close()
                return
            tr.barrier()
            es_scan.close()
            for t in range(NT):
                for h in range(8):
                    tr.lastw[("oacc", t, h)] = None
            tr.lastw = {k: v for k, v in tr.lastw.items() if v is not None}
            if "D_x" in self.taps:
                self.dma(self.scr["D_oc"].rearrange("(t p) c -> p t c", p=128), oacc[:], reads=[("oacc", t, h) for t in range(NT) for h in range(8)])
            gn = self.sb(es, "gn", [128, 64])
            self.dma(gn[:], self.inp["gnorm_rep"][:, l, :], writes=["gn"])
            zr = Ring(self, es, "gz", 2, [128, 512]); sq2 = Ring(self, es, "gsq2", 2, [128, 512]); ssr = Ring(self, es, "gss", 2, [128, 8])
            ycr = Ring(self, es, "gyc", 2, [128, 4, 128], BF16)
            ycv = self.scr["D_yc"].rearrange("(c p) t -> p c t", p=128)
            for t in range(NT):
                z, zk = zr.next()
                self.dma(z[:], self.scr["D_tm"][t * 128:(t + 1) * 128, ZC:ZC + 512], reads=["D_tm"], writes=[zk])
                tr.op("act", lambda e: e.activation(out=z[:], in_=z[:], func=AF.Silu), reads=[zk], writes=[zk])
                ok = [("oacc", t, h) for h in range(8)]
                sq, sk = sq2.next()
                tr.op("pool", lambda e: e.tensor_tensor(out=sq[:], in0=oacc[:, t, :], in1=oacc[:, t, :], op=ALU.mult), reads=ok, writes=[sk])
                ss, ssk = ssr.next()
                tr.op("dve", lambda e: e.tensor_reduce(out=ss[:], in_=sq[:].rearrange("p (h d) -> p h d", d=64), axis=AX.X, op=ALU.add), reads=[sk], writes=[ssk])
                self.rstd_from_ss(ss[:], ssk, ss[:], ssk, 1.0 / 64)
                o3 = oacc[:, t, :].rearrange("p (h d) -> p h d", d=64)
                tr.op("dve", lambda e: e.tensor_tensor(out=sq[:].rearrange("p (h d) -> p h d", d=64), in0=o3, in1=ss[:].unsqueeze(2).to_broadcast([128, 8, 64]), op=ALU.mult),
                      reads=ok + [ssk], writes=[sk])
                tr.op("dve", lambda e: e.tensor_tensor(out=sq[:].rearrange("p (h d) -> p h d", d=64), in0=sq[:].rearrange("p (h d) -> p h d", d=64),
                                                      in1=gn[:].unsqueeze(1).to_broadcast([128, 8, 64]), op=ALU.mult), reads=[sk, "gn"], writes=[sk])
                tr.op("dve", lambda e: e.tensor_tensor(out=sq[:], in0=sq[:], in1=z[:], op=ALU.mult), reads=[sk, zk], writes=[sk])
                pk = ("ps", 4 + t % 2); pt = self.ps[4 + t % 2]
                for c in range(4):
                    tr.op("pe", lambda e: e.transpose(pt[:, c * 128:(c + 1) * 128], sq[:, c * 128:(c + 1) * 128], self.ident[:]), reads=[sk, "ident"], writes=[pk])
                yc, yk = ycr.next()
                self.copy("act", yc[:], pt[:].rearrange("p (c t) -> p c t", c=4), reads=[pk], writes=[yk])
                self.dma(ycv[:, :, t * 128:(t + 1) * 128], yc[:], reads=[yk], writes=["D_yc"])
            tr.barrier()

    def phase_merge(self, l):
        tr = self.tr
        with ExitStack() as es:
            wp = [self.sb(es, f"wp{i}", [128, 4, D], BF16) for i in range(3)]
            wo = self.sb(es, "wo", [128, 8, D], BF16)
            for i, nm in enumerate(("w_pa", "w_pb", "w_pc")):
                self.dma(wp[i][:], self.inp[nm][l].rearrange("(c p) n -> p c n", p=128), writes=[nm], q="gq")
            self.dma(wo[:], self.inp["w_out"][l].rearrange("(c p) n -> p c n", p=128), writes=["w_out"], q="gq")
            gr = Ring(self, es, "mg", 1, [128, 24, 512], BF16)
            yr = [Ring(self, es, f"my{i}", 2, [128, 4, 512], BF16) for i in range(3)]
            mixr = Ring(self, es, "mix", 2, [128, 512]); tmr = Ring(self, es, "mtm", 2, [128, 512])
            mT = Ring(self, es, "mixT", 1, [128, 8, 512], BF16)
            gv = self.scr["D_gate"].rearrange("(c p) t -> p c t", p=128)
            yv = [self.scr[nm].rearrange("(c p) t -> p c t", p=128) for nm in ("D_ya", "D_yb", "D_yc")]
            pi = 0
            for ci, (t0, n) in enumerate(TCH):
                s = 1 if ci == 0 else 0
                gt, gk = gr.next()
                self.dma(gt[:, :, 0:n], gv[:, :, t0:t0 + n], reads=[("D_gate", ci)], writes=[gk])
                ys = []
                for i in range(3):
                    y, yk = yr[i].next()
                    self.dma(y[:, :, 0:n], yv[i][:, :, t0:t0 + n], reads=["D_ya", "D_yb", "D_yc"], writes=[yk])
                    ys.append((y, yk))
                mt, mtk = mT.next()
                for m in range(8):
                    mix, mk = mixr.next()
                    for i in range(3):
                        pk = ("ps", pi % 4); pt = self.ps[pi % 4]; pi += 1
                        y, yk = ys[i]
                        for c in range(4):
                            self.mm(pt[:, 0:n], wp[i][:, c, m * 128:(m + 1) * 128], y[:, c, 0:n], c == 0, c == 3, reads=[("w_pa", "w_pb", "w_pc")[i], yk], writes=[pk])
                        if i == 0:
                            tr.op("dve", lambda e: e.tensor_tensor(out=mix[:, 0:n], in0=pt[:, 0:n], in1=gt[:, m, 0:n], op=ALU.mult), reads=[pk, gk], writes=[mk])
                        else:
                            tm, tk = tmr.next()
                            tr.op("dve", lambda e: e.tensor_tensor(out=tm[:, 0:n], in0=pt[:, 0:n], in1=gt[:, i * 8 + m, 0:n], op=ALU.mult), reads=[pk, gk], writes=[tk])
                            if i == 1:
                                tr.op("pool", lambda e: e.tensor_tensor(out=mix[:, 0:n], in0=mix[:, 0:n], in1=tm[:, 0:n], op=ALU.add), reads=[mk, tk], writes=[mk])
                            else:
                                tr.op("pool", lambda e: e.tensor_tensor(out=mt[:, m, 0:n], in0=mix[:, 0:n], in1=tm[:, 0:n], op=ALU.add), reads=[mk, tk], writes=[(mtk, m)])
                for m in range(8):
                    pk = ("ps", 4 + m % 4); pt = self.ps[4 + m % 4]
                    for c in range(8):
                        self.mm(pt[:, 0:n], wo[:, c, m * 128:(m + 1) * 128], mt[:, c, 0:n], c == 0, c == 7, reads=["w_out", (mtk, c)], writes=[pk])
                    tr.op("dve", lambda e: e.scalar_tensor_tensor(out=self.xT[:, m, t0:t0 + n], in0=pt[:, 0:n], scalar=self.mod[:, l, 16 + m, s:s + 1],
                                                                 in1=self.xT[:, m, t0:t0 + n], op0=ALU.mult, op1=ALU.add),
                          reads=[pk, "mod", ("xT", ci)], writes=[("xT", ci)])
            tr.barrier()

    def phase_ffn(self, l):
        tr = self.tr
        with ExitStack() as es:
            fcw = self.sb(es, "fcw", [128, 44, 3])
            self.dma(fcw[:], self.inp["fconv_fm"][:, l, :, :], writes=["fcw"])
            with ExitStack() as es2:
                xn = self.sb(es2, "xn2", [128, 8, T], BF16)
                self.adaln(es2, l, 1, xn)
                wr = Ring(self, es2, "wup", 2, [128, 8, 2, 128], BF16)
                hr = [Ring(self, es2, f"fh{i}", 2, [128, T]) for i in range(2)]
                cr = [Ring(self, es2, f"fc{i}", 2, [128, T]) for i in range(2)]
                ar = Ring(self, es2, "fact", 2, [128, T], BF16)
                wv = self.inp["ffn_up"][l].rearrange("(c p) n -> p c n", p=128)
                pi = 0
                for cc in range(22):
                    w, wk = wr.next()
                    self.dma(w[:, :, 0, :], wv[:, :, cc * 128:(cc + 1) * 128], writes=[wk], q="gq")
                    self.dma(w[:, :, 1, :], wv[:, :, 2816 + cc * 128:2816 + (cc + 1) * 128], writes=[wk], q="gq")
                    cs_ = []
                    for i in range(2):
                        hb, hk = hr[i].next()
                        for ci, (t0, n) in enumerate(TCH):
                            pk = ("ps", pi % 4); pt = self.ps[pi % 4]; pi += 1
                            for c in range(8):
                                self.mm(pt[:, 0:n], w[:, c, i, :], xn[:, c, t0:t0 + n], c == 0, c == 7, reads=[wk, ("xn", ci)], writes=[pk])
                            self.copy(self.evac_eng(), hb[:, t0:t0 + n], pt[:, 0:n], reads=[pk], writes=[hk])
                        cb, ck = cr[i].next()
                        j = cc + 22 * i
                        eng = "dve" if i == 0 else "pool"
                        tr.op(eng, lambda e: e.tensor_scalar(out=cb[:], in0=hb[:], scalar1=fcw[:, j, 1:2], scalar2=None, op0=ALU.mult), reads=[hk, "fcw"], writes=[ck])
                        for (o0, o1, i0, i1, k) in ((1, CTX, 0, CTX - 1, 0), (CTX + 1, T, CTX, T - 1, 0), (0, CTX - 1, 1, CTX, 2), (CTX, T - 1, CTX + 1, T, 2)):
                            tr.op("dve", lambda e: e.scalar_tensor_tensor(out=cb[:, o0:o1], in0=hb[:, i0:i1], scalar=fcw[:, j, k:k + 1], in1=cb[:, o0:o1],
                                                                       op0=ALU.mult, op1=ALU.add), reads=[hk, ck, "fcw"], writes=[ck])
                        cs_.append((cb, ck))
                    (ca, cak), (cb, cbk) = cs_
                    tr.op("act", lambda e: e.activation(out=ca[:], in_=ca[:], func=AF.Silu), reads=[cak], writes=[cak])
                    a, ak = ar.next()
                    tr.op("dve", lambda e: e.tensor_tensor(out=a[:], in0=ca[:], in1=cb[:], op=ALU.mult), reads=[cak, cbk], writes=[ak])
                    self.dma(self.scr["D_act"][cc * 128:(cc + 1) * 128, :], a[:], reads=[ak], writes=["D_act"])
                tr.barrier()
            with ExitStack() as es2:
                wd = self.sb(es2, "wdn", [128, 22, D], BF16)
                self.dma(wd[:, 0:11, :], self.inp["ffn_down"][l].rearrange("(c p) n -> p c n", p=128)[:, 0:11, :], writes=["wdn"], q="gq")
                self.dma(wd[:, 11:22, :], self.inp["ffn_down"][l].rearrange("(c p) n -> p c n", p=128)[:, 11:22, :], writes=["wdn"], q="gq")
                ar = Ring(self, es2, "fa", 2, [128, 22, 512], BF16)
                av = self.scr["D_act"].rearrange("(c p) t -> p c t", p=128)
                for ci, (t0, n) in enumerate(TCH):
                    s = 1 if ci == 0 else 0
                    a, ak = ar.next()
                    self.dma(a[:, :, 0:n], av[:, :, t0:t0 + n], reads=["D_act"], writes=[ak])
                    for m in range(8):
                        pk = ("ps", m % 4); pt = self.ps[m % 4]
                        for c in range(22):
                            self.mm(pt[:, 0:n], wd[:, c, m * 128:(m + 1) * 128], a[:, c, 0:n], c == 0, c == 21, reads=["wdn", ak], writes=[pk])
                        tr.op("dve", lambda e: e.scalar_tensor_tensor(out=self.xT[:, m, t0:t0 + n], in0=pt[:, 0:n], scalar=self.mod[:, l, 40 + m, s:s + 1],
                                                                     in1=self.xT[:, m, t0:t0 + n], op0=ALU.mult, op1=ALU.add),
                              reads=[pk, "mod", ("xT", ci)], writes=[("xT", ci)])
                tr.barrier()

    def phase_final(self):
        tr = self.tr
        with ExitStack() as es:
            nf = self.sb(es, "nf", [128, D])
            self.dma(nf[:], self.inp["normf_rep"], writes=["nf"])
            xr = Ring(self, es, "fx", 2, [128, D]); sqr = Ring(self, es, "fsq", 2, [128, D]); ssr = Ring(self, es, "fss", 2, [128, 1])
            for t in range(2, NT):
                x, xk = xr.next()
                for half in range(2):
                    pk = ("ps", (2 * t + half) % 8); pt = self.ps[(2 * t + half) % 8]
                    for j in range(4):
                        c = half * 4 + j
                        tr.op("pe", lambda e: e.transpose(pt[:, j * 128:(j + 1) * 128], self.xT[:, c, t * 128:(t + 1) * 128], self.ident[:]),
                              reads=["ident"], writes=[pk])
                    self.copy(self.evac_eng(), x[:, half * 512:(half + 1) * 512], pt[:, :], reads=[pk], writes=[xk])
                sq, sk = sqr.next(); ss, ssk = ssr.next()
                tr.op("pool", lambda e: e.tensor_tensor(out=sq[:], in0=x[:], in1=x[:], op=ALU.mult), reads=[xk], writes=[sk])
                tr.op("dve", lambda e: e.tensor_reduce(out=ss[:], in_=sq[:], axis=AX.X, op=ALU.add), reads=[sk], writes=[ssk])
                self.rstd_from_ss(ss[:], ssk, ss[:], ssk, 1.0 / D)
                tr.op("dve", lambda e: e.scalar_tensor_tensor(out=sq[:], in0=x[:], scalar=ss[:, 0:1], in1=nf[:], op0=ALU.mult, op1=ALU.mult),
                      reads=[xk, ssk, "nf"], writes=[sk])
                self.dma(self.out[(t - 2) * 128:(t - 1) * 128, :], sq[:], reads=[sk])


def _rope_tables(d):
    da = d // 2
    half = da // 2
    inv = (10000.0 ** (-np.arange(half, dtype=np.float32) / half)).astype(np.float32)
    t = np.arange(SEQ)
    rows = (t // 64).astype(np.float32)
    cols = (t % 64).astype(np.float32)
    cos = np.ones((d, T), np.float32)
    sin = np.zeros((d, T), np.float32)
    for i in range(d):
        pos = rows if i < da else cols
        ii = i % da
        ang = pos * inv[ii % half]
        cos[i, CTX:] = np.cos(ang)
        sin[i, CTX:] = np.sin(ang) * (-1.0 if ii < half else 1.0)
    return np.stack([cos, sin])


def _partner(d):
    da = d // 2
    half = da // 2
    idx = np.arange(d)
    ii = idx % da
    return np.where(ii < half, idx + half, idx - half)


def _constants():
    j = np.arange(128)[:, None]
    i = np.arange(128)[None, :]
    tri = np.stack([(j <= i), (j >= i)]).astype(np.float32)
    gmask = np.stack([np.where(i < j, 0.0, BIG),
                      np.where(j <= i, 0.0, -BIG),
                      np.where(i > j, 0.0, BIG),
                      np.where(j >= i, 0.0, -BIG)]).astype(np.float32)
    m_prev = (j >= i).astype(np.float32)
    m_next = (j <= i).astype(np.float32)
    swamask = np.stack([np.tile(m_prev, (1, 4)), np.tile(m_next, (1, 4))]).astype(np.float32)
    b32 = (j // 32 == i // 32)
    b64 = (j // 64 == i // 64)
    bmask = np.stack([b32, b64 & ~b32, ~b64]).astype(np.float32)
    selh = np.zeros((8, 8, 128), np.float32)
    for h in range(8):
        selh[h, h, :] = 1.0
    return dict(ident=np.eye(128, dtype=np.float32), tri=tri, gmask=gmask, bmask=bmask, bd64=b64.astype(np.float32), swamask=swamask, selh=selh,
                ropeM=_rope_tables(32), ropeS=_rope_tables(64))


def _fm(v, p=128):
    v = np.asarray(v, np.float32)
    lead = v.shape[:-1]
    n = v.shape[-1] // p
    return np.ascontiguousarray(np.moveaxis(v.reshape(lead + (n, p)), -1, 0))


def prepare_shared(inp, L=DEPTH):
    f = lambda k: np.asarray(inp[k], np.float32)[:L] if k not in ("norm_f",) else np.asarray(inp[k], np.float32)
    w_in = f("w_in")
    p32 = _partner(32)
    p64 = _partner(64)
    sqp = (np.arange(8)[:, None] * 64 + p64[None, :]).reshape(-1)
    skp = (np.arange(2)[:, None] * 64 + p64[None, :]).reshape(-1)
    w_fm = np.concatenate([w_in[:, :, 0:640], w_in[:, :, 640:672], w_in[:, :, 640:672][:, :, p32],
                           w_in[:, :, 672:1184], w_in[:, :, 672:1184][:, :, sqp],
                           w_in[:, :, 1184:1312], w_in[:, :, 1184:1312][:, :, skp],
                           w_in[:, :, 1440:2976], np.zeros((L, D, 64), np.float32), w_in[:, :, 3520:6592]], axis=2)
    assert w_fm.shape[2] == NFM
    w_tm = np.concatenate([w_in[:, :, 1312:1440], w_in[:, :, 2976:3488], w_in[:, :, 3488:3520]], axis=2)
    wuq = f("w_uq").reshape(L, 384, 8, 96)
    wukv = f("w_ukv").reshape(L, 256, 8, 128)
    sh = dict(
        w_mod=f("w_mod"), b_mod_fm=_fm(f("b_mod")),
        norm1_fm=_fm(f("norm1")), norm2_fm=_fm(f("norm2")), normf_rep=np.ascontiguousarray(np.broadcast_to(f("norm_f"), (128, D))),
        w_fm=np.ascontiguousarray(w_fm), w_tm=np.ascontiguousarray(w_tm),
        qnorm_fm=_fm(f("mla_q_norm")), kvnorm_fm=_fm(f("mla_kv_norm")),
        wuq_n=np.ascontiguousarray(wuq[..., 0:64].reshape(L, 384, 512)),
        wuq_r=np.ascontiguousarray(wuq[..., 64:96].reshape(L, 384, 256)),
        wuq_rp=np.ascontiguousarray(wuq[..., 64:96][..., p32].reshape(L, 384, 256)),
        wukv_k=np.ascontiguousarray(wukv[..., 0:64].reshape(L, 256, 512)),
        wukv_v=np.ascontiguousarray(wukv[..., 64:128].reshape(L, 256, 512)),
        sink_rep=np.ascontiguousarray(np.broadcast_to(f("swa_sink"), (64, L, 8))),
        gconv_fm=np.ascontiguousarray(f("gdn_conv").reshape(L, 3, 12, 128).transpose(3, 0, 2, 1)),
        alog_rep=np.ascontiguousarray(np.broadcast_to(f("gdn_a_log").reshape(L, 16), (128, L, 16))),
        dtb_rep=np.ascontiguousarray(np.broadcast_to(f("gdn_dt_bias").reshape(L, 16), (128, L, 16))),
        gnorm_rep=np.ascontiguousarray(np.broadcast_to(f("gdn_norm"), (128, L, 64))),
        w_pa=f("w_branch_a"), w_pb=f("w_branch_b"), w_pc=f("w_branch_c"), w_out=f("w_out"),
        ffn_up=f("ffn_up"), fconv_fm=np.ascontiguousarray(f("ffn_conv").reshape(L, 3, 44, 128).transpose(3, 0, 2, 1)),
        ffn_down=f("ffn_down"),
    )
    sh.update(_constants())
    return sh


def per_core(inp, b):
    x = np.asarray(inp["x"], np.float32)[b]
    ctx = np.asarray(inp["ctx"], np.float32)[b]
    c = np.asarray(inp["c"], np.float32)[b]
    cc = np.asarray(inp["c_ctx"], np.float32)
    return dict(xin=np.ascontiguousarray(np.concatenate([ctx, x], 0)),
                c_fm=np.ascontiguousarray(np.stack([_fm(c), _fm(cc)], axis=-1)))


_CACHE = {}


def kernel(**inputs):
    if "nc" not in _CACHE:
        _CACHE["nc"] = Builder().build()
    nc = _CACHE["nc"]
    shared = prepare_shared(inputs)
    in_maps = []
    for b in range(8):
        m = dict(shared)
        m.update(per_core(inputs, b))
        in_maps.append(m)
    res = run_bass_kernel_spmd(nc, in_maps, core_ids=list(range(8)))
    return np.stack([np.asarray(r["out"], np.float32) for r in res.results], axis=0)
```

```python
import numpy as np
from contextlib import ExitStack
import concourse.bass as bass
import concourse.mybir as mybir
from concourse.bass_utils import run_bass_kernel_spmd

F32 = mybir.dt.float32
BF16 = mybir.dt.bfloat16
AF = mybir.ActivationFunctionType
ALU = mybir.AluOpType
AX = mybir.AxisListType

D = 1024
SEQ = 2048
CTX = 256
T = SEQ + CTX
NT = T // 128
DEPTH = 4
EPS = 1e-6
TCH = [(0, 256), (256, 512), (768, 512), (1280, 512), (1792, 512)]
BIG = 1.0e5

CQ, CKV, KR, KRP, SQ, SQP, SK, SKP, GQ, PAD, GATE, NFM = 0, 384, 640, 672, 704, 1216, 1728, 1856, 1984, 3520, 3584, 6656
SV, ZC, AB, NTM = 0, 128, 640, 672


def tile_chunk(t):
    return 0 if t < 2 else 1 + (t - 2) // 4


class Trk:
    ISSUER = {"pe": "pe", "act": "act", "dve": "dve", "pool": "pool", "sp": "sp", "gq": "pool"}
    NSLOT = {"sp": 24, "gq": 8}

    def __init__(self, nc, es):
        self.nc = nc
        self.eng = {"pe": nc.tensor, "act": nc.scalar, "dve": nc.vector, "pool": nc.gpsimd, "sp": nc.sync}
        self.sem = {e: es.enter_context(nc.semaphore("sem_" + e)) for e in ("pe", "act", "dve", "pool")}
        for q, k in self.NSLOT.items():
            for i in range(k):
                self.sem[(q, i)] = es.enter_context(nc.semaphore(f"sem_{q}{i}"))
        self.cnt = {e: 0 for e in self.sem}
        self.nq = {q: 0 for q in self.NSLOT}
        self.lastw = {}
        self.readers = {}
        self.waited = {}
        self.n_wait = 0
        self.n_op = 0

    def _need(self, issuer, eng, dep):
        de, ds = dep
        if de == "pe" and eng == "pe":
            return
        k = (issuer, de)
        if self.waited.get(k, 0) >= ds:
            return
        self.waited[k] = ds
        self.eng[issuer].wait_ge(self.sem[de], ds)
        self.n_wait += 1

    def op(self, eng, emit, reads=(), writes=()):
        issuer = self.ISSUER[eng]
        for key in reads:
            w = self.lastw.get(key)
            if w is not None:
                self._need(issuer, eng, w)
            if isinstance(key, tuple) and key[0] == "ps":
                for rk, tok in self.readers.get(key, {}).items():
                    if rk != eng:
                        self._need(issuer, eng, tok)
        for key in writes:
            w = self.lastw.get(key)
            if w is not None:
                self._need(issuer, eng, w)
            for tok in self.readers.get(key, {}).values():
                self._need(issuer, eng, tok)
        if eng in self.NSLOT:
            sk = (eng, self.nq[eng] % self.NSLOT[eng])
            self.nq[eng] += 1
            if self.cnt[sk] > 0:
                self._need(issuer, eng, (sk, self.cnt[sk]))
            inc = 16
        else:
            sk = eng
            inc = 1
        inst = emit(self.eng[issuer])
        self.cnt[sk] += inc
        inst.then_inc(self.sem[sk], inc)
        tok = (sk, self.cnt[sk])
        for key in reads:
            self.readers.setdefault(key, {})[sk] = tok
        for key in writes:
            self.lastw[key] = tok
            self.readers[key] = {}
        self.n_op += 1
        return inst

    def barrier(self):
        for issuer in ("pe", "act", "dve", "pool", "sp"):
            for e, c in self.cnt.items():
                if c > 0:
                    self._need(issuer, "x", (e, c))
        self.lastw.clear()
        self.readers.clear()

    def finish(self, eng="sp"):
        for e, c in self.cnt.items():
            if c > 0:
                self._need(eng, "x", (e, c))


class Ring:
    def __init__(self, b, es, name, n, shape, dt=F32):
        self.tiles = [b.sb(es, f"{name}{i}", shape, dt) for i in range(n)]
        self.name = name
        self.n = n
        self.i = 0

    def next(self):
        j = self.i % self.n
        self.i += 1
        return self.tiles[j], (self.name, j)


class Builder:
    def __init__(self, n_layers=DEPTH, stop=None, taps=()):
        self.n_layers = n_layers
        self.stop = stop
        self.taps = set(taps)
        self.nc = bass.Bass("TRN2", target_bir_lowering=False)
        self.inp = {}
        self.scr = {}
        self.ev = 0

    def din(self, name, shape, dt=F32):
        self.inp[name] = self.nc.dram_tensor(name, list(shape), dt, kind="ExternalInput").ap()
        return self.inp[name]

    def dscr(self, name, shape, dt=F32, force=False):
        kind = "ExternalOutput" if (name in self.taps or force) else "Internal"
        self.scr[name] = self.nc.dram_tensor(name, list(shape), dt, kind=kind).ap()
        return self.scr[name]

    def declare(self):
        L = self.n_layers
        d = self.din
        d("xin", [T, D]); d("c_fm", [128, 8, 2])
        d("w_mod", [L, D, 6 * D]); d("b_mod_fm", [128, L, 48])
        d("norm1_fm", [128, L, 8]); d("norm2_fm", [128, L, 8]); d("normf_rep", [128, D])
        d("w_fm", [L, D, NFM]); d("w_tm", [L, D, NTM])
        d("qnorm_fm", [128, L, 3]); d("kvnorm_fm", [128, L, 2])
        d("wuq_n", [L, 384, 512]); d("wuq_r", [L, 384, 256]); d("wuq_rp", [L, 384, 256])
        d("wukv_k", [L, 256, 512]); d("wukv_v", [L, 256, 512])
        d("sink_rep", [64, L, 8])
        d("gconv_fm", [128, L, 12, 3]); d("bd64", [128, 128]); d("alog_rep", [128, L, 16]); d("dtb_rep", [128, L, 16]); d("gnorm_rep", [128, L, 64])
        d("w_pa", [L, 512, D]); d("w_pb", [L, 512, D]); d("w_pc", [L, 512, D]); d("w_out", [L, D, D])
        d("ffn_up", [L, D, 5632]); d("fconv_fm", [128, L, 44, 3]); d("ffn_down", [L, 2816, D])
        d("ident", [128, 128]); d("ropeM", [2, 32, T]); d("ropeS", [2, 64, T])
        d("tri", [2, 128, 128]); d("gmask", [4, 128, 128]); d("bmask", [3, 128, 128]); d("swamask", [2, 128, 512]); d("selh", [8, 8, 128])
        self.out = self.nc.dram_tensor("out", [SEQ, D], F32, kind="ExternalOutput").ap()
        s = self.dscr
        s("D_fm", [NFM, T]); s("D_tm", [T, NTM]); s("D_gate", [3072, T], BF16)
        s("D_ya", [512, T], BF16); s("D_yb", [512, T], BF16); s("D_yc", [512, T], BF16)
        s("D_gq", [512, T]); s("D_gk", [512, T]); s("D_ktm", [8, T, 64]); s("D_vtm", [8, T, 64])
        s("D_act", [2816, T], BF16)
        if "D_x" in self.taps:
            s("D_x", [D, T], F32, True); s("D_xn", [D, T], BF16, True); s("D_mod", [128, L * 48 * 2], F32, True); s("D_oc", [T, 512], F32, True)

    def sb(self, es, name, shape, dt=F32):
        self.uid = getattr(self, "uid", 0) + 1
        return es.enter_context(self.nc.sbuf_tensor(f"s{self.uid}_{name}", list(shape), dt))

    def evac_eng(self):
        self.ev += 1
        return "act" if self.ev % 2 else "dve"

    def copy(self, eng, out, in_, reads, writes):
        if eng == "act":
            return self.tr.op("act", lambda e: e.copy(out=out, in_=in_), reads=reads, writes=writes)
        return self.tr.op(eng, lambda e: e.tensor_copy(out=out, in_=in_), reads=reads, writes=writes)

    def dma(self, out, in_, reads=(), writes=(), q="sp"):
        return self.tr.op(q, lambda e: e.dma_start(out=out, in_=in_), reads=reads, writes=writes)

    def mm(self, out, lhsT, rhs, start, stop, reads, writes):
        return self.tr.op("pe", lambda e: e.matmul(out, lhsT, rhs, start=start, stop=stop), reads=reads, writes=writes)

    def rstd_from_ss(self, ss_ap, ss_key, r_ap, r_key, scale):
        tr = self.tr
        tr.op("act", lambda e: e.activation(out=r_ap, in_=ss_ap, func=AF.Sqrt, bias=self.cst[0:r_ap.shape[0], 0:1], scale=scale),
              reads=[ss_key, "cst"], writes=[r_key])
        tr.op("dve", lambda e: e.reciprocal(out=r_ap, in_=r_ap), reads=[r_key], writes=[r_key])

    def build(self):
        nc = self.nc
        self.declare()
        with ExitStack() as es:
            self.tr = tr = Trk(nc, es)
            self.ps = [es.enter_context(nc.psum_tensor(f"ps{i}", [128, 512], F32)) for i in range(8)]
            self.xT = self.sb(es, "xT", [128, 8, T])
            self.ident = self.sb(es, "ident", [128, 128])
            self.ones32 = self.sb(es, "ones32", [128, 128])
            self.ones16 = self.sb(es, "ones16", [128, 128], BF16)
            self.cst = self.sb(es, "cst", [128, 4])
            self.mod = self.sb(es, "mod", [128, self.n_layers, 48, 2])
            self.n1 = self.sb(es, "n1", [128, self.n_layers, 8])
            self.n2 = self.sb(es, "n2", [128, self.n_layers, 8])
            self.gs = self.sb(es, "gs", [128, 2, 8, 2])
            self.dma(self.ident[:], self.inp["ident"], writes=["ident"])
            self.dma(self.n1[:], self.inp["norm1_fm"], writes=["n1"])
            self.dma(self.n2[:], self.inp["norm2_fm"], writes=["n2"])
            tr.op("dve", lambda e: e.memset(self.ones32[:], 1.0), writes=["ones32"])
            tr.op("dve", lambda e: e.memset(self.ones16[:], 1.0), writes=["ones16"])
            tr.op("dve", lambda e: e.memset(self.cst[:, 0:1], EPS), writes=["cst"])
            tr.op("dve", lambda e: e.memset(self.cst[:, 1:2], 1.0), writes=["cst"])
            tr.op("dve", lambda e: e.memset(self.cst[:, 2:3], 0.0), writes=["cst"])
            self.phase_load_x()
            self.phase_mod()
            done = self.stop == "mod"
            for l in range(self.n_layers):
                if done:
                    break
                for name, fn in (("norm1", lambda: self.phase_adaln_proj(l)), ("mla", lambda: self.phase_mla(l)),
                                 ("swa", lambda: self.phase_swa(l)), ("gdnprep", lambda: self.phase_gdn_prep(l)),
                                 ("gdn", lambda: self.phase_gdn(l)), ("merge", lambda: self.phase_merge(l)),
                                 ("ffn", lambda: self.phase_ffn(l))):
                    fn()
                    if self.stop == f"{name}{l}" or (self.stop or "").startswith(name + "_"):
                        done = True
                        break
            if "D_x" in self.taps:
                tr.barrier()
                for ci, (t0, n) in enumerate(TCH):
                    self.dma(self.scr["D_x"].rearrange("(c p) t -> p c t", p=128)[:, :, t0:t0 + n], self.xT[:, :, t0:t0 + n])
                self.dma(self.scr["D_mod"], self.mod[:].rearrange("p l j s -> p (l j s)"))
            if not done:
                self.phase_final()
            tr.barrier()
            tr.finish("sp")
        return nc

    def phase_load_x(self):
        tr = self.tr
        with ExitStack() as es:
            ring = Ring(self, es, "xl", 2, [128, D])
            for t in range(NT):
                xt, xk = ring.next()
                self.dma(xt[:], self.inp["xin"][t * 128:(t + 1) * 128, :], writes=[xk])
                for half in range(2):
                    pk = ("ps", (2 * t + half) % 8)
                    pt = self.ps[(2 * t + half) % 8]
                    for j in range(4):
                        c = half * 4 + j
                        tr.op("pe", lambda e: e.transpose(pt[:, j * 128:(j + 1) * 128], xt[:, c * 128:(c + 1) * 128], self.ident[:]),
                              reads=[xk, "ident"], writes=[pk])
                    self.copy(self.evac_eng(), self.xT[:, half * 4:half * 4 + 4, t * 128:(t + 1) * 128],
                              pt[:].rearrange("p (c t) -> p c t", c=4), reads=[pk], writes=[("xT", tile_chunk(t))])
            tr.barrier()

    def phase_mod(self):
        tr = self.tr
        with ExitStack() as es:
            cs = self.sb(es, "cs", [128, 8, 2])
            bm = self.sb(es, "bm", [128, self.n_layers, 48])
            ring = Ring(self, es, "wm", 2, [128, 8, 512])
            self.dma(cs[:], self.inp["c_fm"], writes=["cs"])
            self.dma(bm[:], self.inp["b_mod_fm"], writes=["bm"])
            tr.op("act", lambda e: e.activation(out=cs[:], in_=cs[:], func=AF.Silu), reads=["cs"], writes=["cs"])
            for l in range(self.n_layers):
                wv = self.inp["w_mod"][l].rearrange("(c p) n -> p c n", p=128)
                pk = ("ps", l % 2)
                pt = self.ps[l % 2][:, 0:96].rearrange("p (j s) -> p j s", s=2)
                for g in range(12):
                    w, wk = ring.next()
                    self.dma(w[:], wv[:, :, g * 512:(g + 1) * 512], writes=[wk])
                    for m in range(4):
                        j = g * 4 + m
                        for c in range(8):
                            self.mm(pt[:, j, :], w[:, c, m * 128:(m + 1) * 128], cs[:, c, :], c == 0, c == 7,
                                    reads=[wk, "cs"], writes=[pk])
                for s in range(2):
                    tr.op("dve", lambda e: e.tensor_tensor(out=self.mod[:, l, :, s], in0=pt[:, :, s], in1=bm[:, l, :], op=ALU.add),
                          reads=[pk, "bm"], writes=["mod"])
            tr.barrier()

    def make_gs(self, l, which):
        nrm = self.n1 if which == 0 else self.n2
        sc0 = 8 + which * 24
        for s in range(2):
            self.tr.op("dve", lambda e: e.scalar_tensor_tensor(out=self.gs[:, which, :, s], in0=self.mod[:, l, sc0:sc0 + 8, s], scalar=1.0,
                                                             in1=nrm[:, l, :], op0=ALU.add, op1=ALU.mult),
                       reads=["mod", "n1", "n2"], writes=["gs"])

    def adaln(self, es, l, which, xn):
        tr = self.tr
        self.make_gs(l, which)
        sh0 = which * 24
        es = ExitStack()
        sqr = Ring(self, es, f"sq{which}", 2, [128, 512])
        rr = Ring(self, es, f"rr{which}", 2, [128, 512])
        tmpr = Ring(self, es, f"tm{which}", 2, [128, 512])
        for ci, (t0, n) in enumerate(TCH):
            s = 1 if ci == 0 else 0
            pk = ("ps", ci % 2)
            pt = self.ps[ci % 2]
            for c in range(8):
                sq, sk = sqr.next()
                tr.op("act", lambda e: e.activation(out=sq[:, 0:n], in_=self.xT[:, c, t0:t0 + n], func=AF.Square),
                      reads=[("xT", ci)], writes=[sk])
                self.mm(pt[:, 0:n], self.ones32[:], sq[:, 0:n], c == 0, c == 7, reads=[sk, "ones32"], writes=[pk])
            r, rk = rr.next()
            self.rstd_from_ss(pt[:, 0:n], pk, r[:, 0:n], rk, 1.0 / D)
            for c in range(8):
                tm, tk = tmpr.next()
                tr.op("dve", lambda e: e.scalar_tensor_tensor(out=tm[:, 0:n], in0=self.xT[:, c, t0:t0 + n], scalar=self.gs[:, which, c, s:s + 1],
                                                             in1=r[:, 0:n], op0=ALU.mult, op1=ALU.mult),
                      reads=[("xT", ci), "gs", rk], writes=[tk])
                tr.op("act", lambda e: e.activation(out=xn[:, c, t0:t0 + n], in_=tm[:, 0:n], func=AF.Identity,
                                                    bias=self.mod[:, l, sh0 + c, s:s + 1], scale=1.0),
                      reads=[tk, "mod"], writes=[("xn", ci)])
        keep = {k: v for k, v in tr.lastw.items() if isinstance(k, tuple) and k[0] == "xn"}
        tr.barrier()
        tr.lastw.update(keep)
        es.close()

    def phase_adaln_proj(self, l):
        tr = self.tr
        with ExitStack() as es:
            xn = self.sb(es, "xn", [128, 8, T], BF16)
            self.adaln(es, l, 0, xn)
            if "D_x" in self.taps:
                for ci, (t0, n) in enumerate(TCH):
                    self.dma(self.scr["D_xn"].rearrange("(c p) t -> p c t", p=128)[:, :, t0:t0 + n], xn[:, :, t0:t0 + n], reads=[("xn", ci)])
            wr = Ring(self, es, "wfm", 2, [128, 8, 512], BF16)
            st = Ring(self, es, "stg", 3, [128, 512])
            st16 = Ring(self, es, "stg16", 2, [128, 512], BF16)
            wv = self.inp["w_fm"][l].rearrange("(c p) n -> p c n", p=128)
            pi = 0
            for g in range(NFM // 512):
                w, wk = wr.next()
                self.dma(w[:], wv[:, :, g * 512:(g + 1) * 512], writes=[wk], q="gq")
                col = g * 512
                while col < (g + 1) * 512:
                    if col < KR:
                        mc = 128
                    elif col < SQ:
                        mc = 32
                    elif col < PAD:
                        mc = 64
                    elif col < GATE:
                        col += 64
                        continue
                    else:
                        mc = 128
                    lo = col - g * 512
                    for ci, (t0, n) in enumerate(TCH):
                        pk = ("ps", pi % 4)
                        pt = self.ps[pi % 4]
                        pi += 1
                        for c in range(8):
                            self.mm(pt[0:mc, 0:n], w[:, c, lo:lo + mc], xn[:, c, t0:t0 + n], c == 0, c == 7,
                                    reads=[wk, ("xn", ci)], writes=[pk])
                        if col >= GATE:
                            s16, sk = st16.next()
                            tr.op("act", lambda e: e.activation(out=s16[:, 0:n], in_=pt[:, 0:n], func=AF.Sigmoid), reads=[pk], writes=[sk])
                            self.dma(self.scr["D_gate"][col - GATE:col - GATE + 128, t0:t0 + n], s16[:, 0:n], reads=[sk], writes=[("D_gate", ci)])
                        else:
                            s32, sk = st.next()
                            self.copy(self.evac_eng(), s32[0:mc, 0:n], pt[0:mc, 0:n], reads=[pk], writes=[sk])
                            self.dma(self.scr["D_fm"][col:col + mc, t0:t0 + n], s32[0:mc, 0:n], reads=[sk], writes=["D_fm"])
                    col += mc
            wt = self.sb(es, "wtm", [128, 8, NTM], BF16)
            self.dma(wt[:], self.inp["w_tm"][l].rearrange("(c p) n -> p c n", p=128), writes=["wtm"], q="gq")
            stm = Ring(self, es, "stm", 2, [128, NTM])
            for t in range(NT):
                ci = tile_chunk(t)
                pa, pb = self.ps[4 + (t % 2) * 2], self.ps[5 + (t % 2) * 2]
                ka, kb = ("ps", 4 + (t % 2) * 2), ("ps", 5 + (t % 2) * 2)
                for c in range(8):
                    self.mm(pa[:, 0:512], xn[:, c, t * 128:(t + 1) * 128], wt[:, c, 0:512], c == 0, c == 7, reads=["wtm", ("xn", ci)], writes=[ka])
                for c in range(8):
                    self.mm(pb[:, 0:160], xn[:, c, t * 128:(t + 1) * 128], wt[:, c, 512:672], c == 0, c == 7, reads=["wtm", ("xn", ci)], writes=[kb])
                s, sk = stm.next()
                self.copy("act", s[:, 0:512], pa[:, 0:512], reads=[ka], writes=[sk])
                self.copy("dve", s[:, 512:672], pb[:, 0:160], reads=[kb], writes=[sk])
                self.dma(self.scr["D_tm"][t * 128:(t + 1) * 128, :], s[:], reads=[sk], writes=["D_tm"])
            tr.barrier()

    def rms_fm_from_dram(self, es, row0, nchunk, gain, dst, tag):
        tr = self.tr
        ld = Ring(self, es, f"ld{tag}", 2, [128, nchunk, 512])
        sqr = Ring(self, es, f"sqn{tag}", 2, [128, 512])
        rr = Ring(self, es, f"rn{tag}", 2, [128, 512])
        src = self.scr["D_fm"][row0:row0 + nchunk * 128, :].rearrange("(c p) t -> p c t", p=128)
        for ci, (t0, n) in enumerate(TCH):
            x, xk = ld.next()
            self.dma(x[:, :, 0:n], src[:, :, t0:t0 + n], reads=["D_fm"], writes=[xk])
            pk = ("ps", 6 + ci % 2)
            pt = self.ps[6 + ci % 2]
            for c in range(nchunk):
                sq, sk = sqr.next()
                tr.op("act", lambda e: e.activation(out=sq[:, 0:n], in_=x[:, c, 0:n], func=AF.Square), reads=[xk], writes=[sk])
                self.mm(pt[:, 0:n], self.ones32[:], sq[:, 0:n], c == 0, c == nchunk - 1, reads=[sk, "ones32"], writes=[pk])
            r, rk = rr.next()
            self.rstd_from_ss(pt[:, 0:n], pk, r[:, 0:n], rk, 1.0 / (nchunk * 128))
            for c in range(nchunk):
                tr.op("dve", lambda e: e.scalar_tensor_tensor(out=dst[:, c, t0:t0 + n], in0=x[:, c, 0:n], scalar=gain[:, c:c + 1],
                                                             in1=r[:, 0:n], op0=ALU.mult, op1=ALU.mult),
                      reads=[xk, rk, "gain" + tag], writes=[(tag, ci)])

    def rope_from(self, src_ap, src_key, srcp_ap, srcp_key, cos_ap, sin_ap, dst_ap, dst_key, tmpa, tmpb, ka, kb):
        tr = self.tr
        tr.op("dve", lambda e: e.tensor_tensor(out=tmpa, in0=src_ap, in1=cos_ap, op=ALU.mult), reads=[src_key, "rope"], writes=[ka])
        tr.op("pool" if srcp_key[0] != "ps" else "dve", lambda e: e.tensor_tensor(out=tmpb, in0=srcp_ap, in1=sin_ap, op=ALU.mult),
              reads=[srcp_key, "rope"], writes=[kb])
        tr.op("dve", lambda e: e.tensor_tensor(out=dst_ap, in0=tmpa, in1=tmpb, op=ALU.add), reads=[ka, kb], writes=[dst_key])

    def phase_mla(self, l):
        tr = self.tr
        scale = 96.0 ** -0.5
        with ExitStack() as es:
            qg = self.sb(es, "qg", [128, 3]); kg = self.sb(es, "kg", [128, 2])
            self.dma(qg[:], self.inp["qnorm_fm"][:, l, :], writes=["gaincqn"])
            self.dma(kg[:], self.inp["kvnorm_fm"][:, l, :], writes=["gainckvn"])
            cqn = self.sb(es, "cqn", [128, 3, T], BF16)
            ckvn = self.sb(es, "ckvn", [128, 2, T], BF16)
            cosM = self.sb(es, "cosM", [32, T]); sinM = self.sb(es, "sinM", [32, T])
            self.dma(cosM[:], self.inp["ropeM"][0], writes=["rope"])
            self.dma(sinM[:], self.inp["ropeM"][1], writes=["rope"])
            KrT = self.sb(es, "KrT", [128, T], BF16)
            tr.op("pool", lambda e: e.memset(KrT[32:64, :], 0.0), writes=[("KrT", ci) for ci in range(5)])
            tr.op("pool", lambda e: e.memset(KrT[64:128, :], 0.0), writes=[("KrT", ci) for ci in range(5)])
            Vt = self.sb(es, "Vt", [128, NT, 576], BF16)
            wqn = self.sb(es, "wqn", [128, 3, 512], BF16); wqr = self.sb(es, "wqr", [128, 3, 256], BF16)
            wqp = self.sb(es, "wqp", [128, 3, 256], BF16)
            wkn = self.sb(es, "wkn", [128, 2, 512], BF16); wkv = self.sb(es, "wkv", [128, 2, 512], BF16)
            for wt_, nm in ((wqn, "wuq_n"), (wqr, "wuq_r"), (wqp, "wuq_rp"), (wkn, "wukv_k"), (wkv, "wukv_v")):
                self.dma(wt_[:], self.inp[nm][l].rearrange("(c p) n -> p c n", p=128), writes=[nm], q="gq")
            with ExitStack() as es2:
                self.rms_fm_from_dram(es2, CQ, 3, qg, cqn, "cqn")
                self.rms_fm_from_dram(es2, CKV, 2, kg, ckvn, "ckvn")
                kl = Ring(self, es2, "krl", 2, [32, 2, 512])
                ta = Ring(self, es2, "kta", 2, [32, 512]); tb = Ring(self, es2, "ktb", 2, [32, 512])
                for ci, (t0, n) in enumerate(TCH):
                    k2, kk = kl.next()
                    self.dma(k2[:, :, 0:n], self.scr["D_fm"][KR:KR + 64, :].rearrange("(a p) t -> p a t", p=32)[:, :, t0:t0 + n],
                             reads=["D_fm"], writes=[kk])
                    a_, ak = ta.next(); b_, bk = tb.next()
                    self.rope_from(k2[:, 0, 0:n], kk, k2[:, 1, 0:n], kk, cosM[:, t0:t0 + n], sinM[:, t0:t0 + n],
                                   KrT[0:32, t0:t0 + n], ("KrT", ci), a_[:, 0:n], b_[:, 0:n], ak, bk)
                for t in range(NT):
                    tr.op("pool", lambda e: e.memset(Vt[:, t, 512:576], 0.0), writes=[("Vt", t)])
                for t in range(NT):
                    ci = tile_chunk(t)
                    pk = ("ps", 4 + t % 2); pt = self.ps[4 + t % 2]
                    for c in range(2):
                        self.mm(pt[:, :], ckvn[:, c, t * 128:(t + 1) * 128], wkv[:, c, :], c == 0, c == 1, reads=[("ckvn", ci), "wukv_v"], writes=[pk])
                    self.copy(self.evac_eng(), Vt[:, t, 0:512], pt[:, :], reads=[pk], writes=[("Vt", t)])
                tr.barrier()
            if self.stop == "mla_prep":
                return
            with ExitStack() as es2:
                qnr = Ring(self, es2, "QnT", 2, [128, T], BF16); qrr = Ring(self, es2, "QrT", 2, [128, T], BF16)
                knr = Ring(self, es2, "KnT", 2, [128, T], BF16)
                for rg, lo in ((qnr, 64), (qrr, 32), (knr, 64)):
                    for ti, tl in enumerate(rg.tiles):
                        for p0_, p1_ in (((32, 64), (64, 128)) if lo == 32 else ((64, 128),)):
                            tr.op("pool", lambda e: e.memset(tl[p0_:p1_, :], 0.0), writes=[((rg.name, ti), ci) for ci in range(5)])
                ta = Ring(self, es2, "qta", 2, [32, 512]); tb = Ring(self, es2, "qtb", 2, [32, 512])
                pr = Ring(self, es2, "Pexp", 4, [128, 512], BF16)
                rdr = Ring(self, es2, "rden", 2, [64, 512])
                yar = Ring(self, es2, "yah", 2, [64, T], BF16)
                for h in range(8):
                    Qn, qnk = qnr.next(); Qr, qrk = qrr.next(); Kn, knk = knr.next()
                    for ci, (t0, n) in enumerate(TCH):
                        p0, p1, p2, p3 = self.ps[4], self.ps[5], self.ps[6], self.ps[7]
                        for c in range(3):
                            self.mm(p0[0:64, 0:n], wqn[:, c, h * 64:(h + 1) * 64], cqn[:, c, t0:t0 + n], c == 0, c == 2, reads=["wuq_n", ("cqn", ci)], writes=[("ps", 4)])
                        for c in range(3):
                            self.mm(p1[0:32, 0:n], wqr[:, c, h * 32:(h + 1) * 32], cqn[:, c, t0:t0 + n], c == 0, c == 2, reads=["wuq_r", ("cqn", ci)], writes=[("ps", 5)])
                        for c in range(3):
                            self.mm(p2[0:32, 0:n], wqp[:, c, h * 32:(h + 1) * 32], cqn[:, c, t0:t0 + n], c == 0, c == 2, reads=["wuq_rp", ("cqn", ci)], writes=[("ps", 6)])
                        for c in range(2):
                            self.mm(p3[0:64, 0:n], wkn[:, c, h * 64:(h + 1) * 64], ckvn[:, c, t0:t0 + n], c == 0, c == 1, reads=["wukv_k", ("ckvn", ci)], writes=[("ps", 7)])
                        self.copy("act", Qn[0:64, t0:t0 + n], p0[0:64, 0:n], reads=[("ps", 4)], writes=[(qnk, ci)])
                        self.copy("act", Kn[0:64, t0:t0 + n], p3[0:64, 0:n], reads=[("ps", 7)], writes=[(knk, ci)])
                        a_, ak = ta.next(); b_, bk = tb.next()
                        self.rope_from(p1[0:32, 0:n], ("ps", 5), p2[0:32, 0:n], ("ps", 6), cosM[:, t0:t0 + n], sinM[:, t0:t0 + n],
                                       Qr[0:32, t0:t0 + n], (qrk, ci), a_[:, 0:n], b_[:, 0:n], ak, bk)
                    if self.stop == "mla_proj":
                        break
                    ya, yk = yar.next()
                    for ci, (t0, n) in enumerate(TCH):
                        if self.stop == "mla_ctx" and ci > 0:
                            break
                        kts = [0, 1] if ci == 0 else list(range(NT))
                        po, pd = self.ps[3], self.ps[2]
                        def s_stage(i, kt):
                            kci = tile_chunk(kt)
                            bank = (0, 1, 4)[i % 3]
                            sk_ = ("ps", bank); psn = self.ps[bank]
                            self.mm(psn[:, 0:n], Kn[:, kt * 128:(kt + 1) * 128], Qn[:, t0:t0 + n], True, False,
                                    reads=[(knk, kci), (qnk, ci)], writes=[sk_])
                            self.mm(psn[:, 0:n], KrT[:, kt * 128:(kt + 1) * 128], Qr[:, t0:t0 + n], False, True,
                                    reads=[("KrT", kci), (qrk, ci)], writes=[sk_])
                            P, Pk = pr.next()
                            tr.op("act", lambda e: e.activation(out=P[:, 0:n], in_=psn[:, 0:n], func=AF.Exp, scale=scale), reads=[sk_], writes=[Pk])
                            return P, Pk

                        def pv_stage(i, kt, P, Pk):
                            self.mm(po[:, 0:n], Vt[:, kt, h * 64:h * 64 + 128], P[:, 0:n], i == 0, i == len(kts) - 1, reads=[("Vt", kt), Pk], writes=[("ps", 3)])
                            self.mm(pd[:, 0:n], self.ones16[:, :], P[:, 0:n], i == 0, i == len(kts) - 1, reads=["ones16", Pk], writes=[("ps", 2)])

                        pend = []
                        for i, kt in enumerate(kts):
                            pend.append((i, kt) + s_stage(i, kt))
                            if len(pend) > 2:
                                pv_stage(*pend.pop(0))
                        while pend:
                            pv_stage(*pend.pop(0))
                        rd, rk = rdr.next()
                        tr.op("dve", lambda e: e.reciprocal(out=rd[:, 0:n], in_=pd[0:64, 0:n]), reads=[("ps", 2)], writes=[rk])
                        tr.op("dve", lambda e: e.tensor_tensor(out=ya[:, t0:t0 + n], in0=po[0:64, 0:n], in1=rd[:, 0:n], op=ALU.mult),
                              reads=[("ps", 3), rk], writes=[yk])
                    self.dma(self.scr["D_ya"][h * 64:(h + 1) * 64, :], ya[:], reads=[yk], writes=["D_ya"])
                tr.barrier()

    def phase_swa(self, l):
        tr = self.tr
        with ExitStack() as es:
            QT = self.sb(es, "sQT", [128, 8, T], BF16)
            KT = self.sb(es, "sKT", [128, 2, T], BF16)
            tr.op("pool", lambda e: e.memset(QT[64:128, :, :], 0.0), writes=[("sQK", j) for j in range(8)])
            tr.op("pool", lambda e: e.memset(KT[64:128, :, :], 0.0), writes=[("sQK", 8), ("sQK", 9)])
            Vs = self.sb(es, "sVs", [128, NT, 128], BF16)
            cosS = self.sb(es, "cosS", [64, T]); sinS = self.sb(es, "sinS", [64, T])
            msk = self.sb(es, "smask", [128, 2, 512], BF16)
            snk = self.sb(es, "snk", [64, 8])
            self.dma(cosS[:], self.inp["ropeS"][0], writes=["rope"])
            self.dma(sinS[:], self.inp["ropeS"][1], writes=["rope"])
            self.dma(msk[:], self.inp["swamask"].rearrange("a p n -> p a n"), writes=["smask"], q="gq")
            self.dma(snk[:], self.inp["sink_rep"][:, l, :], writes=["snk"])
            tr.op("act", lambda e: e.activation(out=snk[:], in_=snk[:], func=AF.Exp), reads=["snk"], writes=["snk"])
            self.dma(Vs[:], self.scr["D_tm"][:, SV:SV + 128].rearrange("(t p) c -> p t c", p=128), reads=["D_tm"], writes=["sVs"], q="gq")
            with ExitStack() as es2:
                ld = Ring(self, es2, "sld", 1, [64, T]); ldp = Ring(self, es2, "sldp", 1, [64, T])
                ta = Ring(self, es2, "sta", 1, [64, T]); tb = Ring(self, es2, "stb", 1, [64, T])
                for j in range(10):
                    r0, rp = (SQ + j * 64, SQP + j * 64) if j < 8 else (SK + (j - 8) * 64, SKP + (j - 8) * 64)
                    dst = QT[0:64, j, :] if j < 8 else KT[0:64, j - 8, :]
                    x, xk = ld.next(); xp, xpk = ldp.next()
                    self.dma(x[:], self.scr["D_fm"][r0:r0 + 64, :], reads=["D_fm"], writes=[xk])
                    self.dma(xp[:], self.scr["D_fm"][rp:rp + 64, :], reads=["D_fm"], writes=[xpk])
                    a_, ak = ta.next(); b_, bk = tb.next()
                    self.rope_from(x[:], xk, xp[:], xpk, cosS[:], sinS[:], dst, ("sQK", j), a_[:], b_[:], ak, bk)
                tr.barrier()
            with ExitStack() as es2:
                pr = Ring(self, es2, "sP", 3, [128, 512], BF16)
                dn = Ring(self, es2, "sdn", 2, [64, 512])
                yr = Ring(self, es2, "syb", 2, [64, 4, 128], BF16)
                ybv = self.scr["D_yb"].rearrange("(h d) t -> d h t", d=64)
                it = 0
                for qt in range(NT):
                    if qt < 2:
                        kts = [(0, None), (1, None)]
                    else:
                        kts = [(0, None), (1, None)]
                        if qt > 2:
                            kts.append((qt - 1, 0))
                        kts.append((qt, None))
                        if qt < NT - 1:
                            kts.append((qt + 1, 1))
                    for g in range(2):
                        po, pd = self.ps[3], self.ps[2]
                        rhs = QT[:, 4 * g:4 * g + 4, qt * 128:(qt + 1) * 128]
                        def s_stage(i, kt, mk):
                            nonlocal it
                            sk_ = ("ps", it % 2); psn = self.ps[it % 2]; it += 1
                            self.mm(psn[:, :].rearrange("p (h q) -> p h q", h=4), KT[:, g, kt * 128:(kt + 1) * 128], rhs, True, True,
                                    reads=[("sQK", 8 + g)] + [("sQK", 4 * g + hh) for hh in range(4)], writes=[sk_])
                            P, Pk = pr.next()
                            tr.op("act", lambda e: e.activation(out=P[:], in_=psn[:, :], func=AF.Exp, scale=0.125), reads=[sk_], writes=[Pk])
                            if mk is not None:
                                tr.op("dve", lambda e: e.tensor_tensor(out=P[:], in0=P[:], in1=msk[:, mk, :], op=ALU.mult), reads=[Pk, "smask"], writes=[Pk])
                            return P, Pk

                        def pv_stage(i, kt, P, Pk):
                            self.mm(po[0:64, :], Vs[:, kt, g * 64:(g + 1) * 64], P[:], i == 0, i == len(kts) - 1, reads=["sVs", Pk], writes=[("ps", 3)])
                            self.mm(pd[:, :], self.ones16[:, :], P[:], i == 0, i == len(kts) - 1, reads=["ones16", Pk], writes=[("ps", 2)])

                        prev = None
                        for i, (kt, mk) in enumerate(kts):
                            cur = (i, kt) + s_stage(i, kt, mk)
                            if prev is not None:
                                pv_stage(*prev)
                            prev = cur
                        pv_stage(*prev)
                        d_, dk = dn.next()
                        for hh in range(4):
                            tr.op("dve", lambda e: e.tensor_scalar(out=d_[:, hh * 128:(hh + 1) * 128], in0=pd[0:64, hh * 128:(hh + 1) * 128],
                                                                  scalar1=snk[:, 4 * g + hh:4 * g + hh + 1], scalar2=None, op0=ALU.add),
                                  reads=[("ps", 2), "snk"], writes=[dk])
                        tr.op("dve", lambda e: e.reciprocal(out=d_[:], in_=d_[:]), reads=[dk], writes=[dk])
                        y, yk = yr.next()
                        tr.op("dve", lambda e: e.tensor_tensor(out=y[:].rearrange("p h q -> p (h q)"), in0=po[0:64, :], in1=d_[:], op=ALU.mult),
                              reads=[("ps", 3), dk], writes=[yk])
                        self.dma(ybv[:, 4 * g:4 * g + 4, qt * 128:(qt + 1) * 128], y[:], reads=[yk], writes=["D_yb"])
                tr.barrier()

    def phase_gdn_prep(self, l):
        tr = self.tr
        with ExitStack() as es:
            cw = self.sb(es, "gcw", [128, 12, 3])
            bd = self.sb(es, "gbd64", [128, 128])
            self.dma(cw[:], self.inp["gconv_fm"][:, l, :, :], writes=["gcw"])
            self.dma(bd[:], self.inp["bd64"], writes=["gbd64"])
            ld = Ring(self, es, "gld", 2, [128, T]); yr = Ring(self, es, "gy", 2, [128, T])
            sqr = Ring(self, es, "gsq", 2, [128, 512]); rr = Ring(self, es, "grn", 2, [128, 512])
            tmr = Ring(self, es, "gtm", 2, [128, NT, 128])
            for j in range(12):
                x, xk = ld.next(); y, yk = yr.next()
                self.dma(x[:], self.scr["D_fm"][GQ + j * 128:GQ + (j + 1) * 128, :], reads=["D_fm"], writes=[xk])
                tr.op("pool", lambda e: e.tensor_scalar(out=y[:], in0=x[:], scalar1=cw[:, j, 1:2], scalar2=None, op0=ALU.mult), reads=[xk, "gcw"], writes=[yk])
                for (o0, o1, i0, i1, k) in ((1, CTX, 0, CTX - 1, 0), (CTX + 1, T, CTX, T - 1, 0), (0, CTX - 1, 1, CTX, 2), (CTX, T - 1, CTX + 1, T, 2)):
                    tr.op("dve", lambda e: e.scalar_tensor_tensor(out=y[:, o0:o1], in0=x[:, i0:i1], scalar=cw[:, j, k:k + 1], in1=y[:, o0:o1],
                                                                 op0=ALU.mult, op1=ALU.add), reads=[xk, yk, "gcw"], writes=[yk])
                tr.op("act", lambda e: e.activation(out=y[:], in_=y[:], func=AF.Silu), reads=[yk], writes=[yk])
                if j < 8:
                    for ci, (t0, n) in enumerate(TCH):
                        sq, sk = sqr.next()
                        tr.op("act", lambda e: e.activation(out=sq[:, 0:n], in_=y[:, t0:t0 + n], func=AF.Square), reads=[yk], writes=[sk])
                        pk = ("ps", ci % 2); pt = self.ps[ci % 2]
                        self.mm(pt[:, 0:n], bd[:], sq[:, 0:n], True, True, reads=[sk, "gbd64"], writes=[pk])
                        r, rk = rr.next()
                        self.rstd_from_ss(pt[:, 0:n], pk, r[:, 0:n], rk, 1.0)
                        tr.op("dve", lambda e: e.scalar_tensor_tensor(out=y[:, t0:t0 + n], in0=y[:, t0:t0 + n], scalar=(0.125 if j < 4 else 1.0),
                                                                     in1=r[:, 0:n], op0=ALU.mult, op1=ALU.mult), reads=[yk, rk], writes=[yk])
                    dst = self.scr["D_gq"] if j < 4 else self.scr["D_gk"]
                    self.dma(dst[(j % 4) * 128:(j % 4 + 1) * 128, :], y[:], reads=[yk], writes=["D_gqk"])
                if j >= 4:
                    tm, tk = tmr.next()
                    for t4 in range(0, NT, 4):
                        nt = min(4, NT - t4)
                        pk = ("ps", 2 + (t4 // 4) % 2); pt = self.ps[2 + (t4 // 4) % 2]
                        for tt in range(nt):
                            t = t4 + tt
                            tr.op("pe", lambda e: e.transpose(pt[:, tt * 128:(tt + 1) * 128], y[:, t * 128:(t + 1) * 128], self.ident[:]),
                                  reads=[yk, "ident"], writes=[pk])
                        self.copy(self.evac_eng(), tm[:, t4:t4 + nt, :], pt[:, 0:nt * 128].rearrange("p (t d) -> p t d", d=128), reads=[pk], writes=[tk])
                    dst = self.scr["D_ktm"] if j < 8 else self.scr["D_vtm"]
                    c = (j % 4) * 2
                    for hh in range(2):
                        self.dma(dst[c + hh].rearrange("(t p) d -> p t d", p=128), tm[:, :, hh * 64:(hh + 1) * 64], reads=[tk], writes=["D_kvtm"])
            tr.barrier()

    def phase_gdn(self, l):
        tr = self.tr
        with ExitStack() as es:
            oacc = self.sb(es, "oacc", [128, NT, 512])
            ab = self.sb(es, "gab", [128, NT, 32])
            g = self.sb(es, "gg", [128, NT, 16]); beta = self.sb(es, "gbeta", [128, NT, 16]); nbeta = self.sb(es, "gnbeta", [128, NT, 16])
            gam = self.sb(es, "ggam", [128, NT, 16]); eg = self.sb(es, "geg", [128, NT, 16]); bg = self.sb(es, "gbg", [128, NT, 16])
            edl = self.sb(es, "gedl", [128, NT, 16]); tmp = self.sb(es, "gtmp", [128, NT, 16])
            alog = self.sb(es, "galog", [128, 16]); dtb = self.sb(es, "gdtb", [128, 16])
            tri = self.sb(es, "gtri", [128, 2, 128]); gm = self.sb(es, "gmask", [128, 4, 128]); sel = self.sb(es, "gsel", [128, 8, 128])
            S = self.sb(es, "gS", [128, 2, 8, 64])
            self.dma(ab[:], self.scr["D_tm"][:, AB:AB + 32].rearrange("(t p) c -> p t c", p=128), reads=["D_tm"], writes=["gab"])
            self.dma(alog[:], self.inp["alog_rep"][:, l, :], writes=["galog"])
            self.dma(dtb[:], self.inp["dtb_rep"][:, l, :], writes=["gdtb"])
            self.dma(tri[:], self.inp["tri"].rearrange("a p n -> p a n"), writes=["gtri"])
            self.dma(gm[:], self.inp["gmask"].rearrange("a p n -> p a n"), writes=["gmaskc"])
            tr.op("pool", lambda e: e.memset(sel[:], 0.0), writes=["gsel"])
            self.dma(sel[0:8, :, :], self.inp["selh"].rearrange("h u n -> u h n"), writes=["gsel"])
            bmk = self.sb(es, "gbmask", [128, 3, 128])
            self.dma(bmk[:], self.inp["bmask"].rearrange("a p n -> p a n"), writes=["gbmask"])
            tr.op("dve", lambda e: e.memset(S[:], 0.0), writes=[(f"S{d}", h) for d in range(2) for h in range(8)])
            tr.op("act", lambda e: e.activation(out=alog[:], in_=alog[:], func=AF.Exp), reads=["galog"], writes=["galog"])
            for t in range(NT):
                tr.op("dve", lambda e: e.tensor_tensor(out=tmp[:, t, :], in0=ab[:, t, 0:16], in1=dtb[:], op=ALU.add), reads=["gab", "gdtb"], writes=["gtmp"])
            tr.op("act", lambda e: e.activation(out=tmp[:], in_=tmp[:], func=AF.Exp), reads=["gtmp"], writes=["gtmp"])
            tr.op("act", lambda e: e.activation(out=tmp[:], in_=tmp[:], func=AF.Ln, bias=self.cst[:, 1:2], scale=1.0), reads=["gtmp", "cst"], writes=["gtmp"])
            for t in range(NT):
                tr.op("dve", lambda e: e.scalar_tensor_tensor(out=g[:, t, :], in0=tmp[:, t, :], scalar=-1.0, in1=alog[:], op0=ALU.mult, op1=ALU.mult),
                      reads=["gtmp", "galog"], writes=["gg"])
            tr.op("act", lambda e: e.activation(out=beta[:], in_=ab[:, :, 16:32], func=AF.Sigmoid), reads=["gab"], writes=["gbeta"])
            tr.op("dve", lambda e: e.tensor_scalar(out=nbeta[:], in0=beta[:], scalar1=-1.0, scalar2=None, op0=ALU.mult), reads=["gbeta"], writes=["gnbeta"])
            pg = self.ps[0][:, 0:NT * 16].rearrange("p (t u) -> p t u", u=16)
            pt_ = self.ps[1][:, 0:NT * 16].rearrange("p (t u) -> p t u", u=16)
            for t in range(NT):
                for d in range(2):
                    self.mm(pg[:, t, d * 8:(d + 1) * 8], tri[:, d, :], g[:, t, d * 8:(d + 1) * 8], True, True, reads=["gtri", "gg"], writes=[("ps", 0)])
                self.mm(pt_[:, t, :], self.ones32[:], g[:, t, :], True, True, reads=["ones32", "gg"], writes=[("ps", 1)])
            self.copy("dve", gam[:], pg, reads=[("ps", 0)], writes=["ggam"])
            tr.op("dve", lambda e: e.tensor_tensor(out=edl[:], in0=pt_, in1=gam[:], op=ALU.subtract), reads=[("ps", 1), "ggam"], writes=["gedl"])
            tr.op("act", lambda e: e.activation(out=edl[:], in_=edl[:], func=AF.Exp), reads=["gedl"], writes=["gedl"])
            tr.op("act", lambda e: e.activation(out=eg[:], in_=gam[:], func=AF.Exp), reads=["ggam"], writes=["geg"])
            tr.op("dve", lambda e: e.tensor_tensor(out=bg[:], in0=beta[:], in1=eg[:], op=ALU.mult), reads=["gbeta", "geg"], writes=["gbg"])
            if self.stop == "gdn_gates":
                tr.barrier()
                return
            es_scan = ExitStack()
            NSLOT = 4
            R1 = lambda nm, shp, n=1: [Ring(self, es_scan, f"g{nm}{sl_}", n, shp) for sl_ in range(NSLOT)]
            qTr = R1("iq", [128, 128]); kTr = R1("ik", [128, 128]); ktmr = R1("iktm", [128, 64]); vtmr = R1("ivtm", [128, 64]); kdr = R1("kd", [128, 64])
            t1r = R1("t1", [128, 128]); t2r = R1("t2", [128, 128]); egr = R1("egb", [64, 128])
            a0r = R1("a0", [128, 256]); pxr_ = R1("px", [128, 256], 2); ptr__ = R1("pt", [128, 128], 2); lr = R1("l", [128, 2, 128])
            ybr_ = R1("yb", [128, 128], 5); qkr = R1("qk", [128, 128]); wtr = R1("wt", [128, 128]); qgr = R1("qg", [128, 128]); vnr = R1("vn", [128, 64])
            gTr = Ring(self, es_scan, "ggT", 3, [128, 128])
            for rg in qTr + kTr + wtr + qgr + [gTr]:
                for ti, tl in enumerate(rg.tiles):
                    tr.op("pool", lambda e: e.memset(tl[:], 0.0), writes=[(rg.name, ti)])
            gqv = self.scr["D_gq"].rearrange("(h d) t -> h d t", d=64)
            gkv = self.scr["D_gk"].rearrange("(h d) t -> h d t", d=64)

            def unit(d, t, h, s_, gT, gTk):
                u = d * 8 + h
                Sk = f"S{d}"
                last = 127 if d == 0 else 0
                sl = slice(t * 128, (t + 1) * 128)
                pA, pAk, pB, pBk = self.ps[2 * s_], ("ps", 2 * s_), self.ps[2 * s_ + 1], ("ps", 2 * s_ + 1)
                qT, qk_ = qTr[s_].next(); kT, kk_ = kTr[s_].next(); ktm, ktk = ktmr[s_].next(); vtm, vtk = vtmr[s_].next()
                self.dma(qT[0:64, :], gqv[h, :, sl], reads=["D_gqk"], writes=[qk_])
                self.dma(kT[0:64, :], gkv[h, :, sl], reads=["D_gqk"], writes=[kk_])
                self.dma(ktm[:], self.scr["D_ktm"][h, sl, :], reads=["D_kvtm"], writes=[ktk])
                self.dma(vtm[:], self.scr["D_vtm"][h, sl, :], reads=["D_kvtm"], writes=[vtk])
                yield
                self.mm(pA[:, 0:128], sel[:, h, :], gT[:], True, True, reads=["gsel", gTk], writes=[pAk])
                self.mm(pB[:, 0:128], kT[:], kT[:], True, True, reads=[kk_], writes=[pBk])
                kd, kdk = kdr[s_].next()
                tr.op("act", lambda e: e.activation(out=kd[:], in_=ktm[:], func=AF.Identity, scale=edl[:, t, u:u + 1]), reads=[ktk, "gedl"], writes=[kdk])
                a0, a0k = a0r[s_].next()
                tr.op("act", lambda e: e.activation(out=a0[:, 128:192], in_=vtm[:], func=AF.Identity, scale=beta[:, t, u:u + 1]),
                      reads=[vtk, "gbeta"], writes=[(a0k, "x")])
                tr.op("act", lambda e: e.activation(out=a0[:, 192:256], in_=ktm[:], func=AF.Identity, scale=bg[:, t, u:u + 1]),
                      reads=[ktk, "gbg"], writes=[(a0k, "x")])
                yield
                t1, t1k = t1r[s_].next(); t2, t2k = t2r[s_].next(); egb, egk = egr[s_].next()
                tr.op("dve", lambda e: e.scalar_tensor_tensor(out=t1[:], in0=pA[:, 0:128], scalar=gam[:, t, u:u + 1], in1=gm[:, 2 * d, :],
                                                             op0=ALU.subtract, op1=ALU.add), reads=[pAk, "ggam", "gmaskc"], writes=[t1k])
                tr.op("dve", lambda e: e.scalar_tensor_tensor(out=t2[:], in0=pA[:, 0:128], scalar=gam[:, t, u:u + 1], in1=gm[:, 2 * d + 1, :],
                                                             op0=ALU.subtract, op1=ALU.add), reads=[pAk, "ggam", "gmaskc"], writes=[t2k])
                tr.op("dve", lambda e: e.tensor_copy(out=egb[:], in_=pA[0:64, 0:128]), reads=[pAk], writes=[egk])
                yield
                tr.op("act", lambda e: e.activation(out=t1[:], in_=t1[:], func=AF.Exp, scale=-1.0), reads=[t1k], writes=[t1k])
                tr.op("act", lambda e: e.activation(out=t2[:], in_=t2[:], func=AF.Exp), reads=[t2k], writes=[t2k])
                tr.op("act", lambda e: e.activation(out=egb[:], in_=egb[:], func=AF.Exp), reads=[egk], writes=[egk])
                self.mm(pA[:, 0:128], kT[:], qT[:], True, True, reads=[kk_, qk_], writes=[pAk])
                yield
                tr.op("dve", lambda e: e.scalar_tensor_tensor(out=a0[:, 0:128], in0=pB[:, 0:128], scalar=nbeta[:, t, u:u + 1], in1=t1[:],
                                                             op0=ALU.mult, op1=ALU.mult), reads=[pBk, "gnbeta", t1k], writes=[(a0k, "p")])
                qk, qkk = qkr[s_].next()
                tr.op("dve", lambda e: e.tensor_tensor(out=qk[:], in0=pA[:, 0:128], in1=t2[:], op=ALU.mult), reads=[pAk, t2k], writes=[qkk])
                qg, qgk = qgr[s_].next()
                tr.op("pool", lambda e: e.tensor_tensor(out=qg[0:64, :], in0=qT[0:64, :], in1=egb[:], op=ALU.mult), reads=[qk_, egk], writes=[qgk])
                yield
                tr.op("pe", lambda e: e.transpose(pB[:, 0:128], a0[:, 0:128], self.ident[:]), reads=[(a0k, "p"), "ident"], writes=[pBk])
                px, pxk = pxr_[s_].next(); pT, pTk = ptr__[s_].next(); lt, ltk = lr[s_].next()
                tr.op("pool", lambda e: e.tensor_tensor(out=pT[:], in0=a0[:, 0:128], in1=bmk[:, 0, :], op=ALU.mult), reads=[(a0k, "p"), "gbmask"], writes=[pTk])
                self.copy("act", px[:, 128:256], self.ident[:], reads=["ident"], writes=[(pxk, "x")])
                yield
                tr.op("dve", lambda e: e.tensor_tensor(out=px[:, 0:128], in0=pB[:, 0:128], in1=bmk[:, 0, :], op=ALU.mult), reads=[pBk, "gbmask"], writes=[(pxk, "p")])
                tr.op("dve", lambda e: e.tensor_tensor(out=lt[:, 0, :], in0=pB[:, 0:128], in1=bmk[:, 1, :], op=ALU.mult), reads=[pBk, "gbmask"], writes=[(ltk, 0)])
                tr.op("dve", lambda e: e.tensor_tensor(out=lt[:, 1, :], in0=pB[:, 0:128], in1=bmk[:, 2, :], op=ALU.mult), reads=[pBk, "gbmask"], writes=[(ltk, 1)])
                yield
                for lev in range(5):
                    if lev < 4:
                        self.mm(pA[:, 0:256], pT[:], px[:, 0:256], True, True, reads=[pTk, (pxk, "p"), (pxk, "x")], writes=[pAk])
                        self.mm(pB[:, 0:128], px[:, 0:128], pT[:], True, True, reads=[pTk, (pxk, "p")], writes=[pBk])
                        yield
                        nx, nxk = pxr_[s_].next(); nT, nTk = ptr__[s_].next()
                        self.copy("act", nT[:], pB[:, 0:128], reads=[pBk], writes=[nTk])
                        tr.op("dve", lambda e: e.tensor_tensor(out=nx[:, 128:256], in0=pA[:, 128:256], in1=px[:, 128:256], op=ALU.add),
                              reads=[pAk, (pxk, "x")], writes=[(nxk, "x")])
                        self.copy("dve", nx[:, 0:128], pA[:, 0:128], reads=[pAk], writes=[(nxk, "p")])
                        px, pxk, pT, pTk = nx, nxk, nT, nTk
                        yield
                    else:
                        self.mm(pA[:, 0:128], pT[:], px[:, 128:256], True, True, reads=[pTk, (pxk, "x")], writes=[pAk])
                        yield
                        nx, nxk = pxr_[s_].next()
                        tr.op("dve", lambda e: e.tensor_tensor(out=nx[:, 128:256], in0=pA[:, 0:128], in1=px[:, 128:256], op=ALU.add),
                              reads=[pAk, (pxk, "x")], writes=[(nxk, "x")])
                        px, pxk = nx, nxk
                        yield
                MT, MTk = px[:, 128:256], (pxk, "x")
                cnt = [0]

                def apply(lhsT_ap, lhs_key, rhs_ap, rhs_key, add_ap=None, add_key=None):
                    pb, pbk = (pA, pAk) if cnt[0] % 2 == 0 else (pB, pBk)
                    eng = "act" if cnt[0] % 2 == 0 else "dve"
                    cnt[0] += 1
                    self.mm(pb[:, 0:128], lhsT_ap, rhs_ap, True, True, reads=[lhs_key, rhs_key], writes=[pbk])
                    yield
                    yb, ybk = ybr_[s_].next()
                    if add_ap is None:
                        self.copy(eng, yb[:], pb[:, 0:128], reads=[pbk], writes=[ybk])
                    else:
                        tr.op("dve", lambda e: e.tensor_tensor(out=yb[:], in0=pb[:, 0:128], in1=add_ap, op=ALU.add), reads=[pbk, add_key], writes=[ybk])
                    yield
                    return yb, ybk

                def s64(z_ap, z_key):
                    y1, y1k = yield from apply(MT, MTk, z_ap, z_key)
                    z1, z1k = yield from apply(lt[:, 0, :], (ltk, 0), y1[:], y1k)
                    r_ = yield from apply(MT, MTk, z1[:], z1k, add_ap=y1[:], add_key=y1k)
                    return r_

                y2, y2k = yield from s64(a0[:, 128:256], (a0k, "x"))
                z2, z2k = yield from apply(lt[:, 1, :], (ltk, 1), y2[:], y2k)
                y4, y4k = yield from s64(z2[:], z2k)
                px, pxk = pxr_[s_].next()
                tr.op("pool", lambda e: e.tensor_tensor(out=px[:, 128:256], in0=y2[:], in1=y4[:], op=ALU.add), reads=[y2k, y4k], writes=[(pxk, "x")])
                yield
                tr.op("pe", lambda e: e.transpose(pA[0:64, 0:128], px[:, 192:256], self.ident[:]), reads=[(pxk, "x"), "ident"], writes=[pAk])
                yield
                wt_, wtk = wtr[s_].next()
                self.copy("act", wt_[0:64, :], pA[0:64, 0:128], reads=[pAk], writes=[wtk])
                yield
                self.mm(pB[:, 0:64], wt_[:], S[:, d, h, :], True, True, reads=[wtk, (Sk, h)], writes=[pBk])
                yield
                vn, vnk = vnr[s_].next()
                tr.op("dve", lambda e: e.tensor_tensor(out=vn[:], in0=px[:, 128:192], in1=pB[:, 0:64], op=ALU.subtract),
                      reads=[(pxk, "x"), pBk], writes=[vnk])
                yield
                self.mm(pA[:, 0:64], qg[:], S[:, d, h, :], True, False, reads=[qgk, (Sk, h)], writes=[pAk])
                self.mm(pA[:, 0:64], qk[:], vn[:], False, True, reads=[qkk, vnk], writes=[pAk])
                self.mm(pB[0:64, 0:64], kd[:], vn[:], True, True, reads=[kdk, vnk], writes=[pBk])
                yield
                if d == 0:
                    self.copy("act", oacc[:, t, h * 64:(h + 1) * 64], pA[:, 0:64], reads=[pAk], writes=[("oacc", t, h)])
                else:
                    tr.op("dve", lambda e: e.tensor_tensor(out=oacc[:, t, h * 64:(h + 1) * 64], in0=pA[:, 0:64], in1=oacc[:, t, h * 64:(h + 1) * 64], op=ALU.add),
                          reads=[pAk, ("oacc", t, h)], writes=[("oacc", t, h)])
                tr.op("dve", lambda e: e.scalar_tensor_tensor(out=S[0:64, d, h, :], in0=S[0:64, d, h, :], scalar=egb[:, last:last + 1], in1=pB[0:64, 0:64],
                                                             op0=ALU.mult, op1=ALU.add), reads=[(Sk, h), egk, pBk], writes=[(Sk, h)])

            def unit_stream():
                for d in range(2):
                    order = list(range(NT)) if d == 0 else [1, 0] + list(range(NT - 1, 1, -1))
                    for t in order:
                        gT, gTk = gTr.next()
                        pk = ("ps", 7)
                        self.mm(self.ps[7][0:8, 256:384], g[:, t, d * 8:(d + 1) * 8], tri[:, d, :], True, True, reads=["gg", "gtri"], writes=[pk])
                        self.copy("act", gT[0:8, :], self.ps[7][0:8, 256:384], reads=[pk], writes=[gTk])
                        for h in range(8):
                            yield (d, t, h, gT, gTk)

            stream = unit_stream()
            active = {}
            free_slots = list(range(NSLOT))
            exhausted = False
            while True:
                while free_slots and not exhausted:
                    try:
                        d_, t_, h_, gT_, gTk_ = next(stream)
                    except StopIteration:
                        exhausted = True
                        break
                    s_ = free_slots.pop(0)
                    active[s_] = unit(d_, t_, h_, s_, gT_, gTk_)
                if not active:
                    break
                for s_ in sorted(active):
                    try:
                        next(active[s_])
                    except StopIteration:
                        del active[s_]
                        free_slots.append(s_)
            if (self.stop or "").startswith("gdn_u1"):
                tr.barrier()
                es_scan.close()
                return
            tr.barrier()
            es_scan.close()
            for t in range(NT):
                for h in range(8):
                    tr.lastw[("oacc", t, h)] = None
            tr.lastw = {k: v for k, v in tr.lastw.items() if v is not None}
            if "D_x" in self.taps:
                self.dma(self.scr["D_oc"].rearrange("(t p) c -> p t c", p=128), oacc[:], reads=[("oacc", t, h) for t in range(NT) for h in range(8)])
            gn = self.sb(es, "gn", [128, 64])
            self.dma(gn[:], self.inp["gnorm_rep"][:, l, :], writes=["gn"])
            zr = Ring(self, es, "gz", 2, [128, 512]); sq2 = Ring(self, es, "gsq2", 2, [128, 512]); ssr = Ring(self, es, "gss", 2, [128, 8])
            ycr = Ring(self, es, "gyc", 2, [128, 4, 128], BF16)
            ycv = self.scr["D_yc"].rearrange("(c p) t -> p c t", p=128)
            for t in range(NT):
                z, zk = zr.next()
                self.dma(z[:], self.scr["D_tm"][t * 128:(t + 1) * 128, ZC:ZC + 512], reads=["D_tm"], writes=[zk])
                tr.op("act", lambda e: e.activation(out=z[:], in_=z[:], func=AF.Silu), reads=[zk], writes=[zk])
                ok = [("oacc", t, h) for h in range(8)]
                sq, sk = sq2.next()
                tr.op("pool", lambda e: e.tensor_tensor(out=sq[:], in0=oacc[:, t, :], in1=oacc[:, t, :], op=ALU.mult), reads=ok, writes=[sk])
                ss, ssk = ssr.next()
                tr.op("dve", lambda e: e.tensor_reduce(out=ss[:], in_=sq[:].rearrange("p (h d) -> p h d", d=64), axis=AX.X, op=ALU.add), reads=[sk], writes=[ssk])
                self.rstd_from_ss(ss[:], ssk, ss[:], ssk, 1.0 / 64)
                o3 = oacc[:, t, :].rearrange("p (h d) -> p h d", d=64)
                tr.op("dve", lambda e: e.tensor_tensor(out=sq[:].rearrange("p (h d) -> p h d", d=64), in0=o3, in1=ss[:].unsqueeze(2).to_broadcast([128, 8, 64]), op=ALU.mult),
                      reads=ok + [ssk], writes=[sk])
                tr.op("dve", lambda e: e.tensor_tensor(out=sq[:].rearrange("p (h d) -> p h d", d=64), in0=sq[:].rearrange("p (h d) -> p h d", d=64),
                                                      in1=gn[:].unsqueeze(1).to_broadcast([128, 8, 64]), op=ALU.mult), reads=[sk, "gn"], writes=[sk])
                tr.op("dve", lambda e: e.tensor_tensor(out=sq[:], in0=sq[:], in1=z[:], op=ALU.mult), reads=[sk, zk], writes=[sk])
                pk = ("ps", 4 + t % 2); pt = self.ps[4 + t % 2]
                for c in range(4):
                    tr.op("pe", lambda e: e.transpose(pt[:, c * 128:(c + 1) * 128], sq[:, c * 128:(c + 1) * 128], self.ident[:]), reads=[sk, "ident"], writes=[pk])
                yc, yk = ycr.next()
                self.copy("act", yc[:], pt[:].rearrange("p (c t) -> p c t", c=4), reads=[pk], writes=[yk])
                self.dma(ycv[:, :, t * 128:(t + 1) * 128], yc[:], reads=[yk], writes=["D_yc"])
            tr.barrier()

    def phase_merge(self, l):
        tr = self.tr
        with ExitStack() as es:
            wp = [self.sb(es, f"wp{i}", [128, 4, D], BF16) for i in range(3)]
            wo = self.sb(es, "wo", [128, 8, D], BF16)
            for i, nm in enumerate(("w_pa", "w_pb", "w_pc")):
                self.dma(wp[i][:], self.inp[nm][l].rearrange("(c p) n -> p c n", p=128), writes=[nm], q="gq")
            self.dma(wo[:], self.inp["w_out"][l].rearrange("(c p) n -> p c n", p=128), writes=["w_out"], q="gq")
            gr = Ring(self, es, "mg", 1, [128, 24, 512], BF16)
            yr = [Ring(self, es, f"my{i}", 2, [128, 4, 512], BF16) for i in range(3)]
            mixr = Ring(self, es, "mix", 2, [128, 512]); tmr = Ring(self, es, "mtm", 2, [128, 512])
            mT = Ring(self, es, "mixT", 1, [128, 8, 512], BF16)
            gv = self.scr["D_gate"].rearrange("(c p) t -> p c t", p=128)
            yv = [self.scr[nm].rearrange("(c p) t -> p c t", p=128) for nm in ("D_ya", "D_yb", "D_yc")]
            pi = 0
            for ci, (t0, n) in enumerate(TCH):
                s = 1 if ci == 0 else 0
                gt, gk = gr.next()
                self.dma(gt[:, :, 0:n], gv[:, :, t0:t0 + n], reads=[("D_gate", ci)], writes=[gk])
                ys = []
                for i in range(3):
                    y, yk = yr[i].next()
                    self.dma(y[:, :, 0:n], yv[i][:, :, t0:t0 + n], reads=["D_ya", "D_yb", "D_yc"], writes=[yk])
                    ys.append((y, yk))
                mt, mtk = mT.next()
                for m in range(8):
                    mix, mk = mixr.next()
                    for i in range(3):
                        pk = ("ps", pi % 4); pt = self.ps[pi % 4]; pi += 1
                        y, yk = ys[i]
                        for c in range(4):
                            self.mm(pt[:, 0:n], wp[i][:, c, m * 128:(m + 1) * 128], y[:, c, 0:n], c == 0, c == 3, reads=[("w_pa", "w_pb", "w_pc")[i], yk], writes=[pk])
                        if i == 0:
                            tr.op("dve", lambda e: e.tensor_tensor(out=mix[:, 0:n], in0=pt[:, 0:n], in1=gt[:, m, 0:n], op=ALU.mult), reads=[pk, gk], writes=[mk])
                        else:
                            tm, tk = tmr.next()
                            tr.op("dve", lambda e: e.tensor_tensor(out=tm[:, 0:n], in0=pt[:, 0:n], in1=gt[:, i * 8 + m, 0:n], op=ALU.mult), reads=[pk, gk], writes=[tk])
                            if i == 1:
                                tr.op("pool", lambda e: e.tensor_tensor(out=mix[:, 0:n], in0=mix[:, 0:n], in1=tm[:, 0:n], op=ALU.add), reads=[mk, tk], writes=[mk])
                            else:
                                tr.op("pool", lambda e: e.tensor_tensor(out=mt[:, m, 0:n], in0=mix[:, 0:n], in1=tm[:, 0:n], op=ALU.add), reads=[mk, tk], writes=[(mtk, m)])
                for m in range(8):
                    pk = ("ps", 4 + m % 4); pt = self.ps[4 + m % 4]
                    for c in range(8):
                        self.mm(pt[:, 0:n], wo[:, c, m * 128:(m + 1) * 128], mt[:, c, 0:n], c == 0, c == 7, reads=["w_out", (mtk, c)], writes=[pk])
                    tr.op("dve", lambda e: e.scalar_tensor_tensor(out=self.xT[:, m, t0:t0 + n], in0=pt[:, 0:n], scalar=self.mod[:, l, 16 + m, s:s + 1],
                                                                 in1=self.xT[:, m, t0:t0 + n], op0=ALU.mult, op1=ALU.add),
                          reads=[pk, "mod", ("xT", ci)], writes=[("xT", ci)])
            tr.barrier()

    def phase_ffn(self, l):
        tr = self.tr
        with ExitStack() as es:
            fcw = self.sb(es, "fcw", [128, 44, 3])
            self.dma(fcw[:], self.inp["fconv_fm"][:, l, :, :], writes=["fcw"])
            with ExitStack() as es2:
                xn = self.sb(es2, "xn2", [128, 8, T], BF16)
                self.adaln(es2, l, 1, xn)
                wr = Ring(self, es2, "wup", 2, [128, 8, 2, 128], BF16)
                hr = [Ring(self, es2, f"fh{i}", 2, [128, T]) for i in range(2)]
                cr = [Ring(self, es2, f"fc{i}", 2, [128, T]) for i in range(2)]
                ar = Ring(self, es2, "fact", 2, [128, T], BF16)
                wv = self.inp["ffn_up"][l].rearrange("(c p) n -> p c n", p=128)
                pi = 0
                for cc in range(22):
                    w, wk = wr.next()
                    self.dma(w[:, :, 0, :], wv[:, :, cc * 128:(cc + 1) * 128], writes=[wk], q="gq")
                    self.dma(w[:, :, 1, :], wv[:, :, 2816 + cc * 128:2816 + (cc + 1) * 128], writes=[wk], q="gq")
                    cs_ = []
                    for i in range(2):
                        hb, hk = hr[i].next()
                        for ci, (t0, n) in enumerate(TCH):
                            pk = ("ps", pi % 4); pt = self.ps[pi % 4]; pi += 1
                            for c in range(8):
                                self.mm(pt[:, 0:n], w[:, c, i, :], xn[:, c, t0:t0 + n], c == 0, c == 7, reads=[wk, ("xn", ci)], writes=[pk])
                            self.copy(self.evac_eng(), hb[:, t0:t0 + n], pt[:, 0:n], reads=[pk], writes=[hk])
                        cb, ck = cr[i].next()
                        j = cc + 22 * i
                        tr.op("act", lambda e: e.activation(out=cb[:], in_=hb[:], func=AF.Identity, scale=fcw[:, j, 1:2]), reads=[hk, "fcw"], writes=[ck])
                        for (o0, o1, i0, i1, k) in ((1, CTX, 0, CTX - 1, 0), (CTX + 1, T, CTX, T - 1, 0), (0, CTX - 1, 1, CTX, 2), (CTX, T - 1, CTX + 1, T, 2)):
                            tr.op("dve", lambda e: e.scalar_tensor_tensor(out=cb[:, o0:o1], in0=hb[:, i0:i1], scalar=fcw[:, j, k:k + 1], in1=cb[:, o0:o1],
                                                                       op0=ALU.mult, op1=ALU.add), reads=[hk, ck, "fcw"], writes=[ck])
                        cs_.append((cb, ck))
                    (ca, cak), (cb, cbk) = cs_
                    tr.op("act", lambda e: e.activation(out=ca[:], in_=ca[:], func=AF.Silu), reads=[cak], writes=[cak])
                    a, ak = ar.next()
                    tr.op("dve", lambda e: e.tensor_tensor(out=a[:], in0=ca[:], in1=cb[:], op=ALU.mult), reads=[cak, cbk], writes=[ak])
                    self.dma(self.scr["D_act"][cc * 128:(cc + 1) * 128, :], a[:], reads=[ak], writes=["D_act"])
                tr.barrier()
            with ExitStack() as es2:
                wd = self.sb(es2, "wdn", [128, 22, D], BF16)
                self.dma(wd[:, 0:11, :], self.inp["ffn_down"][l].rearrange("(c p) n -> p c n", p=128)[:, 0:11, :], writes=["wdn"], q="gq")
                self.dma(wd[:, 11:22, :], self.inp["ffn_down"][l].rearrange("(c p) n -> p c n", p=128)[:, 11:22, :], writes=["wdn"], q="gq")
                ar = Ring(self, es2, "fa", 2, [128, 22, 512], BF16)
                av = self.scr["D_act"].rearrange("(c p) t -> p c t", p=128)
                for ci, (t0, n) in enumerate(TCH):
                    s = 1 if ci == 0 else 0
                    a, ak = ar.next()
                    self.dma(a[:, :, 0:n], av[:, :, t0:t0 + n], reads=["D_act"], writes=[ak])
                    for m in range(8):
                        pk = ("ps", m % 4); pt = self.ps[m % 4]
                        for c in range(22):
                            self.mm(pt[:, 0:n], wd[:, c, m * 128:(m + 1) * 128], a[:, c, 0:n], c == 0, c == 21, reads=["wdn", ak], writes=[pk])
                        tr.op("dve", lambda e: e.scalar_tensor_tensor(out=self.xT[:, m, t0:t0 + n], in0=pt[:, 0:n], scalar=self.mod[:, l, 40 + m, s:s + 1],
                                                                     in1=self.xT[:, m, t0:t0 + n], op0=ALU.mult, op1=ALU.add),
                              reads=[pk, "mod", ("xT", ci)], writes=[("xT", ci)])
                tr.barrier()

    def phase_final(self):
        tr = self.tr
        with ExitStack() as es:
            nf = self.sb(es, "nf", [128, D])
            self.dma(nf[:], self.inp["normf_rep"], writes=["nf"])
            xr = Ring(self, es, "fx", 2, [128, D]); sqr = Ring(self, es, "fsq", 2, [128, D]); ssr = Ring(self, es, "fss", 2, [128, 1])
            for t in range(2, NT):
                x, xk = xr.next()
                for half in range(2):
                    pk = ("ps", (2 * t + half) % 8); pt = self.ps[(2 * t + half) % 8]
                    for j in range(4):
                        c = half * 4 + j
                        tr.op("pe", lambda e: e.transpose(pt[:, j * 128:(j + 1) * 128], self.xT[:, c, t * 128:(t + 1) * 128], self.ident[:]),
                              reads=["ident"], writes=[pk])
                    self.copy(self.evac_eng(), x[:, half * 512:(half + 1) * 512], pt[:, :], reads=[pk], writes=[xk])
                sq, sk = sqr.next(); ss, ssk = ssr.next()
                tr.op("pool", lambda e: e.tensor_tensor(out=sq[:], in0=x[:], in1=x[:], op=ALU.mult), reads=[xk], writes=[sk])
                tr.op("dve", lambda e: e.tensor_reduce(out=ss[:], in_=sq[:], axis=AX.X, op=ALU.add), reads=[sk], writes=[ssk])
                self.rstd_from_ss(ss[:], ssk, ss[:], ssk, 1.0 / D)
                tr.op("dve", lambda e: e.scalar_tensor_tensor(out=sq[:], in0=x[:], scalar=ss[:, 0:1], in1=nf[:], op0=ALU.mult, op1=ALU.mult),
                      reads=[xk, ssk, "nf"], writes=[sk])
                self.dma(self.out[(t - 2) * 128:(t - 1) * 128, :], sq[:], reads=[sk])


def _rope_tables(d):
    da = d // 2
    half = da // 2
    inv = (10000.0 ** (-np.arange(half, dtype=np.float32) / half)).astype(np.float32)
    t = np.arange(SEQ)
    rows = (t // 64).astype(np.float32)
    cols = (t % 64).astype(np.float32)
    cos = np.ones((d, T), np.float32)
    sin = np.zeros((d, T), np.float32)
    for i in range(d):
        pos = rows if i < da else cols
        ii = i % da
        ang = pos * inv[ii % half]
        cos[i, CTX:] = np.cos(ang)
        sin[i, CTX:] = np.sin(ang) * (-1.0 if ii < half else 1.0)
    return np.stack([cos, sin])


def _partner(d):
    da = d // 2
    half = da // 2
    idx = np.arange(d)
    ii = idx % da
    return np.where(ii < half, idx + half, idx - half)


def _constants():
    j = np.arange(128)[:, None]
    i = np.arange(128)[None, :]
    tri = np.stack([(j <= i), (j >= i)]).astype(np.float32)
    gmask = np.stack([np.where(i < j, 0.0, BIG),
                      np.where(j <= i, 0.0, -BIG),
                      np.where(i > j, 0.0, BIG),
                      np.where(j >= i, 0.0, -BIG)]).astype(np.float32)
    m_prev = (j >= i).astype(np.float32)
    m_next = (j <= i).astype(np.float32)
    swamask = np.stack([np.tile(m_prev, (1, 4)), np.tile(m_next, (1, 4))]).astype(np.float32)
    b32 = (j // 32 == i // 32)
    b64 = (j // 64 == i // 64)
    bmask = np.stack([b32, b64 & ~b32, ~b64]).astype(np.float32)
    selh = np.zeros((8, 8, 128), np.float32)
    for h in range(8):
        selh[h, h, :] = 1.0
    return dict(ident=np.eye(128, dtype=np.float32), tri=tri, gmask=gmask, bmask=bmask, bd64=b64.astype(np.float32), swamask=swamask, selh=selh,
                ropeM=_rope_tables(32), ropeS=_rope_tables(64))


def _fm(v, p=128):
    v = np.asarray(v, np.float32)
    lead = v.shape[:-1]
    n = v.shape[-1] // p
    return np.ascontiguousarray(np.moveaxis(v.reshape(lead + (n, p)), -1, 0))


def prepare_shared(inp, L=DEPTH):
    f = lambda k: np.asarray(inp[k], np.float32)[:L] if k not in ("norm_f",) else np.asarray(inp[k], np.float32)
    w_in = f("w_in")
    p32 = _partner(32)
    p64 = _partner(64)
    sqp = (np.arange(8)[:, None] * 64 + p64[None, :]).reshape(-1)
    skp = (np.arange(2)[:, None] * 64 + p64[None, :]).reshape(-1)
    w_fm = np.concatenate([w_in[:, :, 0:640], w_in[:, :, 640:672], w_in[:, :, 640:672][:, :, p32],
                           w_in[:, :, 672:1184], w_in[:, :, 672:1184][:, :, sqp],
                           w_in[:, :, 1184:1312], w_in[:, :, 1184:1312][:, :, skp],
                           w_in[:, :, 1440:2976], np.zeros((L, D, 64), np.float32), w_in[:, :, 3520:6592]], axis=2)
    assert w_fm.shape[2] == NFM
    w_tm = np.concatenate([w_in[:, :, 1312:1440], w_in[:, :, 2976:3488], w_in[:, :, 3488:3520]], axis=2)
    wuq = f("w_uq").reshape(L, 384, 8, 96)
    wukv = f("w_ukv").reshape(L, 256, 8, 128)
    sh = dict(
        w_mod=f("w_mod"), b_mod_fm=_fm(f("b_mod")),
        norm1_fm=_fm(f("norm1")), norm2_fm=_fm(f("norm2")), normf_rep=np.ascontiguousarray(np.broadcast_to(f("norm_f"), (128, D))),
        w_fm=np.ascontiguousarray(w_fm), w_tm=np.ascontiguousarray(w_tm),
        qnorm_fm=_fm(f("mla_q_norm")), kvnorm_fm=_fm(f("mla_kv_norm")),
        wuq_n=np.ascontiguousarray(wuq[..., 0:64].reshape(L, 384, 512)),
        wuq_r=np.ascontiguousarray(wuq[..., 64:96].reshape(L, 384, 256)),
        wuq_rp=np.ascontiguousarray(wuq[..., 64:96][..., p32].reshape(L, 384, 256)),
        wukv_k=np.ascontiguousarray(wukv[..., 0:64].reshape(L, 256, 512)),
        wukv_v=np.ascontiguousarray(wukv[..., 64:128].reshape(L, 256, 512)),
        sink_rep=np.ascontiguousarray(np.broadcast_to(f("swa_sink"), (64, L, 8))),
        gconv_fm=np.ascontiguousarray(f("gdn_conv").reshape(L, 3, 12, 128).transpose(3, 0, 2, 1)),
        alog_rep=np.ascontiguousarray(np.broadcast_to(f("gdn_a_log").reshape(L, 16), (128, L, 16))),
        dtb_rep=np.ascontiguousarray(np.broadcast_to(f("gdn_dt_bias").reshape(L, 16), (128, L, 16))),
        gnorm_rep=np.ascontiguousarray(np.broadcast_to(f("gdn_norm"), (128, L, 64))),
        w_pa=f("w_branch_a"), w_pb=f("w_branch_b"), w_pc=f("w_branch_c"), w_out=f("w_out"),
        ffn_up=f("ffn_up"), fconv_fm=np.ascontiguousarray(f("ffn_conv").reshape(L, 3, 44, 128).transpose(3, 0, 2, 1)),
        ffn_down=f("ffn_down"),
    )
    sh.update(_constants())
    return sh


def per_core(inp, b):
    x = np.asarray(inp["x"], np.float32)[b]
    ctx = np.asarray(inp["ctx"], np.float32)[b]
    c = np.asarray(inp["c"], np.float32)[b]
    cc = np.asarray(inp["c_ctx"], np.float32)
    return dict(xin=np.ascontiguousarray(np.concatenate([ctx, x], 0)),
                c_fm=np.ascontiguousarray(np.stack([_fm(c), _fm(cc)], axis=-1)))


_CACHE = {}


def kernel(**inputs):
    if "nc" not in _CACHE:
        _CACHE["nc"] = Builder().build()
    nc = _CACHE["nc"]
    shared = prepare_shared(inputs)
    in_maps = []
    for b in range(8):
        m = dict(shared)
        m.update(per_core(inputs, b))
        in_maps.append(m)
    res = run_bass_kernel_spmd(nc, in_maps, core_ids=list(range(8)))
    return np.stack([np.asarray(r["out"], np.float32) for r in res.results], axis=0)
```

```python
import numpy as np
from contextlib import ExitStack
import concourse.bass as bass
import concourse.mybir as mybir
from concourse.bass_utils import run_bass_kernel_spmd

F32 = mybir.dt.float32
BF16 = mybir.dt.bfloat16
F32R = mybir.dt.float32r
USE_F32R = False
AF = mybir.ActivationFunctionType
ALU = mybir.AluOpType
AX = mybir.AxisListType

D = 1024
SEQ = 2048
CTX = 256
T = SEQ + CTX
NT = T // 128
DEPTH = 4
EPS = 1e-6
TCH = [(0, 256), (256, 512), (768, 512), (1280, 512), (1792, 512)]
BIG = 1.0e5

CQ, CKV, KR, KRP, SQ, SQP, SK, SKP, GQ, PAD, GATE, NFM = 0, 384, 640, 672, 704, 1216, 1728, 1856, 1984, 3520, 3584, 6656
SV, ZC, AB, NTM = 0, 128, 640, 672


def tile_chunk(t):
    return 0 if t < 2 else 1 + (t - 2) // 4


class Trk:
    ISSUER = {"pe": "pe", "act": "act", "dve": "dve", "pool": "pool", "sp": "sp", "gq": "pool"}
    NSLOT = {"sp": 24, "gq": 8}

    def __init__(self, nc, es):
        self.nc = nc
        self.eng = {"pe": nc.tensor, "act": nc.scalar, "dve": nc.vector, "pool": nc.gpsimd, "sp": nc.sync}
        self.sem = {e: es.enter_context(nc.semaphore("sem_" + e)) for e in ("pe", "act", "dve", "pool")}
        for q, k in self.NSLOT.items():
            for i in range(k):
                self.sem[(q, i)] = es.enter_context(nc.semaphore(f"sem_{q}{i}"))
        self.cnt = {e: 0 for e in self.sem}
        self.nq = {q: 0 for q in self.NSLOT}
        self.lastw = {}
        self.readers = {}
        self.waited = {}
        self.n_wait = 0
        self.n_op = 0

    def _need(self, issuer, eng, dep):
        de, ds = dep
        if de == "pe" and eng == "pe":
            return
        k = (issuer, de)
        if self.waited.get(k, 0) >= ds:
            return
        self.waited[k] = ds
        self.eng[issuer].wait_ge(self.sem[de], ds)
        self.n_wait += 1

    def op(self, eng, emit, reads=(), writes=()):
        issuer = self.ISSUER[eng]
        for key in reads:
            w = self.lastw.get(key)
            if w is not None:
                self._need(issuer, eng, w)
            if isinstance(key, tuple) and key[0] == "ps":
                for rk, tok in self.readers.get(key, {}).items():
                    if rk != eng:
                        self._need(issuer, eng, tok)
        for key in writes:
            w = self.lastw.get(key)
            if w is not None:
                self._need(issuer, eng, w)
            for tok in self.readers.get(key, {}).values():
                self._need(issuer, eng, tok)
        if eng in self.NSLOT:
            sk = (eng, self.nq[eng] % self.NSLOT[eng])
            self.nq[eng] += 1
            if self.cnt[sk] > 0:
                self._need(issuer, eng, (sk, self.cnt[sk]))
            inc = 16
        else:
            sk = eng
            inc = 1
        inst = emit(self.eng[issuer])
        self.cnt[sk] += inc
        inst.then_inc(self.sem[sk], inc)
        tok = (sk, self.cnt[sk])
        for key in reads:
            self.readers.setdefault(key, {})[sk] = tok
        for key in writes:
            self.lastw[key] = tok
            self.readers[key] = {}
        self.n_op += 1
        return inst

    def barrier(self):
        for issuer in ("pe", "act", "dve", "pool", "sp"):
            for e, c in self.cnt.items():
                if c > 0:
                    self._need(issuer, "x", (e, c))
        self.lastw.clear()
        self.readers.clear()

    def finish(self, eng="sp"):
        for e, c in self.cnt.items():
            if c > 0:
                self._need(eng, "x", (e, c))


class Ring:
    def __init__(self, b, es, name, n, shape, dt=F32):
        self.tiles = [b.sb(es, f"{name}{i}", shape, dt) for i in range(n)]
        self.name = name
        self.n = n
        self.i = 0

    def next(self):
        j = self.i % self.n
        self.i += 1
        return self.tiles[j], (self.name, j)


class Builder:
    def __init__(self, n_layers=DEPTH, stop=None, taps=()):
        self.n_layers = n_layers
        self.stop = stop
        self.taps = set(taps)
        self.nc = bass.Bass("TRN2", target_bir_lowering=False)
        self.inp = {}
        self.scr = {}
        self.ev = 0
        self.use_f32r = USE_F32R

    def din(self, name, shape, dt=F32):
        self.inp[name] = self.nc.dram_tensor(name, list(shape), dt, kind="ExternalInput").ap()
        return self.inp[name]

    def dscr(self, name, shape, dt=F32, force=False):
        kind = "ExternalOutput" if (name in self.taps or force) else "Internal"
        self.scr[name] = self.nc.dram_tensor(name, list(shape), dt, kind=kind).ap()
        return self.scr[name]

    def declare(self):
        L = self.n_layers
        d = self.din
        d("xin", [T, D]); d("c_fm", [128, 8, 2])
        d("w_mod", [L, D, 6 * D]); d("b_mod_fm", [128, L, 48])
        d("norm1_fm", [128, L, 8]); d("norm2_fm", [128, L, 8]); d("normf_rep", [128, D])
        d("w_fm", [L, D, NFM]); d("w_tm", [L, D, NTM])
        d("qnorm_fm", [128, L, 3]); d("kvnorm_fm", [128, L, 2])
        d("wuq_n", [L, 384, 512]); d("wuq_r", [L, 384, 256]); d("wuq_rp", [L, 384, 256])
        d("wukv_k", [L, 256, 512]); d("wukv_v", [L, 256, 512])
        d("sink_rep", [64, L, 8])
        d("gconv_fm", [128, L, 12, 3]); d("bd64", [128, 128]); d("alog_rep", [128, L, 16]); d("dtb_rep", [128, L, 16]); d("gnorm_rep", [128, L, 64])
        d("w_pa", [L, 512, D]); d("w_pb", [L, 512, D]); d("w_pc", [L, 512, D]); d("w_out", [L, D, D])
        d("ffn_up", [L, D, 5632]); d("fconv_fm", [128, L, 44, 3]); d("ffn_down", [L, 2816, D])
        d("ident", [128, 128]); d("ropeM", [2, 32, T]); d("ropeS", [2, 64, T])
        d("tri", [2, 128, 128]); d("gmask", [4, 128, 128]); d("bmask", [3, 128, 128]); d("swamask", [2, 128, 512]); d("selh", [8, 8, 128])
        self.out = self.nc.dram_tensor("out", [SEQ, D], F32, kind="ExternalOutput").ap()
        s = self.dscr
        s("D_fm", [NFM, T]); s("D_tm", [T, NTM]); s("D_gate", [3072, T], BF16)
        s("D_ya", [512, T], BF16); s("D_yb", [512, T], BF16); s("D_yc", [512, T], BF16)
        s("D_gq", [512, T]); s("D_gk", [512, T]); s("D_ktm", [8, T, 64]); s("D_vtm", [8, T, 64])
        s("D_act", [2816, T], BF16)
        if "D_x" in self.taps:
            s("D_x", [D, T], F32, True); s("D_xn", [D, T], BF16, True); s("D_mod", [128, L * 48 * 2], F32, True); s("D_oc", [T, 512], F32, True)

    def sb(self, es, name, shape, dt=F32):
        self.uid = getattr(self, "uid", 0) + 1
        return es.enter_context(self.nc.sbuf_tensor(f"s{self.uid}_{name}", list(shape), dt))

    def evac_eng(self):
        self.ev += 1
        return "act" if self.ev % 2 else "dve"

    def copy(self, eng, out, in_, reads, writes):
        if eng == "act":
            return self.tr.op("act", lambda e: e.copy(out=out, in_=in_), reads=reads, writes=writes)
        return self.tr.op(eng, lambda e: e.tensor_copy(out=out, in_=in_), reads=reads, writes=writes)

    def dma(self, out, in_, reads=(), writes=(), q="sp"):
        return self.tr.op(q, lambda e: e.dma_start(out=out, in_=in_), reads=reads, writes=writes)

    def mm(self, out, lhsT, rhs, start, stop, reads, writes):
        return self.tr.op("pe", lambda e: e.matmul(out, lhsT, rhs, start=start, stop=stop), reads=reads, writes=writes)

    def rstd_from_ss(self, ss_ap, ss_key, r_ap, r_key, scale):
        tr = self.tr
        tr.op("act", lambda e: e.activation(out=r_ap, in_=ss_ap, func=AF.Sqrt, bias=self.cst[0:r_ap.shape[0], 0:1], scale=scale),
              reads=[ss_key, "cst"], writes=[r_key])
        tr.op("dve", lambda e: e.reciprocal(out=r_ap, in_=r_ap), reads=[r_key], writes=[r_key])

    def build(self):
        nc = self.nc
        self.declare()
        with ExitStack() as es:
            self.tr = tr = Trk(nc, es)
            self.ps = [es.enter_context(nc.psum_tensor(f"ps{i}", [128, 512], F32)) for i in range(8)]
            self.xT = self.sb(es, "xT", [128, 8, T])
            self.ident = self.sb(es, "ident", [128, 128])
            self.ones32 = self.sb(es, "ones32", [128, 128])
            self.ones16 = self.sb(es, "ones16", [128, 128], BF16)
            self.cst = self.sb(es, "cst", [128, 4])
            self.mod = self.sb(es, "mod", [128, self.n_layers, 48, 2])
            self.n1 = self.sb(es, "n1", [128, self.n_layers, 8])
            self.n2 = self.sb(es, "n2", [128, self.n_layers, 8])
            self.gs = self.sb(es, "gs", [128, 2, 8, 2])
            self.dma(self.ident[:], self.inp["ident"], writes=["ident"])
            self.dma(self.n1[:], self.inp["norm1_fm"], writes=["n1"])
            self.dma(self.n2[:], self.inp["norm2_fm"], writes=["n2"])
            tr.op("dve", lambda e: e.memset(self.ones32[:], 1.0), writes=["ones32"])
            tr.op("dve", lambda e: e.memset(self.ones16[:], 1.0), writes=["ones16"])
            tr.op("dve", lambda e: e.memset(self.cst[:, 0:1], EPS), writes=["cst"])
            tr.op("dve", lambda e: e.memset(self.cst[:, 1:2], 1.0), writes=["cst"])
            tr.op("dve", lambda e: e.memset(self.cst[:, 2:3], 0.0), writes=["cst"])
            self.phase_load_x()
            self.phase_mod()
            done = self.stop == "mod"
            for l in range(self.n_layers):
                if done:
                    break
                for name, fn in (("norm1", lambda: self.phase_adaln_proj(l)), ("mla", lambda: self.phase_mla(l)),
                                 ("swa", lambda: self.phase_swa(l)), ("gdnprep", lambda: self.phase_gdn_prep(l)),
                                 ("gdn", lambda: self.phase_gdn(l)), ("merge", lambda: self.phase_merge(l)),
                                 ("ffn", lambda: self.phase_ffn(l))):
                    fn()
                    if self.stop == f"{name}{l}" or (self.stop or "").startswith(name + "_"):
                        done = True
                        break
            if "D_x" in self.taps:
                tr.barrier()
                for ci, (t0, n) in enumerate(TCH):
                    self.dma(self.scr["D_x"].rearrange("(c p) t -> p c t", p=128)[:, :, t0:t0 + n], self.xT[:, :, t0:t0 + n])
                self.dma(self.scr["D_mod"], self.mod[:].rearrange("p l j s -> p (l j s)"))
            if not done:
                self.phase_final()
            tr.barrier()
            tr.finish("sp")
        return nc

    def phase_load_x(self):
        tr = self.tr
        with ExitStack() as es:
            ring = Ring(self, es, "xl", 2, [128, D])
            for t in range(NT):
                xt, xk = ring.next()
                self.dma(xt[:], self.inp["xin"][t * 128:(t + 1) * 128, :], writes=[xk])
                for half in range(2):
                    pk = ("ps", (2 * t + half) % 8)
                    pt = self.ps[(2 * t + half) % 8]
                    for j in range(4):
                        c = half * 4 + j
                        tr.op("pe", lambda e: e.transpose(pt[:, j * 128:(j + 1) * 128], xt[:, c * 128:(c + 1) * 128], self.ident[:]),
                              reads=[xk, "ident"], writes=[pk])
                    self.copy(self.evac_eng(), self.xT[:, half * 4:half * 4 + 4, t * 128:(t + 1) * 128],
                              pt[:].rearrange("p (c t) -> p c t", c=4), reads=[pk], writes=[("xT", tile_chunk(t))])
            tr.barrier()

    def phase_mod(self):
        tr = self.tr
        with ExitStack() as es:
            cs = self.sb(es, "cs", [128, 8, 2])
            bm = self.sb(es, "bm", [128, self.n_layers, 48])
            ring = Ring(self, es, "wm", 2, [128, 8, 512])
            self.dma(cs[:], self.inp["c_fm"], writes=["cs"])
            self.dma(bm[:], self.inp["b_mod_fm"], writes=["bm"])
            tr.op("act", lambda e: e.activation(out=cs[:], in_=cs[:], func=AF.Silu), reads=["cs"], writes=["cs"])
            for l in range(self.n_layers):
                wv = self.inp["w_mod"][l].rearrange("(c p) n -> p c n", p=128)
                pk = ("ps", l % 2)
                pt = self.ps[l % 2][:, 0:96].rearrange("p (j s) -> p j s", s=2)
                for g in range(12):
                    w, wk = ring.next()
                    self.dma(w[:], wv[:, :, g * 512:(g + 1) * 512], writes=[wk])
                    for m in range(4):
                        j = g * 4 + m
                        for c in range(8):
                            self.mm(pt[:, j, :], w[:, c, m * 128:(m + 1) * 128], cs[:, c, :], c == 0, c == 7,
                                    reads=[wk, "cs"], writes=[pk])
                for s in range(2):
                    tr.op("dve", lambda e: e.tensor_tensor(out=self.mod[:, l, :, s], in0=pt[:, :, s], in1=bm[:, l, :], op=ALU.add),
                          reads=[pk, "bm"], writes=["mod"])
            tr.barrier()

    def make_gs(self, l, which):
        nrm = self.n1 if which == 0 else self.n2
        sc0 = 8 + which * 24
        for s in range(2):
            self.tr.op("dve", lambda e: e.scalar_tensor_tensor(out=self.gs[:, which, :, s], in0=self.mod[:, l, sc0:sc0 + 8, s], scalar=1.0,
                                                             in1=nrm[:, l, :], op0=ALU.add, op1=ALU.mult),
                       reads=["mod", "n1", "n2"], writes=["gs"])

    def adaln(self, es, l, which, xn):
        tr = self.tr
        self.make_gs(l, which)
        sh0 = which * 24
        es = ExitStack()
        sqr = Ring(self, es, f"sq{which}", 2, [128, 512])
        rr = Ring(self, es, f"rr{which}", 2, [128, 512])
        tmpr = Ring(self, es, f"tm{which}", 2, [128, 512])
        for ci, (t0, n) in enumerate(TCH):
            s = 1 if ci == 0 else 0
            pk = ("ps", ci % 2)
            pt = self.ps[ci % 2]
            for c in range(8):
                sq, sk = sqr.next()
                tr.op("act", lambda e: e.activation(out=sq[:, 0:n], in_=self.xT[:, c, t0:t0 + n], func=AF.Square),
                      reads=[("xT", ci)], writes=[sk])
                self.mm(pt[:, 0:n], self.ones32[:], sq[:, 0:n], c == 0, c == 7, reads=[sk, "ones32"], writes=[pk])
            r, rk = rr.next()
            self.rstd_from_ss(pt[:, 0:n], pk, r[:, 0:n], rk, 1.0 / D)
            for c in range(8):
                tm, tk = tmpr.next()
                tr.op("dve", lambda e: e.scalar_tensor_tensor(out=tm[:, 0:n], in0=self.xT[:, c, t0:t0 + n], scalar=self.gs[:, which, c, s:s + 1],
                                                             in1=r[:, 0:n], op0=ALU.mult, op1=ALU.mult),
                      reads=[("xT", ci), "gs", rk], writes=[tk])
                tr.op("act", lambda e: e.activation(out=xn[:, c, t0:t0 + n], in_=tm[:, 0:n], func=AF.Identity,
                                                    bias=self.mod[:, l, sh0 + c, s:s + 1], scale=1.0),
                      reads=[tk, "mod"], writes=[("xn", ci)])
        keep = {k: v for k, v in tr.lastw.items() if isinstance(k, tuple) and k[0] == "xn"}
        tr.barrier()
        tr.lastw.update(keep)
        es.close()

    def phase_adaln_proj(self, l):
        tr = self.tr
        with ExitStack() as es:
            xn = self.sb(es, "xn", [128, 8, T], BF16)
            self.adaln(es, l, 0, xn)
            if "D_x" in self.taps:
                for ci, (t0, n) in enumerate(TCH):
                    self.dma(self.scr["D_xn"].rearrange("(c p) t -> p c t", p=128)[:, :, t0:t0 + n], xn[:, :, t0:t0 + n], reads=[("xn", ci)])
            wr = Ring(self, es, "wfm", 2, [128, 8, 512], BF16)
            st = Ring(self, es, "stg", 3, [128, 512])
            st16 = Ring(self, es, "stg16", 2, [128, 512], BF16)
            wv = self.inp["w_fm"][l].rearrange("(c p) n -> p c n", p=128)
            pi = 0
            for g in range(NFM // 512):
                w, wk = wr.next()
                self.dma(w[:], wv[:, :, g * 512:(g + 1) * 512], writes=[wk], q="gq")
                col = g * 512
                while col < (g + 1) * 512:
                    if col < KR:
                        mc = 128
                    elif col < SQ:
                        mc = 32
                    elif col < PAD:
                        mc = 64
                    elif col < GATE:
                        col += 64
                        continue
                    else:
                        mc = 128
                    lo = col - g * 512
                    for ci, (t0, n) in enumerate(TCH):
                        pk = ("ps", pi % 4)
                        pt = self.ps[pi % 4]
                        pi += 1
                        for c in range(8):
                            self.mm(pt[0:mc, 0:n], w[:, c, lo:lo + mc], xn[:, c, t0:t0 + n], c == 0, c == 7,
                                    reads=[wk, ("xn", ci)], writes=[pk])
                        if col >= GATE:
                            s16, sk = st16.next()
                            tr.op("act", lambda e: e.activation(out=s16[:, 0:n], in_=pt[:, 0:n], func=AF.Sigmoid), reads=[pk], writes=[sk])
                            self.dma(self.scr["D_gate"][col - GATE:col - GATE + 128, t0:t0 + n], s16[:, 0:n], reads=[sk], writes=[("D_gate", ci)])
                        else:
                            s32, sk = st.next()
                            self.copy(self.evac_eng(), s32[0:mc, 0:n], pt[0:mc, 0:n], reads=[pk], writes=[sk])
                            self.dma(self.scr["D_fm"][col:col + mc, t0:t0 + n], s32[0:mc, 0:n], reads=[sk], writes=["D_fm"])
                    col += mc
            wt = self.sb(es, "wtm", [128, 8, NTM], BF16)
            self.dma(wt[:], self.inp["w_tm"][l].rearrange("(c p) n -> p c n", p=128), writes=["wtm"], q="gq")
            stm = Ring(self, es, "stm", 2, [128, NTM])
            for t in range(NT):
                ci = tile_chunk(t)
                pa, pb = self.ps[4 + (t % 2) * 2], self.ps[5 + (t % 2) * 2]
                ka, kb = ("ps", 4 + (t % 2) * 2), ("ps", 5 + (t % 2) * 2)
                for c in range(8):
                    self.mm(pa[:, 0:512], xn[:, c, t * 128:(t + 1) * 128], wt[:, c, 0:512], c == 0, c == 7, reads=["wtm", ("xn", ci)], writes=[ka])
                for c in range(8):
                    self.mm(pb[:, 0:160], xn[:, c, t * 128:(t + 1) * 128], wt[:, c, 512:672], c == 0, c == 7, reads=["wtm", ("xn", ci)], writes=[kb])
                s, sk = stm.next()
                self.copy("act", s[:, 0:512], pa[:, 0:512], reads=[ka], writes=[sk])
                self.copy("dve", s[:, 512:672], pb[:, 0:160], reads=[kb], writes=[sk])
                self.dma(self.scr["D_tm"][t * 128:(t + 1) * 128, :], s[:], reads=[sk], writes=["D_tm"])
            tr.barrier()

    def rms_fm_from_dram(self, es, row0, nchunk, gain, dst, tag):
        tr = self.tr
        ld = Ring(self, es, f"ld{tag}", 2, [128, nchunk, 512])
        sqr = Ring(self, es, f"sqn{tag}", 2, [128, 512])
        rr = Ring(self, es, f"rn{tag}", 2, [128, 512])
        src = self.scr["D_fm"][row0:row0 + nchunk * 128, :].rearrange("(c p) t -> p c t", p=128)
        for ci, (t0, n) in enumerate(TCH):
            x, xk = ld.next()
            self.dma(x[:, :, 0:n], src[:, :, t0:t0 + n], reads=["D_fm"], writes=[xk])
            pk = ("ps", 6 + ci % 2)
            pt = self.ps[6 + ci % 2]
            for c in range(nchunk):
                sq, sk = sqr.next()
                tr.op("act", lambda e: e.activation(out=sq[:, 0:n], in_=x[:, c, 0:n], func=AF.Square), reads=[xk], writes=[sk])
                self.mm(pt[:, 0:n], self.ones32[:], sq[:, 0:n], c == 0, c == nchunk - 1, reads=[sk, "ones32"], writes=[pk])
            r, rk = rr.next()
            self.rstd_from_ss(pt[:, 0:n], pk, r[:, 0:n], rk, 1.0 / (nchunk * 128))
            for c in range(nchunk):
                tr.op("dve", lambda e: e.scalar_tensor_tensor(out=dst[:, c, t0:t0 + n], in0=x[:, c, 0:n], scalar=gain[:, c:c + 1],
                                                             in1=r[:, 0:n], op0=ALU.mult, op1=ALU.mult),
                      reads=[xk, rk, "gain" + tag], writes=[(tag, ci)])

    def rope_from(self, src_ap, src_key, srcp_ap, srcp_key, cos_ap, sin_ap, dst_ap, dst_key, tmpa, tmpb, ka, kb):
        tr = self.tr
        tr.op("dve", lambda e: e.tensor_tensor(out=tmpa, in0=src_ap, in1=cos_ap, op=ALU.mult), reads=[src_key, "rope"], writes=[ka])
        tr.op("pool" if srcp_key[0] != "ps" else "dve", lambda e: e.tensor_tensor(out=tmpb, in0=srcp_ap, in1=sin_ap, op=ALU.mult),
              reads=[srcp_key, "rope"], writes=[kb])
        tr.op("dve", lambda e: e.tensor_tensor(out=dst_ap, in0=tmpa, in1=tmpb, op=ALU.add), reads=[ka, kb], writes=[dst_key])

    def phase_mla(self, l):
        tr = self.tr
        scale = 96.0 ** -0.5
        with ExitStack() as es:
            qg = self.sb(es, "qg", [128, 3]); kg = self.sb(es, "kg", [128, 2])
            self.dma(qg[:], self.inp["qnorm_fm"][:, l, :], writes=["gaincqn"])
            self.dma(kg[:], self.inp["kvnorm_fm"][:, l, :], writes=["gainckvn"])
            cqn = self.sb(es, "cqn", [128, 3, T], BF16)
            ckvn = self.sb(es, "ckvn", [128, 2, T], BF16)
            cosM = self.sb(es, "cosM", [32, T]); sinM = self.sb(es, "sinM", [32, T])
            self.dma(cosM[:], self.inp["ropeM"][0], writes=["rope"])
            self.dma(sinM[:], self.inp["ropeM"][1], writes=["rope"])
            KrT = self.sb(es, "KrT", [128, T], BF16)
            tr.op("pool", lambda e: e.memset(KrT[32:64, :], 0.0), writes=[("KrT", ci) for ci in range(5)])
            tr.op("pool", lambda e: e.memset(KrT[64:128, :], 0.0), writes=[("KrT", ci) for ci in range(5)])
            Vt = self.sb(es, "Vt", [128, NT, 576], BF16)
            wqn = self.sb(es, "wqn", [128, 3, 512], BF16); wqr = self.sb(es, "wqr", [128, 3, 256], BF16)
            wqp = self.sb(es, "wqp", [128, 3, 256], BF16)
            wkn = self.sb(es, "wkn", [128, 2, 512], BF16); wkv = self.sb(es, "wkv", [128, 2, 512], BF16)
            for wt_, nm in ((wqn, "wuq_n"), (wqr, "wuq_r"), (wqp, "wuq_rp"), (wkn, "wukv_k"), (wkv, "wukv_v")):
                self.dma(wt_[:], self.inp[nm][l].rearrange("(c p) n -> p c n", p=128), writes=[nm], q="gq")
            with ExitStack() as es2:
                self.rms_fm_from_dram(es2, CQ, 3, qg, cqn, "cqn")
                self.rms_fm_from_dram(es2, CKV, 2, kg, ckvn, "ckvn")
                kl = Ring(self, es2, "krl", 2, [32, 2, 512])
                ta = Ring(self, es2, "kta", 2, [32, 512]); tb = Ring(self, es2, "ktb", 2, [32, 512])
                for ci, (t0, n) in enumerate(TCH):
                    k2, kk = kl.next()
                    self.dma(k2[:, :, 0:n], self.scr["D_fm"][KR:KR + 64, :].rearrange("(a p) t -> p a t", p=32)[:, :, t0:t0 + n],
                             reads=["D_fm"], writes=[kk])
                    a_, ak = ta.next(); b_, bk = tb.next()
                    self.rope_from(k2[:, 0, 0:n], kk, k2[:, 1, 0:n], kk, cosM[:, t0:t0 + n], sinM[:, t0:t0 + n],
                                   KrT[0:32, t0:t0 + n], ("KrT", ci), a_[:, 0:n], b_[:, 0:n], ak, bk)
                for t in range(NT):
                    tr.op("pool", lambda e: e.memset(Vt[:, t, 512:576], 0.0), writes=[("Vt", t)])
                for t in range(NT):
                    ci = tile_chunk(t)
                    pk = ("ps", 4 + t % 2); pt = self.ps[4 + t % 2]
                    for c in range(2):
                        self.mm(pt[:, :], ckvn[:, c, t * 128:(t + 1) * 128], wkv[:, c, :], c == 0, c == 1, reads=[("ckvn", ci), "wukv_v"], writes=[pk])
                    self.copy(self.evac_eng(), Vt[:, t, 0:512], pt[:, :], reads=[pk], writes=[("Vt", t)])
                tr.barrier()
            if self.stop == "mla_prep":
                return
            with ExitStack() as es2:
                qnr = Ring(self, es2, "QnT", 2, [128, T], BF16); qrr = Ring(self, es2, "QrT", 2, [128, T], BF16)
                knr = Ring(self, es2, "KnT", 2, [128, T], BF16)
                for rg, lo in ((qnr, 64), (qrr, 32), (knr, 64)):
                    for ti, tl in enumerate(rg.tiles):
                        for p0_, p1_ in (((32, 64), (64, 128)) if lo == 32 else ((64, 128),)):
                            tr.op("pool", lambda e: e.memset(tl[p0_:p1_, :], 0.0), writes=[((rg.name, ti), ci) for ci in range(5)])
                ta = Ring(self, es2, "qta", 2, [32, 512]); tb = Ring(self, es2, "qtb", 2, [32, 512])
                pr = Ring(self, es2, "Pexp", 4, [128, 512], BF16)
                rdr = Ring(self, es2, "rden", 2, [64, 512])
                yar = Ring(self, es2, "yah", 2, [64, T], BF16)
                for h in range(8):
                    Qn, qnk = qnr.next(); Qr, qrk = qrr.next(); Kn, knk = knr.next()
                    for ci, (t0, n) in enumerate(TCH):
                        p0, p1, p2, p3 = self.ps[4], self.ps[5], self.ps[6], self.ps[7]
                        for c in range(3):
                            self.mm(p0[0:64, 0:n], wqn[:, c, h * 64:(h + 1) * 64], cqn[:, c, t0:t0 + n], c == 0, c == 2, reads=["wuq_n", ("cqn", ci)], writes=[("ps", 4)])
                        for c in range(3):
                            self.mm(p1[0:32, 0:n], wqr[:, c, h * 32:(h + 1) * 32], cqn[:, c, t0:t0 + n], c == 0, c == 2, reads=["wuq_r", ("cqn", ci)], writes=[("ps", 5)])
                        for c in range(3):
                            self.mm(p2[0:32, 0:n], wqp[:, c, h * 32:(h + 1) * 32], cqn[:, c, t0:t0 + n], c == 0, c == 2, reads=["wuq_rp", ("cqn", ci)], writes=[("ps", 6)])
                        for c in range(2):
                            self.mm(p3[0:64, 0:n], wkn[:, c, h * 64:(h + 1) * 64], ckvn[:, c, t0:t0 + n], c == 0, c == 1, reads=["wukv_k", ("ckvn", ci)], writes=[("ps", 7)])
                        self.copy("act", Qn[0:64, t0:t0 + n], p0[0:64, 0:n], reads=[("ps", 4)], writes=[(qnk, ci)])
                        self.copy("act", Kn[0:64, t0:t0 + n], p3[0:64, 0:n], reads=[("ps", 7)], writes=[(knk, ci)])
                        a_, ak = ta.next(); b_, bk = tb.next()
                        self.rope_from(p1[0:32, 0:n], ("ps", 5), p2[0:32, 0:n], ("ps", 6), cosM[:, t0:t0 + n], sinM[:, t0:t0 + n],
                                       Qr[0:32, t0:t0 + n], (qrk, ci), a_[:, 0:n], b_[:, 0:n], ak, bk)
                    if self.stop == "mla_proj":
                        break
                    ya, yk = yar.next()
                    for ci, (t0, n) in enumerate(TCH):
                        if self.stop == "mla_ctx" and ci > 0:
                            break
                        kts = [0, 1] if ci == 0 else list(range(NT))
                        po, pd = self.ps[3], self.ps[2]
                        def s_stage(i, kt):
                            kci = tile_chunk(kt)
                            bank = (0, 1, 4)[i % 3]
                            sk_ = ("ps", bank); psn = self.ps[bank]
                            self.mm(psn[:, 0:n], Kn[:, kt * 128:(kt + 1) * 128], Qn[:, t0:t0 + n], True, False,
                                    reads=[(knk, kci), (qnk, ci)], writes=[sk_])
                            self.mm(psn[:, 0:n], KrT[:, kt * 128:(kt + 1) * 128], Qr[:, t0:t0 + n], False, True,
                                    reads=[("KrT", kci), (qrk, ci)], writes=[sk_])
                            P, Pk = pr.next()
                            tr.op("act", lambda e: e.activation(out=P[:, 0:n], in_=psn[:, 0:n], func=AF.Exp, scale=scale), reads=[sk_], writes=[Pk])
                            return P, Pk

                        def pv_stage(i, kt, P, Pk):
                            self.mm(po[:, 0:n], Vt[:, kt, h * 64:h * 64 + 128], P[:, 0:n], i == 0, i == len(kts) - 1, reads=[("Vt", kt), Pk], writes=[("ps", 3)])
                            self.mm(pd[:, 0:n], self.ones16[:, :], P[:, 0:n], i == 0, i == len(kts) - 1, reads=["ones16", Pk], writes=[("ps", 2)])

                        pend = []
                        for i, kt in enumerate(kts):
                            pend.append((i, kt) + s_stage(i, kt))
                            if len(pend) > 2:
                                pv_stage(*pend.pop(0))
                        while pend:
                            pv_stage(*pend.pop(0))
                        rd, rk = rdr.next()
                        tr.op("dve", lambda e: e.reciprocal(out=rd[:, 0:n], in_=pd[0:64, 0:n]), reads=[("ps", 2)], writes=[rk])
                        tr.op("dve", lambda e: e.tensor_tensor(out=ya[:, t0:t0 + n], in0=po[0:64, 0:n], in1=rd[:, 0:n], op=ALU.mult),
                              reads=[("ps", 3), rk], writes=[yk])
                    self.dma(self.scr["D_ya"][h * 64:(h + 1) * 64, :], ya[:], reads=[yk], writes=["D_ya"])
                tr.barrier()

    def phase_swa(self, l):
        tr = self.tr
        with ExitStack() as es:
            QT = self.sb(es, "sQT", [128, 8, T], BF16)
            KT = self.sb(es, "sKT", [128, 2, T], BF16)
            tr.op("pool", lambda e: e.memset(QT[64:128, :, :], 0.0), writes=[("sQK", j) for j in range(8)])
            tr.op("pool", lambda e: e.memset(KT[64:128, :, :], 0.0), writes=[("sQK", 8), ("sQK", 9)])
            Vs = self.sb(es, "sVs", [128, NT, 128], BF16)
            cosS = self.sb(es, "cosS", [64, T]); sinS = self.sb(es, "sinS", [64, T])
            msk = self.sb(es, "smask", [128, 2, 512], BF16)
            snk = self.sb(es, "snk", [64, 8])
            self.dma(cosS[:], self.inp["ropeS"][0], writes=["rope"])
            self.dma(sinS[:], self.inp["ropeS"][1], writes=["rope"])
            self.dma(msk[:], self.inp["swamask"].rearrange("a p n -> p a n"), writes=["smask"], q="gq")
            self.dma(snk[:], self.inp["sink_rep"][:, l, :], writes=["snk"])
            tr.op("act", lambda e: e.activation(out=snk[:], in_=snk[:], func=AF.Exp), reads=["snk"], writes=["snk"])
            self.dma(Vs[:], self.scr["D_tm"][:, SV:SV + 128].rearrange("(t p) c -> p t c", p=128), reads=["D_tm"], writes=["sVs"], q="gq")
            with ExitStack() as es2:
                ld = Ring(self, es2, "sld", 1, [64, T]); ldp = Ring(self, es2, "sldp", 1, [64, T])
                ta = Ring(self, es2, "sta", 1, [64, T]); tb = Ring(self, es2, "stb", 1, [64, T])
                for j in range(10):
                    r0, rp = (SQ + j * 64, SQP + j * 64) if j < 8 else (SK + (j - 8) * 64, SKP + (j - 8) * 64)
                    dst = QT[0:64, j, :] if j < 8 else KT[0:64, j - 8, :]
                    x, xk = ld.next(); xp, xpk = ldp.next()
                    self.dma(x[:], self.scr["D_fm"][r0:r0 + 64, :], reads=["D_fm"], writes=[xk])
                    self.dma(xp[:], self.scr["D_fm"][rp:rp + 64, :], reads=["D_fm"], writes=[xpk])
                    a_, ak = ta.next(); b_, bk = tb.next()
                    self.rope_from(x[:], xk, xp[:], xpk, cosS[:], sinS[:], dst, ("sQK", j), a_[:], b_[:], ak, bk)
                tr.barrier()
            with ExitStack() as es2:
                pr = Ring(self, es2, "sP", 3, [128, 512], BF16)
                dn = Ring(self, es2, "sdn", 2, [64, 512])
                yr = Ring(self, es2, "syb", 2, [64, 4, 128], BF16)
                ybv = self.scr["D_yb"].rearrange("(h d) t -> d h t", d=64)
                it = 0
                for qt in range(NT):
                    if qt < 2:
                        kts = [(0, None), (1, None)]
                    else:
                        kts = [(0, None), (1, None)]
                        if qt > 2:
                            kts.append((qt - 1, 0))
                        kts.append((qt, None))
                        if qt < NT - 1:
                            kts.append((qt + 1, 1))
                    for g in range(2):
                        po, pd = self.ps[3], self.ps[2]
                        rhs = QT[:, 4 * g:4 * g + 4, qt * 128:(qt + 1) * 128]
                        def s_stage(i, kt, mk):
                            nonlocal it
                            sk_ = ("ps", it % 2); psn = self.ps[it % 2]; it += 1
                            self.mm(psn[:, :].rearrange("p (h q) -> p h q", h=4), KT[:, g, kt * 128:(kt + 1) * 128], rhs, True, True,
                                    reads=[("sQK", 8 + g)] + [("sQK", 4 * g + hh) for hh in range(4)], writes=[sk_])
                            P, Pk = pr.next()
                            tr.op("act", lambda e: e.activation(out=P[:], in_=psn[:, :], func=AF.Exp, scale=0.125), reads=[sk_], writes=[Pk])
                            if mk is not None:
                                tr.op("dve", lambda e: e.tensor_tensor(out=P[:], in0=P[:], in1=msk[:, mk, :], op=ALU.mult), reads=[Pk, "smask"], writes=[Pk])
                            return P, Pk

                        def pv_stage(i, kt, P, Pk):
                            self.mm(po[0:64, :], Vs[:, kt, g * 64:(g + 1) * 64], P[:], i == 0, i == len(kts) - 1, reads=["sVs", Pk], writes=[("ps", 3)])
                            self.mm(pd[:, :], self.ones16[:, :], P[:], i == 0, i == len(kts) - 1, reads=["ones16", Pk], writes=[("ps", 2)])

                        prev = None
                        for i, (kt, mk) in enumerate(kts):
                            cur = (i, kt) + s_stage(i, kt, mk)
                            if prev is not None:
                                pv_stage(*prev)
                            prev = cur
                        pv_stage(*prev)
                        d_, dk = dn.next()
                        for hh in range(4):
                            tr.op("dve", lambda e: e.tensor_scalar(out=d_[:, hh * 128:(hh + 1) * 128], in0=pd[0:64, hh * 128:(hh + 1) * 128],
                                                                  scalar1=snk[:, 4 * g + hh:4 * g + hh + 1], scalar2=None, op0=ALU.add),
                                  reads=[("ps", 2), "snk"], writes=[dk])
                        tr.op("dve", lambda e: e.reciprocal(out=d_[:], in_=d_[:]), reads=[dk], writes=[dk])
                        y, yk = yr.next()
                        tr.op("dve", lambda e: e.tensor_tensor(out=y[:].rearrange("p h q -> p (h q)"), in0=po[0:64, :], in1=d_[:], op=ALU.mult),
                              reads=[("ps", 3), dk], writes=[yk])
                        self.dma(ybv[:, 4 * g:4 * g + 4, qt * 128:(qt + 1) * 128], y[:], reads=[yk], writes=["D_yb"])
                tr.barrier()

    def phase_gdn_prep(self, l):
        tr = self.tr
        with ExitStack() as es:
            cw = self.sb(es, "gcw", [128, 12, 3])
            bd = self.sb(es, "gbd64", [128, 128])
            self.dma(cw[:], self.inp["gconv_fm"][:, l, :, :], writes=["gcw"])
            self.dma(bd[:], self.inp["bd64"], writes=["gbd64"])
            ld = Ring(self, es, "gld", 2, [128, T]); yr = Ring(self, es, "gy", 2, [128, T])
            sqr = Ring(self, es, "gsq", 2, [128, 512]); rr = Ring(self, es, "grn", 2, [128, 512])
            tmr = Ring(self, es, "gtm", 2, [128, NT, 128])
            for j in range(12):
                x, xk = ld.next(); y, yk = yr.next()
                self.dma(x[:], self.scr["D_fm"][GQ + j * 128:GQ + (j + 1) * 128, :], reads=["D_fm"], writes=[xk])
                tr.op("act", lambda e: e.activation(out=y[:], in_=x[:], func=AF.Identity, scale=cw[:, j, 1:2]), reads=[xk, "gcw"], writes=[yk])
                for (o0, o1, i0, i1, k) in ((1, CTX, 0, CTX - 1, 0), (CTX + 1, T, CTX, T - 1, 0), (0, CTX - 1, 1, CTX, 2), (CTX, T - 1, CTX + 1, T, 2)):
                    tr.op("dve", lambda e: e.scalar_tensor_tensor(out=y[:, o0:o1], in0=x[:, i0:i1], scalar=cw[:, j, k:k + 1], in1=y[:, o0:o1],
                                                                 op0=ALU.mult, op1=ALU.add), reads=[xk, yk, "gcw"], writes=[yk])
                tr.op("act", lambda e: e.activation(out=y[:], in_=y[:], func=AF.Silu), reads=[yk], writes=[yk])
                if j < 8:
                    for ci, (t0, n) in enumerate(TCH):
                        sq, sk = sqr.next()
                        tr.op("act", lambda e: e.activation(out=sq[:, 0:n], in_=y[:, t0:t0 + n], func=AF.Square), reads=[yk], writes=[sk])
                        pk = ("ps", ci % 2); pt = self.ps[ci % 2]
                        self.mm(pt[:, 0:n], bd[:], sq[:, 0:n], True, True, reads=[sk, "gbd64"], writes=[pk])
                        r, rk = rr.next()
                        self.rstd_from_ss(pt[:, 0:n], pk, r[:, 0:n], rk, 1.0)
                        tr.op("dve", lambda e: e.scalar_tensor_tensor(out=y[:, t0:t0 + n], in0=y[:, t0:t0 + n], scalar=(0.125 if j < 4 else 1.0),
                                                                     in1=r[:, 0:n], op0=ALU.mult, op1=ALU.mult), reads=[yk, rk], writes=[yk])
                    dst = self.scr["D_gq"] if j < 4 else self.scr["D_gk"]
                    self.dma(dst[(j % 4) * 128:(j % 4 + 1) * 128, :], y[:], reads=[yk], writes=["D_gqk"])
                if j >= 4:
                    tm, tk = tmr.next()
                    for t4 in range(0, NT, 4):
                        nt = min(4, NT - t4)
                        pk = ("ps", 2 + (t4 // 4) % 2); pt = self.ps[2 + (t4 // 4) % 2]
                        for tt in range(nt):
                            t = t4 + tt
                            tr.op("pe", lambda e: e.transpose(pt[:, tt * 128:(tt + 1) * 128], y[:, t * 128:(t + 1) * 128], self.ident[:]),
                                  reads=[yk, "ident"], writes=[pk])
                        self.copy(self.evac_eng(), tm[:, t4:t4 + nt, :], pt[:, 0:nt * 128].rearrange("p (t d) -> p t d", d=128), reads=[pk], writes=[tk])
                    dst = self.scr["D_ktm"] if j < 8 else self.scr["D_vtm"]
                    c = (j % 4) * 2
                    for hh in range(2):
                        self.dma(dst[c + hh].rearrange("(t p) d -> p t d", p=128), tm[:, :, hh * 64:(hh + 1) * 64], reads=[tk], writes=["D_kvtm"])
            tr.barrier()

    def phase_gdn(self, l):
        tr = self.tr
        with ExitStack() as es:
            oacc = self.sb(es, "oacc", [128, NT, 512])
            ab = self.sb(es, "gab", [128, NT, 32])
            g = self.sb(es, "gg", [128, NT, 16]); beta = self.sb(es, "gbeta", [128, NT, 16]); nbeta = self.sb(es, "gnbeta", [128, NT, 16])
            gam = self.sb(es, "ggam", [128, NT, 16]); eg = self.sb(es, "geg", [128, NT, 16]); bg = self.sb(es, "gbg", [128, NT, 16])
            edl = self.sb(es, "gedl", [128, NT, 16]); tmp = self.sb(es, "gtmp", [128, NT, 16])
            alog = self.sb(es, "galog", [128, 16]); dtb = self.sb(es, "gdtb", [128, 16])
            tri = self.sb(es, "gtri", [128, 2, 128]); gm = self.sb(es, "gmask", [128, 4, 128]); sel = self.sb(es, "gsel", [128, 8, 128])
            S = self.sb(es, "gS", [128, 2, 8, 64])
            self.dma(ab[:], self.scr["D_tm"][:, AB:AB + 32].rearrange("(t p) c -> p t c", p=128), reads=["D_tm"], writes=["gab"])
            self.dma(alog[:], self.inp["alog_rep"][:, l, :], writes=["galog"])
            self.dma(dtb[:], self.inp["dtb_rep"][:, l, :], writes=["gdtb"])
            self.dma(tri[:], self.inp["tri"].rearrange("a p n -> p a n"), writes=["gtri"])
            self.dma(gm[:], self.inp["gmask"].rearrange("a p n -> p a n"), writes=["gmaskc"])
            tr.op("pool", lambda e: e.memset(sel[:], 0.0), writes=["gsel"])
            self.dma(sel[0:8, :, :], self.inp["selh"].rearrange("h u n -> u h n"), writes=["gsel"])
            bmk = self.sb(es, "gbmask", [128, 3, 128])
            self.dma(bmk[:], self.inp["bmask"].rearrange("a p n -> p a n"), writes=["gbmask"])
            tr.op("dve", lambda e: e.memset(S[:], 0.0), writes=[(f"S{d}", h) for d in range(2) for h in range(8)])
            tr.op("act", lambda e: e.activation(out=alog[:], in_=alog[:], func=AF.Exp), reads=["galog"], writes=["galog"])
            for t in range(NT):
                tr.op("dve", lambda e: e.tensor_tensor(out=tmp[:, t, :], in0=ab[:, t, 0:16], in1=dtb[:], op=ALU.add), reads=["gab", "gdtb"], writes=["gtmp"])
            tr.op("act", lambda e: e.activation(out=tmp[:], in_=tmp[:], func=AF.Exp), reads=["gtmp"], writes=["gtmp"])
            tr.op("act", lambda e: e.activation(out=tmp[:], in_=tmp[:], func=AF.Ln, bias=self.cst[:, 1:2], scale=1.0), reads=["gtmp", "cst"], writes=["gtmp"])
            for t in range(NT):
                tr.op("dve", lambda e: e.scalar_tensor_tensor(out=g[:, t, :], in0=tmp[:, t, :], scalar=-1.0, in1=alog[:], op0=ALU.mult, op1=ALU.mult),
                      reads=["gtmp", "galog"], writes=["gg"])
            tr.op("act", lambda e: e.activation(out=beta[:], in_=ab[:, :, 16:32], func=AF.Sigmoid), reads=["gab"], writes=["gbeta"])
            tr.op("dve", lambda e: e.tensor_scalar(out=nbeta[:], in0=beta[:], scalar1=-1.0, scalar2=None, op0=ALU.mult), reads=["gbeta"], writes=["gnbeta"])
            pg = self.ps[0][:, 0:NT * 16].rearrange("p (t u) -> p t u", u=16)
            pt_ = self.ps[1][:, 0:NT * 16].rearrange("p (t u) -> p t u", u=16)
            for t in range(NT):
                for d in range(2):
                    self.mm(pg[:, t, d * 8:(d + 1) * 8], tri[:, d, :], g[:, t, d * 8:(d + 1) * 8], True, True, reads=["gtri", "gg"], writes=[("ps", 0)])
                self.mm(pt_[:, t, :], self.ones32[:], g[:, t, :], True, True, reads=["ones32", "gg"], writes=[("ps", 1)])
            self.copy("dve", gam[:], pg, reads=[("ps", 0)], writes=["ggam"])
            tr.op("dve", lambda e: e.tensor_tensor(out=edl[:], in0=pt_, in1=gam[:], op=ALU.subtract), reads=[("ps", 1), "ggam"], writes=["gedl"])
            tr.op("act", lambda e: e.activation(out=edl[:], in_=edl[:], func=AF.Exp), reads=["gedl"], writes=["gedl"])
            tr.op("act", lambda e: e.activation(out=eg[:], in_=gam[:], func=AF.Exp), reads=["ggam"], writes=["geg"])
            tr.op("dve", lambda e: e.tensor_tensor(out=bg[:], in0=beta[:], in1=eg[:], op=ALU.mult), reads=["gbeta", "geg"], writes=["gbg"])
            if self.stop == "gdn_gates":
                tr.barrier()
                return
            es_scan = ExitStack()
            NSLOT = 4
            R1 = lambda nm, shp, n=1: [Ring(self, es_scan, f"g{nm}{sl_}", n, shp) for sl_ in range(NSLOT)]
            qTr = R1("iq", [128, 128]); kTr = R1("ik", [128, 128]); ktmr = R1("iktm", [128, 64]); vtmr = R1("ivtm", [128, 64]); kdr = R1("kd", [128, 64])
            t1r = R1("t1", [128, 128]); t2r = R1("t2", [128, 128]); egr = R1("egb", [64, 128])
            a0r = R1("a0", [128, 256]); pxr_ = R1("px", [128, 256], 2); ptr__ = R1("pt", [128, 128], 2); lr = R1("l", [128, 2, 128])
            ybr_ = R1("yb", [128, 128], 5); qkr = R1("qk", [128, 128]); wtr = R1("wt", [128, 128]); qgr = R1("qg", [128, 128]); vnr = R1("vn", [128, 64])
            gTr = Ring(self, es_scan, "ggT", 3, [128, 128])
            for rg in qTr + kTr + wtr + qgr + [gTr]:
                for ti, tl in enumerate(rg.tiles):
                    tr.op("pool", lambda e: e.memset(tl[:], 0.0), writes=[(rg.name, ti)])
            gqv = self.scr["D_gq"].rearrange("(h d) t -> h d t", d=64)
            gkv = self.scr["D_gk"].rearrange("(h d) t -> h d t", d=64)

            def unit(d, t, h, s_, gT, gTk):
                u = d * 8 + h
                Sk = f"S{d}"
                last = 127 if d == 0 else 0
                sl = slice(t * 128, (t + 1) * 128)
                pA, pAk, pB, pBk = self.ps[2 * s_], ("ps", 2 * s_), self.ps[2 * s_ + 1], ("ps", 2 * s_ + 1)
                qT, qk_ = qTr[s_].next(); kT, kk_ = kTr[s_].next(); ktm, ktk = ktmr[s_].next(); vtm, vtk = vtmr[s_].next()
                self.dma(qT[0:64, :], gqv[h, :, sl], reads=["D_gqk"], writes=[qk_])
                self.dma(kT[0:64, :], gkv[h, :, sl], reads=["D_gqk"], writes=[kk_])
                self.dma(ktm[:], self.scr["D_ktm"][h, sl, :], reads=["D_kvtm"], writes=[ktk])
                self.dma(vtm[:], self.scr["D_vtm"][h, sl, :], reads=["D_kvtm"], writes=[vtk])
                yield
                self.mm(pA[:, 0:128], sel[:, h, :], gT[:], True, True, reads=["gsel", gTk], writes=[pAk])
                self.mm(pB[:, 0:128], kT[:], kT[:], True, True, reads=[kk_], writes=[pBk])
                kd, kdk = kdr[s_].next()
                tr.op("act", lambda e: e.activation(out=kd[:], in_=ktm[:], func=AF.Identity, scale=edl[:, t, u:u + 1]), reads=[ktk, "gedl"], writes=[kdk])
                a0, a0k = a0r[s_].next()
                tr.op("act", lambda e: e.activation(out=a0[:, 128:192], in_=vtm[:], func=AF.Identity, scale=beta[:, t, u:u + 1]),
                      reads=[vtk, "gbeta"], writes=[(a0k, "x")])
                tr.op("act", lambda e: e.activation(out=a0[:, 192:256], in_=ktm[:], func=AF.Identity, scale=bg[:, t, u:u + 1]),
                      reads=[ktk, "gbg"], writes=[(a0k, "x")])
                yield
                t1, t1k = t1r[s_].next(); t2, t2k = t2r[s_].next(); egb, egk = egr[s_].next()
                tr.op("dve", lambda e: e.scalar_tensor_tensor(out=t1[:], in0=pA[:, 0:128], scalar=gam[:, t, u:u + 1], in1=gm[:, 2 * d, :],
                                                             op0=ALU.subtract, op1=ALU.add), reads=[pAk, "ggam", "gmaskc"], writes=[t1k])
                tr.op("dve", lambda e: e.scalar_tensor_tensor(out=t2[:], in0=pA[:, 0:128], scalar=gam[:, t, u:u + 1], in1=gm[:, 2 * d + 1, :],
                                                             op0=ALU.subtract, op1=ALU.add), reads=[pAk, "ggam", "gmaskc"], writes=[t2k])
                tr.op("dve", lambda e: e.tensor_copy(out=egb[:], in_=pA[0:64, 0:128]), reads=[pAk], writes=[egk])
                yield
                tr.op("act", lambda e: e.activation(out=t1[:], in_=t1[:], func=AF.Exp, scale=-1.0), reads=[t1k], writes=[t1k])
                tr.op("act", lambda e: e.activation(out=t2[:], in_=t2[:], func=AF.Exp), reads=[t2k], writes=[t2k])
                tr.op("act", lambda e: e.activation(out=egb[:], in_=egb[:], func=AF.Exp), reads=[egk], writes=[egk])
                self.mm(pA[:, 0:128], kT[:], qT[:], True, True, reads=[kk_, qk_], writes=[pAk])
                yield
                tr.op("dve", lambda e: e.scalar_tensor_tensor(out=a0[:, 0:128], in0=pB[:, 0:128], scalar=nbeta[:, t, u:u + 1], in1=t1[:],
                                                             op0=ALU.mult, op1=ALU.mult), reads=[pBk, "gnbeta", t1k], writes=[(a0k, "p")])
                qk, qkk = qkr[s_].next()
                tr.op("dve", lambda e: e.tensor_tensor(out=qk[:], in0=pA[:, 0:128], in1=t2[:], op=ALU.mult), reads=[pAk, t2k], writes=[qkk])
                qg, qgk = qgr[s_].next()
                tr.op("pool", lambda e: e.tensor_tensor(out=qg[0:64, :], in0=qT[0:64, :], in1=egb[:], op=ALU.mult), reads=[qk_, egk], writes=[qgk])
                yield
                tr.op("pe", lambda e: e.transpose(pB[:, 0:128], a0[:, 0:128], self.ident[:]), reads=[(a0k, "p"), "ident"], writes=[pBk])
                px, pxk = pxr_[s_].next(); pT, pTk = ptr__[s_].next(); lt, ltk = lr[s_].next()
                tr.op("pool", lambda e: e.tensor_tensor(out=pT[:], in0=a0[:, 0:128], in1=bmk[:, 0, :], op=ALU.mult), reads=[(a0k, "p"), "gbmask"], writes=[pTk])
                self.copy("act", px[:, 128:256], self.ident[:], reads=["ident"], writes=[(pxk, "x")])
                yield
                tr.op("dve", lambda e: e.tensor_tensor(out=px[:, 0:128], in0=pB[:, 0:128], in1=bmk[:, 0, :], op=ALU.mult), reads=[pBk, "gbmask"], writes=[(pxk, "p")])
                tr.op("dve", lambda e: e.tensor_tensor(out=lt[:, 0, :], in0=pB[:, 0:128], in1=bmk[:, 1, :], op=ALU.mult), reads=[pBk, "gbmask"], writes=[(ltk, 0)])
                tr.op("dve", lambda e: e.tensor_tensor(out=lt[:, 1, :], in0=pB[:, 0:128], in1=bmk[:, 2, :], op=ALU.mult), reads=[pBk, "gbmask"], writes=[(ltk, 1)])
                yield
                for lev in range(5):
                    if lev < 4:
                        self.mm(pA[:, 0:256], pT[:].bitcast(F32R) if self.use_f32r else pT[:], px[:, 0:256].bitcast(F32R) if self.use_f32r else px[:, 0:256],
                                True, True, reads=[pTk, (pxk, "p"), (pxk, "x")], writes=[pAk])
                        self.mm(pB[:, 0:128], px[:, 0:128], pT[:], True, True, reads=[pTk, (pxk, "p")], writes=[pBk])
                        yield
                        nx, nxk = pxr_[s_].next(); nT, nTk = ptr__[s_].next()
                        self.copy("act", nT[:], pB[:, 0:128], reads=[pBk], writes=[nTk])
                        tr.op("dve", lambda e: e.tensor_tensor(out=nx[:, 128:256], in0=pA[:, 128:256], in1=px[:, 128:256], op=ALU.add),
                              reads=[pAk, (pxk, "x")], writes=[(nxk, "x")])
                        self.copy("dve", nx[:, 0:128], pA[:, 0:128], reads=[pAk], writes=[(nxk, "p")])
                        px, pxk, pT, pTk = nx, nxk, nT, nTk
                        yield
                    else:
                        self.mm(pA[:, 0:128], pT[:], px[:, 128:256], True, True, reads=[pTk, (pxk, "x")], writes=[pAk])
                        yield
                        nx, nxk = pxr_[s_].next()
                        tr.op("dve", lambda e: e.tensor_tensor(out=nx[:, 128:256], in0=pA[:, 0:128], in1=px[:, 128:256], op=ALU.add),
                              reads=[pAk, (pxk, "x")], writes=[(nxk, "x")])
                        px, pxk = nx, nxk
                        yield
                MT, MTk = px[:, 128:256], (pxk, "x")
                cnt = [0]

                def apply(lhsT_ap, lhs_key, rhs_ap, rhs_key, add_ap=None, add_key=None):
                    pb, pbk = (pA, pAk) if cnt[0] % 2 == 0 else (pB, pBk)
                    eng = "act" if cnt[0] % 2 == 0 else "dve"
                    cnt[0] += 1
                    self.mm(pb[:, 0:128], lhsT_ap, rhs_ap, True, True, reads=[lhs_key, rhs_key], writes=[pbk])
                    yield
                    yb, ybk = ybr_[s_].next()
                    if add_ap is None:
                        self.copy(eng, yb[:], pb[:, 0:128], reads=[pbk], writes=[ybk])
                    else:
                        tr.op("dve", lambda e: e.tensor_tensor(out=yb[:], in0=pb[:, 0:128], in1=add_ap, op=ALU.add), reads=[pbk, add_key], writes=[ybk])
                    yield
                    return yb, ybk

                def s64(z_ap, z_key):
                    y1, y1k = yield from apply(MT, MTk, z_ap, z_key)
                    z1, z1k = yield from apply(lt[:, 0, :], (ltk, 0), y1[:], y1k)
                    r_ = yield from apply(MT, MTk, z1[:], z1k, add_ap=y1[:], add_key=y1k)
                    return r_

                y2, y2k = yield from s64(a0[:, 128:256], (a0k, "x"))
                z2, z2k = yield from apply(lt[:, 1, :], (ltk, 1), y2[:], y2k)
                y4, y4k = yield from s64(z2[:], z2k)
                px, pxk = pxr_[s_].next()
                tr.op("pool", lambda e: e.tensor_tensor(out=px[:, 128:256], in0=y2[:], in1=y4[:], op=ALU.add), reads=[y2k, y4k], writes=[(pxk, "x")])
                yield
                tr.op("pe", lambda e: e.transpose(pA[0:64, 0:128], px[:, 192:256], self.ident[:]), reads=[(pxk, "x"), "ident"], writes=[pAk])
                yield
                wt_, wtk = wtr[s_].next()
                self.copy("act", wt_[0:64, :], pA[0:64, 0:128], reads=[pAk], writes=[wtk])
                yield
                self.mm(pB[:, 0:64], wt_[:], S[:, d, h, :], True, True, reads=[wtk, (Sk, h)], writes=[pBk])
                yield
                vn, vnk = vnr[s_].next()
                tr.op("dve", lambda e: e.tensor_tensor(out=vn[:], in0=px[:, 128:192], in1=pB[:, 0:64], op=ALU.subtract),
                      reads=[(pxk, "x"), pBk], writes=[vnk])
                yield
                self.mm(pA[:, 0:64], qg[:], S[:, d, h, :], True, False, reads=[qgk, (Sk, h)], writes=[pAk])
                self.mm(pA[:, 0:64], qk[:], vn[:], False, True, reads=[qkk, vnk], writes=[pAk])
                self.mm(pB[0:64, 0:64], kd[:], vn[:], True, True, reads=[kdk, vnk], writes=[pBk])
                yield
                if d == 0:
                    self.copy("act", oacc[:, t, h * 64:(h + 1) * 64], pA[:, 0:64], reads=[pAk], writes=[("oacc", t, h)])
                else:
                    tr.op("dve", lambda e: e.tensor_tensor(out=oacc[:, t, h * 64:(h + 1) * 64], in0=pA[:, 0:64], in1=oacc[:, t, h * 64:(h + 1) * 64], op=ALU.add),
                          reads=[pAk, ("oacc", t, h)], writes=[("oacc", t, h)])
                tr.op("dve", lambda e: e.scalar_tensor_tensor(out=S[0:64, d, h, :], in0=S[0:64, d, h, :], scalar=egb[:, last:last + 1], in1=pB[0:64, 0:64],
                                                             op0=ALU.mult, op1=ALU.add), reads=[(Sk, h), egk, pBk], writes=[(Sk, h)])

            def unit_stream():
                for d in range(2):
                    order = list(range(NT)) if d == 0 else [1, 0] + list(range(NT - 1, 1, -1))
                    for t in order:
                        gT, gTk = gTr.next()
                        pk = ("ps", 7)
                        self.mm(self.ps[7][0:8, 256:384], g[:, t, d * 8:(d + 1) * 8], tri[:, d, :], True, True, reads=["gg", "gtri"], writes=[pk])
                        self.copy("act", gT[0:8, :], self.ps[7][0:8, 256:384], reads=[pk], writes=[gTk])
                        for h in range(8):
                            yield (d, t, h, gT, gTk)

            stream = unit_stream()
            active = {}
            free_slots = list(range(NSLOT))
            exhausted = False
            while True:
                while free_slots and not exhausted:
                    try:
                        d_, t_, h_, gT_, gTk_ = next(stream)
                    except StopIteration:
                        exhausted = True
                        break
                    s_ = free_slots.pop(0)
                    active[s_] = unit(d_, t_, h_, s_, gT_, gTk_)
                if not active:
                    break
                for s_ in sorted(active):
                    try:
                        next(active[s_])
                    except StopIteration:
                        del active[s_]
                        free_slots.append(s_)
            if (self.stop or "").startswith("gdn_u1"):
                tr.barrier()
                es_scan.close()
                return
            tr.barrier()
            es_scan.close()
            for t in range(NT):
                for h in range(8):
                    tr.lastw[("oacc", t, h)] = None
            tr.lastw = {k: v for k, v in tr.lastw.items() if v is not None}
            if "D_x" in self.taps:
                self.dma(self.scr["D_oc"].rearrange("(t p) c -> p t c", p=128), oacc[:], reads=[("oacc", t, h) for t in range(NT) for h in range(8)])
            gn = self.sb(es, "gn", [128, 64])
            self.dma(gn[:], self.inp["gnorm_rep"][:, l, :], writes=["gn"])
            zr = Ring(self, es, "gz", 2, [128, 512]); sq2 = Ring(self, es, "gsq2", 2, [128, 512]); ssr = Ring(self, es, "gss", 2, [128, 8])
            ycr = Ring(self, es, "gyc", 2, [128, 4, 128], BF16)
            ycv = self.scr["D_yc"].rearrange("(c p) t -> p c t", p=128)
            for t in range(NT):
                z, zk = zr.next()
                self.dma(z[:], self.scr["D_tm"][t * 128:(t + 1) * 128, ZC:ZC + 512], reads=["D_tm"], writes=[zk])
                tr.op("act", lambda e: e.activation(out=z[:], in_=z[:], func=AF.Silu), reads=[zk], writes=[zk])
                ok = [("oacc", t, h) for h in range(8)]
                sq, sk = sq2.next()
                tr.op("pool", lambda e: e.tensor_tensor(out=sq[:], in0=oacc[:, t, :], in1=oacc[:, t, :], op=ALU.mult), reads=ok, writes=[sk])
                ss, ssk = ssr.next()
                tr.op("dve", lambda e: e.tensor_reduce(out=ss[:], in_=sq[:].rearrange("p (h d) -> p h d", d=64), axis=AX.X, op=ALU.add), reads=[sk], writes=[ssk])
                self.rstd_from_ss(ss[:], ssk, ss[:], ssk, 1.0 / 64)
                o3 = oacc[:, t, :].rearrange("p (h d) -> p h d", d=64)
                tr.op("dve", lambda e: e.tensor_tensor(out=sq[:].rearrange("p (h d) -> p h d", d=64), in0=o3, in1=ss[:].unsqueeze(2).to_broadcast([128, 8, 64]), op=ALU.mult),
                      reads=ok + [ssk], writes=[sk])
                tr.op("dve", lambda e: e.tensor_tensor(out=sq[:].rearrange("p (h d) -> p h d", d=64), in0=sq[:].rearrange("p (h d) -> p h d", d=64),
                                                      in1=gn[:].unsqueeze(1).to_broadcast([128, 8, 64]), op=ALU.mult), reads=[sk, "gn"], writes=[sk])
                tr.op("dve", lambda e: e.tensor_tensor(out=sq[:], in0=sq[:], in1=z[:], op=ALU.mult), reads=[sk, zk], writes=[sk])
                pk = ("ps", 4 + t % 2); pt = self.ps[4 + t % 2]
                for c in range(4):
                    tr.op("pe", lambda e: e.transpose(pt[:, c * 128:(c + 1) * 128], sq[:, c * 128:(c + 1) * 128], self.ident[:]), reads=[sk, "ident"], writes=[pk])
                yc, yk = ycr.next()
                self.copy("act", yc[:], pt[:].rearrange("p (c t) -> p c t", c=4), reads=[pk], writes=[yk])
                self.dma(ycv[:, :, t * 128:(t + 1) * 128], yc[:], reads=[yk], writes=["D_yc"])
            tr.barrier()

    def phase_merge(self, l):
        tr = self.tr
        with ExitStack() as es:
            wp = [self.sb(es, f"wp{i}", [128, 4, D], BF16) for i in range(3)]
            wo = self.sb(es, "wo", [128, 8, D], BF16)
            for i, nm in enumerate(("w_pa", "w_pb", "w_pc")):
                self.dma(wp[i][:], self.inp[nm][l].rearrange("(c p) n -> p c n", p=128), writes=[nm], q="gq")
            self.dma(wo[:], self.inp["w_out"][l].rearrange("(c p) n -> p c n", p=128), writes=["w_out"], q="gq")
            gr = Ring(self, es, "mg", 1, [128, 24, 512], BF16)
            yr = [Ring(self, es, f"my{i}", 2, [128, 4, 512], BF16) for i in range(3)]
            mixr = Ring(self, es, "mix", 2, [128, 512]); tmr = Ring(self, es, "mtm", 2, [128, 512])
            mT = Ring(self, es, "mixT", 1, [128, 8, 512], BF16)
            gv = self.scr["D_gate"].rearrange("(c p) t -> p c t", p=128)
            yv = [self.scr[nm].rearrange("(c p) t -> p c t", p=128) for nm in ("D_ya", "D_yb", "D_yc")]
            pi = 0
            for ci, (t0, n) in enumerate(TCH):
                s = 1 if ci == 0 else 0
                gt, gk = gr.next()
                self.dma(gt[:, :, 0:n], gv[:, :, t0:t0 + n], reads=[("D_gate", ci)], writes=[gk])
                ys = []
                for i in range(3):
                    y, yk = yr[i].next()
                    self.dma(y[:, :, 0:n], yv[i][:, :, t0:t0 + n], reads=["D_ya", "D_yb", "D_yc"], writes=[yk])
                    ys.append((y, yk))
                mt, mtk = mT.next()
                for m in range(8):
                    mix, mk = mixr.next()
                    for i in range(3):
                        pk = ("ps", pi % 4); pt = self.ps[pi % 4]; pi += 1
                        y, yk = ys[i]
                        for c in range(4):
                            self.mm(pt[:, 0:n], wp[i][:, c, m * 128:(m + 1) * 128], y[:, c, 0:n], c == 0, c == 3, reads=[("w_pa", "w_pb", "w_pc")[i], yk], writes=[pk])
                        if i == 0:
                            tr.op("dve", lambda e: e.tensor_tensor(out=mix[:, 0:n], in0=pt[:, 0:n], in1=gt[:, m, 0:n], op=ALU.mult), reads=[pk, gk], writes=[mk])
                        else:
                            tm, tk = tmr.next()
                            tr.op("dve", lambda e: e.tensor_tensor(out=tm[:, 0:n], in0=pt[:, 0:n], in1=gt[:, i * 8 + m, 0:n], op=ALU.mult), reads=[pk, gk], writes=[tk])
                            if i == 1:
                                tr.op("pool", lambda e: e.tensor_tensor(out=mix[:, 0:n], in0=mix[:, 0:n], in1=tm[:, 0:n], op=ALU.add), reads=[mk, tk], writes=[mk])
                            else:
                                tr.op("pool", lambda e: e.tensor_tensor(out=mt[:, m, 0:n], in0=mix[:, 0:n], in1=tm[:, 0:n], op=ALU.add), reads=[mk, tk], writes=[(mtk, m)])
                for m in range(8):
                    pk = ("ps", 4 + m % 4); pt = self.ps[4 + m % 4]
                    for c in range(8):
                        self.mm(pt[:, 0:n], wo[:, c, m * 128:(m + 1) * 128], mt[:, c, 0:n], c == 0, c == 7, reads=["w_out", (mtk, c)], writes=[pk])
                    tr.op("dve", lambda e: e.scalar_tensor_tensor(out=self.xT[:, m, t0:t0 + n], in0=pt[:, 0:n], scalar=self.mod[:, l, 16 + m, s:s + 1],
                                                                 in1=self.xT[:, m, t0:t0 + n], op0=ALU.mult, op1=ALU.add),
                          reads=[pk, "mod", ("xT", ci)], writes=[("xT", ci)])
            tr.barrier()

    def phase_ffn(self, l):
        tr = self.tr
        with ExitStack() as es:
            fcw = self.sb(es, "fcw", [128, 44, 3])
            self.dma(fcw[:], self.inp["fconv_fm"][:, l, :, :], writes=["fcw"])
            with ExitStack() as es2:
                xn = self.sb(es2, "xn2", [128, 8, T], BF16)
                self.adaln(es2, l, 1, xn)
                wr = Ring(self, es2, "wup", 2, [128, 8, 2, 128], BF16)
                hr = [Ring(self, es2, f"fh{i}", 2, [128, T]) for i in range(2)]
                cr = [Ring(self, es2, f"fc{i}", 2, [128, T]) for i in range(2)]
                ar = Ring(self, es2, "fact", 2, [128, T], BF16)
                wv = self.inp["ffn_up"][l].rearrange("(c p) n -> p c n", p=128)
                pi = 0
                for cc in range(22):
                    w, wk = wr.next()
                    self.dma(w[:, :, 0, :], wv[:, :, cc * 128:(cc + 1) * 128], writes=[wk], q="gq")
                    self.dma(w[:, :, 1, :], wv[:, :, 2816 + cc * 128:2816 + (cc + 1) * 128], writes=[wk], q="gq")
                    cs_ = []
                    for i in range(2):
                        hb, hk = hr[i].next()
                        for ci, (t0, n) in enumerate(TCH):
                            pk = ("ps", pi % 4); pt = self.ps[pi % 4]; pi += 1
                            for c in range(8):
                                self.mm(pt[:, 0:n], w[:, c, i, :], xn[:, c, t0:t0 + n], c == 0, c == 7, reads=[wk, ("xn", ci)], writes=[pk])
                            self.copy(self.evac_eng(), hb[:, t0:t0 + n], pt[:, 0:n], reads=[pk], writes=[hk])
                        cb, ck = cr[i].next()
                        j = cc + 22 * i
                        tr.op("act", lambda e: e.activation(out=cb[:], in_=hb[:], func=AF.Identity, scale=fcw[:, j, 1:2]), reads=[hk, "fcw"], writes=[ck])
                        for (o0, o1, i0, i1, k) in ((1, CTX, 0, CTX - 1, 0), (CTX + 1, T, CTX, T - 1, 0), (0, CTX - 1, 1, CTX, 2), (CTX, T - 1, CTX + 1, T, 2)):
                            tr.op("dve", lambda e: e.scalar_tensor_tensor(out=cb[:, o0:o1], in0=hb[:, i0:i1], scalar=fcw[:, j, k:k + 1], in1=cb[:, o0:o1],
                                                                       op0=ALU.mult, op1=ALU.add), reads=[hk, ck, "fcw"], writes=[ck])
                        cs_.append((cb, ck))
                    (ca, cak), (cb, cbk) = cs_
                    tr.op("act", lambda e: e.activation(out=ca[:], in_=ca[:], func=AF.Silu), reads=[cak], writes=[cak])
                    a, ak = ar.next()
                    tr.op("dve", lambda e: e.tensor_tensor(out=a[:], in0=ca[:], in1=cb[:], op=ALU.mult), reads=[cak, cbk], writes=[ak])
                    self.dma(self.scr["D_act"][cc * 128:(cc + 1) * 128, :], a[:], reads=[ak], writes=["D_act"])
                tr.barrier()
            with ExitStack() as es2:
                wd = self.sb(es2, "wdn", [128, 22, D], BF16)
                self.dma(wd[:, 0:11, :], self.inp["ffn_down"][l].rearrange("(c p) n -> p c n", p=128)[:, 0:11, :], writes=["wdn"], q="gq")
                self.dma(wd[:, 11:22, :], self.inp["ffn_down"][l].rearrange("(c p) n -> p c n", p=128)[:, 11:22, :], writes=["wdn"], q="gq")
                ar = Ring(self, es2, "fa", 2, [128, 22, 512], BF16)
                av = self.scr["D_act"].rearrange("(c p) t -> p c t", p=128)
                for ci, (t0, n) in enumerate(TCH):
                    s = 1 if ci == 0 else 0
                    a, ak = ar.next()
                    self.dma(a[:, :, 0:n], av[:, :, t0:t0 + n], reads=["D_act"], writes=[ak])
                    for m in range(8):
                        pk = ("ps", m % 4); pt = self.ps[m % 4]
                        for c in range(22):
                            self.mm(pt[:, 0:n], wd[:, c, m * 128:(m + 1) * 128], a[:, c, 0:n], c == 0, c == 21, reads=["wdn", ak], writes=[pk])
                        tr.op("dve", lambda e: e.scalar_tensor_tensor(out=self.xT[:, m, t0:t0 + n], in0=pt[:, 0:n], scalar=self.mod[:, l, 40 + m, s:s + 1],
                                                                     in1=self.xT[:, m, t0:t0 + n], op0=ALU.mult, op1=ALU.add),
                              reads=[pk, "mod", ("xT", ci)], writes=[("xT", ci)])
                tr.barrier()

    def phase_final(self):
        tr = self.tr
        with ExitStack() as es:
            nf = self.sb(es, "nf", [128, D])
            self.dma(nf[:], self.inp["normf_rep"], writes=["nf"])
            xr = Ring(self, es, "fx", 2, [128, D]); sqr = Ring(self, es, "fsq", 2, [128, D]); ssr = Ring(self, es, "fss", 2, [128, 1])
            for t in range(2, NT):
                x, xk = xr.next()
                for half in range(2):
                    pk = ("ps", (2 * t + half) % 8); pt = self.ps[(2 * t + half) % 8]
                    for j in range(4):
                        c = half * 4 + j
                        tr.op("pe", lambda e: e.transpose(pt[:, j * 128:(j + 1) * 128], self.xT[:, c, t * 128:(t + 1) * 128], self.ident[:]),
                              reads=["ident"], writes=[pk])
                    self.copy(self.evac_eng(), x[:, half * 512:(half + 1) * 512], pt[:, :], reads=[pk], writes=[xk])
                sq, sk = sqr.next(); ss, ssk = ssr.next()
                tr.op("pool", lambda e: e.tensor_tensor(out=sq[:], in0=x[:], in1=x[:], op=ALU.mult), reads=[xk], writes=[sk])
                tr.op("dve", lambda e: e.tensor_reduce(out=ss[:], in_=sq[:], axis=AX.X, op=ALU.add), reads=[sk], writes=[ssk])
                self.rstd_from_ss(ss[:], ssk, ss[:], ssk, 1.0 / D)
                tr.op("dve", lambda e: e.scalar_tensor_tensor(out=sq[:], in0=x[:], scalar=ss[:, 0:1], in1=nf[:], op0=ALU.mult, op1=ALU.mult),
                      reads=[xk, ssk, "nf"], writes=[sk])
                self.dma(self.out[(t - 2) * 128:(t - 1) * 128, :], sq[:], reads=[sk])


def _rope_tables(d):
    da = d // 2
    half = da // 2
    inv = (10000.0 ** (-np.arange(half, dtype=np.float32) / half)).astype(np.float32)
    t = np.arange(SEQ)
    rows = (t // 64).astype(np.float32)
    cols = (t % 64).astype(np.float32)
    cos = np.ones((d, T), np.float32)
    sin = np.zeros((d, T), np.float32)
    for i in range(d):
        pos = rows if i < da else cols
        ii = i % da
        ang = pos * inv[ii % half]
        cos[i, CTX:] = np.cos(ang)
        sin[i, CTX:] = np.sin(ang) * (-1.0 if ii < half else 1.0)
    return np.stack([cos, sin])


def _partner(d):
    da = d // 2
    half = da // 2
    idx = np.arange(d)
    ii = idx % da
    return np.where(ii < half, idx + half, idx - half)


def _constants():
    j = np.arange(128)[:, None]
    i = np.arange(128)[None, :]
    tri = np.stack([(j <= i), (j >= i)]).astype(np.float32)
    gmask = np.stack([np.where(i < j, 0.0, BIG),
                      np.where(j <= i, 0.0, -BIG),
                      np.where(i > j, 0.0, BIG),
                      np.where(j >= i, 0.0, -BIG)]).astype(np.float32)
    m_prev = (j >= i).astype(np.float32)
    m_next = (j <= i).astype(np.float32)
    swamask = np.stack([np.tile(m_prev, (1, 4)), np.tile(m_next, (1, 4))]).astype(np.float32)
    b32 = (j // 32 == i // 32)
    b64 = (j // 64 == i // 64)
    bmask = np.stack([b32, b64 & ~b32, ~b64]).astype(np.float32)
    selh = np.zeros((8, 8, 128), np.float32)
    for h in range(8):
        selh[h, h, :] = 1.0
    return dict(ident=np.eye(128, dtype=np.float32), tri=tri, gmask=gmask, bmask=bmask, bd64=b64.astype(np.float32), swamask=swamask, selh=selh,
                ropeM=_rope_tables(32), ropeS=_rope_tables(64))


def _fm(v, p=128):
    v = np.asarray(v, np.float32)
    lead = v.shape[:-1]
    n = v.shape[-1] // p
    return np.ascontiguousarray(np.moveaxis(v.reshape(lead + (n, p)), -1, 0))


def prepare_shared(inp, L=DEPTH):
    f = lambda k: np.asarray(inp[k], np.float32)[:L] if k not in ("norm_f",) else np.asarray(inp[k], np.float32)
    w_in = f("w_in")
    p32 = _partner(32)
    p64 = _partner(64)
    sqp = (np.arange(8)[:, None] * 64 + p64[None, :]).reshape(-1)
    skp = (np.arange(2)[:, None] * 64 + p64[None, :]).reshape(-1)
    w_fm = np.concatenate([w_in[:, :, 0:640], w_in[:, :, 640:672], w_in[:, :, 640:672][:, :, p32],
                           w_in[:, :, 672:1184], w_in[:, :, 672:1184][:, :, sqp],
                           w_in[:, :, 1184:1312], w_in[:, :, 1184:1312][:, :, skp],
                           w_in[:, :, 1440:2976], np.zeros((L, D, 64), np.float32), w_in[:, :, 3520:6592]], axis=2)
    assert w_fm.shape[2] == NFM
    w_tm = np.concatenate([w_in[:, :, 1312:1440], w_in[:, :, 2976:3488], w_in[:, :, 3488:3520]], axis=2)
    wuq = f("w_uq").reshape(L, 384, 8, 96)
    wukv = f("w_ukv").reshape(L, 256, 8, 128)
    sh = dict(
        w_mod=f("w_mod"), b_mod_fm=_fm(f("b_mod")),
        norm1_fm=_fm(f("norm1")), norm2_fm=_fm(f("norm2")), normf_rep=np.ascontiguousarray(np.broadcast_to(f("norm_f"), (128, D))),
        w_fm=np.ascontiguousarray(w_fm), w_tm=np.ascontiguousarray(w_tm),
        qnorm_fm=_fm(f("mla_q_norm")), kvnorm_fm=_fm(f("mla_kv_norm")),
        wuq_n=np.ascontiguousarray(wuq[..., 0:64].reshape(L, 384, 512)),
        wuq_r=np.ascontiguousarray(wuq[..., 64:96].reshape(L, 384, 256)),
        wuq_rp=np.ascontiguousarray(wuq[..., 64:96][..., p32].reshape(L, 384, 256)),
        wukv_k=np.ascontiguousarray(wukv[..., 0:64].reshape(L, 256, 512)),
        wukv_v=np.ascontiguousarray(wukv[..., 64:128].reshape(L, 256, 512)),
        sink_rep=np.ascontiguousarray(np.broadcast_to(f("swa_sink"), (64, L, 8))),
        gconv_fm=np.ascontiguousarray(f("gdn_conv").reshape(L, 3, 12, 128).transpose(3, 0, 2, 1)),
        alog_rep=np.ascontiguousarray(np.broadcast_to(f("gdn_a_log").reshape(L, 16), (128, L, 16))),
        dtb_rep=np.ascontiguousarray(np.broadcast_to(f("gdn_dt_bias").reshape(L, 16), (128, L, 16))),
        gnorm_rep=np.ascontiguousarray(np.broadcast_to(f("gdn_norm"), (128, L, 64))),
        w_pa=f("w_branch_a"), w_pb=f("w_branch_b"), w_pc=f("w_branch_c"), w_out=f("w_out"),
        ffn_up=f("ffn_up"), fconv_fm=np.ascontiguousarray(f("ffn_conv").reshape(L, 3, 44, 128).transpose(3, 0, 2, 1)),
        ffn_down=f("ffn_down"),
    )
    sh.update(_constants())
    return sh


def per_core(inp, b):
    x = np.asarray(inp["x"], np.float32)[b]
    ctx = np.asarray(inp["ctx"], np.float32)[b]
    c = np.asarray(inp["c"], np.float32)[b]
    cc = np.asarray(inp["c_ctx"], np.float32)
    return dict(xin=np.ascontiguousarray(np.concatenate([ctx, x], 0)),
                c_fm=np.ascontiguousarray(np.stack([_fm(c), _fm(cc)], axis=-1)))


_CACHE = {}


def kernel(**inputs):
    if "nc" not in _CACHE:
        _CACHE["nc"] = Builder().build()
    nc = _CACHE["nc"]
    shared = prepare_shared(inputs)
    in_maps = []
    for b in range(8):
        m = dict(shared)
        m.update(per_core(inputs, b))
        in_maps.append(m)
    res = run_bass_kernel_spmd(nc, in_maps, core_ids=list(range(8)))
    return np.stack([np.asarray(r["out"], np.float32) for r in res.results], axis=0)
```

```python
import numpy as np
from contextlib import ExitStack
import concourse.bass as bass
import concourse.mybir as mybir
from concourse.bass_utils import run_bass_kernel_spmd

F32 = mybir.dt.float32
BF16 = mybir.dt.bfloat16
F32R = mybir.dt.float32r
USE_F32R = False
AF = mybir.ActivationFunctionType
ALU = mybir.AluOpType
AX = mybir.AxisListType

D = 1024
SEQ = 2048
CTX = 256
T = SEQ + CTX
NT = T // 128
DEPTH = 4
EPS = 1e-6
TCH = [(0, 256), (256, 512), (768, 512), (1280, 512), (1792, 512)]
BIG = 1.0e5

CQ, CKV, KR, KRP, SQ, SQP, SK, SKP, GQ, PAD, GATE, NFM = 0, 384, 640, 672, 704, 1216, 1728, 1856, 1984, 3520, 3584, 6656
SV, ZC, AB, NTM = 0, 128, 640, 672


def tile_chunk(t):
    return 0 if t < 2 else 1 + (t - 2) // 4


class Trk:
    ISSUER = {"pe": "pe", "act": "act", "dve": "dve", "pool": "pool", "sp": "sp", "gq": "pool"}
    NSLOT = {"sp": 24, "gq": 8}

    def __init__(self, nc, es):
        self.nc = nc
        self.eng = {"pe": nc.tensor, "act": nc.scalar, "dve": nc.vector, "pool": nc.gpsimd, "sp": nc.sync}
        self.sem = {e: es.enter_context(nc.semaphore("sem_" + e)) for e in ("pe", "act", "dve", "pool")}
        for q, k in self.NSLOT.items():
            for i in range(k):
                self.sem[(q, i)] = es.enter_context(nc.semaphore(f"sem_{q}{i}"))
        self.cnt = {e: 0 for e in self.sem}
        self.nq = {q: 0 for q in self.NSLOT}
        self.lastw = {}
        self.readers = {}
        self.waited = {}
        self.n_wait = 0
        self.n_op = 0

    def _need(self, issuer, eng, dep):
        de, ds = dep
        if de == "pe" and eng == "pe":
            return
        k = (issuer, de)
        if self.waited.get(k, 0) >= ds:
            return
        self.waited[k] = ds
        self.eng[issuer].wait_ge(self.sem[de], ds)
        self.n_wait += 1

    def op(self, eng, emit, reads=(), writes=()):
        issuer = self.ISSUER[eng]
        for key in reads:
            w = self.lastw.get(key)
            if w is not None:
                self._need(issuer, eng, w)
            if isinstance(key, tuple) and key[0] == "ps":
                for rk, tok in self.readers.get(key, {}).items():
                    if rk != eng:
                        self._need(issuer, eng, tok)
        for key in writes:
            w = self.lastw.get(key)
            if w is not None:
                self._need(issuer, eng, w)
            for tok in self.readers.get(key, {}).values():
                self._need(issuer, eng, tok)
        if eng in self.NSLOT:
            sk = (eng, self.nq[eng] % self.NSLOT[eng])
            self.nq[eng] += 1
            if self.cnt[sk] > 0:
                self._need(issuer, eng, (sk, self.cnt[sk]))
            inc = 16
        else:
            sk = eng
            inc = 1
        inst = emit(self.eng[issuer])
        self.cnt[sk] += inc
        inst.then_inc(self.sem[sk], inc)
        tok = (sk, self.cnt[sk])
        for key in reads:
            self.readers.setdefault(key, {})[sk] = tok
        for key in writes:
            self.lastw[key] = tok
            self.readers[key] = {}
        self.n_op += 1
        return inst

    def barrier(self):
        for issuer in ("pe", "act", "dve", "pool", "sp"):
            for e, c in self.cnt.items():
                if c > 0:
                    self._need(issuer, "x", (e, c))
        self.lastw.clear()
        self.readers.clear()

    def finish(self, eng="sp"):
        for e, c in self.cnt.items():
            if c > 0:
                self._need(eng, "x", (e, c))


class Ring:
    def __init__(self, b, es, name, n, shape, dt=F32):
        self.tiles = [b.sb(es, f"{name}{i}", shape, dt) for i in range(n)]
        self.name = name
        self.n = n
        self.i = 0

    def next(self):
        j = self.i % self.n
        self.i += 1
        return self.tiles[j], (self.name, j)


class Builder:
    def __init__(self, n_layers=DEPTH, stop=None, taps=()):
        self.n_layers = n_layers
        self.stop = stop
        self.taps = set(taps)
        self.nc = bass.Bass("TRN2", target_bir_lowering=False)
        self.inp = {}
        self.scr = {}
        self.ev = 0
        self.use_f32r = USE_F32R

    def din(self, name, shape, dt=F32):
        self.inp[name] = self.nc.dram_tensor(name, list(shape), dt, kind="ExternalInput").ap()
        return self.inp[name]

    def dscr(self, name, shape, dt=F32, force=False):
        kind = "ExternalOutput" if (name in self.taps or force) else "Internal"
        self.scr[name] = self.nc.dram_tensor(name, list(shape), dt, kind=kind).ap()
        return self.scr[name]

    def declare(self):
        L = self.n_layers
        d = self.din
        d("xin", [T, D]); d("c_fm", [128, 8, 2])
        d("w_mod", [L, D, 6 * D]); d("b_mod_fm", [128, L, 48])
        d("norm1_fm", [128, L, 8]); d("norm2_fm", [128, L, 8]); d("normf_rep", [128, D])
        d("w_fm", [L, D, NFM]); d("w_tm", [L, D, NTM])
        d("qnorm_fm", [128, L, 3]); d("kvnorm_fm", [128, L, 2])
        d("wuq_n", [L, 384, 512]); d("wuq_r", [L, 384, 256]); d("wuq_rp", [L, 384, 256])
        d("wukv_k", [L, 256, 512]); d("wukv_v", [L, 256, 512])
        d("sink_rep", [64, L, 8])
        d("gconv_fm", [128, L, 12, 3]); d("bd64", [128, 128]); d("alog_rep", [128, L, 16]); d("dtb_rep", [128, L, 16]); d("gnorm_rep", [128, L, 64])
        d("w_pa", [L, 512, D]); d("w_pb", [L, 512, D]); d("w_pc", [L, 512, D]); d("w_out", [L, D, D])
        d("ffn_up", [L, D, 5632]); d("fconv_fm", [128, L, 44, 3]); d("ffn_down", [L, 2816, D])
        d("ident", [128, 128]); d("ropeM", [2, 32, T]); d("ropeS", [2, 64, T])
        d("tri", [2, 128, 128]); d("gmask", [4, 128, 128]); d("bmask", [3, 128, 128]); d("swamask", [2, 128, 512]); d("selh", [8, 8, 128])
        self.out = self.nc.dram_tensor("out", [SEQ, D], F32, kind="ExternalOutput").ap()
        s = self.dscr
        s("D_fm", [NFM, T]); s("D_tm", [T, NTM]); s("D_gate", [3072, T], BF16)
        s("D_ya", [512, T], BF16); s("D_yb", [512, T], BF16); s("D_yc", [512, T], BF16)
        s("D_gq", [512, T]); s("D_gk", [512, T]); s("D_ktm", [8, T, 64]); s("D_vtm", [8, T, 64])
        s("D_act", [2816, T], BF16)
        if "D_x" in self.taps:
            s("D_x", [D, T], F32, True); s("D_xn", [D, T], BF16, True); s("D_mod", [128, L * 48 * 2], F32, True); s("D_oc", [T, 512], F32, True)

    def sb(self, es, name, shape, dt=F32):
        self.uid = getattr(self, "uid", 0) + 1
        return es.enter_context(self.nc.sbuf_tensor(f"s{self.uid}_{name}", list(shape), dt))

    def evac_eng(self):
        self.ev += 1
        return "act" if self.ev % 2 else "dve"

    def copy(self, eng, out, in_, reads, writes):
        if eng == "act":
            return self.tr.op("act", lambda e: e.copy(out=out, in_=in_), reads=reads, writes=writes)
        return self.tr.op(eng, lambda e: e.tensor_copy(out=out, in_=in_), reads=reads, writes=writes)

    def dma(self, out, in_, reads=(), writes=(), q="sp"):
        return self.tr.op(q, lambda e: e.dma_start(out=out, in_=in_), reads=reads, writes=writes)

    def mm(self, out, lhsT, rhs, start, stop, reads, writes):
        return self.tr.op("pe", lambda e: e.matmul(out, lhsT, rhs, start=start, stop=stop), reads=reads, writes=writes)

    def rstd_from_ss(self, ss_ap, ss_key, r_ap, r_key, scale):
        tr = self.tr
        tr.op("act", lambda e: e.activation(out=r_ap, in_=ss_ap, func=AF.Sqrt, bias=self.cst[0:r_ap.shape[0], 0:1], scale=scale),
              reads=[ss_key, "cst"], writes=[r_key])
        tr.op("dve", lambda e: e.reciprocal(out=r_ap, in_=r_ap), reads=[r_key], writes=[r_key])

    def build(self):
        nc = self.nc
        self.declare()
        with ExitStack() as es:
            self.tr = tr = Trk(nc, es)
            self.ps = [es.enter_context(nc.psum_tensor(f"ps{i}", [128, 512], F32)) for i in range(8)]
            self.xT = self.sb(es, "xT", [128, 8, T])
            self.ident = self.sb(es, "ident", [128, 128])
            self.ones32 = self.sb(es, "ones32", [128, 128])
            self.ones16 = self.sb(es, "ones16", [128, 128], BF16)
            self.cst = self.sb(es, "cst", [128, 4])
            self.mod = self.sb(es, "mod", [128, self.n_layers, 48, 2])
            self.n1 = self.sb(es, "n1", [128, self.n_layers, 8])
            self.n2 = self.sb(es, "n2", [128, self.n_layers, 8])
            self.gs = self.sb(es, "gs", [128, 2, 8, 2])
            self.dma(self.ident[:], self.inp["ident"], writes=["ident"])
            self.dma(self.n1[:], self.inp["norm1_fm"], writes=["n1"])
            self.dma(self.n2[:], self.inp["norm2_fm"], writes=["n2"])
            tr.op("dve", lambda e: e.memset(self.ones32[:], 1.0), writes=["ones32"])
            tr.op("dve", lambda e: e.memset(self.ones16[:], 1.0), writes=["ones16"])
            tr.op("dve", lambda e: e.memset(self.cst[:, 0:1], EPS), writes=["cst"])
            tr.op("dve", lambda e: e.memset(self.cst[:, 1:2], 1.0), writes=["cst"])
            tr.op("dve", lambda e: e.memset(self.cst[:, 2:3], 0.0), writes=["cst"])
            self.phase_load_x()
            self.phase_mod()
            done = self.stop == "mod"
            for l in range(self.n_layers):
                if done:
                    break
                for name, fn in (("norm1", lambda: self.phase_adaln_proj(l)), ("mla", lambda: self.phase_mla(l)),
                                 ("swa", lambda: self.phase_swa(l)), ("gdnprep", lambda: self.phase_gdn_prep(l)),
                                 ("gdn", lambda: self.phase_gdn(l)), ("merge", lambda: self.phase_merge(l)),
                                 ("ffn", lambda: self.phase_ffn(l))):
                    fn()
                    if self.stop == f"{name}{l}" or (self.stop or "").startswith(name + "_"):
                        done = True
                        break
            if "D_x" in self.taps:
                tr.barrier()
                for ci, (t0, n) in enumerate(TCH):
                    self.dma(self.scr["D_x"].rearrange("(c p) t -> p c t", p=128)[:, :, t0:t0 + n], self.xT[:, :, t0:t0 + n])
                self.dma(self.scr["D_mod"], self.mod[:].rearrange("p l j s -> p (l j s)"))
            if not done:
                self.phase_final()
            tr.barrier()
            tr.finish("sp")
        return nc

    def phase_load_x(self):
        tr = self.tr
        with ExitStack() as es:
            ring = Ring(self, es, "xl", 2, [128, D])
            for t in range(NT):
                xt, xk = ring.next()
                self.dma(xt[:], self.inp["xin"][t * 128:(t + 1) * 128, :], writes=[xk])
                for half in range(2):
                    pk = ("ps", (2 * t + half) % 8)
                    pt = self.ps[(2 * t + half) % 8]
                    for j in range(4):
                        c = half * 4 + j
                        tr.op("pe", lambda e: e.transpose(pt[:, j * 128:(j + 1) * 128], xt[:, c * 128:(c + 1) * 128], self.ident[:]),
                              reads=[xk, "ident"], writes=[pk])
                    self.copy(self.evac_eng(), self.xT[:, half * 4:half * 4 + 4, t * 128:(t + 1) * 128],
                              pt[:].rearrange("p (c t) -> p c t", c=4), reads=[pk], writes=[("xT", tile_chunk(t))])
            tr.barrier()

    def phase_mod(self):
        tr = self.tr
        with ExitStack() as es:
            cs = self.sb(es, "cs", [128, 8, 2])
            bm = self.sb(es, "bm", [128, self.n_layers, 48])
            ring = Ring(self, es, "wm", 2, [128, 8, 512])
            self.dma(cs[:], self.inp["c_fm"], writes=["cs"])
            self.dma(bm[:], self.inp["b_mod_fm"], writes=["bm"])
            tr.op("act", lambda e: e.activation(out=cs[:], in_=cs[:], func=AF.Silu), reads=["cs"], writes=["cs"])
            for l in range(self.n_layers):
                wv = self.inp["w_mod"][l].rearrange("(c p) n -> p c n", p=128)
                pk = ("ps", l % 2)
                pt = self.ps[l % 2][:, 0:96].rearrange("p (j s) -> p j s", s=2)
                for g in range(12):
                    w, wk = ring.next()
                    self.dma(w[:], wv[:, :, g * 512:(g + 1) * 512], writes=[wk])
                    for m in range(4):
                        j = g * 4 + m
                        for c in range(8):
                            self.mm(pt[:, j, :], w[:, c, m * 128:(m + 1) * 128], cs[:, c, :], c == 0, c == 7,
                                    reads=[wk, "cs"], writes=[pk])
                for s in range(2):
                    tr.op("dve", lambda e: e.tensor_tensor(out=self.mod[:, l, :, s], in0=pt[:, :, s], in1=bm[:, l, :], op=ALU.add),
                          reads=[pk, "bm"], writes=["mod"])
            tr.barrier()

    def make_gs(self, l, which):
        nrm = self.n1 if which == 0 else self.n2
        sc0 = 8 + which * 24
        for s in range(2):
            self.tr.op("dve", lambda e: e.scalar_tensor_tensor(out=self.gs[:, which, :, s], in0=self.mod[:, l, sc0:sc0 + 8, s], scalar=1.0,
                                                             in1=nrm[:, l, :], op0=ALU.add, op1=ALU.mult),
                       reads=["mod", "n1", "n2"], writes=["gs"])

    def adaln(self, es, l, which, xn):
        tr = self.tr
        self.make_gs(l, which)
        sh0 = which * 24
        es = ExitStack()
        sqr = Ring(self, es, f"sq{which}", 2, [128, 512])
        rr = Ring(self, es, f"rr{which}", 2, [128, 512])
        tmpr = Ring(self, es, f"tm{which}", 2, [128, 512])
        for ci, (t0, n) in enumerate(TCH):
            s = 1 if ci == 0 else 0
            pk = ("ps", ci % 2)
            pt = self.ps[ci % 2]
            for c in range(8):
                sq, sk = sqr.next()
                tr.op("act", lambda e: e.activation(out=sq[:, 0:n], in_=self.xT[:, c, t0:t0 + n], func=AF.Square),
                      reads=[("xT", ci)], writes=[sk])
                self.mm(pt[:, 0:n], self.ones32[:], sq[:, 0:n], c == 0, c == 7, reads=[sk, "ones32"], writes=[pk])
            r, rk = rr.next()
            self.rstd_from_ss(pt[:, 0:n], pk, r[:, 0:n], rk, 1.0 / D)
            for c in range(8):
                tm, tk = tmpr.next()
                tr.op("dve", lambda e: e.scalar_tensor_tensor(out=tm[:, 0:n], in0=self.xT[:, c, t0:t0 + n], scalar=self.gs[:, which, c, s:s + 1],
                                                             in1=r[:, 0:n], op0=ALU.mult, op1=ALU.mult),
                      reads=[("xT", ci), "gs", rk], writes=[tk])
                tr.op("act", lambda e: e.activation(out=xn[:, c, t0:t0 + n], in_=tm[:, 0:n], func=AF.Identity,
                                                    bias=self.mod[:, l, sh0 + c, s:s + 1], scale=1.0),
                      reads=[tk, "mod"], writes=[("xn", ci)])
        keep = {k: v for k, v in tr.lastw.items() if isinstance(k, tuple) and k[0] == "xn"}
        tr.barrier()
        tr.lastw.update(keep)
        es.close()

    def phase_adaln_proj(self, l):
        tr = self.tr
        with ExitStack() as es:
            xn = self.sb(es, "xn", [128, 8, T], BF16)
            self.adaln(es, l, 0, xn)
            if "D_x" in self.taps:
                for ci, (t0, n) in enumerate(TCH):
                    self.dma(self.scr["D_xn"].rearrange("(c p) t -> p c t", p=128)[:, :, t0:t0 + n], xn[:, :, t0:t0 + n], reads=[("xn", ci)])
            wr = Ring(self, es, "wfm", 2, [128, 8, 512], BF16)
            st = Ring(self, es, "stg", 3, [128, 512])
            st16 = Ring(self, es, "stg16", 2, [128, 512], BF16)
            wv = self.inp["w_fm"][l].rearrange("(c p) n -> p c n", p=128)
            pi = 0
            for g in range(NFM // 512):
                w, wk = wr.next()
                self.dma(w[:], wv[:, :, g * 512:(g + 1) * 512], writes=[wk], q="gq")
                col = g * 512
                while col < (g + 1) * 512:
                    if col < KR:
                        mc = 128
                    elif col < SQ:
                        mc = 32
                    elif col < PAD:
                        mc = 64
                    elif col < GATE:
                        col += 64
                        continue
                    else:
                        mc = 128
                    lo = col - g * 512
                    for ci, (t0, n) in enumerate(TCH):
                        pk = ("ps", pi % 4)
                        pt = self.ps[pi % 4]
                        pi += 1
                        for c in range(8):
                            self.mm(pt[0:mc, 0:n], w[:, c, lo:lo + mc], xn[:, c, t0:t0 + n], c == 0, c == 7,
                                    reads=[wk, ("xn", ci)], writes=[pk])
                        if col >= GATE:
                            s16, sk = st16.next()
                            tr.op("act", lambda e: e.activation(out=s16[:, 0:n], in_=pt[:, 0:n], func=AF.Sigmoid), reads=[pk], writes=[sk])
                            self.dma(self.scr["D_gate"][col - GATE:col - GATE + 128, t0:t0 + n], s16[:, 0:n], reads=[sk], writes=[("D_gate", ci)])
                        else:
                            s32, sk = st.next()
                            self.copy(self.evac_eng(), s32[0:mc, 0:n], pt[0:mc, 0:n], reads=[pk], writes=[sk])
                            self.dma(self.scr["D_fm"][col:col + mc, t0:t0 + n], s32[0:mc, 0:n], reads=[sk], writes=["D_fm"])
                    col += mc
            wt = self.sb(es, "wtm", [128, 8, NTM], BF16)
            self.dma(wt[:], self.inp["w_tm"][l].rearrange("(c p) n -> p c n", p=128), writes=["wtm"], q="gq")
            stm = Ring(self, es, "stm", 2, [128, NTM])
            for t in range(NT):
                ci = tile_chunk(t)
                pa, pb = self.ps[4 + (t % 2) * 2], self.ps[5 + (t % 2) * 2]
                ka, kb = ("ps", 4 + (t % 2) * 2), ("ps", 5 + (t % 2) * 2)
                for c in range(8):
                    self.mm(pa[:, 0:512], xn[:, c, t * 128:(t + 1) * 128], wt[:, c, 0:512], c == 0, c == 7, reads=["wtm", ("xn", ci)], writes=[ka])
                for c in range(8):
                    self.mm(pb[:, 0:160], xn[:, c, t * 128:(t + 1) * 128], wt[:, c, 512:672], c == 0, c == 7, reads=["wtm", ("xn", ci)], writes=[kb])
                s, sk = stm.next()
                self.copy("act", s[:, 0:512], pa[:, 0:512], reads=[ka], writes=[sk])
                self.copy("dve", s[:, 512:672], pb[:, 0:160], reads=[kb], writes=[sk])
                self.dma(self.scr["D_tm"][t * 128:(t + 1) * 128, :], s[:], reads=[sk], writes=["D_tm"])
            tr.barrier()

    def rms_fm_from_dram(self, es, row0, nchunk, gain, dst, tag):
        tr = self.tr
        ld = Ring(self, es, f"ld{tag}", 2, [128, nchunk, 512])
        sqr = Ring(self, es, f"sqn{tag}", 2, [128, 512])
        rr = Ring(self, es, f"rn{tag}", 2, [128, 512])
        src = self.scr["D_fm"][row0:row0 + nchunk * 128, :].rearrange("(c p) t -> p c t", p=128)
        for ci, (t0, n) in enumerate(TCH):
            x, xk = ld.next()
            self.dma(x[:, :, 0:n], src[:, :, t0:t0 + n], reads=["D_fm"], writes=[xk])
            pk = ("ps", 6 + ci % 2)
            pt = self.ps[6 + ci % 2]
            for c in range(nchunk):
                sq, sk = sqr.next()
                tr.op("act", lambda e: e.activation(out=sq[:, 0:n], in_=x[:, c, 0:n], func=AF.Square), reads=[xk], writes=[sk])
                self.mm(pt[:, 0:n], self.ones32[:], sq[:, 0:n], c == 0, c == nchunk - 1, reads=[sk, "ones32"], writes=[pk])
            r, rk = rr.next()
            self.rstd_from_ss(pt[:, 0:n], pk, r[:, 0:n], rk, 1.0 / (nchunk * 128))
            for c in range(nchunk):
                tr.op("dve", lambda e: e.scalar_tensor_tensor(out=dst[:, c, t0:t0 + n], in0=x[:, c, 0:n], scalar=gain[:, c:c + 1],
                                                             in1=r[:, 0:n], op0=ALU.mult, op1=ALU.mult),
                      reads=[xk, rk, "gain" + tag], writes=[(tag, ci)])

    def rope_from(self, src_ap, src_key, srcp_ap, srcp_key, cos_ap, sin_ap, dst_ap, dst_key, tmpa, tmpb, ka, kb):
        tr = self.tr
        tr.op("dve", lambda e: e.tensor_tensor(out=tmpa, in0=src_ap, in1=cos_ap, op=ALU.mult), reads=[src_key, "rope"], writes=[ka])
        tr.op("pool" if srcp_key[0] != "ps" else "dve", lambda e: e.tensor_tensor(out=tmpb, in0=srcp_ap, in1=sin_ap, op=ALU.mult),
              reads=[srcp_key, "rope"], writes=[kb])
        tr.op("dve", lambda e: e.tensor_tensor(out=dst_ap, in0=tmpa, in1=tmpb, op=ALU.add), reads=[ka, kb], writes=[dst_key])

    def phase_mla(self, l):
        tr = self.tr
        scale = 96.0 ** -0.5
        with ExitStack() as es:
            qg = self.sb(es, "qg", [128, 3]); kg = self.sb(es, "kg", [128, 2])
            self.dma(qg[:], self.inp["qnorm_fm"][:, l, :], writes=["gaincqn"])
            self.dma(kg[:], self.inp["kvnorm_fm"][:, l, :], writes=["gainckvn"])
            cqn = self.sb(es, "cqn", [128, 3, T], BF16)
            ckvn = self.sb(es, "ckvn", [128, 2, T], BF16)
            cosM = self.sb(es, "cosM", [32, T]); sinM = self.sb(es, "sinM", [32, T])
            self.dma(cosM[:], self.inp["ropeM"][0], writes=["rope"])
            self.dma(sinM[:], self.inp["ropeM"][1], writes=["rope"])
            KrT = self.sb(es, "KrT", [128, T], BF16)
            tr.op("pool", lambda e: e.memset(KrT[32:64, :], 0.0), writes=[("KrT", ci) for ci in range(5)])
            tr.op("pool", lambda e: e.memset(KrT[64:128, :], 0.0), writes=[("KrT", ci) for ci in range(5)])
            Vt = self.sb(es, "Vt", [128, NT, 576], BF16)
            wqn = self.sb(es, "wqn", [128, 3, 512], BF16); wqr = self.sb(es, "wqr", [128, 3, 256], BF16)
            wqp = self.sb(es, "wqp", [128, 3, 256], BF16)
            wkn = self.sb(es, "wkn", [128, 2, 512], BF16); wkv = self.sb(es, "wkv", [128, 2, 512], BF16)
            for wt_, nm in ((wqn, "wuq_n"), (wqr, "wuq_r"), (wqp, "wuq_rp"), (wkn, "wukv_k"), (wkv, "wukv_v")):
                self.dma(wt_[:], self.inp[nm][l].rearrange("(c p) n -> p c n", p=128), writes=[nm], q="gq")
            with ExitStack() as es2:
                self.rms_fm_from_dram(es2, CQ, 3, qg, cqn, "cqn")
                self.rms_fm_from_dram(es2, CKV, 2, kg, ckvn, "ckvn")
                kl = Ring(self, es2, "krl", 2, [32, 2, 512])
                ta = Ring(self, es2, "kta", 2, [32, 512]); tb = Ring(self, es2, "ktb", 2, [32, 512])
                for ci, (t0, n) in enumerate(TCH):
                    k2, kk = kl.next()
                    self.dma(k2[:, :, 0:n], self.scr["D_fm"][KR:KR + 64, :].rearrange("(a p) t -> p a t", p=32)[:, :, t0:t0 + n],
                             reads=["D_fm"], writes=[kk])
                    a_, ak = ta.next(); b_, bk = tb.next()
                    self.rope_from(k2[:, 0, 0:n], kk, k2[:, 1, 0:n], kk, cosM[:, t0:t0 + n], sinM[:, t0:t0 + n],
                                   KrT[0:32, t0:t0 + n], ("KrT", ci), a_[:, 0:n], b_[:, 0:n], ak, bk)
                for t in range(NT):
                    tr.op("pool", lambda e: e.memset(Vt[:, t, 512:576], 0.0), writes=[("Vt", t)])
                for t in range(NT):
                    ci = tile_chunk(t)
                    pk = ("ps", 4 + t % 2); pt = self.ps[4 + t % 2]
                    for c in range(2):
                        self.mm(pt[:, :], ckvn[:, c, t * 128:(t + 1) * 128], wkv[:, c, :], c == 0, c == 1, reads=[("ckvn", ci), "wukv_v"], writes=[pk])
                    self.copy(self.evac_eng(), Vt[:, t, 0:512], pt[:, :], reads=[pk], writes=[("Vt", t)])
                tr.barrier()
            if self.stop == "mla_prep":
                return
            with ExitStack() as es2:
                qnr = Ring(self, es2, "QnT", 2, [128, T], BF16); qrr = Ring(self, es2, "QrT", 2, [128, T], BF16)
                knr = Ring(self, es2, "KnT", 2, [128, T], BF16)
                for rg, lo in ((qnr, 64), (qrr, 32), (knr, 64)):
                    for ti, tl in enumerate(rg.tiles):
                        for p0_, p1_ in (((32, 64), (64, 128)) if lo == 32 else ((64, 128),)):
                            tr.op("pool", lambda e: e.memset(tl[p0_:p1_, :], 0.0), writes=[((rg.name, ti), ci) for ci in range(5)])
                ta = Ring(self, es2, "qta", 2, [32, 512]); tb = Ring(self, es2, "qtb", 2, [32, 512])
                pr = Ring(self, es2, "Pexp", 4, [128, 512], BF16)
                rdr = Ring(self, es2, "rden", 2, [64, 512])
                yar = Ring(self, es2, "yah", 2, [64, T], BF16)
                for h in range(8):
                    Qn, qnk = qnr.next(); Qr, qrk = qrr.next(); Kn, knk = knr.next()
                    for ci, (t0, n) in enumerate(TCH):
                        p0, p1, p2, p3 = self.ps[4], self.ps[5], self.ps[6], self.ps[7]
                        for c in range(3):
                            self.mm(p0[0:64, 0:n], wqn[:, c, h * 64:(h + 1) * 64], cqn[:, c, t0:t0 + n], c == 0, c == 2, reads=["wuq_n", ("cqn", ci)], writes=[("ps", 4)])
                        for c in range(3):
                            self.mm(p1[0:32, 0:n], wqr[:, c, h * 32:(h + 1) * 32], cqn[:, c, t0:t0 + n], c == 0, c == 2, reads=["wuq_r", ("cqn", ci)], writes=[("ps", 5)])
                        for c in range(3):
                            self.mm(p2[0:32, 0:n], wqp[:, c, h * 32:(h + 1) * 32], cqn[:, c, t0:t0 + n], c == 0, c == 2, reads=["wuq_rp", ("cqn", ci)], writes=[("ps", 6)])
                        for c in range(2):
                            self.mm(p3[0:64, 0:n], wkn[:, c, h * 64:(h + 1) * 64], ckvn[:, c, t0:t0 + n], c == 0, c == 1, reads=["wukv_k", ("ckvn", ci)], writes=[("ps", 7)])
                        self.copy("act", Qn[0:64, t0:t0 + n], p0[0:64, 0:n], reads=[("ps", 4)], writes=[(qnk, ci)])
                        self.copy("act", Kn[0:64, t0:t0 + n], p3[0:64, 0:n], reads=[("ps", 7)], writes=[(knk, ci)])
                        a_, ak = ta.next(); b_, bk = tb.next()
                        self.rope_from(p1[0:32, 0:n], ("ps", 5), p2[0:32, 0:n], ("ps", 6), cosM[:, t0:t0 + n], sinM[:, t0:t0 + n],
                                       Qr[0:32, t0:t0 + n], (qrk, ci), a_[:, 0:n], b_[:, 0:n], ak, bk)
                    if self.stop == "mla_proj":
                        break
                    ya, yk = yar.next()
                    for ci, (t0, n) in enumerate(TCH):
                        if self.stop == "mla_ctx" and ci > 0:
                            break
                        kts = [0, 1] if ci == 0 else list(range(NT))
                        ob, db = (3, 2) if ci % 2 == 0 else (6, 5)
                        po, pd = self.ps[ob], self.ps[db]
                        def s_stage(i, kt):
                            kci = tile_chunk(kt)
                            bank = (0, 1, 4)[i % 3]
                            sk_ = ("ps", bank); psn = self.ps[bank]
                            self.mm(psn[:, 0:n], Kn[:, kt * 128:(kt + 1) * 128], Qn[:, t0:t0 + n], True, False,
                                    reads=[(knk, kci), (qnk, ci)], writes=[sk_])
                            self.mm(psn[:, 0:n], KrT[:, kt * 128:(kt + 1) * 128], Qr[:, t0:t0 + n], False, True,
                                    reads=[("KrT", kci), (qrk, ci)], writes=[sk_])
                            P, Pk = pr.next()
                            tr.op("act", lambda e: e.activation(out=P[:, 0:n], in_=psn[:, 0:n], func=AF.Exp, scale=scale), reads=[sk_], writes=[Pk])
                            return P, Pk

                        def pv_stage(i, kt, P, Pk):
                            self.mm(po[:, 0:n], Vt[:, kt, h * 64:h * 64 + 128], P[:, 0:n], i == 0, i == len(kts) - 1, reads=[("Vt", kt), Pk], writes=[("ps", ob)])
                            self.mm(pd[:, 0:n], self.ones16[:, :], P[:, 0:n], i == 0, i == len(kts) - 1, reads=["ones16", Pk], writes=[("ps", db)])

                        pend = []
                        for i, kt in enumerate(kts):
                            pend.append((i, kt) + s_stage(i, kt))
                            if len(pend) > 2:
                                pv_stage(*pend.pop(0))
                        while pend:
                            pv_stage(*pend.pop(0))
                        rd, rk = rdr.next()
                        tr.op("dve", lambda e: e.reciprocal(out=rd[:, 0:n], in_=pd[0:64, 0:n]), reads=[("ps", db)], writes=[rk])
                        tr.op("dve", lambda e: e.tensor_tensor(out=ya[:, t0:t0 + n], in0=po[0:64, 0:n], in1=rd[:, 0:n], op=ALU.mult),
                              reads=[("ps", ob), rk], writes=[yk])
                    self.dma(self.scr["D_ya"][h * 64:(h + 1) * 64, :], ya[:], reads=[yk], writes=["D_ya"])
                tr.barrier()

    def phase_swa(self, l):
        tr = self.tr
        with ExitStack() as es:
            QT = self.sb(es, "sQT", [128, 8, T], BF16)
            KT = self.sb(es, "sKT", [128, 2, T], BF16)
            tr.op("pool", lambda e: e.memset(QT[64:128, :, :], 0.0), writes=[("sQK", j) for j in range(8)])
            tr.op("pool", lambda e: e.memset(KT[64:128, :, :], 0.0), writes=[("sQK", 8), ("sQK", 9)])
            Vs = self.sb(es, "sVs", [128, NT, 128], BF16)
            cosS = self.sb(es, "cosS", [64, T]); sinS = self.sb(es, "sinS", [64, T])
            msk = self.sb(es, "smask", [128, 2, 512], BF16)
            snk = self.sb(es, "snk", [64, 8])
            self.dma(cosS[:], self.inp["ropeS"][0], writes=["rope"])
            self.dma(sinS[:], self.inp["ropeS"][1], writes=["rope"])
            self.dma(msk[:], self.inp["swamask"].rearrange("a p n -> p a n"), writes=["smask"], q="gq")
            self.dma(snk[:], self.inp["sink_rep"][:, l, :], writes=["snk"])
            tr.op("act", lambda e: e.activation(out=snk[:], in_=snk[:], func=AF.Exp), reads=["snk"], writes=["snk"])
            self.dma(Vs[:], self.scr["D_tm"][:, SV:SV + 128].rearrange("(t p) c -> p t c", p=128), reads=["D_tm"], writes=["sVs"], q="gq")
            with ExitStack() as es2:
                ld = Ring(self, es2, "sld", 1, [64, T]); ldp = Ring(self, es2, "sldp", 1, [64, T])
                ta = Ring(self, es2, "sta", 1, [64, T]); tb = Ring(self, es2, "stb", 1, [64, T])
                for j in range(10):
                    r0, rp = (SQ + j * 64, SQP + j * 64) if j < 8 else (SK + (j - 8) * 64, SKP + (j - 8) * 64)
                    dst = QT[0:64, j, :] if j < 8 else KT[0:64, j - 8, :]
                    x, xk = ld.next(); xp, xpk = ldp.next()
                    self.dma(x[:], self.scr["D_fm"][r0:r0 + 64, :], reads=["D_fm"], writes=[xk])
                    self.dma(xp[:], self.scr["D_fm"][rp:rp + 64, :], reads=["D_fm"], writes=[xpk])
                    a_, ak = ta.next(); b_, bk = tb.next()
                    self.rope_from(x[:], xk, xp[:], xpk, cosS[:], sinS[:], dst, ("sQK", j), a_[:], b_[:], ak, bk)
                tr.barrier()
            with ExitStack() as es2:
                pr = Ring(self, es2, "sP", 3, [128, 512], BF16)
                dn = Ring(self, es2, "sdn", 2, [64, 512])
                yr = Ring(self, es2, "syb", 2, [64, 4, 128], BF16)
                ybv = self.scr["D_yb"].rearrange("(h d) t -> d h t", d=64)
                it = 0
                for qt in range(NT):
                    if qt < 2:
                        kts = [(0, None), (1, None)]
                    else:
                        kts = [(0, None), (1, None)]
                        if qt > 2:
                            kts.append((qt - 1, 0))
                        kts.append((qt, None))
                        if qt < NT - 1:
                            kts.append((qt + 1, 1))
                    for g in range(2):
                        ob, db = (3, 2) if g == 0 else (6, 5)
                        po, pd = self.ps[ob], self.ps[db]
                        rhs = QT[:, 4 * g:4 * g + 4, qt * 128:(qt + 1) * 128]
                        def s_stage(i, kt, mk):
                            nonlocal it
                            sk_ = ("ps", it % 2); psn = self.ps[it % 2]; it += 1
                            self.mm(psn[:, :].rearrange("p (h q) -> p h q", h=4), KT[:, g, kt * 128:(kt + 1) * 128], rhs, True, True,
                                    reads=[("sQK", 8 + g)] + [("sQK", 4 * g + hh) for hh in range(4)], writes=[sk_])
                            P, Pk = pr.next()
                            tr.op("act", lambda e: e.activation(out=P[:], in_=psn[:, :], func=AF.Exp, scale=0.125), reads=[sk_], writes=[Pk])
                            if mk is not None:
                                tr.op("dve", lambda e: e.tensor_tensor(out=P[:], in0=P[:], in1=msk[:, mk, :], op=ALU.mult), reads=[Pk, "smask"], writes=[Pk])
                            return P, Pk

                        def pv_stage(i, kt, P, Pk):
                            self.mm(po[0:64, :], Vs[:, kt, g * 64:(g + 1) * 64], P[:], i == 0, i == len(kts) - 1, reads=["sVs", Pk], writes=[("ps", ob)])
                            self.mm(pd[:, :], self.ones16[:, :], P[:], i == 0, i == len(kts) - 1, reads=["ones16", Pk], writes=[("ps", db)])

                        prev = None
                        for i, (kt, mk) in enumerate(kts):
                            cur = (i, kt) + s_stage(i, kt, mk)
                            if prev is not None:
                                pv_stage(*prev)
                            prev = cur
                        pv_stage(*prev)
                        d_, dk = dn.next()
                        for hh in range(4):
                            tr.op("dve", lambda e: e.tensor_scalar(out=d_[:, hh * 128:(hh + 1) * 128], in0=pd[0:64, hh * 128:(hh + 1) * 128],
                                                                  scalar1=snk[:, 4 * g + hh:4 * g + hh + 1], scalar2=None, op0=ALU.add),
                                  reads=[("ps", db), "snk"], writes=[dk])
                        tr.op("dve", lambda e: e.reciprocal(out=d_[:], in_=d_[:]), reads=[dk], writes=[dk])
                        y, yk = yr.next()
                        tr.op("dve", lambda e: e.tensor_tensor(out=y[:].rearrange("p h q -> p (h q)"), in0=po[0:64, :], in1=d_[:], op=ALU.mult),
                              reads=[("ps", ob), dk], writes=[yk])
                        self.dma(ybv[:, 4 * g:4 * g + 4, qt * 128:(qt + 1) * 128], y[:], reads=[yk], writes=["D_yb"])
                tr.barrier()

    def phase_gdn_prep(self, l):
        tr = self.tr
        with ExitStack() as es:
            cw = self.sb(es, "gcw", [128, 12, 3])
            bd = self.sb(es, "gbd64", [128, 128])
            self.dma(cw[:], self.inp["gconv_fm"][:, l, :, :], writes=["gcw"])
            self.dma(bd[:], self.inp["bd64"], writes=["gbd64"])
            ld = Ring(self, es, "gld", 2, [128, T]); yr = Ring(self, es, "gy", 2, [128, T])
            sqr = Ring(self, es, "gsq", 2, [128, 512]); rr = Ring(self, es, "grn", 2, [128, 512])
            tmr = Ring(self, es, "gtm", 2, [128, NT, 128])
            for j in range(12):
                x, xk = ld.next(); y, yk = yr.next()
                self.dma(x[:], self.scr["D_fm"][GQ + j * 128:GQ + (j + 1) * 128, :], reads=["D_fm"], writes=[xk])
                tr.op("act", lambda e: e.activation(out=y[:], in_=x[:], func=AF.Identity, scale=cw[:, j, 1:2]), reads=[xk, "gcw"], writes=[yk])
                for (o0, o1, i0, i1, k) in ((1, CTX, 0, CTX - 1, 0), (CTX + 1, T, CTX, T - 1, 0), (0, CTX - 1, 1, CTX, 2), (CTX, T - 1, CTX + 1, T, 2)):
                    tr.op("dve", lambda e: e.scalar_tensor_tensor(out=y[:, o0:o1], in0=x[:, i0:i1], scalar=cw[:, j, k:k + 1], in1=y[:, o0:o1],
                                                                 op0=ALU.mult, op1=ALU.add), reads=[xk, yk, "gcw"], writes=[yk])
                tr.op("act", lambda e: e.activation(out=y[:], in_=y[:], func=AF.Silu), reads=[yk], writes=[yk])
                if j < 8:
                    for ci, (t0, n) in enumerate(TCH):
                        sq, sk = sqr.next()
                        tr.op("act", lambda e: e.activation(out=sq[:, 0:n], in_=y[:, t0:t0 + n], func=AF.Square), reads=[yk], writes=[sk])
                        pk = ("ps", ci % 2); pt = self.ps[ci % 2]
                        self.mm(pt[:, 0:n], bd[:], sq[:, 0:n], True, True, reads=[sk, "gbd64"], writes=[pk])
                        r, rk = rr.next()
                        self.rstd_from_ss(pt[:, 0:n], pk, r[:, 0:n], rk, 1.0)
                        tr.op("dve", lambda e: e.scalar_tensor_tensor(out=y[:, t0:t0 + n], in0=y[:, t0:t0 + n], scalar=(0.125 if j < 4 else 1.0),
                                                                     in1=r[:, 0:n], op0=ALU.mult, op1=ALU.mult), reads=[yk, rk], writes=[yk])
                    dst = self.scr["D_gq"] if j < 4 else self.scr["D_gk"]
                    self.dma(dst[(j % 4) * 128:(j % 4 + 1) * 128, :], y[:], reads=[yk], writes=["D_gqk"])
                if j >= 4:
                    tm, tk = tmr.next()
                    for t4 in range(0, NT, 4):
                        nt = min(4, NT - t4)
                        pk = ("ps", 2 + (t4 // 4) % 2); pt = self.ps[2 + (t4 // 4) % 2]
                        for tt in range(nt):
                            t = t4 + tt
                            tr.op("pe", lambda e: e.transpose(pt[:, tt * 128:(tt + 1) * 128], y[:, t * 128:(t + 1) * 128], self.ident[:]),
                                  reads=[yk, "ident"], writes=[pk])
                        self.copy(self.evac_eng(), tm[:, t4:t4 + nt, :], pt[:, 0:nt * 128].rearrange("p (t d) -> p t d", d=128), reads=[pk], writes=[tk])
                    dst = self.scr["D_ktm"] if j < 8 else self.scr["D_vtm"]
                    c = (j % 4) * 2
                    for hh in range(2):
                        self.dma(dst[c + hh].rearrange("(t p) d -> p t d", p=128), tm[:, :, hh * 64:(hh + 1) * 64], reads=[tk], writes=["D_kvtm"])
            tr.barrier()

    def phase_gdn(self, l):
        tr = self.tr
        with ExitStack() as es:
            oacc = self.sb(es, "oacc", [128, NT, 512])
            ab = self.sb(es, "gab", [128, NT, 32])
            g = self.sb(es, "gg", [128, NT, 16]); beta = self.sb(es, "gbeta", [128, NT, 16]); nbeta = self.sb(es, "gnbeta", [128, NT, 16])
            gam = self.sb(es, "ggam", [128, NT, 16]); eg = self.sb(es, "geg", [128, NT, 16]); bg = self.sb(es, "gbg", [128, NT, 16])
            edl = self.sb(es, "gedl", [128, NT, 16]); tmp = self.sb(es, "gtmp", [128, NT, 16])
            alog = self.sb(es, "galog", [128, 16]); dtb = self.sb(es, "gdtb", [128, 16])
            tri = self.sb(es, "gtri", [128, 2, 128]); gm = self.sb(es, "gmask", [128, 4, 128]); sel = self.sb(es, "gsel", [128, 8, 128])
            S = self.sb(es, "gS", [128, 2, 8, 64])
            self.dma(ab[:], self.scr["D_tm"][:, AB:AB + 32].rearrange("(t p) c -> p t c", p=128), reads=["D_tm"], writes=["gab"])
            self.dma(alog[:], self.inp["alog_rep"][:, l, :], writes=["galog"])
            self.dma(dtb[:], self.inp["dtb_rep"][:, l, :], writes=["gdtb"])
            self.dma(tri[:], self.inp["tri"].rearrange("a p n -> p a n"), writes=["gtri"])
            self.dma(gm[:], self.inp["gmask"].rearrange("a p n -> p a n"), writes=["gmaskc"])
            tr.op("pool", lambda e: e.memset(sel[:], 0.0), writes=["gsel"])
            self.dma(sel[0:8, :, :], self.inp["selh"].rearrange("h u n -> u h n"), writes=["gsel"])
            bmk = self.sb(es, "gbmask", [128, 3, 128])
            self.dma(bmk[:], self.inp["bmask"].rearrange("a p n -> p a n"), writes=["gbmask"])
            tr.op("dve", lambda e: e.memset(S[:], 0.0), writes=[(f"S{d}", h) for d in range(2) for h in range(8)])
            tr.op("act", lambda e: e.activation(out=alog[:], in_=alog[:], func=AF.Exp), reads=["galog"], writes=["galog"])
            for t in range(NT):
                tr.op("dve", lambda e: e.tensor_tensor(out=tmp[:, t, :], in0=ab[:, t, 0:16], in1=dtb[:], op=ALU.add), reads=["gab", "gdtb"], writes=["gtmp"])
            tr.op("act", lambda e: e.activation(out=tmp[:], in_=tmp[:], func=AF.Exp), reads=["gtmp"], writes=["gtmp"])
            tr.op("act", lambda e: e.activation(out=tmp[:], in_=tmp[:], func=AF.Ln, bias=self.cst[:, 1:2], scale=1.0), reads=["gtmp", "cst"], writes=["gtmp"])
            for t in range(NT):
                tr.op("dve", lambda e: e.scalar_tensor_tensor(out=g[:, t, :], in0=tmp[:, t, :], scalar=-1.0, in1=alog[:], op0=ALU.mult, op1=ALU.mult),
                      reads=["gtmp", "galog"], writes=["gg"])
            tr.op("act", lambda e: e.activation(out=beta[:], in_=ab[:, :, 16:32], func=AF.Sigmoid), reads=["gab"], writes=["gbeta"])
            tr.op("dve", lambda e: e.tensor_scalar(out=nbeta[:], in0=beta[:], scalar1=-1.0, scalar2=None, op0=ALU.mult), reads=["gbeta"], writes=["gnbeta"])
            pg = self.ps[0][:, 0:NT * 16].rearrange("p (t u) -> p t u", u=16)
            pt_ = self.ps[1][:, 0:NT * 16].rearrange("p (t u) -> p t u", u=16)
            for t in range(NT):
                for d in range(2):
                    self.mm(pg[:, t, d * 8:(d + 1) * 8], tri[:, d, :], g[:, t, d * 8:(d + 1) * 8], True, True, reads=["gtri", "gg"], writes=[("ps", 0)])
                self.mm(pt_[:, t, :], self.ones32[:], g[:, t, :], True, True, reads=["ones32", "gg"], writes=[("ps", 1)])
            self.copy("dve", gam[:], pg, reads=[("ps", 0)], writes=["ggam"])
            tr.op("dve", lambda e: e.tensor_tensor(out=edl[:], in0=pt_, in1=gam[:], op=ALU.subtract), reads=[("ps", 1), "ggam"], writes=["gedl"])
            tr.op("act", lambda e: e.activation(out=edl[:], in_=edl[:], func=AF.Exp), reads=["gedl"], writes=["gedl"])
            tr.op("act", lambda e: e.activation(out=eg[:], in_=gam[:], func=AF.Exp), reads=["ggam"], writes=["geg"])
            tr.op("dve", lambda e: e.tensor_tensor(out=bg[:], in0=beta[:], in1=eg[:], op=ALU.mult), reads=["gbeta", "geg"], writes=["gbg"])
            if self.stop == "gdn_gates":
                tr.barrier()
                return
            es_scan = ExitStack()
            NSLOT = 4
            R1 = lambda nm, shp, n=1: [Ring(self, es_scan, f"g{nm}{sl_}", n, shp) for sl_ in range(NSLOT)]
            qTr = R1("iq", [128, 128]); kTr = R1("ik", [128, 128]); ktmr = R1("iktm", [128, 64]); vtmr = R1("ivtm", [128, 64]); kdr = R1("kd", [128, 64])
            t1r = R1("t1", [128, 128]); t2r = R1("t2", [128, 128]); egr = R1("egb", [64, 128])
            a0r = R1("a0", [128, 256]); pxr_ = R1("px", [128, 256], 2); ptr__ = R1("pt", [128, 128], 2); lr = R1("l", [128, 2, 128])
            ybr_ = R1("yb", [128, 128], 5); qkr = R1("qk", [128, 128]); wtr = R1("wt", [128, 128]); qgr = R1("qg", [128, 128]); vnr = R1("vn", [128, 64])
            gTr = Ring(self, es_scan, "ggT", 3, [128, 128])
            for rg in qTr + kTr + wtr + qgr + [gTr]:
                for ti, tl in enumerate(rg.tiles):
                    tr.op("pool", lambda e: e.memset(tl[:], 0.0), writes=[(rg.name, ti)])
            gqv = self.scr["D_gq"].rearrange("(h d) t -> h d t", d=64)
            gkv = self.scr["D_gk"].rearrange("(h d) t -> h d t", d=64)

            def unit(d, t, h, s_, gT, gTk):
                u = d * 8 + h
                Sk = f"S{d}"
                last = 127 if d == 0 else 0
                sl = slice(t * 128, (t + 1) * 128)
                pA, pAk, pB, pBk = self.ps[2 * s_], ("ps", 2 * s_), self.ps[2 * s_ + 1], ("ps", 2 * s_ + 1)
                qT, qk_ = qTr[s_].next(); kT, kk_ = kTr[s_].next(); ktm, ktk = ktmr[s_].next(); vtm, vtk = vtmr[s_].next()
                self.dma(qT[0:64, :], gqv[h, :, sl], reads=["D_gqk"], writes=[qk_])
                self.dma(kT[0:64, :], gkv[h, :, sl], reads=["D_gqk"], writes=[kk_])
                self.dma(ktm[:], self.scr["D_ktm"][h, sl, :], reads=["D_kvtm"], writes=[ktk])
                self.dma(vtm[:], self.scr["D_vtm"][h, sl, :], reads=["D_kvtm"], writes=[vtk])
                yield
                self.mm(pA[:, 0:128], sel[:, h, :], gT[:], True, True, reads=["gsel", gTk], writes=[pAk])
                self.mm(pB[:, 0:128], kT[:], kT[:], True, True, reads=[kk_], writes=[pBk])
                kd, kdk = kdr[s_].next()
                tr.op("act", lambda e: e.activation(out=kd[:], in_=ktm[:], func=AF.Identity, scale=edl[:, t, u:u + 1]), reads=[ktk, "gedl"], writes=[kdk])
                a0, a0k = a0r[s_].next()
                tr.op("act", lambda e: e.activation(out=a0[:, 128:192], in_=vtm[:], func=AF.Identity, scale=beta[:, t, u:u + 1]),
                      reads=[vtk, "gbeta"], writes=[(a0k, "x")])
                tr.op("act", lambda e: e.activation(out=a0[:, 192:256], in_=ktm[:], func=AF.Identity, scale=bg[:, t, u:u + 1]),
                      reads=[ktk, "gbg"], writes=[(a0k, "x")])
                yield
                t1, t1k = t1r[s_].next(); t2, t2k = t2r[s_].next(); egb, egk = egr[s_].next()
                tr.op("dve", lambda e: e.scalar_tensor_tensor(out=t1[:], in0=pA[:, 0:128], scalar=gam[:, t, u:u + 1], in1=gm[:, 2 * d, :],
                                                             op0=ALU.subtract, op1=ALU.add), reads=[pAk, "ggam", "gmaskc"], writes=[t1k])
                tr.op("dve", lambda e: e.scalar_tensor_tensor(out=t2[:], in0=pA[:, 0:128], scalar=gam[:, t, u:u + 1], in1=gm[:, 2 * d + 1, :],
                                                             op0=ALU.subtract, op1=ALU.add), reads=[pAk, "ggam", "gmaskc"], writes=[t2k])
                tr.op("dve", lambda e: e.tensor_copy(out=egb[:], in_=pA[0:64, 0:128]), reads=[pAk], writes=[egk])
                yield
                tr.op("act", lambda e: e.activation(out=t1[:], in_=t1[:], func=AF.Exp, scale=-1.0), reads=[t1k], writes=[t1k])
                tr.op("act", lambda e: e.activation(out=t2[:], in_=t2[:], func=AF.Exp), reads=[t2k], writes=[t2k])
                tr.op("act", lambda e: e.activation(out=egb[:], in_=egb[:], func=AF.Exp), reads=[egk], writes=[egk])
                self.mm(pA[:, 0:128], kT[:], qT[:], True, True, reads=[kk_, qk_], writes=[pAk])
                yield
                tr.op("dve", lambda e: e.scalar_tensor_tensor(out=a0[:, 0:128], in0=pB[:, 0:128], scalar=nbeta[:, t, u:u + 1], in1=t1[:],
                                                             op0=ALU.mult, op1=ALU.mult), reads=[pBk, "gnbeta", t1k], writes=[(a0k, "p")])
                qk, qkk = qkr[s_].next()
                tr.op("dve", lambda e: e.tensor_tensor(out=qk[:], in0=pA[:, 0:128], in1=t2[:], op=ALU.mult), reads=[pAk, t2k], writes=[qkk])
                qg, qgk = qgr[s_].next()
                tr.op("pool", lambda e: e.tensor_tensor(out=qg[0:64, :], in0=qT[0:64, :], in1=egb[:], op=ALU.mult), reads=[qk_, egk], writes=[qgk])
                yield
                tr.op("pe", lambda e: e.transpose(pB[:, 0:128], a0[:, 0:128], self.ident[:]), reads=[(a0k, "p"), "ident"], writes=[pBk])
                px, pxk = pxr_[s_].next(); pT, pTk = ptr__[s_].next(); lt, ltk = lr[s_].next()
                tr.op("pool", lambda e: e.tensor_tensor(out=pT[:], in0=a0[:, 0:128], in1=bmk[:, 0, :], op=ALU.mult), reads=[(a0k, "p"), "gbmask"], writes=[pTk])
                self.copy("act", px[:, 128:256], self.ident[:], reads=["ident"], writes=[(pxk, "x")])
                yield
                tr.op("dve", lambda e: e.tensor_tensor(out=px[:, 0:128], in0=pB[:, 0:128], in1=bmk[:, 0, :], op=ALU.mult), reads=[pBk, "gbmask"], writes=[(pxk, "p")])
                tr.op("dve", lambda e: e.tensor_tensor(out=lt[:, 0, :], in0=pB[:, 0:128], in1=bmk[:, 1, :], op=ALU.mult), reads=[pBk, "gbmask"], writes=[(ltk, 0)])
                tr.op("dve", lambda e: e.tensor_tensor(out=lt[:, 1, :], in0=pB[:, 0:128], in1=bmk[:, 2, :], op=ALU.mult), reads=[pBk, "gbmask"], writes=[(ltk, 1)])
                yield
                for lev in range(5):
                    if lev < 4:
                        self.mm(pA[:, 0:256], pT[:].bitcast(F32R) if self.use_f32r else pT[:], px[:, 0:256].bitcast(F32R) if self.use_f32r else px[:, 0:256],
                                True, True, reads=[pTk, (pxk, "p"), (pxk, "x")], writes=[pAk])
                        self.mm(pB[:, 0:128], px[:, 0:128], pT[:], True, True, reads=[pTk, (pxk, "p")], writes=[pBk])
                        yield
                        nx, nxk = pxr_[s_].next(); nT, nTk = ptr__[s_].next()
                        self.copy("act", nT[:], pB[:, 0:128], reads=[pBk], writes=[nTk])
                        tr.op("dve", lambda e: e.tensor_tensor(out=nx[:, 128:256], in0=pA[:, 128:256], in1=px[:, 128:256], op=ALU.add),
                              reads=[pAk, (pxk, "x")], writes=[(nxk, "x")])
                        self.copy("dve", nx[:, 0:128], pA[:, 0:128], reads=[pAk], writes=[(nxk, "p")])
                        px, pxk, pT, pTk = nx, nxk, nT, nTk
                        yield
                    else:
                        self.mm(pA[:, 0:128], pT[:], px[:, 128:256], True, True, reads=[pTk, (pxk, "x")], writes=[pAk])
                        yield
                        nx, nxk = pxr_[s_].next()
                        tr.op("dve", lambda e: e.tensor_tensor(out=nx[:, 128:256], in0=pA[:, 0:128], in1=px[:, 128:256], op=ALU.add),
                              reads=[pAk, (pxk, "x")], writes=[(nxk, "x")])
                        px, pxk = nx, nxk
                        yield
                MT, MTk = px[:, 128:256], (pxk, "x")
                cnt = [0]

                def apply(lhsT_ap, lhs_key, rhs_ap, rhs_key, add_ap=None, add_key=None):
                    pb, pbk = (pA, pAk) if cnt[0] % 2 == 0 else (pB, pBk)
                    eng = "act" if cnt[0] % 2 == 0 else "dve"
                    cnt[0] += 1
                    self.mm(pb[:, 0:128], lhsT_ap, rhs_ap, True, True, reads=[lhs_key, rhs_key], writes=[pbk])
                    yield
                    yb, ybk = ybr_[s_].next()
                    if add_ap is None:
                        self.copy(eng, yb[:], pb[:, 0:128], reads=[pbk], writes=[ybk])
                    else:
                        tr.op("dve", lambda e: e.tensor_tensor(out=yb[:], in0=pb[:, 0:128], in1=add_ap, op=ALU.add), reads=[pbk, add_key], writes=[ybk])
                    yield
                    return yb, ybk

                def s64(z_ap, z_key):
                    y1, y1k = yield from apply(MT, MTk, z_ap, z_key)
                    z1, z1k = yield from apply(lt[:, 0, :], (ltk, 0), y1[:], y1k)
                    r_ = yield from apply(MT, MTk, z1[:], z1k, add_ap=y1[:], add_key=y1k)
                    return r_

                y2, y2k = yield from s64(a0[:, 128:256], (a0k, "x"))
                z2, z2k = yield from apply(lt[:, 1, :], (ltk, 1), y2[:], y2k)
                y4, y4k = yield from s64(z2[:], z2k)
                px, pxk = pxr_[s_].next()
                tr.op("pool", lambda e: e.tensor_tensor(out=px[:, 128:256], in0=y2[:], in1=y4[:], op=ALU.add), reads=[y2k, y4k], writes=[(pxk, "x")])
                yield
                tr.op("pe", lambda e: e.transpose(pA[0:64, 0:128], px[:, 192:256], self.ident[:]), reads=[(pxk, "x"), "ident"], writes=[pAk])
                yield
                wt_, wtk = wtr[s_].next()
                self.copy("act", wt_[0:64, :], pA[0:64, 0:128], reads=[pAk], writes=[wtk])
                yield
                self.mm(pB[:, 0:64], wt_[:], S[:, d, h, :], True, True, reads=[wtk, (Sk, h)], writes=[pBk])
                yield
                vn, vnk = vnr[s_].next()
                tr.op("dve", lambda e: e.tensor_tensor(out=vn[:], in0=px[:, 128:192], in1=pB[:, 0:64], op=ALU.subtract),
                      reads=[(pxk, "x"), pBk], writes=[vnk])
                yield
                self.mm(pA[:, 0:64], qg[:], S[:, d, h, :], True, False, reads=[qgk, (Sk, h)], writes=[pAk])
                self.mm(pA[:, 0:64], qk[:], vn[:], False, True, reads=[qkk, vnk], writes=[pAk])
                self.mm(pB[0:64, 0:64], kd[:], vn[:], True, True, reads=[kdk, vnk], writes=[pBk])
                yield
                if d == 0:
                    self.copy("act", oacc[:, t, h * 64:(h + 1) * 64], pA[:, 0:64], reads=[pAk], writes=[("oacc", t, h)])
                else:
                    tr.op("dve", lambda e: e.tensor_tensor(out=oacc[:, t, h * 64:(h + 1) * 64], in0=pA[:, 0:64], in1=oacc[:, t, h * 64:(h + 1) * 64], op=ALU.add),
                          reads=[pAk, ("oacc", t, h)], writes=[("oacc", t, h)])
                tr.op("dve", lambda e: e.scalar_tensor_tensor(out=S[0:64, d, h, :], in0=S[0:64, d, h, :], scalar=egb[:, last:last + 1], in1=pB[0:64, 0:64],
                                                             op0=ALU.mult, op1=ALU.add), reads=[(Sk, h), egk, pBk], writes=[(Sk, h)])

            def unit_stream():
                for d in range(2):
                    order = list(range(NT)) if d == 0 else [1, 0] + list(range(NT - 1, 1, -1))
                    for t in order:
                        gT, gTk = gTr.next()
                        pk = ("ps", 7)
                        self.mm(self.ps[7][0:8, 256:384], g[:, t, d * 8:(d + 1) * 8], tri[:, d, :], True, True, reads=["gg", "gtri"], writes=[pk])
                        self.copy("act", gT[0:8, :], self.ps[7][0:8, 256:384], reads=[pk], writes=[gTk])
                        for h in range(8):
                            yield (d, t, h, gT, gTk)

            stream = unit_stream()
            active = {}
            free_slots = list(range(NSLOT))
            exhausted = False
            while True:
                while free_slots and not exhausted:
                    try:
                        d_, t_, h_, gT_, gTk_ = next(stream)
                    except StopIteration:
                        exhausted = True
                        break
                    s_ = free_slots.pop(0)
                    active[s_] = unit(d_, t_, h_, s_, gT_, gTk_)
                if not active:
                    break
                for s_ in sorted(active):
                    try:
                        next(active[s_])
                    except StopIteration:
                        del active[s_]
                        free_slots.append(s_)
            if (self.stop or "").startswith("gdn_u1"):
                tr.barrier()
                es_scan.close()
                return
            tr.barrier()
            es_scan.close()
            for t in range(NT):
                for h in range(8):
                    tr.lastw[("oacc", t, h)] = None
            tr.lastw = {k: v for k, v in tr.lastw.items() if v is not None}
            if "D_x" in self.taps:
                self.dma(self.scr["D_oc"].rearrange("(t p) c -> p t c", p=128), oacc[:], reads=[("oacc", t, h) for t in range(NT) for h in range(8)])
            gn = self.sb(es, "gn", [128, 64])
            self.dma(gn[:], self.inp["gnorm_rep"][:, l, :], writes=["gn"])
            zr = Ring(self, es, "gz", 2, [128, 512]); sq2 = Ring(self, es, "gsq2", 2, [128, 512]); ssr = Ring(self, es, "gss", 2, [128, 8])
            ycr = Ring(self, es, "gyc", 2, [128, 4, 128], BF16)
            ycv = self.scr["D_yc"].rearrange("(c p) t -> p c t", p=128)
            for t in range(NT):
                z, zk = zr.next()
                self.dma(z[:], self.scr["D_tm"][t * 128:(t + 1) * 128, ZC:ZC + 512], reads=["D_tm"], writes=[zk])
                tr.op("act", lambda e: e.activation(out=z[:], in_=z[:], func=AF.Silu), reads=[zk], writes=[zk])
                ok = [("oacc", t, h) for h in range(8)]
                sq, sk = sq2.next()
                tr.op("pool", lambda e: e.tensor_tensor(out=sq[:], in0=oacc[:, t, :], in1=oacc[:, t, :], op=ALU.mult), reads=ok, writes=[sk])
                ss, ssk = ssr.next()
                tr.op("dve", lambda e: e.tensor_reduce(out=ss[:], in_=sq[:].rearrange("p (h d) -> p h d", d=64), axis=AX.X, op=ALU.add), reads=[sk], writes=[ssk])
                self.rstd_from_ss(ss[:], ssk, ss[:], ssk, 1.0 / 64)
                o3 = oacc[:, t, :].rearrange("p (h d) -> p h d", d=64)
                tr.op("dve", lambda e: e.tensor_tensor(out=sq[:].rearrange("p (h d) -> p h d", d=64), in0=o3, in1=ss[:].unsqueeze(2).to_broadcast([128, 8, 64]), op=ALU.mult),
                      reads=ok + [ssk], writes=[sk])
                tr.op("dve", lambda e: e.tensor_tensor(out=sq[:].rearrange("p (h d) -> p h d", d=64), in0=sq[:].rearrange("p (h d) -> p h d", d=64),
                                                      in1=gn[:].unsqueeze(1).to_broadcast([128, 8, 64]), op=ALU.mult), reads=[sk, "gn"], writes=[sk])
                tr.op("dve", lambda e: e.tensor_tensor(out=sq[:], in0=sq[:], in1=z[:], op=ALU.mult), reads=[sk, zk], writes=[sk])
                pk = ("ps", 4 + t % 2); pt = self.ps[4 + t % 2]
                for c in range(4):
                    tr.op("pe", lambda e: e.transpose(pt[:, c * 128:(c + 1) * 128], sq[:, c * 128:(c + 1) * 128], self.ident[:]), reads=[sk, "ident"], writes=[pk])
                yc, yk = ycr.next()
                self.copy("act", yc[:], pt[:].rearrange("p (c t) -> p c t", c=4), reads=[pk], writes=[yk])
                self.dma(ycv[:, :, t * 128:(t + 1) * 128], yc[:], reads=[yk], writes=["D_yc"])
            tr.barrier()

    def phase_merge(self, l):
        tr = self.tr
        with ExitStack() as es:
            wp = [self.sb(es, f"wp{i}", [128, 4, D], BF16) for i in range(3)]
            wo = self.sb(es, "wo", [128, 8, D], BF16)
            for i, nm in enumerate(("w_pa", "w_pb", "w_pc")):
                self.dma(wp[i][:], self.inp[nm][l].rearrange("(c p) n -> p c n", p=128), writes=[nm], q="gq")
            self.dma(wo[:], self.inp["w_out"][l].rearrange("(c p) n -> p c n", p=128), writes=["w_out"], q="gq")
            gr = Ring(self, es, "mg", 1, [128, 24, 512], BF16)
            yr = [Ring(self, es, f"my{i}", 2, [128, 4, 512], BF16) for i in range(3)]
            mixr = Ring(self, es, "mix", 2, [128, 512]); tmr = Ring(self, es, "mtm", 2, [128, 512])
            mT = Ring(self, es, "mixT", 1, [128, 8, 512], BF16)
            gv = self.scr["D_gate"].rearrange("(c p) t -> p c t", p=128)
            yv = [self.scr[nm].rearrange("(c p) t -> p c t", p=128) for nm in ("D_ya", "D_yb", "D_yc")]
            pi = 0
            for ci, (t0, n) in enumerate(TCH):
                s = 1 if ci == 0 else 0
                gt, gk = gr.next()
                self.dma(gt[:, :, 0:n], gv[:, :, t0:t0 + n], reads=[("D_gate", ci)], writes=[gk])
                ys = []
                for i in range(3):
                    y, yk = yr[i].next()
                    self.dma(y[:, :, 0:n], yv[i][:, :, t0:t0 + n], reads=["D_ya", "D_yb", "D_yc"], writes=[yk])
                    ys.append((y, yk))
                mt, mtk = mT.next()
                for m in range(8):
                    mix, mk = mixr.next()
                    for i in range(3):
                        pk = ("ps", pi % 4); pt = self.ps[pi % 4]; pi += 1
                        y, yk = ys[i]
                        for c in range(4):
                            self.mm(pt[:, 0:n], wp[i][:, c, m * 128:(m + 1) * 128], y[:, c, 0:n], c == 0, c == 3, reads=[("w_pa", "w_pb", "w_pc")[i], yk], writes=[pk])
                        if i == 0:
                            tr.op("dve", lambda e: e.tensor_tensor(out=mix[:, 0:n], in0=pt[:, 0:n], in1=gt[:, m, 0:n], op=ALU.mult), reads=[pk, gk], writes=[mk])
                        else:
                            tm, tk = tmr.next()
                            tr.op("dve", lambda e: e.tensor_tensor(out=tm[:, 0:n], in0=pt[:, 0:n], in1=gt[:, i * 8 + m, 0:n], op=ALU.mult), reads=[pk, gk], writes=[tk])
                            if i == 1:
                                tr.op("pool", lambda e: e.tensor_tensor(out=mix[:, 0:n], in0=mix[:, 0:n], in1=tm[:, 0:n], op=ALU.add), reads=[mk, tk], writes=[mk])
                            else:
                                tr.op("pool", lambda e: e.tensor_tensor(out=mt[:, m, 0:n], in0=mix[:, 0:n], in1=tm[:, 0:n], op=ALU.add), reads=[mk, tk], writes=[(mtk, m)])
                for m in range(8):
                    pk = ("ps", 4 + m % 4); pt = self.ps[4 + m % 4]
                    for c in range(8):
                        self.mm(pt[:, 0:n], wo[:, c, m * 128:(m + 1) * 128], mt[:, c, 0:n], c == 0, c == 7, reads=["w_out", (mtk, c)], writes=[pk])
                    tr.op("dve", lambda e: e.scalar_tensor_tensor(out=self.xT[:, m, t0:t0 + n], in0=pt[:, 0:n], scalar=self.mod[:, l, 16 + m, s:s + 1],
                                                                 in1=self.xT[:, m, t0:t0 + n], op0=ALU.mult, op1=ALU.add),
                          reads=[pk, "mod", ("xT", ci)], writes=[("xT", ci)])
            tr.barrier()

    def phase_ffn(self, l):
        tr = self.tr
        with ExitStack() as es:
            fcw = self.sb(es, "fcw", [128, 44, 3])
            self.dma(fcw[:], self.inp["fconv_fm"][:, l, :, :], writes=["fcw"])
            with ExitStack() as es2:
                xn = self.sb(es2, "xn2", [128, 8, T], BF16)
                self.adaln(es2, l, 1, xn)
                wr = Ring(self, es2, "wup", 2, [128, 8, 2, 128], BF16)
                hr = [Ring(self, es2, f"fh{i}", 2, [128, T]) for i in range(2)]
                cr = [Ring(self, es2, f"fc{i}", 2, [128, T]) for i in range(2)]
                ar = Ring(self, es2, "fact", 2, [128, T], BF16)
                wv = self.inp["ffn_up"][l].rearrange("(c p) n -> p c n", p=128)
                pi = 0
                for cc in range(22):
                    w, wk = wr.next()
                    self.dma(w[:, :, 0, :], wv[:, :, cc * 128:(cc + 1) * 128], writes=[wk], q="gq")
                    self.dma(w[:, :, 1, :], wv[:, :, 2816 + cc * 128:2816 + (cc + 1) * 128], writes=[wk], q="gq")
                    cs_ = []
                    for i in range(2):
                        hb, hk = hr[i].next()
                        for ci, (t0, n) in enumerate(TCH):
                            pk = ("ps", pi % 4); pt = self.ps[pi % 4]; pi += 1
                            for c in range(8):
                                self.mm(pt[:, 0:n], w[:, c, i, :], xn[:, c, t0:t0 + n], c == 0, c == 7, reads=[wk, ("xn", ci)], writes=[pk])
                            self.copy(self.evac_eng(), hb[:, t0:t0 + n], pt[:, 0:n], reads=[pk], writes=[hk])
                        cb, ck = cr[i].next()
                        j = cc + 22 * i
                        tr.op("act", lambda e: e.activation(out=cb[:], in_=hb[:], func=AF.Identity, scale=fcw[:, j, 1:2]), reads=[hk, "fcw"], writes=[ck])
                        for (o0, o1, i0, i1, k) in ((1, CTX, 0, CTX - 1, 0), (CTX + 1, T, CTX, T - 1, 0), (0, CTX - 1, 1, CTX, 2), (CTX, T - 1, CTX + 1, T, 2)):
                            tr.op("dve", lambda e: e.scalar_tensor_tensor(out=cb[:, o0:o1], in0=hb[:, i0:i1], scalar=fcw[:, j, k:k + 1], in1=cb[:, o0:o1],
                                                                       op0=ALU.mult, op1=ALU.add), reads=[hk, ck, "fcw"], writes=[ck])
                        cs_.append((cb, ck))
                    (ca, cak), (cb, cbk) = cs_
                    tr.op("act", lambda e: e.activation(out=ca[:], in_=ca[:], func=AF.Silu), reads=[cak], writes=[cak])
                    a, ak = ar.next()
                    tr.op("dve", lambda e: e.tensor_tensor(out=a[:], in0=ca[:], in1=cb[:], op=ALU.mult), reads=[cak, cbk], writes=[ak])
                    self.dma(self.scr["D_act"][cc * 128:(cc + 1) * 128, :], a[:], reads=[ak], writes=["D_act"])
                tr.barrier()
            with ExitStack() as es2:
                wd = self.sb(es2, "wdn", [128, 22, D], BF16)
                self.dma(wd[:, 0:11, :], self.inp["ffn_down"][l].rearrange("(c p) n -> p c n", p=128)[:, 0:11, :], writes=["wdn"], q="gq")
                self.dma(wd[:, 11:22, :], self.inp["ffn_down"][l].rearrange("(c p) n -> p c n", p=128)[:, 11:22, :], writes=["wdn"], q="gq")
                ar = Ring(self, es2, "fa", 2, [128, 22, 512], BF16)
                av = self.scr["D_act"].rearrange("(c p) t -> p c t", p=128)
                for ci, (t0, n) in enumerate(TCH):
                    s = 1 if ci == 0 else 0
                    a, ak = ar.next()
                    self.dma(a[:, :, 0:n], av[:, :, t0:t0 + n], reads=["D_act"], writes=[ak])
                    for m in range(8):
                        pk = ("ps", m % 4); pt = self.ps[m % 4]
                        for c in range(22):
                            self.mm(pt[:, 0:n], wd[:, c, m * 128:(m + 1) * 128], a[:, c, 0:n], c == 0, c == 21, reads=["wdn", ak], writes=[pk])
                        tr.op("dve", lambda e: e.scalar_tensor_tensor(out=self.xT[:, m, t0:t0 + n], in0=pt[:, 0:n], scalar=self.mod[:, l, 40 + m, s:s + 1],
                                                                     in1=self.xT[:, m, t0:t0 + n], op0=ALU.mult, op1=ALU.add),
                              reads=[pk, "mod", ("xT", ci)], writes=[("xT", ci)])
                tr.barrier()

    def phase_final(self):
        tr = self.tr
        with ExitStack() as es:
            nf = self.sb(es, "nf", [128, D])
            self.dma(nf[:], self.inp["normf_rep"], writes=["nf"])
            xr = Ring(self, es, "fx", 2, [128, D]); sqr = Ring(self, es, "fsq", 2, [128, D]); ssr = Ring(self, es, "fss", 2, [128, 1])
            for t in range(2, NT):
                x, xk = xr.next()
                for half in range(2):
                    pk = ("ps", (2 * t + half) % 8); pt = self.ps[(2 * t + half) % 8]
                    for j in range(4):
                        c = half * 4 + j
                        tr.op("pe", lambda e: e.transpose(pt[:, j * 128:(j + 1) * 128], self.xT[:, c, t * 128:(t + 1) * 128], self.ident[:]),
                              reads=["ident"], writes=[pk])
                    self.copy(self.evac_eng(), x[:, half * 512:(half + 1) * 512], pt[:, :], reads=[pk], writes=[xk])
                sq, sk = sqr.next(); ss, ssk = ssr.next()
                tr.op("pool", lambda e: e.tensor_tensor(out=sq[:], in0=x[:], in1=x[:], op=ALU.mult), reads=[xk], writes=[sk])
                tr.op("dve", lambda e: e.tensor_reduce(out=ss[:], in_=sq[:], axis=AX.X, op=ALU.add), reads=[sk], writes=[ssk])
                self.rstd_from_ss(ss[:], ssk, ss[:], ssk, 1.0 / D)
                tr.op("dve", lambda e: e.scalar_tensor_tensor(out=sq[:], in0=x[:], scalar=ss[:, 0:1], in1=nf[:], op0=ALU.mult, op1=ALU.mult),
                      reads=[xk, ssk, "nf"], writes=[sk])
                self.dma(self.out[(t - 2) * 128:(t - 1) * 128, :], sq[:], reads=[sk])


def _rope_tables(d):
    da = d // 2
    half = da // 2
    inv = (10000.0 ** (-np.arange(half, dtype=np.float32) / half)).astype(np.float32)
    t = np.arange(SEQ)
    rows = (t // 64).astype(np.float32)
    cols = (t % 64).astype(np.float32)
    cos = np.ones((d, T), np.float32)
    sin = np.zeros((d, T), np.float32)
    for i in range(d):
        pos = rows if i < da else cols
        ii = i % da
        ang = pos * inv[ii % half]
        cos[i, CTX:] = np.cos(ang)
        sin[i, CTX:] = np.sin(ang) * (-1.0 if ii < half else 1.0)
    return np.stack([cos, sin])


def _partner(d):
    da = d // 2
    half = da // 2
    idx = np.arange(d)
    ii = idx % da
    return np.where(ii < half, idx + half, idx - half)


def _constants():
    j = np.arange(128)[:, None]
    i = np.arange(128)[None, :]
    tri = np.stack([(j <= i), (j >= i)]).astype(np.float32)
    gmask = np.stack([np.where(i < j, 0.0, BIG),
                      np.where(j <= i, 0.0, -BIG),
                      np.where(i > j, 0.0, BIG),
                      np.where(j >= i, 0.0, -BIG)]).astype(np.float32)
    m_prev = (j >= i).astype(np.float32)
    m_next = (j <= i).astype(np.float32)
    swamask = np.stack([np.tile(m_prev, (1, 4)), np.tile(m_next, (1, 4))]).astype(np.float32)
    b32 = (j // 32 == i // 32)
    b64 = (j // 64 == i // 64)
    bmask = np.stack([b32, b64 & ~b32, ~b64]).astype(np.float32)
    selh = np.zeros((8, 8, 128), np.float32)
    for h in range(8):
        selh[h, h, :] = 1.0
    return dict(ident=np.eye(128, dtype=np.float32), tri=tri, gmask=gmask, bmask=bmask, bd64=b64.astype(np.float32), swamask=swamask, selh=selh,
                ropeM=_rope_tables(32), ropeS=_rope_tables(64))


def _fm(v, p=128):
    v = np.asarray(v, np.float32)
    lead = v.shape[:-1]
    n = v.shape[-1] // p
    return np.ascontiguousarray(np.moveaxis(v.reshape(lead + (n, p)), -1, 0))


def prepare_shared(inp, L=DEPTH):
    f = lambda k: np.asarray(inp[k], np.float32)[:L] if k not in ("norm_f",) else np.asarray(inp[k], np.float32)
    w_in = f("w_in")
    p32 = _partner(32)
    p64 = _partner(64)
    sqp = (np.arange(8)[:, None] * 64 + p64[None, :]).reshape(-1)
    skp = (np.arange(2)[:, None] * 64 + p64[None, :]).reshape(-1)
    w_fm = np.concatenate([w_in[:, :, 0:640], w_in[:, :, 640:672], w_in[:, :, 640:672][:, :, p32],
                           w_in[:, :, 672:1184], w_in[:, :, 672:1184][:, :, sqp],
                           w_in[:, :, 1184:1312], w_in[:, :, 1184:1312][:, :, skp],
                           w_in[:, :, 1440:2976], np.zeros((L, D, 64), np.float32), w_in[:, :, 3520:6592]], axis=2)
    assert w_fm.shape[2] == NFM
    w_tm = np.concatenate([w_in[:, :, 1312:1440], w_in[:, :, 2976:3488], w_in[:, :, 3488:3520]], axis=2)
    wuq = f("w_uq").reshape(L, 384, 8, 96)
    wukv = f("w_ukv").reshape(L, 256, 8, 128)
    sh = dict(
        w_mod=f("w_mod"), b_mod_fm=_fm(f("b_mod")),
        norm1_fm=_fm(f("norm1")), norm2_fm=_fm(f("norm2")), normf_rep=np.ascontiguousarray(np.broadcast_to(f("norm_f"), (128, D))),
        w_fm=np.ascontiguousarray(w_fm), w_tm=np.ascontiguousarray(w_tm),
        qnorm_fm=_fm(f("mla_q_norm")), kvnorm_fm=_fm(f("mla_kv_norm")),
        wuq_n=np.ascontiguousarray(wuq[..., 0:64].reshape(L, 384, 512)),
        wuq_r=np.ascontiguousarray(wuq[..., 64:96].reshape(L, 384, 256)),
        wuq_rp=np.ascontiguousarray(wuq[..., 64:96][..., p32].reshape(L, 384, 256)),
        wukv_k=np.ascontiguousarray(wukv[..., 0:64].reshape(L, 256, 512)),
        wukv_v=np.ascontiguousarray(wukv[..., 64:128].reshape(L, 256, 512)),
        sink_rep=np.ascontiguousarray(np.broadcast_to(f("swa_sink"), (64, L, 8))),
        gconv_fm=np.ascontiguousarray(f("gdn_conv").reshape(L, 3, 12, 128).transpose(3, 0, 2, 1)),
        alog_rep=np.ascontiguousarray(np.broadcast_to(f("gdn_a_log").reshape(L, 16), (128, L, 16))),
        dtb_rep=np.ascontiguousarray(np.broadcast_to(f("gdn_dt_bias").reshape(L, 16), (128, L, 16))),
        gnorm_rep=np.ascontiguousarray(np.broadcast_to(f("gdn_norm"), (128, L, 64))),
        w_pa=f("w_branch_a"), w_pb=f("w_branch_b"), w_pc=f("w_branch_c"), w_out=f("w_out"),
        ffn_up=f("ffn_up"), fconv_fm=np.ascontiguousarray(f("ffn_conv").reshape(L, 3, 44, 128).transpose(3, 0, 2, 1)),
        ffn_down=f("ffn_down"),
    )
    sh.update(_constants())
    return sh


def per_core(inp, b):
    x = np.asarray(inp["x"], np.float32)[b]
    ctx = np.asarray(inp["ctx"], np.float32)[b]
    c = np.asarray(inp["c"], np.float32)[b]
    cc = np.asarray(inp["c_ctx"], np.float32)
    return dict(xin=np.ascontiguousarray(np.concatenate([ctx, x], 0)),
                c_fm=np.ascontiguousarray(np.stack([_fm(c), _fm(cc)], axis=-1)))


_CACHE = {}


def kernel(**inputs):
    if "nc" not in _CACHE:
        _CACHE["nc"] = Builder().build()
    nc = _CACHE["nc"]
    shared = prepare_shared(inputs)
    in_maps = []
    for b in range(8):
        m = dict(shared)
        m.update(per_core(inputs, b))
        in_maps.append(m)
    res = run_bass_kernel_spmd(nc, in_maps, core_ids=list(range(8)))
    return np.stack([np.asarray(r["out"], np.float32) for r in res.results], axis=0)
```

```python
import numpy as np
from contextlib import ExitStack
import concourse.bass as bass
import concourse.mybir as mybir
from concourse.bass_utils import run_bass_kernel_spmd

F32 = mybir.dt.float32
BF16 = mybir.dt.bfloat16
F32R = mybir.dt.float32r
USE_F32R = False
AF = mybir.ActivationFunctionType
ALU = mybir.AluOpType
AX = mybir.AxisListType

D = 1024
SEQ = 2048
CTX = 256
T = SEQ + CTX
NT = T // 128
DEPTH = 4
EPS = 1e-6
TCH = [(0, 256), (256, 512), (768, 512), (1280, 512), (1792, 512)]
BIG = 1.0e5

CQ, CKV, KR, KRP, SQ, SQP, SK, SKP, GQ, PAD, GATE, NFM = 0, 384, 640, 672, 704, 1216, 1728, 1856, 1984, 3520, 3584, 6656
SV, ZC, AB, NTM = 0, 128, 640, 672


def tile_chunk(t):
    return 0 if t < 2 else 1 + (t - 2) // 4


class Trk:
    ISSUER = {"pe": "pe", "act": "act", "dve": "dve", "pool": "pool", "sp": "sp", "gq": "pool"}
    NSLOT = {"sp": 24, "gq": 8}

    def __init__(self, nc, es):
        self.nc = nc
        self.eng = {"pe": nc.tensor, "act": nc.scalar, "dve": nc.vector, "pool": nc.gpsimd, "sp": nc.sync}
        self.sem = {e: es.enter_context(nc.semaphore("sem_" + e)) for e in ("pe", "act", "dve", "pool")}
        for q, k in self.NSLOT.items():
            for i in range(k):
                self.sem[(q, i)] = es.enter_context(nc.semaphore(f"sem_{q}{i}"))
        self.cnt = {e: 0 for e in self.sem}
        self.nq = {q: 0 for q in self.NSLOT}
        self.lastw = {}
        self.readers = {}
        self.waited = {}
        self.n_wait = 0
        self.n_op = 0

    def _need(self, issuer, eng, dep):
        de, ds = dep
        if de == "pe" and eng == "pe":
            return
        k = (issuer, de)
        if self.waited.get(k, 0) >= ds:
            return
        self.waited[k] = ds
        self.eng[issuer].wait_ge(self.sem[de], ds)
        self.n_wait += 1

    def op(self, eng, emit, reads=(), writes=()):
        issuer = self.ISSUER[eng]
        for key in reads:
            w = self.lastw.get(key)
            if w is not None:
                self._need(issuer, eng, w)
            if isinstance(key, tuple) and key[0] == "ps":
                for rk, tok in self.readers.get(key, {}).items():
                    if rk != eng:
                        self._need(issuer, eng, tok)
        for key in writes:
            w = self.lastw.get(key)
            if w is not None:
                self._need(issuer, eng, w)
            for tok in self.readers.get(key, {}).values():
                self._need(issuer, eng, tok)
        if eng in self.NSLOT:
            sk = (eng, self.nq[eng] % self.NSLOT[eng])
            self.nq[eng] += 1
            if self.cnt[sk] > 0:
                self._need(issuer, eng, (sk, self.cnt[sk]))
            inc = 16
        else:
            sk = eng
            inc = 1
        inst = emit(self.eng[issuer])
        self.cnt[sk] += inc
        inst.then_inc(self.sem[sk], inc)
        tok = (sk, self.cnt[sk])
        for key in reads:
            self.readers.setdefault(key, {})[sk] = tok
        for key in writes:
            self.lastw[key] = tok
            self.readers[key] = {}
        self.n_op += 1
        return inst

    def barrier(self):
        for issuer in ("pe", "act", "dve", "pool", "sp"):
            for e, c in self.cnt.items():
                if c > 0:
                    self._need(issuer, "x", (e, c))
        self.lastw.clear()
        self.readers.clear()

    def finish(self, eng="sp"):
        for e, c in self.cnt.items():
            if c > 0:
                self._need(eng, "x", (e, c))


class Ring:
    def __init__(self, b, es, name, n, shape, dt=F32):
        self.tiles = [b.sb(es, f"{name}{i}", shape, dt) for i in range(n)]
        self.name = name
        self.n = n
        self.i = 0

    def next(self):
        j = self.i % self.n
        self.i += 1
        return self.tiles[j], (self.name, j)


class Builder:
    def __init__(self, n_layers=DEPTH, stop=None, taps=()):
        self.n_layers = n_layers
        self.stop = stop
        self.taps = set(taps)
        self.nc = bass.Bass("TRN2", target_bir_lowering=False)
        self.inp = {}
        self.scr = {}
        self.ev = 0
        self.use_f32r = USE_F32R

    def din(self, name, shape, dt=F32):
        self.inp[name] = self.nc.dram_tensor(name, list(shape), dt, kind="ExternalInput").ap()
        return self.inp[name]

    def dscr(self, name, shape, dt=F32, force=False):
        kind = "ExternalOutput" if (name in self.taps or force) else "Internal"
        self.scr[name] = self.nc.dram_tensor(name, list(shape), dt, kind=kind).ap()
        return self.scr[name]

    def declare(self):
        L = self.n_layers
        d = self.din
        d("xin", [T, D]); d("c_fm", [128, 8, 2])
        d("w_mod", [L, D, 6 * D]); d("b_mod_fm", [128, L, 48])
        d("norm1_fm", [128, L, 8]); d("norm2_fm", [128, L, 8]); d("normf_rep", [128, D])
        d("w_fm", [L, D, NFM]); d("w_tm", [L, D, NTM])
        d("qnorm_fm", [128, L, 3]); d("kvnorm_fm", [128, L, 2])
        d("wuq_n", [L, 384, 512]); d("wuq_r", [L, 384, 256]); d("wuq_rp", [L, 384, 256])
        d("wukv_k", [L, 256, 512]); d("wukv_v", [L, 256, 512])
        d("sink_rep", [64, L, 8])
        d("gconv_fm", [128, L, 12, 3]); d("bd64", [128, 128]); d("alog_rep", [128, L, 16]); d("dtb_rep", [128, L, 16]); d("gnorm_rep", [128, L, 64])
        d("w_pa", [L, 512, D]); d("w_pb", [L, 512, D]); d("w_pc", [L, 512, D]); d("w_out", [L, D, D])
        d("ffn_up", [L, D, 5632]); d("fconv_fm", [128, L, 44, 3]); d("ffn_down", [L, 2816, D])
        d("ident", [128, 128]); d("ropeM", [2, 32, T]); d("ropeS", [2, 64, T])
        d("tri", [2, 128, 128]); d("gmask", [4, 128, 128]); d("bmask", [3, 128, 128]); d("swamask", [2, 128, 512]); d("selh", [8, 8, 128])
        self.out = self.nc.dram_tensor("out", [SEQ, D], F32, kind="ExternalOutput").ap()
        s = self.dscr
        s("D_fm", [NFM, T]); s("D_tm", [T, NTM]); s("D_gate", [3072, T], BF16)
        s("D_ya", [512, T], BF16); s("D_yb", [512, T], BF16); s("D_yc", [512, T], BF16)
        s("D_gq", [512, T]); s("D_gk", [512, T]); s("D_ktm", [8, T, 64]); s("D_vtm", [8, T, 64])
        s("D_act", [2816, T], BF16)
        if "D_x" in self.taps:
            s("D_x", [D, T], F32, True); s("D_xn", [D, T], BF16, True); s("D_mod", [128, L * 48 * 2], F32, True); s("D_oc", [T, 512], F32, True)

    def sb(self, es, name, shape, dt=F32):
        self.uid = getattr(self, "uid", 0) + 1
        return es.enter_context(self.nc.sbuf_tensor(f"s{self.uid}_{name}", list(shape), dt))

    def evac_eng(self):
        self.ev += 1
        return "act" if self.ev % 2 else "dve"

    def copy(self, eng, out, in_, reads, writes):
        if eng == "act":
            return self.tr.op("act", lambda e: e.copy(out=out, in_=in_), reads=reads, writes=writes)
        return self.tr.op(eng, lambda e: e.tensor_copy(out=out, in_=in_), reads=reads, writes=writes)

    def dma(self, out, in_, reads=(), writes=(), q="sp"):
        return self.tr.op(q, lambda e: e.dma_start(out=out, in_=in_), reads=reads, writes=writes)

    def mm(self, out, lhsT, rhs, start, stop, reads, writes):
        return self.tr.op("pe", lambda e: e.matmul(out, lhsT, rhs, start=start, stop=stop), reads=reads, writes=writes)

    def rstd_from_ss(self, ss_ap, ss_key, r_ap, r_key, scale):
        tr = self.tr
        tr.op("act", lambda e: e.activation(out=r_ap, in_=ss_ap, func=AF.Sqrt, bias=self.cst[0:r_ap.shape[0], 0:1], scale=scale),
              reads=[ss_key, "cst"], writes=[r_key])
        tr.op("dve", lambda e: e.reciprocal(out=r_ap, in_=r_ap), reads=[r_key], writes=[r_key])

    def build(self):
        nc = self.nc
        self.declare()
        with ExitStack() as es:
            self.tr = tr = Trk(nc, es)
            self.ps = [es.enter_context(nc.psum_tensor(f"ps{i}", [128, 512], F32)) for i in range(8)]
            self.xT = self.sb(es, "xT", [128, 8, T])
            self.ident = self.sb(es, "ident", [128, 128])
            self.ones32 = self.sb(es, "ones32", [128, 128])
            self.ones16 = self.sb(es, "ones16", [128, 128], BF16)
            self.cst = self.sb(es, "cst", [128, 4])
            self.mod = self.sb(es, "mod", [128, self.n_layers, 48, 2])
            self.n1 = self.sb(es, "n1", [128, self.n_layers, 8])
            self.n2 = self.sb(es, "n2", [128, self.n_layers, 8])
            self.gs = self.sb(es, "gs", [128, 2, 8, 2])
            self.dma(self.ident[:], self.inp["ident"], writes=["ident"])
            self.dma(self.n1[:], self.inp["norm1_fm"], writes=["n1"])
            self.dma(self.n2[:], self.inp["norm2_fm"], writes=["n2"])
            tr.op("dve", lambda e: e.memset(self.ones32[:], 1.0), writes=["ones32"])
            tr.op("dve", lambda e: e.memset(self.ones16[:], 1.0), writes=["ones16"])
            tr.op("dve", lambda e: e.memset(self.cst[:, 0:1], EPS), writes=["cst"])
            tr.op("dve", lambda e: e.memset(self.cst[:, 1:2], 1.0), writes=["cst"])
            tr.op("dve", lambda e: e.memset(self.cst[:, 2:3], 0.0), writes=["cst"])
            self.phase_load_x()
            self.phase_mod()
            done = self.stop == "mod"
            for l in range(self.n_layers):
                if done:
                    break
                for name, fn in (("norm1", lambda: self.phase_adaln_proj(l)), ("mla", lambda: self.phase_mla(l)),
                                 ("swa", lambda: self.phase_swa(l)), ("gdnprep", lambda: self.phase_gdn_prep(l)),
                                 ("gdn", lambda: self.phase_gdn(l)), ("merge", lambda: self.phase_merge(l)),
                                 ("ffn", lambda: self.phase_ffn(l))):
                    fn()
                    if self.stop == f"{name}{l}" or (self.stop or "").startswith(name + "_"):
                        done = True
                        break
            if "D_x" in self.taps:
                tr.barrier()
                for ci, (t0, n) in enumerate(TCH):
                    self.dma(self.scr["D_x"].rearrange("(c p) t -> p c t", p=128)[:, :, t0:t0 + n], self.xT[:, :, t0:t0 + n])
                self.dma(self.scr["D_mod"], self.mod[:].rearrange("p l j s -> p (l j s)"))
            if not done:
                self.phase_final()
            tr.barrier()
            tr.finish("sp")
        return nc

    def phase_load_x(self):
        tr = self.tr
        with ExitStack() as es:
            ring = Ring(self, es, "xl", 2, [128, D])
            for t in range(NT):
                xt, xk = ring.next()
                self.dma(xt[:], self.inp["xin"][t * 128:(t + 1) * 128, :], writes=[xk])
                for half in range(2):
                    pk = ("ps", (2 * t + half) % 8)
                    pt = self.ps[(2 * t + half) % 8]
                    for j in range(4):
                        c = half * 4 + j
                        tr.op("pe", lambda e: e.transpose(pt[:, j * 128:(j + 1) * 128], xt[:, c * 128:(c + 1) * 128], self.ident[:]),
                              reads=[xk, "ident"], writes=[pk])
                    self.copy(self.evac_eng(), self.xT[:, half * 4:half * 4 + 4, t * 128:(t + 1) * 128],
                              pt[:].rearrange("p (c t) -> p c t", c=4), reads=[pk], writes=[("xT", tile_chunk(t))])
            tr.barrier()

    def phase_mod(self):
        tr = self.tr
        with ExitStack() as es:
            cs = self.sb(es, "cs", [128, 8, 2])
            bm = self.sb(es, "bm", [128, self.n_layers, 48])
            ring = Ring(self, es, "wm", 2, [128, 8, 512])
            self.dma(cs[:], self.inp["c_fm"], writes=["cs"])
            self.dma(bm[:], self.inp["b_mod_fm"], writes=["bm"])
            tr.op("act", lambda e: e.activation(out=cs[:], in_=cs[:], func=AF.Silu), reads=["cs"], writes=["cs"])
            for l in range(self.n_layers):
                wv = self.inp["w_mod"][l].rearrange("(c p) n -> p c n", p=128)
                pk = ("ps", l % 2)
                pt = self.ps[l % 2][:, 0:96].rearrange("p (j s) -> p j s", s=2)
                for g in range(12):
                    w, wk = ring.next()
                    self.dma(w[:], wv[:, :, g * 512:(g + 1) * 512], writes=[wk])
                    for m in range(4):
                        j = g * 4 + m
                        for c in range(8):
                            self.mm(pt[:, j, :], w[:, c, m * 128:(m + 1) * 128], cs[:, c, :], c == 0, c == 7,
                                    reads=[wk, "cs"], writes=[pk])
                for s in range(2):
                    tr.op("dve", lambda e: e.tensor_tensor(out=self.mod[:, l, :, s], in0=pt[:, :, s], in1=bm[:, l, :], op=ALU.add),
                          reads=[pk, "bm"], writes=["mod"])
            tr.barrier()

    def make_gs(self, l, which):
        nrm = self.n1 if which == 0 else self.n2
        sc0 = 8 + which * 24
        for s in range(2):
            self.tr.op("dve", lambda e: e.scalar_tensor_tensor(out=self.gs[:, which, :, s], in0=self.mod[:, l, sc0:sc0 + 8, s], scalar=1.0,
                                                             in1=nrm[:, l, :], op0=ALU.add, op1=ALU.mult),
                       reads=["mod", "n1", "n2"], writes=["gs"])

    def adaln(self, es, l, which, xn):
        tr = self.tr
        self.make_gs(l, which)
        sh0 = which * 24
        es = ExitStack()
        sqr = Ring(self, es, f"sq{which}", 2, [128, 512])
        rr = Ring(self, es, f"rr{which}", 2, [128, 512])
        tmpr = Ring(self, es, f"tm{which}", 2, [128, 512])
        for ci, (t0, n) in enumerate(TCH):
            s = 1 if ci == 0 else 0
            pk = ("ps", ci % 2)
            pt = self.ps[ci % 2]
            for c in range(8):
                sq, sk = sqr.next()
                tr.op("act", lambda e: e.activation(out=sq[:, 0:n], in_=self.xT[:, c, t0:t0 + n], func=AF.Square),
                      reads=[("xT", ci)], writes=[sk])
                self.mm(pt[:, 0:n], self.ones32[:], sq[:, 0:n], c == 0, c == 7, reads=[sk, "ones32"], writes=[pk])
            r, rk = rr.next()
            self.rstd_from_ss(pt[:, 0:n], pk, r[:, 0:n], rk, 1.0 / D)
            for c in range(8):
                tm, tk = tmpr.next()
                tr.op("dve", lambda e: e.scalar_tensor_tensor(out=tm[:, 0:n], in0=self.xT[:, c, t0:t0 + n], scalar=self.gs[:, which, c, s:s + 1],
                                                             in1=r[:, 0:n], op0=ALU.mult, op1=ALU.mult),
                      reads=[("xT", ci), "gs", rk], writes=[tk])
                tr.op("act", lambda e: e.activation(out=xn[:, c, t0:t0 + n], in_=tm[:, 0:n], func=AF.Identity,
                                                    bias=self.mod[:, l, sh0 + c, s:s + 1], scale=1.0),
                      reads=[tk, "mod"], writes=[("xn", ci)])
        keep = {k: v for k, v in tr.lastw.items() if isinstance(k, tuple) and k[0] == "xn"}
        tr.barrier()
        tr.lastw.update(keep)
        es.close()

    def phase_adaln_proj(self, l):
        tr = self.tr
        with ExitStack() as es:
            xn = self.sb(es, "xn", [128, 8, T], BF16)
            self.adaln(es, l, 0, xn)
            if "D_x" in self.taps:
                for ci, (t0, n) in enumerate(TCH):
                    self.dma(self.scr["D_xn"].rearrange("(c p) t -> p c t", p=128)[:, :, t0:t0 + n], xn[:, :, t0:t0 + n], reads=[("xn", ci)])
            wr = Ring(self, es, "wfm", 2, [128, 8, 512], BF16)
            st = Ring(self, es, "stg", 3, [128, 512])
            st16 = Ring(self, es, "stg16", 2, [128, 512], BF16)
            wv = self.inp["w_fm"][l].rearrange("(c p) n -> p c n", p=128)
            pi = 0
            for g in range(NFM // 512):
                w, wk = wr.next()
                self.dma(w[:], wv[:, :, g * 512:(g + 1) * 512], writes=[wk], q="gq")
                col = g * 512
                while col < (g + 1) * 512:
                    if col < KR:
                        mc = 128
                    elif col < SQ:
                        mc = 32
                    elif col < PAD:
                        mc = 64
                    elif col < GATE:
                        col += 64
                        continue
                    else:
                        mc = 128
                    lo = col - g * 512
                    for ci, (t0, n) in enumerate(TCH):
                        pk = ("ps", pi % 4)
                        pt = self.ps[pi % 4]
                        pi += 1
                        for c in range(8):
                            self.mm(pt[0:mc, 0:n], w[:, c, lo:lo + mc], xn[:, c, t0:t0 + n], c == 0, c == 7,
                                    reads=[wk, ("xn", ci)], writes=[pk])
                        if col >= GATE:
                            s16, sk = st16.next()
                            tr.op("act", lambda e: e.activation(out=s16[:, 0:n], in_=pt[:, 0:n], func=AF.Sigmoid), reads=[pk], writes=[sk])
                            self.dma(self.scr["D_gate"][col - GATE:col - GATE + 128, t0:t0 + n], s16[:, 0:n], reads=[sk], writes=[("D_gate", ci)])
                        else:
                            s32, sk = st.next()
                            self.copy(self.evac_eng(), s32[0:mc, 0:n], pt[0:mc, 0:n], reads=[pk], writes=[sk])
                            self.dma(self.scr["D_fm"][col:col + mc, t0:t0 + n], s32[0:mc, 0:n], reads=[sk], writes=["D_fm"])
                    col += mc
            wt = self.sb(es, "wtm", [128, 8, NTM], BF16)
            self.dma(wt[:], self.inp["w_tm"][l].rearrange("(c p) n -> p c n", p=128), writes=["wtm"], q="gq")
            stm = Ring(self, es, "stm", 2, [128, NTM])
            for t in range(NT):
                ci = tile_chunk(t)
                pa, pb = self.ps[4 + (t % 2) * 2], self.ps[5 + (t % 2) * 2]
                ka, kb = ("ps", 4 + (t % 2) * 2), ("ps", 5 + (t % 2) * 2)
                for c in range(8):
                    self.mm(pa[:, 0:512], xn[:, c, t * 128:(t + 1) * 128], wt[:, c, 0:512], c == 0, c == 7, reads=["wtm", ("xn", ci)], writes=[ka])
                for c in range(8):
                    self.mm(pb[:, 0:160], xn[:, c, t * 128:(t + 1) * 128], wt[:, c, 512:672], c == 0, c == 7, reads=["wtm", ("xn", ci)], writes=[kb])
                s, sk = stm.next()
                self.copy("act", s[:, 0:512], pa[:, 0:512], reads=[ka], writes=[sk])
                self.copy("dve", s[:, 512:672], pb[:, 0:160], reads=[kb], writes=[sk])
                self.dma(self.scr["D_tm"][t * 128:(t + 1) * 128, :], s[:], reads=[sk], writes=["D_tm"])
            tr.barrier()

    def rms_fm_from_dram(self, es, row0, nchunk, gain, dst, tag):
        tr = self.tr
        ld = Ring(self, es, f"ld{tag}", 2, [128, nchunk, 512])
        sqr = Ring(self, es, f"sqn{tag}", 2, [128, 512])
        rr = Ring(self, es, f"rn{tag}", 2, [128, 512])
        src = self.scr["D_fm"][row0:row0 + nchunk * 128, :].rearrange("(c p) t -> p c t", p=128)
        for ci, (t0, n) in enumerate(TCH):
            x, xk = ld.next()
            self.dma(x[:, :, 0:n], src[:, :, t0:t0 + n], reads=["D_fm"], writes=[xk])
            pk = ("ps", 6 + ci % 2)
            pt = self.ps[6 + ci % 2]
            for c in range(nchunk):
                sq, sk = sqr.next()
                tr.op("act", lambda e: e.activation(out=sq[:, 0:n], in_=x[:, c, 0:n], func=AF.Square), reads=[xk], writes=[sk])
                self.mm(pt[:, 0:n], self.ones32[:], sq[:, 0:n], c == 0, c == nchunk - 1, reads=[sk, "ones32"], writes=[pk])
            r, rk = rr.next()
            self.rstd_from_ss(pt[:, 0:n], pk, r[:, 0:n], rk, 1.0 / (nchunk * 128))
            for c in range(nchunk):
                tr.op("dve", lambda e: e.scalar_tensor_tensor(out=dst[:, c, t0:t0 + n], in0=x[:, c, 0:n], scalar=gain[:, c:c + 1],
                                                             in1=r[:, 0:n], op0=ALU.mult, op1=ALU.mult),
                      reads=[xk, rk, "gain" + tag], writes=[(tag, ci)])

    def rope_from(self, src_ap, src_key, srcp_ap, srcp_key, cos_ap, sin_ap, dst_ap, dst_key, tmpa, tmpb, ka, kb):
        tr = self.tr
        tr.op("dve", lambda e: e.tensor_tensor(out=tmpa, in0=src_ap, in1=cos_ap, op=ALU.mult), reads=[src_key, "rope"], writes=[ka])
        tr.op("pool" if srcp_key[0] != "ps" else "dve", lambda e: e.tensor_tensor(out=tmpb, in0=srcp_ap, in1=sin_ap, op=ALU.mult),
              reads=[srcp_key, "rope"], writes=[kb])
        tr.op("dve", lambda e: e.tensor_tensor(out=dst_ap, in0=tmpa, in1=tmpb, op=ALU.add), reads=[ka, kb], writes=[dst_key])

    def phase_mla(self, l):
        tr = self.tr
        scale = 96.0 ** -0.5
        with ExitStack() as es:
            qg = self.sb(es, "qg", [128, 3]); kg = self.sb(es, "kg", [128, 2])
            self.dma(qg[:], self.inp["qnorm_fm"][:, l, :], writes=["gaincqn"])
            self.dma(kg[:], self.inp["kvnorm_fm"][:, l, :], writes=["gainckvn"])
            cqn = self.sb(es, "cqn", [128, 3, T], BF16)
            ckvn = self.sb(es, "ckvn", [128, 2, T], BF16)
            cosM = self.sb(es, "cosM", [32, T]); sinM = self.sb(es, "sinM", [32, T])
            self.dma(cosM[:], self.inp["ropeM"][0], writes=["rope"])
            self.dma(sinM[:], self.inp["ropeM"][1], writes=["rope"])
            KrT = self.sb(es, "KrT", [128, T], BF16)
            tr.op("pool", lambda e: e.memset(KrT[32:64, :], 0.0), writes=[("KrT", ci) for ci in range(5)])
            tr.op("pool", lambda e: e.memset(KrT[64:128, :], 0.0), writes=[("KrT", ci) for ci in range(5)])
            Vt = self.sb(es, "Vt", [128, NT, 576], BF16)
            wqn = self.sb(es, "wqn", [128, 3, 512], BF16); wqr = self.sb(es, "wqr", [128, 3, 256], BF16)
            wqp = self.sb(es, "wqp", [128, 3, 256], BF16)
            wkn = self.sb(es, "wkn", [128, 2, 512], BF16); wkv = self.sb(es, "wkv", [128, 2, 512], BF16)
            for wt_, nm in ((wqn, "wuq_n"), (wqr, "wuq_r"), (wqp, "wuq_rp"), (wkn, "wukv_k"), (wkv, "wukv_v")):
                self.dma(wt_[:], self.inp[nm][l].rearrange("(c p) n -> p c n", p=128), writes=[nm], q="gq")
            with ExitStack() as es2:
                self.rms_fm_from_dram(es2, CQ, 3, qg, cqn, "cqn")
                self.rms_fm_from_dram(es2, CKV, 2, kg, ckvn, "ckvn")
                kl = Ring(self, es2, "krl", 2, [32, 2, 512])
                ta = Ring(self, es2, "kta", 2, [32, 512]); tb = Ring(self, es2, "ktb", 2, [32, 512])
                for ci, (t0, n) in enumerate(TCH):
                    k2, kk = kl.next()
                    self.dma(k2[:, :, 0:n], self.scr["D_fm"][KR:KR + 64, :].rearrange("(a p) t -> p a t", p=32)[:, :, t0:t0 + n],
                             reads=["D_fm"], writes=[kk])
                    a_, ak = ta.next(); b_, bk = tb.next()
                    self.rope_from(k2[:, 0, 0:n], kk, k2[:, 1, 0:n], kk, cosM[:, t0:t0 + n], sinM[:, t0:t0 + n],
                                   KrT[0:32, t0:t0 + n], ("KrT", ci), a_[:, 0:n], b_[:, 0:n], ak, bk)
                for t in range(NT):
                    tr.op("pool", lambda e: e.memset(Vt[:, t, 512:576], 0.0), writes=[("Vt", t)])
                for t in range(NT):
                    ci = tile_chunk(t)
                    pk = ("ps", 4 + t % 2); pt = self.ps[4 + t % 2]
                    for c in range(2):
                        self.mm(pt[:, :], ckvn[:, c, t * 128:(t + 1) * 128], wkv[:, c, :], c == 0, c == 1, reads=[("ckvn", ci), "wukv_v"], writes=[pk])
                    self.copy(self.evac_eng(), Vt[:, t, 0:512], pt[:, :], reads=[pk], writes=[("Vt", t)])
                tr.barrier()
            if self.stop == "mla_prep":
                return
            with ExitStack() as es2:
                qnr = Ring(self, es2, "QnT", 2, [128, T], BF16); qrr = Ring(self, es2, "QrT", 2, [128, T], BF16)
                knr = Ring(self, es2, "KnT", 2, [128, T], BF16)
                for rg, lo in ((qnr, 64), (qrr, 32), (knr, 64)):
                    for ti, tl in enumerate(rg.tiles):
                        for p0_, p1_ in (((32, 64), (64, 128)) if lo == 32 else ((64, 128),)):
                            tr.op("pool", lambda e: e.memset(tl[p0_:p1_, :], 0.0), writes=[((rg.name, ti), ci) for ci in range(5)])
                ta = Ring(self, es2, "qta", 2, [32, 512]); tb = Ring(self, es2, "qtb", 2, [32, 512])
                pr = Ring(self, es2, "Pexp", 4, [128, 512], BF16)
                rdr = Ring(self, es2, "rden", 2, [64, 512])
                yar = Ring(self, es2, "yah", 2, [64, T], BF16)
                for h in range(8):
                    Qn, qnk = qnr.next(); Qr, qrk = qrr.next(); Kn, knk = knr.next()
                    for ci, (t0, n) in enumerate(TCH):
                        p0, p1, p2, p3 = self.ps[4], self.ps[5], self.ps[6], self.ps[7]
                        for c in range(3):
                            self.mm(p0[0:64, 0:n], wqn[:, c, h * 64:(h + 1) * 64], cqn[:, c, t0:t0 + n], c == 0, c == 2, reads=["wuq_n", ("cqn", ci)], writes=[("ps", 4)])
                        for c in range(3):
                            self.mm(p1[0:32, 0:n], wqr[:, c, h * 32:(h + 1) * 32], cqn[:, c, t0:t0 + n], c == 0, c == 2, reads=["wuq_r", ("cqn", ci)], writes=[("ps", 5)])
                        for c in range(3):
                            self.mm(p2[0:32, 0:n], wqp[:, c, h * 32:(h + 1) * 32], cqn[:, c, t0:t0 + n], c == 0, c == 2, reads=["wuq_rp", ("cqn", ci)], writes=[("ps", 6)])
                        for c in range(2):
                            self.mm(p3[0:64, 0:n], wkn[:, c, h * 64:(h + 1) * 64], ckvn[:, c, t0:t0 + n], c == 0, c == 1, reads=["wukv_k", ("ckvn", ci)], writes=[("ps", 7)])
                        self.copy("act", Qn[0:64, t0:t0 + n], p0[0:64, 0:n], reads=[("ps", 4)], writes=[(qnk, ci)])
                        self.copy("act", Kn[0:64, t0:t0 + n], p3[0:64, 0:n], reads=[("ps", 7)], writes=[(knk, ci)])
                        a_, ak = ta.next(); b_, bk = tb.next()
                        self.rope_from(p1[0:32, 0:n], ("ps", 5), p2[0:32, 0:n], ("ps", 6), cosM[:, t0:t0 + n], sinM[:, t0:t0 + n],
                                       Qr[0:32, t0:t0 + n], (qrk, ci), a_[:, 0:n], b_[:, 0:n], ak, bk)
                    if self.stop == "mla_proj":
                        break
                    ya, yk = yar.next()
                    for ci, (t0, n) in enumerate(TCH):
                        if self.stop == "mla_ctx" and ci > 0:
                            break
                        kts = [0, 1] if ci == 0 else list(range(NT))
                        ob, db = (3, 2) if ci % 2 == 0 else (6, 5)
                        po, pd = self.ps[ob], self.ps[db]
                        def s_stage(i, kt):
                            kci = tile_chunk(kt)
                            bank = (0, 1, 4)[i % 3]
                            sk_ = ("ps", bank); psn = self.ps[bank]
                            self.mm(psn[:, 0:n], Kn[:, kt * 128:(kt + 1) * 128], Qn[:, t0:t0 + n], True, False,
                                    reads=[(knk, kci), (qnk, ci)], writes=[sk_])
                            self.mm(psn[:, 0:n], KrT[:, kt * 128:(kt + 1) * 128], Qr[:, t0:t0 + n], False, True,
                                    reads=[("KrT", kci), (qrk, ci)], writes=[sk_])
                            P, Pk = pr.next()
                            tr.op("act", lambda e: e.activation(out=P[:, 0:n], in_=psn[:, 0:n], func=AF.Exp, scale=scale), reads=[sk_], writes=[Pk])
                            return P, Pk

                        def pv_stage(i, kt, P, Pk):
                            self.mm(po[:, 0:n], Vt[:, kt, h * 64:h * 64 + 128], P[:, 0:n], i == 0, i == len(kts) - 1, reads=[("Vt", kt), Pk], writes=[("ps", ob)])
                            self.mm(pd[:, 0:n], self.ones16[:, :], P[:, 0:n], i == 0, i == len(kts) - 1, reads=["ones16", Pk], writes=[("ps", db)])

                        pend = []
                        for i, kt in enumerate(kts):
                            pend.append((i, kt) + s_stage(i, kt))
                            if len(pend) > 2:
                                pv_stage(*pend.pop(0))
                        while pend:
                            pv_stage(*pend.pop(0))
                        rd, rk = rdr.next()
                        tr.op("dve", lambda e: e.reciprocal(out=rd[:, 0:n], in_=pd[0:64, 0:n]), reads=[("ps", db)], writes=[rk])
                        tr.op("dve", lambda e: e.tensor_tensor(out=ya[:, t0:t0 + n], in0=po[0:64, 0:n], in1=rd[:, 0:n], op=ALU.mult),
                              reads=[("ps", ob), rk], writes=[yk])
                    self.dma(self.scr["D_ya"][h * 64:(h + 1) * 64, :], ya[:], reads=[yk], writes=["D_ya"])
                tr.barrier()

    def phase_swa(self, l):
        tr = self.tr
        with ExitStack() as es:
            QT = self.sb(es, "sQT", [128, 8, T], BF16)
            KT = self.sb(es, "sKT", [128, 2, T], BF16)
            tr.op("pool", lambda e: e.memset(QT[64:128, :, :], 0.0), writes=[("sQK", j) for j in range(8)])
            tr.op("pool", lambda e: e.memset(KT[64:128, :, :], 0.0), writes=[("sQK", 8), ("sQK", 9)])
            Vs = self.sb(es, "sVs", [128, NT, 128], BF16)
            cosS = self.sb(es, "cosS", [64, T]); sinS = self.sb(es, "sinS", [64, T])
            msk = self.sb(es, "smask", [128, 2, 512], BF16)
            snk = self.sb(es, "snk", [64, 8])
            self.dma(cosS[:], self.inp["ropeS"][0], writes=["rope"])
            self.dma(sinS[:], self.inp["ropeS"][1], writes=["rope"])
            self.dma(msk[:], self.inp["swamask"].rearrange("a p n -> p a n"), writes=["smask"], q="gq")
            self.dma(snk[:], self.inp["sink_rep"][:, l, :], writes=["snk"])
            tr.op("act", lambda e: e.activation(out=snk[:], in_=snk[:], func=AF.Exp), reads=["snk"], writes=["snk"])
            self.dma(Vs[:], self.scr["D_tm"][:, SV:SV + 128].rearrange("(t p) c -> p t c", p=128), reads=["D_tm"], writes=["sVs"], q="gq")
            with ExitStack() as es2:
                ld = Ring(self, es2, "sld", 1, [64, T]); ldp = Ring(self, es2, "sldp", 1, [64, T])
                ta = Ring(self, es2, "sta", 1, [64, T]); tb = Ring(self, es2, "stb", 1, [64, T])
                for j in range(10):
                    r0, rp = (SQ + j * 64, SQP + j * 64) if j < 8 else (SK + (j - 8) * 64, SKP + (j - 8) * 64)
                    dst = QT[0:64, j, :] if j < 8 else KT[0:64, j - 8, :]
                    x, xk = ld.next(); xp, xpk = ldp.next()
                    self.dma(x[:], self.scr["D_fm"][r0:r0 + 64, :], reads=["D_fm"], writes=[xk])
                    self.dma(xp[:], self.scr["D_fm"][rp:rp + 64, :], reads=["D_fm"], writes=[xpk])
                    a_, ak = ta.next(); b_, bk = tb.next()
                    self.rope_from(x[:], xk, xp[:], xpk, cosS[:], sinS[:], dst, ("sQK", j), a_[:], b_[:], ak, bk)
                tr.barrier()
            with ExitStack() as es2:
                pr = Ring(self, es2, "sP", 3, [128, 512], BF16)
                dn = Ring(self, es2, "sdn", 2, [64, 512])
                yr = Ring(self, es2, "syb", 2, [64, 4, 128], BF16)
                ybv = self.scr["D_yb"].rearrange("(h d) t -> d h t", d=64)
                it = 0
                for qt in range(NT):
                    if qt < 2:
                        kts = [(0, None), (1, None)]
                    else:
                        kts = [(0, None), (1, None)]
                        if qt > 2:
                            kts.append((qt - 1, 0))
                        kts.append((qt, None))
                        if qt < NT - 1:
                            kts.append((qt + 1, 1))
                    for g in range(2):
                        ob, db = (3, 2) if g == 0 else (6, 5)
                        po, pd = self.ps[ob], self.ps[db]
                        rhs = QT[:, 4 * g:4 * g + 4, qt * 128:(qt + 1) * 128]
                        def s_stage(i, kt, mk):
                            nonlocal it
                            sk_ = ("ps", it % 2); psn = self.ps[it % 2]; it += 1
                            self.mm(psn[:, :].rearrange("p (h q) -> p h q", h=4), KT[:, g, kt * 128:(kt + 1) * 128], rhs, True, True,
                                    reads=[("sQK", 8 + g)] + [("sQK", 4 * g + hh) for hh in range(4)], writes=[sk_])
                            P, Pk = pr.next()
                            tr.op("act", lambda e: e.activation(out=P[:], in_=psn[:, :], func=AF.Exp, scale=0.125), reads=[sk_], writes=[Pk])
                            if mk is not None:
                                tr.op("dve", lambda e: e.tensor_tensor(out=P[:], in0=P[:], in1=msk[:, mk, :], op=ALU.mult), reads=[Pk, "smask"], writes=[Pk])
                            return P, Pk

                        def pv_stage(i, kt, P, Pk):
                            self.mm(po[0:64, :], Vs[:, kt, g * 64:(g + 1) * 64], P[:], i == 0, i == len(kts) - 1, reads=["sVs", Pk], writes=[("ps", ob)])
                            self.mm(pd[:, :], self.ones16[:, :], P[:], i == 0, i == len(kts) - 1, reads=["ones16", Pk], writes=[("ps", db)])

                        prev = None
                        for i, (kt, mk) in enumerate(kts):
                            cur = (i, kt) + s_stage(i, kt, mk)
                            if prev is not None:
                                pv_stage(*prev)
                            prev = cur
                        pv_stage(*prev)
                        d_, dk = dn.next()
                        for hh in range(4):
                            tr.op("dve", lambda e: e.tensor_scalar(out=d_[:, hh * 128:(hh + 1) * 128], in0=pd[0:64, hh * 128:(hh + 1) * 128],
                                                                  scalar1=snk[:, 4 * g + hh:4 * g + hh + 1], scalar2=None, op0=ALU.add),
                                  reads=[("ps", db), "snk"], writes=[dk])
                        tr.op("dve", lambda e: e.reciprocal(out=d_[:], in_=d_[:]), reads=[dk], writes=[dk])
                        y, yk = yr.next()
                        tr.op("dve", lambda e: e.tensor_tensor(out=y[:].rearrange("p h q -> p (h q)"), in0=po[0:64, :], in1=d_[:], op=ALU.mult),
                              reads=[("ps", ob), dk], writes=[yk])
                        self.dma(ybv[:, 4 * g:4 * g + 4, qt * 128:(qt + 1) * 128], y[:], reads=[yk], writes=["D_yb"])
                tr.barrier()

    def phase_gdn_prep(self, l):
        tr = self.tr
        with ExitStack() as es:
            cw = self.sb(es, "gcw", [128, 12, 3])
            bd = self.sb(es, "gbd64", [128, 128])
            self.dma(cw[:], self.inp["gconv_fm"][:, l, :, :], writes=["gcw"])
            self.dma(bd[:], self.inp["bd64"], writes=["gbd64"])
            ld = Ring(self, es, "gld", 2, [128, T]); yr = Ring(self, es, "gy", 2, [128, T])
            sqr = Ring(self, es, "gsq", 2, [128, 512]); rr = Ring(self, es, "grn", 2, [128, 512])
            tmr = Ring(self, es, "gtm", 2, [128, NT, 128])
            for j in range(12):
                x, xk = ld.next(); y, yk = yr.next()
                self.dma(x[:], self.scr["D_fm"][GQ + j * 128:GQ + (j + 1) * 128, :], reads=["D_fm"], writes=[xk])
                tr.op("act", lambda e: e.activation(out=y[:], in_=x[:], func=AF.Identity, scale=cw[:, j, 1:2]), reads=[xk, "gcw"], writes=[yk])
                for (o0, o1, i0, i1, k) in ((1, CTX, 0, CTX - 1, 0), (CTX + 1, T, CTX, T - 1, 0), (0, CTX - 1, 1, CTX, 2), (CTX, T - 1, CTX + 1, T, 2)):
                    tr.op("dve", lambda e: e.scalar_tensor_tensor(out=y[:, o0:o1], in0=x[:, i0:i1], scalar=cw[:, j, k:k + 1], in1=y[:, o0:o1],
                                                                 op0=ALU.mult, op1=ALU.add), reads=[xk, yk, "gcw"], writes=[yk])
                tr.op("act", lambda e: e.activation(out=y[:], in_=y[:], func=AF.Silu), reads=[yk], writes=[yk])
                if j < 8:
                    for ci, (t0, n) in enumerate(TCH):
                        sq, sk = sqr.next()
                        tr.op("act", lambda e: e.activation(out=sq[:, 0:n], in_=y[:, t0:t0 + n], func=AF.Square), reads=[yk], writes=[sk])
                        pk = ("ps", ci % 2); pt = self.ps[ci % 2]
                        self.mm(pt[:, 0:n], bd[:], sq[:, 0:n], True, True, reads=[sk, "gbd64"], writes=[pk])
                        r, rk = rr.next()
                        self.rstd_from_ss(pt[:, 0:n], pk, r[:, 0:n], rk, 1.0)
                        tr.op("dve", lambda e: e.scalar_tensor_tensor(out=y[:, t0:t0 + n], in0=y[:, t0:t0 + n], scalar=(0.125 if j < 4 else 1.0),
                                                                     in1=r[:, 0:n], op0=ALU.mult, op1=ALU.mult), reads=[yk, rk], writes=[yk])
                    dst = self.scr["D_gq"] if j < 4 else self.scr["D_gk"]
                    self.dma(dst[(j % 4) * 128:(j % 4 + 1) * 128, :], y[:], reads=[yk], writes=["D_gqk"])
                if j >= 4:
                    tm, tk = tmr.next()
                    for t4 in range(0, NT, 4):
                        nt = min(4, NT - t4)
                        pk = ("ps", 2 + (t4 // 4) % 2); pt = self.ps[2 + (t4 // 4) % 2]
                        for tt in range(nt):
                            t = t4 + tt
                            tr.op("pe", lambda e: e.transpose(pt[:, tt * 128:(tt + 1) * 128], y[:, t * 128:(t + 1) * 128], self.ident[:]),
                                  reads=[yk, "ident"], writes=[pk])
                        self.copy(self.evac_eng(), tm[:, t4:t4 + nt, :], pt[:, 0:nt * 128].rearrange("p (t d) -> p t d", d=128), reads=[pk], writes=[tk])
                    dst = self.scr["D_ktm"] if j < 8 else self.scr["D_vtm"]
                    c = (j % 4) * 2
                    for hh in range(2):
                        self.dma(dst[c + hh].rearrange("(t p) d -> p t d", p=128), tm[:, :, hh * 64:(hh + 1) * 64], reads=[tk], writes=["D_kvtm"])
            tr.barrier()

    def phase_gdn(self, l):
        tr = self.tr
        with ExitStack() as es:
            oacc = self.sb(es, "oacc", [128, NT, 512])
            ab = self.sb(es, "gab", [128, NT, 32])
            g = self.sb(es, "gg", [128, NT, 16]); beta = self.sb(es, "gbeta", [128, NT, 16]); nbeta = self.sb(es, "gnbeta", [128, NT, 16])
            gam = self.sb(es, "ggam", [128, NT, 16]); eg = self.sb(es, "geg", [128, NT, 16]); bg = self.sb(es, "gbg", [128, NT, 16])
            edl = self.sb(es, "gedl", [128, NT, 16]); tmp = self.sb(es, "gtmp", [128, NT, 16])
            alog = self.sb(es, "galog", [128, 16]); dtb = self.sb(es, "gdtb", [128, 16])
            tri = self.sb(es, "gtri", [128, 2, 128]); gm = self.sb(es, "gmask", [128, 4, 128]); sel = self.sb(es, "gsel", [128, 8, 128])
            S = self.sb(es, "gS", [128, 2, 8, 64])
            self.dma(ab[:], self.scr["D_tm"][:, AB:AB + 32].rearrange("(t p) c -> p t c", p=128), reads=["D_tm"], writes=["gab"])
            self.dma(alog[:], self.inp["alog_rep"][:, l, :], writes=["galog"])
            self.dma(dtb[:], self.inp["dtb_rep"][:, l, :], writes=["gdtb"])
            self.dma(tri[:], self.inp["tri"].rearrange("a p n -> p a n"), writes=["gtri"])
            self.dma(gm[:], self.inp["gmask"].rearrange("a p n -> p a n"), writes=["gmaskc"])
            tr.op("pool", lambda e: e.memset(sel[:], 0.0), writes=["gsel"])
            self.dma(sel[0:8, :, :], self.inp["selh"].rearrange("h u n -> u h n"), writes=["gsel"])
            bmk = self.sb(es, "gbmask", [128, 3, 128])
            self.dma(bmk[:], self.inp["bmask"].rearrange("a p n -> p a n"), writes=["gbmask"])
            tr.op("dve", lambda e: e.memset(S[:], 0.0), writes=[(f"S{d}", h) for d in range(2) for h in range(8)])
            tr.op("act", lambda e: e.activation(out=alog[:], in_=alog[:], func=AF.Exp), reads=["galog"], writes=["galog"])
            for t in range(NT):
                tr.op("dve", lambda e: e.tensor_tensor(out=tmp[:, t, :], in0=ab[:, t, 0:16], in1=dtb[:], op=ALU.add), reads=["gab", "gdtb"], writes=["gtmp"])
            tr.op("act", lambda e: e.activation(out=tmp[:], in_=tmp[:], func=AF.Exp), reads=["gtmp"], writes=["gtmp"])
            tr.op("act", lambda e: e.activation(out=tmp[:], in_=tmp[:], func=AF.Ln, bias=self.cst[:, 1:2], scale=1.0), reads=["gtmp", "cst"], writes=["gtmp"])
            for t in range(NT):
                tr.op("dve", lambda e: e.scalar_tensor_tensor(out=g[:, t, :], in0=tmp[:, t, :], scalar=-1.0, in1=alog[:], op0=ALU.mult, op1=ALU.mult),
                      reads=["gtmp", "galog"], writes=["gg"])
            tr.op("act", lambda e: e.activation(out=beta[:], in_=ab[:, :, 16:32], func=AF.Sigmoid), reads=["gab"], writes=["gbeta"])
            tr.op("dve", lambda e: e.tensor_scalar(out=nbeta[:], in0=beta[:], scalar1=-1.0, scalar2=None, op0=ALU.mult), reads=["gbeta"], writes=["gnbeta"])
            pg = self.ps[0][:, 0:NT * 16].rearrange("p (t u) -> p t u", u=16)
            pt_ = self.ps[1][:, 0:NT * 16].rearrange("p (t u) -> p t u", u=16)
            for t in range(NT):
                for d in range(2):
                    self.mm(pg[:, t, d * 8:(d + 1) * 8], tri[:, d, :], g[:, t, d * 8:(d + 1) * 8], True, True, reads=["gtri", "gg"], writes=[("ps", 0)])
                self.mm(pt_[:, t, :], self.ones32[:], g[:, t, :], True, True, reads=["ones32", "gg"], writes=[("ps", 1)])
            self.copy("dve", gam[:], pg, reads=[("ps", 0)], writes=["ggam"])
            tr.op("dve", lambda e: e.tensor_tensor(out=edl[:], in0=pt_, in1=gam[:], op=ALU.subtract), reads=[("ps", 1), "ggam"], writes=["gedl"])
            tr.op("act", lambda e: e.activation(out=edl[:], in_=edl[:], func=AF.Exp), reads=["gedl"], writes=["gedl"])
            tr.op("act", lambda e: e.activation(out=eg[:], in_=gam[:], func=AF.Exp), reads=["ggam"], writes=["geg"])
            tr.op("dve", lambda e: e.tensor_tensor(out=bg[:], in0=beta[:], in1=eg[:], op=ALU.mult), reads=["gbeta", "geg"], writes=["gbg"])
            if self.stop == "gdn_gates":
                tr.barrier()
                return
            es_scan = ExitStack()
            NSLOT = 4
            R1 = lambda nm, shp, n=1: [Ring(self, es_scan, f"g{nm}{sl_}", n, shp) for sl_ in range(NSLOT)]
            qTr = R1("iq", [128, 128]); kTr = R1("ik", [128, 128]); ktmr = R1("iktm", [128, 64]); vtmr = R1("ivtm", [128, 64]); kdr = R1("kd", [128, 64])
            t1r = R1("t1", [128, 128]); t2r = R1("t2", [128, 128]); egr = R1("egb", [64, 128])
            a0r = R1("a0", [128, 256]); pxr_ = R1("px", [128, 256], 2); ptr__ = R1("pt", [128, 128], 2); lr = R1("l", [128, 2, 128])
            ybr_ = R1("yb", [128, 128], 5); qkr = R1("qk", [128, 128]); wtr = R1("wt", [128, 128]); qgr = R1("qg", [128, 128]); vnr = R1("vn", [128, 64])
            gTr = Ring(self, es_scan, "ggT", 3, [128, 128])
            for rg in qTr + kTr + wtr + qgr + [gTr]:
                for ti, tl in enumerate(rg.tiles):
                    tr.op("pool", lambda e: e.memset(tl[:], 0.0), writes=[(rg.name, ti)])
            gqv = self.scr["D_gq"].rearrange("(h d) t -> h d t", d=64)
            gkv = self.scr["D_gk"].rearrange("(h d) t -> h d t", d=64)

            def unit(d, t, h, s_, gT, gTk):
                u = d * 8 + h
                Sk = f"S{d}"
                last = 127 if d == 0 else 0
                sl = slice(t * 128, (t + 1) * 128)
                pA, pAk, pB, pBk = self.ps[2 * s_], ("ps", 2 * s_), self.ps[2 * s_ + 1], ("ps", 2 * s_ + 1)
                qT, qk_ = qTr[s_].next(); kT, kk_ = kTr[s_].next(); ktm, ktk = ktmr[s_].next(); vtm, vtk = vtmr[s_].next()
                self.dma(qT[0:64, :], gqv[h, :, sl], reads=["D_gqk"], writes=[qk_])
                self.dma(kT[0:64, :], gkv[h, :, sl], reads=["D_gqk"], writes=[kk_])
                self.dma(ktm[:], self.scr["D_ktm"][h, sl, :], reads=["D_kvtm"], writes=[ktk])
                self.dma(vtm[:], self.scr["D_vtm"][h, sl, :], reads=["D_kvtm"], writes=[vtk])
                yield
                self.mm(pA[:, 0:128], sel[:, h, :], gT[:], True, True, reads=["gsel", gTk], writes=[pAk])
                self.mm(pB[:, 0:128], kT[:], kT[:], True, True, reads=[kk_], writes=[pBk])
                kd, kdk = kdr[s_].next()
                tr.op("act", lambda e: e.activation(out=kd[:], in_=ktm[:], func=AF.Identity, scale=edl[:, t, u:u + 1]), reads=[ktk, "gedl"], writes=[kdk])
                a0, a0k = a0r[s_].next()
                tr.op("act", lambda e: e.activation(out=a0[:, 128:192], in_=vtm[:], func=AF.Identity, scale=beta[:, t, u:u + 1]),
                      reads=[vtk, "gbeta"], writes=[(a0k, "x")])
                tr.op("act", lambda e: e.activation(out=a0[:, 192:256], in_=ktm[:], func=AF.Identity, scale=bg[:, t, u:u + 1]),
                      reads=[ktk, "gbg"], writes=[(a0k, "x")])
                yield
                t1, t1k = t1r[s_].next(); t2, t2k = t2r[s_].next(); egb, egk = egr[s_].next()
                tr.op("dve", lambda e: e.scalar_tensor_tensor(out=t1[:], in0=pA[:, 0:128], scalar=gam[:, t, u:u + 1], in1=gm[:, 2 * d, :],
                                                             op0=ALU.subtract, op1=ALU.add), reads=[pAk, "ggam", "gmaskc"], writes=[t1k])
                tr.op("dve", lambda e: e.scalar_tensor_tensor(out=t2[:], in0=pA[:, 0:128], scalar=gam[:, t, u:u + 1], in1=gm[:, 2 * d + 1, :],
                                                             op0=ALU.subtract, op1=ALU.add), reads=[pAk, "ggam", "gmaskc"], writes=[t2k])
                tr.op("dve", lambda e: e.tensor_copy(out=egb[:], in_=pA[0:64, 0:128]), reads=[pAk], writes=[egk])
                yield
                tr.op("act", lambda e: e.activation(out=t1[:], in_=t1[:], func=AF.Exp, scale=-1.0), reads=[t1k], writes=[t1k])
                tr.op("act", lambda e: e.activation(out=t2[:], in_=t2[:], func=AF.Exp), reads=[t2k], writes=[t2k])
                tr.op("act", lambda e: e.activation(out=egb[:], in_=egb[:], func=AF.Exp), reads=[egk], writes=[egk])
                self.mm(pA[:, 0:128], kT[:], qT[:], True, True, reads=[kk_, qk_], writes=[pAk])
                yield
                tr.op("dve", lambda e: e.scalar_tensor_tensor(out=a0[:, 0:128], in0=pB[:, 0:128], scalar=nbeta[:, t, u:u + 1], in1=t1[:],
                                                             op0=ALU.mult, op1=ALU.mult), reads=[pBk, "gnbeta", t1k], writes=[(a0k, "p")])
                qk, qkk = qkr[s_].next()
                tr.op("dve", lambda e: e.tensor_tensor(out=qk[:], in0=pA[:, 0:128], in1=t2[:], op=ALU.mult), reads=[pAk, t2k], writes=[qkk])
                qg, qgk = qgr[s_].next()
                tr.op("pool", lambda e: e.tensor_tensor(out=qg[0:64, :], in0=qT[0:64, :], in1=egb[:], op=ALU.mult), reads=[qk_, egk], writes=[qgk])
                yield
                tr.op("pe", lambda e: e.transpose(pB[:, 0:128], a0[:, 0:128], self.ident[:]), reads=[(a0k, "p"), "ident"], writes=[pBk])
                px, pxk = pxr_[s_].next(); pT, pTk = ptr__[s_].next(); lt, ltk = lr[s_].next()
                tr.op("pool", lambda e: e.tensor_tensor(out=pT[:], in0=a0[:, 0:128], in1=bmk[:, 0, :], op=ALU.mult), reads=[(a0k, "p"), "gbmask"], writes=[pTk])
                self.copy("act", px[:, 128:256], self.ident[:], reads=["ident"], writes=[(pxk, "x")])
                yield
                tr.op("dve", lambda e: e.tensor_tensor(out=px[:, 0:128], in0=pB[:, 0:128], in1=bmk[:, 0, :], op=ALU.mult), reads=[pBk, "gbmask"], writes=[(pxk, "p")])
                tr.op("dve", lambda e: e.tensor_tensor(out=lt[:, 0, :], in0=pB[:, 0:128], in1=bmk[:, 1, :], op=ALU.mult), reads=[pBk, "gbmask"], writes=[(ltk, 0)])
                tr.op("dve", lambda e: e.tensor_tensor(out=lt[:, 1, :], in0=pB[:, 0:128], in1=bmk[:, 2, :], op=ALU.mult), reads=[pBk, "gbmask"], writes=[(ltk, 1)])
                yield
                for lev in range(5):
                    if lev < 4:
                        self.mm(pA[:, 0:256], pT[:].bitcast(F32R) if self.use_f32r else pT[:], px[:, 0:256].bitcast(F32R) if self.use_f32r else px[:, 0:256],
                                True, True, reads=[pTk, (pxk, "p"), (pxk, "x")], writes=[pAk])
                        self.mm(pB[:, 0:128], px[:, 0:128], pT[:], True, True, reads=[pTk, (pxk, "p")], writes=[pBk])
                        yield
                        nx, nxk = pxr_[s_].next(); nT, nTk = ptr__[s_].next()
                        self.copy("act", nT[:], pB[:, 0:128], reads=[pBk], writes=[nTk])
                        tr.op("dve", lambda e: e.tensor_tensor(out=nx[:, 128:256], in0=pA[:, 128:256], in1=px[:, 128:256], op=ALU.add),
                              reads=[pAk, (pxk, "x")], writes=[(nxk, "x")])
                        self.copy("dve", nx[:, 0:128], pA[:, 0:128], reads=[pAk], writes=[(nxk, "p")])
                        px, pxk, pT, pTk = nx, nxk, nT, nTk
                        yield
                    else:
                        self.mm(pA[:, 0:128], pT[:], px[:, 128:256], True, True, reads=[pTk, (pxk, "x")], writes=[pAk])
                        yield
                        nx, nxk = pxr_[s_].next()
                        tr.op("dve", lambda e: e.tensor_tensor(out=nx[:, 128:256], in0=pA[:, 0:128], in1=px[:, 128:256], op=ALU.add),
                              reads=[pAk, (pxk, "x")], writes=[(nxk, "x")])
                        px, pxk = nx, nxk
                        yield
                MT, MTk = px[:, 128:256], (pxk, "x")
                cnt = [0]

                def apply(lhsT_ap, lhs_key, rhs_ap, rhs_key, add_ap=None, add_key=None):
                    pb, pbk = (pA, pAk) if cnt[0] % 2 == 0 else (pB, pBk)
                    eng = "act" if cnt[0] % 2 == 0 else "dve"
                    cnt[0] += 1
                    self.mm(pb[:, 0:128], lhsT_ap, rhs_ap, True, True, reads=[lhs_key, rhs_key], writes=[pbk])
                    yield
                    yb, ybk = ybr_[s_].next()
                    if add_ap is None:
                        self.copy(eng, yb[:], pb[:, 0:128], reads=[pbk], writes=[ybk])
                    else:
                        tr.op("dve", lambda e: e.tensor_tensor(out=yb[:], in0=pb[:, 0:128], in1=add_ap, op=ALU.add), reads=[pbk, add_key], writes=[ybk])
                    yield
                    return yb, ybk

                def s64(z_ap, z_key):
                    y1, y1k = yield from apply(MT, MTk, z_ap, z_key)
                    z1, z1k = yield from apply(lt[:, 0, :], (ltk, 0), y1[:], y1k)
                    r_ = yield from apply(MT, MTk, z1[:], z1k, add_ap=y1[:], add_key=y1k)
                    return r_

                y2, y2k = yield from s64(a0[:, 128:256], (a0k, "x"))
                z2, z2k = yield from apply(lt[:, 1, :], (ltk, 1), y2[:], y2k)
                y4, y4k = yield from s64(z2[:], z2k)
                px, pxk = pxr_[s_].next()
                tr.op("pool", lambda e: e.tensor_tensor(out=px[:, 128:256], in0=y2[:], in1=y4[:], op=ALU.add), reads=[y2k, y4k], writes=[(pxk, "x")])
                yield
                tr.op("pe", lambda e: e.transpose(pA[0:64, 0:128], px[:, 192:256], self.ident[:]), reads=[(pxk, "x"), "ident"], writes=[pAk])
                yield
                wt_, wtk = wtr[s_].next()
                self.copy("act", wt_[0:64, :], pA[0:64, 0:128], reads=[pAk], writes=[wtk])
                yield
                self.mm(pB[:, 0:64], wt_[:], S[:, d, h, :], True, True, reads=[wtk, (Sk, h)], writes=[pBk])
                yield
                vn, vnk = vnr[s_].next()
                tr.op("dve", lambda e: e.tensor_tensor(out=vn[:], in0=px[:, 128:192], in1=pB[:, 0:64], op=ALU.subtract),
                      reads=[(pxk, "x"), pBk], writes=[vnk])
                yield
                self.mm(pA[:, 0:64], qg[:], S[:, d, h, :], True, False, reads=[qgk, (Sk, h)], writes=[pAk])
                self.mm(pA[:, 0:64], qk[:], vn[:], False, True, reads=[qkk, vnk], writes=[pAk])
                self.mm(pB[0:64, 0:64], kd[:], vn[:], True, True, reads=[kdk, vnk], writes=[pBk])
                yield
                if d == 0:
                    self.copy("act", oacc[:, t, h * 64:(h + 1) * 64], pA[:, 0:64], reads=[pAk], writes=[("oacc", t, h)])
                else:
                    tr.op("dve", lambda e: e.tensor_tensor(out=oacc[:, t, h * 64:(h + 1) * 64], in0=pA[:, 0:64], in1=oacc[:, t, h * 64:(h + 1) * 64], op=ALU.add),
                          reads=[pAk, ("oacc", t, h)], writes=[("oacc", t, h)])
                tr.op("dve", lambda e: e.scalar_tensor_tensor(out=S[0:64, d, h, :], in0=S[0:64, d, h, :], scalar=egb[:, last:last + 1], in1=pB[0:64, 0:64],
                                                             op0=ALU.mult, op1=ALU.add), reads=[(Sk, h), egk, pBk], writes=[(Sk, h)])

            def unit_stream():
                for d in range(2):
                    order = list(range(NT)) if d == 0 else [1, 0] + list(range(NT - 1, 1, -1))
                    for t in order:
                        gT, gTk = gTr.next()
                        pk = ("ps", 7)
                        self.mm(self.ps[7][0:8, 256:384], g[:, t, d * 8:(d + 1) * 8], tri[:, d, :], True, True, reads=["gg", "gtri"], writes=[pk])
                        self.copy("act", gT[0:8, :], self.ps[7][0:8, 256:384], reads=[pk], writes=[gTk])
                        for h in range(8):
                            yield (d, t, h, gT, gTk)

            stream = unit_stream()
            active = {}
            free_slots = list(range(NSLOT))
            exhausted = False
            while True:
                while free_slots and not exhausted:
                    try:
                        d_, t_, h_, gT_, gTk_ = next(stream)
                    except StopIteration:
                        exhausted = True
                        break
                    s_ = free_slots.pop(0)
                    active[s_] = unit(d_, t_, h_, s_, gT_, gTk_)
                if not active:
                    break
                for s_ in sorted(active):
                    try:
                        next(active[s_])
                    except StopIteration:
                        del active[s_]
                        free_slots.append(s_)
            if (self.stop or "").startswith("gdn_u1"):
                tr.barrier()
                es_scan.close()
                return
            tr.barrier()
            es_scan.close()
            for t in range(NT):
                for h in range(8):
                    tr.lastw[("oacc", t, h)] = None
            tr.lastw = {k: v for k, v in tr.lastw.items() if v is not None}
            if "D_x" in self.taps:
                self.dma(self.scr["D_oc"].rearrange("(t p) c -> p t c", p=128), oacc[:], reads=[("oacc", t, h) for t in range(NT) for h in range(8)])
            gn = self.sb(es, "gn", [128, 64])
            self.dma(gn[:], self.inp["gnorm_rep"][:, l, :], writes=["gn"])
            zr = Ring(self, es, "gz", 2, [128, 512]); sq2 = Ring(self, es, "gsq2", 2, [128, 512]); ssr = Ring(self, es, "gss", 2, [128, 8])
            ycr = Ring(self, es, "gyc", 2, [128, 4, 128], BF16)
            ycv = self.scr["D_yc"].rearrange("(c p) t -> p c t", p=128)
            for t in range(NT):
                z, zk = zr.next()
                self.dma(z[:], self.scr["D_tm"][t * 128:(t + 1) * 128, ZC:ZC + 512], reads=["D_tm"], writes=[zk])
                tr.op("act", lambda e: e.activation(out=z[:], in_=z[:], func=AF.Silu), reads=[zk], writes=[zk])
                ok = [("oacc", t, h) for h in range(8)]
                sq, sk = sq2.next()
                tr.op("pool", lambda e: e.tensor_tensor(out=sq[:], in0=oacc[:, t, :], in1=oacc[:, t, :], op=ALU.mult), reads=ok, writes=[sk])
                ss, ssk = ssr.next()
                tr.op("dve", lambda e: e.tensor_reduce(out=ss[:], in_=sq[:].rearrange("p (h d) -> p h d", d=64), axis=AX.X, op=ALU.add), reads=[sk], writes=[ssk])
                self.rstd_from_ss(ss[:], ssk, ss[:], ssk, 1.0 / 64)
                o3 = oacc[:, t, :].rearrange("p (h d) -> p h d", d=64)
                tr.op("dve", lambda e: e.tensor_tensor(out=sq[:].rearrange("p (h d) -> p h d", d=64), in0=o3, in1=ss[:].unsqueeze(2).to_broadcast([128, 8, 64]), op=ALU.mult),
                      reads=ok + [ssk], writes=[sk])
                tr.op("dve", lambda e: e.tensor_tensor(out=sq[:].rearrange("p (h d) -> p h d", d=64), in0=sq[:].rearrange("p (h d) -> p h d", d=64),
                                                      in1=gn[:].unsqueeze(1).to_broadcast([128, 8, 64]), op=ALU.mult), reads=[sk, "gn"], writes=[sk])
                tr.op("dve", lambda e: e.tensor_tensor(out=sq[:], in0=sq[:], in1=z[:], op=ALU.mult), reads=[sk, zk], writes=[sk])
                pk = ("ps", 4 + t % 2); pt = self.ps[4 + t % 2]
                for c in range(4):
                    tr.op("pe", lambda e: e.transpose(pt[:, c * 128:(c + 1) * 128], sq[:, c * 128:(c + 1) * 128], self.ident[:]), reads=[sk, "ident"], writes=[pk])
                yc, yk = ycr.next()
                self.copy("act", yc[:], pt[:].rearrange("p (c t) -> p c t", c=4), reads=[pk], writes=[yk])
                self.dma(ycv[:, :, t * 128:(t + 1) * 128], yc[:], reads=[yk], writes=["D_yc"])
            tr.barrier()

    def phase_merge(self, l):
        tr = self.tr
        with ExitStack() as es:
            wp = [self.sb(es, f"wp{i}", [128, 4, D], BF16) for i in range(3)]
            wo = self.sb(es, "wo", [128, 8, D], BF16)
            for m in range(8):
                for i, nm in enumerate(("w_pa", "w_pb", "w_pc")):
                    self.dma(wp[i][:, :, m * 128:(m + 1) * 128], self.inp[nm][l].rearrange("(c p) n -> p c n", p=128)[:, :, m * 128:(m + 1) * 128],
                             writes=[(nm, m)], q="gq")
            for m in range(8):
                self.dma(wo[:, :, m * 128:(m + 1) * 128], self.inp["w_out"][l].rearrange("(c p) n -> p c n", p=128)[:, :, m * 128:(m + 1) * 128],
                         writes=[("w_out", m)], q="gq")
            gr = Ring(self, es, "mg", 1, [128, 24, 512], BF16)
            yr = [Ring(self, es, f"my{i}", 2, [128, 4, 512], BF16) for i in range(3)]
            mixr = Ring(self, es, "mix", 2, [128, 512]); tmr = Ring(self, es, "mtm", 2, [128, 512])
            mT = Ring(self, es, "mixT", 1, [128, 8, 512], BF16)
            gv = self.scr["D_gate"].rearrange("(c p) t -> p c t", p=128)
            yv = [self.scr[nm].rearrange("(c p) t -> p c t", p=128) for nm in ("D_ya", "D_yb", "D_yc")]
            pi = 0
            for ci, (t0, n) in enumerate(TCH):
                s = 1 if ci == 0 else 0
                gt, gk = gr.next()
                self.dma(gt[:, :, 0:n], gv[:, :, t0:t0 + n], reads=[("D_gate", ci)], writes=[gk])
                ys = []
                for i in range(3):
                    y, yk = yr[i].next()
                    self.dma(y[:, :, 0:n], yv[i][:, :, t0:t0 + n], reads=["D_ya", "D_yb", "D_yc"], writes=[yk])
                    ys.append((y, yk))
                mt, mtk = mT.next()
                for m in range(8):
                    mix, mk = mixr.next()
                    for i in range(3):
                        pk = ("ps", pi % 4); pt = self.ps[pi % 4]; pi += 1
                        y, yk = ys[i]
                        for c in range(4):
                            self.mm(pt[:, 0:n], wp[i][:, c, m * 128:(m + 1) * 128], y[:, c, 0:n], c == 0, c == 3, reads=[(("w_pa", "w_pb", "w_pc")[i], m), yk], writes=[pk])
                        if i == 0:
                            tr.op("dve", lambda e: e.tensor_tensor(out=mix[:, 0:n], in0=pt[:, 0:n], in1=gt[:, m, 0:n], op=ALU.mult), reads=[pk, gk], writes=[mk])
                        else:
                            tm, tk = tmr.next()
                            tr.op("dve", lambda e: e.tensor_tensor(out=tm[:, 0:n], in0=pt[:, 0:n], in1=gt[:, i * 8 + m, 0:n], op=ALU.mult), reads=[pk, gk], writes=[tk])
                            if i == 1:
                                tr.op("pool", lambda e: e.tensor_tensor(out=mix[:, 0:n], in0=mix[:, 0:n], in1=tm[:, 0:n], op=ALU.add), reads=[mk, tk], writes=[mk])
                            else:
                                tr.op("pool", lambda e: e.tensor_tensor(out=mt[:, m, 0:n], in0=mix[:, 0:n], in1=tm[:, 0:n], op=ALU.add), reads=[mk, tk], writes=[(mtk, m)])
                for m in range(8):
                    pk = ("ps", 4 + m % 4); pt = self.ps[4 + m % 4]
                    for c in range(8):
                        self.mm(pt[:, 0:n], wo[:, c, m * 128:(m + 1) * 128], mt[:, c, 0:n], c == 0, c == 7, reads=[("w_out", m), (mtk, c)], writes=[pk])
                    tr.op("dve", lambda e: e.scalar_tensor_tensor(out=self.xT[:, m, t0:t0 + n], in0=pt[:, 0:n], scalar=self.mod[:, l, 16 + m, s:s + 1],
                                                                 in1=self.xT[:, m, t0:t0 + n], op0=ALU.mult, op1=ALU.add),
                          reads=[pk, "mod", ("xT", ci)], writes=[("xT", ci)])
            tr.barrier()

    def phase_ffn(self, l):
        tr = self.tr
        with ExitStack() as es:
            fcw = self.sb(es, "fcw", [128, 44, 3])
            self.dma(fcw[:], self.inp["fconv_fm"][:, l, :, :], writes=["fcw"])
            with ExitStack() as es2:
                xn = self.sb(es2, "xn2", [128, 8, T], BF16)
                self.adaln(es2, l, 1, xn)
                wr = Ring(self, es2, "wup", 2, [128, 8, 2, 128], BF16)
                hr = [Ring(self, es2, f"fh{i}", 2, [128, T]) for i in range(2)]
                cr = [Ring(self, es2, f"fc{i}", 2, [128, T]) for i in range(2)]
                ar = Ring(self, es2, "fact", 2, [128, T], BF16)
                wv = self.inp["ffn_up"][l].rearrange("(c p) n -> p c n", p=128)
                pi = 0
                for cc in range(22):
                    w, wk = wr.next()
                    self.dma(w[:, :, 0, :], wv[:, :, cc * 128:(cc + 1) * 128], writes=[wk], q="gq")
                    self.dma(w[:, :, 1, :], wv[:, :, 2816 + cc * 128:2816 + (cc + 1) * 128], writes=[wk], q="gq")
                    cs_ = []
                    for i in range(2):
                        hb, hk = hr[i].next()
                        for ci, (t0, n) in enumerate(TCH):
                            pk = ("ps", pi % 4); pt = self.ps[pi % 4]; pi += 1
                            for c in range(8):
                                self.mm(pt[:, 0:n], w[:, c, i, :], xn[:, c, t0:t0 + n], c == 0, c == 7, reads=[wk, ("xn", ci)], writes=[pk])
                            self.copy(self.evac_eng(), hb[:, t0:t0 + n], pt[:, 0:n], reads=[pk], writes=[hk])
                        cb, ck = cr[i].next()
                        j = cc + 22 * i
                        tr.op("act", lambda e: e.activation(out=cb[:], in_=hb[:], func=AF.Identity, scale=fcw[:, j, 1:2]), reads=[hk, "fcw"], writes=[ck])
                        for (o0, o1, i0, i1, k) in ((1, CTX, 0, CTX - 1, 0), (CTX + 1, T, CTX, T - 1, 0), (0, CTX - 1, 1, CTX, 2), (CTX, T - 1, CTX + 1, T, 2)):
                            tr.op("dve", lambda e: e.scalar_tensor_tensor(out=cb[:, o0:o1], in0=hb[:, i0:i1], scalar=fcw[:, j, k:k + 1], in1=cb[:, o0:o1],
                                                                       op0=ALU.mult, op1=ALU.add), reads=[hk, ck, "fcw"], writes=[ck])
                        cs_.append((cb, ck))
                    (ca, cak), (cb, cbk) = cs_
                    tr.op("act", lambda e: e.activation(out=ca[:], in_=ca[:], func=AF.Silu), reads=[cak], writes=[cak])
                    a, ak = ar.next()
                    tr.op("dve", lambda e: e.tensor_tensor(out=a[:], in0=ca[:], in1=cb[:], op=ALU.mult), reads=[cak, cbk], writes=[ak])
                    self.dma(self.scr["D_act"][cc * 128:(cc + 1) * 128, :], a[:], reads=[ak], writes=["D_act"])
                tr.barrier()
            with ExitStack() as es2:
                wd = self.sb(es2, "wdn", [128, 22, D], BF16)
                for m in range(8):
                    self.dma(wd[:, :, m * 128:(m + 1) * 128], self.inp["ffn_down"][l].rearrange("(c p) n -> p c n", p=128)[:, :, m * 128:(m + 1) * 128],
                             writes=[("wdn", m)], q="gq")
                ar = Ring(self, es2, "fa", 2, [128, 22, 512], BF16)
                av = self.scr["D_act"].rearrange("(c p) t -> p c t", p=128)
                for ci, (t0, n) in enumerate(TCH):
                    s = 1 if ci == 0 else 0
                    a, ak = ar.next()
                    self.dma(a[:, :, 0:n], av[:, :, t0:t0 + n], reads=["D_act"], writes=[ak])
                    for m in range(8):
                        pk = ("ps", m % 4); pt = self.ps[m % 4]
                        for c in range(22):
                            self.mm(pt[:, 0:n], wd[:, c, m * 128:(m + 1) * 128], a[:, c, 0:n], c == 0, c == 21, reads=[("wdn", m), ak], writes=[pk])
                        tr.op("dve", lambda e: e.scalar_tensor_tensor(out=self.xT[:, m, t0:t0 + n], in0=pt[:, 0:n], scalar=self.mod[:, l, 40 + m, s:s + 1],
                                                                     in1=self.xT[:, m, t0:t0 + n], op0=ALU.mult, op1=ALU.add),
                              reads=[pk, "mod", ("xT", ci)], writes=[("xT", ci)])
                tr.barrier()

    def phase_final(self):
        tr = self.tr
        with ExitStack() as es:
            nf = self.sb(es, "nf", [128, D])
            self.dma(nf[:], self.inp["normf_rep"], writes=["nf"])
            xr = Ring(self, es, "fx", 2, [128, D]); sqr = Ring(self, es, "fsq", 2, [128, D]); ssr = Ring(self, es, "fss", 2, [128, 1])
            for t in range(2, NT):
                x, xk = xr.next()
                for half in range(2):
                    pk = ("ps", (2 * t + half) % 8); pt = self.ps[(2 * t + half) % 8]
                    for j in range(4):
                        c = half * 4 + j
                        tr.op("pe", lambda e: e.transpose(pt[:, j * 128:(j + 1) * 128], self.xT[:, c, t * 128:(t + 1) * 128], self.ident[:]),
                              reads=["ident"], writes=[pk])
                    self.copy(self.evac_eng(), x[:, half * 512:(half + 1) * 512], pt[:, :], reads=[pk], writes=[xk])
                sq, sk = sqr.next(); ss, ssk = ssr.next()
                tr.op("pool", lambda e: e.tensor_tensor(out=sq[:], in0=x[:], in1=x[:], op=ALU.mult), reads=[xk], writes=[sk])
                tr.op("dve", lambda e: e.tensor_reduce(out=ss[:], in_=sq[:], axis=AX.X, op=ALU.add), reads=[sk], writes=[ssk])
                self.rstd_from_ss(ss[:], ssk, ss[:], ssk, 1.0 / D)
                tr.op("dve", lambda e: e.scalar_tensor_tensor(out=sq[:], in0=x[:], scalar=ss[:, 0:1], in1=nf[:], op0=ALU.mult, op1=ALU.mult),
                      reads=[xk, ssk, "nf"], writes=[sk])
                self.dma(self.out[(t - 2) * 128:(t - 1) * 128, :], sq[:], reads=[sk])


def _rope_tables(d):
    da = d // 2
    half = da // 2
    inv = (10000.0 ** (-np.arange(half, dtype=np.float32) / half)).astype(np.float32)
    t = np.arange(SEQ)
    rows = (t // 64).astype(np.float32)
    cols = (t % 64).astype(np.float32)
    cos = np.ones((d, T), np.float32)
    sin = np.zeros((d, T), np.float32)
    for i in range(d):
        pos = rows if i < da else cols
        ii = i % da
        ang = pos * inv[ii % half]
        cos[i, CTX:] = np.cos(ang)
        sin[i, CTX:] = np.sin(ang) * (-1.0 if ii < half else 1.0)
    return np.stack([cos, sin])


def _partner(d):
    da = d // 2
    half = da // 2
    idx = np.arange(d)
    ii = idx % da
    return np.where(ii < half, idx + half, idx - half)


def _constants():
    j = np.arange(128)[:, None]
    i = np.arange(128)[None, :]
    tri = np.stack([(j <= i), (j >= i)]).astype(np.float32)
    gmask = np.stack([np.where(i < j, 0.0, BIG),
                      np.where(j <= i, 0.0, -BIG),
                      np.where(i > j, 0.0, BIG),
                      np.where(j >= i, 0.0, -BIG)]).astype(np.float32)
    m_prev = (j >= i).astype(np.float32)
    m_next = (j <= i).astype(np.float32)
    swamask = np.stack([np.tile(m_prev, (1, 4)), np.tile(m_next, (1, 4))]).astype(np.float32)
    b32 = (j // 32 == i // 32)
    b64 = (j // 64 == i // 64)
    bmask = np.stack([b32, b64 & ~b32, ~b64]).astype(np.float32)
    selh = np.zeros((8, 8, 128), np.float32)
    for h in range(8):
        selh[h, h, :] = 1.0
    return dict(ident=np.eye(128, dtype=np.float32), tri=tri, gmask=gmask, bmask=bmask, bd64=b64.astype(np.float32), swamask=swamask, selh=selh,
                ropeM=_rope_tables(32), ropeS=_rope_tables(64))


def _fm(v, p=128):
    v = np.asarray(v, np.float32)
    lead = v.shape[:-1]
    n = v.shape[-1] // p
    return np.ascontiguousarray(np.moveaxis(v.reshape(lead + (n, p)), -1, 0))


def prepare_shared(inp, L=DEPTH):
    f = lambda k: np.asarray(inp[k], np.float32)[:L] if k not in ("norm_f",) else np.asarray(inp[k], np.float32)
    w_in = f("w_in")
    p32 = _partner(32)
    p64 = _partner(64)
    sqp = (np.arange(8)[:, None] * 64 + p64[None, :]).reshape(-1)
    skp = (np.arange(2)[:, None] * 64 + p64[None, :]).reshape(-1)
    w_fm = np.concatenate([w_in[:, :, 0:640], w_in[:, :, 640:672], w_in[:, :, 640:672][:, :, p32],
                           w_in[:, :, 672:1184], w_in[:, :, 672:1184][:, :, sqp],
                           w_in[:, :, 1184:1312], w_in[:, :, 1184:1312][:, :, skp],
                           w_in[:, :, 1440:2976], np.zeros((L, D, 64), np.float32), w_in[:, :, 3520:6592]], axis=2)
    assert w_fm.shape[2] == NFM
    w_tm = np.concatenate([w_in[:, :, 1312:1440], w_in[:, :, 2976:3488], w_in[:, :, 3488:3520]], axis=2)
    wuq = f("w_uq").reshape(L, 384, 8, 96)
    wukv = f("w_ukv").reshape(L, 256, 8, 128)
    sh = dict(
        w_mod=f("w_mod"), b_mod_fm=_fm(f("b_mod")),
        norm1_fm=_fm(f("norm1")), norm2_fm=_fm(f("norm2")), normf_rep=np.ascontiguousarray(np.broadcast_to(f("norm_f"), (128, D))),
        w_fm=np.ascontiguousarray(w_fm), w_tm=np.ascontiguousarray(w_tm),
        qnorm_fm=_fm(f("mla_q_norm")), kvnorm_fm=_fm(f("mla_kv_norm")),
        wuq_n=np.ascontiguousarray(wuq[..., 0:64].reshape(L, 384, 512)),
        wuq_r=np.ascontiguousarray(wuq[..., 64:96].reshape(L, 384, 256)),
        wuq_rp=np.ascontiguousarray(wuq[..., 64:96][..., p32].reshape(L, 384, 256)),
        wukv_k=np.ascontiguousarray(wukv[..., 0:64].reshape(L, 256, 512)),
        wukv_v=np.ascontiguousarray(wukv[..., 64:128].reshape(L, 256, 512)),
        sink_rep=np.ascontiguousarray(np.broadcast_to(f("swa_sink"), (64, L, 8))),
        gconv_fm=np.ascontiguousarray(f("gdn_conv").reshape(L, 3, 12, 128).transpose(3, 0, 2, 1)),
        alog_rep=np.ascontiguousarray(np.broadcast_to(f("gdn_a_log").reshape(L, 16), (128, L, 16))),
        dtb_rep=np.ascontiguousarray(np.broadcast_to(f("gdn_dt_bias").reshape(L, 16), (128, L, 16))),
        gnorm_rep=np.ascontiguousarray(np.broadcast_to(f("gdn_norm"), (128, L, 64))),
        w_pa=f("w_branch_a"), w_pb=f("w_branch_b"), w_pc=f("w_branch_c"), w_out=f("w_out"),
        ffn_up=f("ffn_up"), fconv_fm=np.ascontiguousarray(f("ffn_conv").reshape(L, 3, 44, 128).transpose(3, 0, 2, 1)),
        ffn_down=f("ffn_down"),
    )
    sh.update(_constants())
    return sh


def per_core(inp, b):
    x = np.asarray(inp["x"], np.float32)[b]
    ctx = np.asarray(inp["ctx"], np.float32)[b]
    c = np.asarray(inp["c"], np.float32)[b]
    cc = np.asarray(inp["c_ctx"], np.float32)
    return dict(xin=np.ascontiguousarray(np.concatenate([ctx, x], 0)),
                c_fm=np.ascontiguousarray(np.stack([_fm(c), _fm(cc)], axis=-1)))


_CACHE = {}


def kernel(**inputs):
    if "nc" not in _CACHE:
        _CACHE["nc"] = Builder().build()
    nc = _CACHE["nc"]
    shared = prepare_shared(inputs)
    in_maps = []
    for b in range(8):
        m = dict(shared)
        m.update(per_core(inputs, b))
        in_maps.append(m)
    res = run_bass_kernel_spmd(nc, in_maps, core_ids=list(range(8)))
    return np.stack([np.asarray(r["out"], np.float32) for r in res.results], axis=0)
```

```python
import numpy as np
from contextlib import ExitStack
import concourse.bass as bass
import concourse.mybir as mybir
from concourse.bass_utils import run_bass_kernel_spmd

F32 = mybir.dt.float32
BF16 = mybir.dt.bfloat16
F32R = mybir.dt.float32r
USE_F32R = False
AF = mybir.ActivationFunctionType
ALU = mybir.AluOpType
AX = mybir.AxisListType

D = 1024
SEQ = 2048
CTX = 256
T = SEQ + CTX
NT = T // 128
DEPTH = 4
EPS = 1e-6
TCH = [(0, 256), (256, 512), (768, 512), (1280, 512), (1792, 512)]
BIG = 1.0e5

CQ, CKV, KR, KRP, SQ, SQP, SK, SKP, GQ, PAD, GATE, NFM = 0, 384, 640, 672, 704, 1216, 1728, 1856, 1984, 3520, 3584, 6656
SV, ZC, AB, NTM = 0, 128, 640, 672


def tile_chunk(t):
    return 0 if t < 2 else 1 + (t - 2) // 4


class Trk:
    ISSUER = {"pe": "pe", "act": "act", "dve": "dve", "pool": "pool", "sp": "sp", "gq": "pool"}
    NSLOT = {"sp": 24, "gq": 8}

    def __init__(self, nc, es):
        self.nc = nc
        self.eng = {"pe": nc.tensor, "act": nc.scalar, "dve": nc.vector, "pool": nc.gpsimd, "sp": nc.sync}
        self.sem = {e: es.enter_context(nc.semaphore("sem_" + e)) for e in ("pe", "act", "dve", "pool")}
        for q, k in self.NSLOT.items():
            for i in range(k):
                self.sem[(q, i)] = es.enter_context(nc.semaphore(f"sem_{q}{i}"))
        self.cnt = {e: 0 for e in self.sem}
        self.nq = {q: 0 for q in self.NSLOT}
        self.lastw = {}
        self.readers = {}
        self.waited = {}
        self.n_wait = 0
        self.n_op = 0

    def _need(self, issuer, eng, dep):
        de, ds = dep
        if de == "pe" and eng == "pe":
            return
        k = (issuer, de)
        if self.waited.get(k, 0) >= ds:
            return
        self.waited[k] = ds
        self.eng[issuer].wait_ge(self.sem[de], ds)
        self.n_wait += 1

    def op(self, eng, emit, reads=(), writes=()):
        issuer = self.ISSUER[eng]
        for key in reads:
            w = self.lastw.get(key)
            if w is not None:
                self._need(issuer, eng, w)
            if isinstance(key, tuple) and key[0] == "ps":
                for rk, tok in self.readers.get(key, {}).items():
                    if rk != eng:
                        self._need(issuer, eng, tok)
        for key in writes:
            w = self.lastw.get(key)
            if w is not None:
                self._need(issuer, eng, w)
            for tok in self.readers.get(key, {}).values():
                self._need(issuer, eng, tok)
        if eng in self.NSLOT:
            sk = (eng, self.nq[eng] % self.NSLOT[eng])
            self.nq[eng] += 1
            if self.cnt[sk] > 0:
                self._need(issuer, eng, (sk, self.cnt[sk]))
            inc = 16
        else:
            sk = eng
            inc = 1
        inst = emit(self.eng[issuer])
        self.cnt[sk] += inc
        inst.then_inc(self.sem[sk], inc)
        tok = (sk, self.cnt[sk])
        for key in reads:
            self.readers.setdefault(key, {})[sk] = tok
        for key in writes:
            self.lastw[key] = tok
            self.readers[key] = {}
        self.n_op += 1
        return inst

    def barrier(self):
        for issuer in ("pe", "act", "dve", "pool", "sp"):
            for e, c in self.cnt.items():
                if c > 0:
                    self._need(issuer, "x", (e, c))
        self.lastw.clear()
        self.readers.clear()

    def finish(self, eng="sp"):
        for e, c in self.cnt.items():
            if c > 0:
                self._need(eng, "x", (e, c))


class Ring:
    def __init__(self, b, es, name, n, shape, dt=F32):
        self.tiles = [b.sb(es, f"{name}{i}", shape, dt) for i in range(n)]
        self.name = name
        self.n = n
        self.i = 0

    def next(self):
        j = self.i % self.n
        self.i += 1
        return self.tiles[j], (self.name, j)


class Builder:
    def __init__(self, n_layers=DEPTH, stop=None, taps=()):
        self.n_layers = n_layers
        self.stop = stop
        self.taps = set(taps)
        self.nc = bass.Bass("TRN2", target_bir_lowering=False)
        self.inp = {}
        self.scr = {}
        self.ev = 0
        self.use_f32r = USE_F32R

    def din(self, name, shape, dt=F32):
        self.inp[name] = self.nc.dram_tensor(name, list(shape), dt, kind="ExternalInput").ap()
        return self.inp[name]

    def dscr(self, name, shape, dt=F32, force=False):
        kind = "ExternalOutput" if (name in self.taps or force) else "Internal"
        self.scr[name] = self.nc.dram_tensor(name, list(shape), dt, kind=kind).ap()
        return self.scr[name]

    def declare(self):
        L = self.n_layers
        d = self.din
        d("xin", [T, D]); d("c_fm", [128, 8, 2])
        d("w_mod", [L, D, 6 * D]); d("b_mod_fm", [128, L, 48])
        d("norm1_fm", [128, L, 8]); d("norm2_fm", [128, L, 8]); d("normf_rep", [128, D])
        d("w_fm", [L, D, NFM]); d("w_tm", [L, D, NTM])
        d("qnorm_fm", [128, L, 3]); d("kvnorm_fm", [128, L, 2])
        d("wuq_n", [L, 384, 512]); d("wuq_r", [L, 384, 256]); d("wuq_rp", [L, 384, 256])
        d("wukv_k", [L, 256, 512]); d("wukv_v", [L, 256, 512])
        d("sink_rep", [64, L, 8])
        d("gconv_fm", [128, L, 12, 3]); d("bd64", [128, 128]); d("alog_rep", [128, L, 16]); d("dtb_rep", [128, L, 16]); d("gnorm_rep", [128, L, 64])
        d("w_pa", [L, 512, D]); d("w_pb", [L, 512, D]); d("w_pc", [L, 512, D]); d("w_out", [L, D, D])
        d("ffn_up", [L, D, 5632]); d("fconv_fm", [128, L, 44, 3]); d("ffn_down", [L, 2816, D])
        d("ident", [128, 128]); d("ropeM", [2, 32, T]); d("ropeS", [2, 64, T])
        d("tri", [2, 128, 128]); d("gmask", [4, 128, 128]); d("bmask", [3, 128, 128]); d("swamask", [2, 128, 512]); d("selh", [8, 8, 128])
        self.out = self.nc.dram_tensor("out", [SEQ, D], F32, kind="ExternalOutput").ap()
        s = self.dscr
        s("D_fm", [NFM, T]); s("D_tm", [T, NTM]); s("D_gate", [3072, T], BF16)
        s("D_ya", [512, T], BF16); s("D_yb", [512, T], BF16); s("D_yc", [512, T], BF16)
        s("D_gq", [512, T]); s("D_gk", [512, T]); s("D_ktm", [8, T, 64]); s("D_vtm", [8, T, 64])
        s("D_act", [2816, T], BF16)
        if "D_x" in self.taps:
            s("D_x", [D, T], F32, True); s("D_xn", [D, T], BF16, True); s("D_mod", [128, L * 48 * 2], F32, True); s("D_oc", [T, 512], F32, True)

    def sb(self, es, name, shape, dt=F32):
        self.uid = getattr(self, "uid", 0) + 1
        return es.enter_context(self.nc.sbuf_tensor(f"s{self.uid}_{name}", list(shape), dt))

    def evac_eng(self):
        self.ev += 1
        return "act" if self.ev % 2 else "dve"

    def copy(self, eng, out, in_, reads, writes):
        if eng == "act":
            return self.tr.op("act", lambda e: e.copy(out=out, in_=in_), reads=reads, writes=writes)
        return self.tr.op(eng, lambda e: e.tensor_copy(out=out, in_=in_), reads=reads, writes=writes)

    def dma(self, out, in_, reads=(), writes=(), q="sp"):
        return self.tr.op(q, lambda e: e.dma_start(out=out, in_=in_), reads=reads, writes=writes)

    def mm(self, out, lhsT, rhs, start, stop, reads, writes):
        return self.tr.op("pe", lambda e: e.matmul(out, lhsT, rhs, start=start, stop=stop), reads=reads, writes=writes)

    def rstd_from_ss(self, ss_ap, ss_key, r_ap, r_key, scale):
        tr = self.tr
        tr.op("act", lambda e: e.activation(out=r_ap, in_=ss_ap, func=AF.Sqrt, bias=self.cst[0:r_ap.shape[0], 0:1], scale=scale),
              reads=[ss_key, "cst"], writes=[r_key])
        tr.op("dve", lambda e: e.reciprocal(out=r_ap, in_=r_ap), reads=[r_key], writes=[r_key])

    def build(self):
        nc = self.nc
        self.declare()
        with ExitStack() as es:
            self.tr = tr = Trk(nc, es)
            self.ps = [es.enter_context(nc.psum_tensor(f"ps{i}", [128, 512], F32)) for i in range(8)]
            self.xT = self.sb(es, "xT", [128, 8, T])
            self.ident = self.sb(es, "ident", [128, 128])
            self.ones32 = self.sb(es, "ones32", [128, 128])
            self.ones16 = self.sb(es, "ones16", [128, 128], BF16)
            self.cst = self.sb(es, "cst", [128, 4])
            self.mod = self.sb(es, "mod", [128, self.n_layers, 48, 2])
            self.n1 = self.sb(es, "n1", [128, self.n_layers, 8])
            self.n2 = self.sb(es, "n2", [128, self.n_layers, 8])
            self.gs = self.sb(es, "gs", [128, 2, 8, 2])
            self.dma(self.ident[:], self.inp["ident"], writes=["ident"])
            self.dma(self.n1[:], self.inp["norm1_fm"], writes=["n1"])
            self.dma(self.n2[:], self.inp["norm2_fm"], writes=["n2"])
            tr.op("dve", lambda e: e.memset(self.ones32[:], 1.0), writes=["ones32"])
            tr.op("dve", lambda e: e.memset(self.ones16[:], 1.0), writes=["ones16"])
            tr.op("dve", lambda e: e.memset(self.cst[:, 0:1], EPS), writes=["cst"])
            tr.op("dve", lambda e: e.memset(self.cst[:, 1:2], 1.0), writes=["cst"])
            tr.op("dve", lambda e: e.memset(self.cst[:, 2:3], 0.0), writes=["cst"])
            self.phase_load_x()
            self.phase_mod()
            done = self.stop == "mod"
            for l in range(self.n_layers):
                if done:
                    break
                for name, fn in (("norm1", lambda: self.phase_adaln_proj(l)), ("mla", lambda: self.phase_mla(l)),
                                 ("swa", lambda: self.phase_swa(l)), ("gdnprep", lambda: self.phase_gdn_prep(l)),
                                 ("gdn", lambda: self.phase_gdn(l)), ("merge", lambda: self.phase_merge(l)),
                                 ("ffn", lambda: self.phase_ffn(l))):
                    fn()
                    if self.stop == f"{name}{l}" or (self.stop or "").startswith(name + "_"):
                        done = True
                        break
            if "D_x" in self.taps:
                tr.barrier()
                for ci, (t0, n) in enumerate(TCH):
                    self.dma(self.scr["D_x"].rearrange("(c p) t -> p c t", p=128)[:, :, t0:t0 + n], self.xT[:, :, t0:t0 + n])
                self.dma(self.scr["D_mod"], self.mod[:].rearrange("p l j s -> p (l j s)"))
            if not done:
                self.phase_final()
            tr.barrier()
            tr.finish("sp")
        return nc

    def phase_load_x(self):
        tr = self.tr
        with ExitStack() as es:
            ring = Ring(self, es, "xl", 2, [128, D])
            for t in range(NT):
                xt, xk = ring.next()
                self.dma(xt[:], self.inp["xin"][t * 128:(t + 1) * 128, :], writes=[xk])
                for half in range(2):
                    pk = ("ps", (2 * t + half) % 8)
                    pt = self.ps[(2 * t + half) % 8]
                    for j in range(4):
                        c = half * 4 + j
                        tr.op("pe", lambda e: e.transpose(pt[:, j * 128:(j + 1) * 128], xt[:, c * 128:(c + 1) * 128], self.ident[:]),
                              reads=[xk, "ident"], writes=[pk])
                    self.copy(self.evac_eng(), self.xT[:, half * 4:half * 4 + 4, t * 128:(t + 1) * 128],
                              pt[:].rearrange("p (c t) -> p c t", c=4), reads=[pk], writes=[("xT", tile_chunk(t))])
            tr.barrier()

    def phase_mod(self):
        tr = self.tr
        with ExitStack() as es:
            cs = self.sb(es, "cs", [128, 8, 2])
            bm = self.sb(es, "bm", [128, self.n_layers, 48])
            ring = Ring(self, es, "wm", 2, [128, 8, 512])
            self.dma(cs[:], self.inp["c_fm"], writes=["cs"])
            self.dma(bm[:], self.inp["b_mod_fm"], writes=["bm"])
            tr.op("act", lambda e: e.activation(out=cs[:], in_=cs[:], func=AF.Silu), reads=["cs"], writes=["cs"])
            for l in range(self.n_layers):
                wv = self.inp["w_mod"][l].rearrange("(c p) n -> p c n", p=128)
                pk = ("ps", l % 2)
                pt = self.ps[l % 2][:, 0:96].rearrange("p (j s) -> p j s", s=2)
                for g in range(12):
                    w, wk = ring.next()
                    self.dma(w[:], wv[:, :, g * 512:(g + 1) * 512], writes=[wk])
                    for m in range(4):
                        j = g * 4 + m
                        for c in range(8):
                            self.mm(pt[:, j, :], w[:, c, m * 128:(m + 1) * 128], cs[:, c, :], c == 0, c == 7,
                                    reads=[wk, "cs"], writes=[pk])
                for s in range(2):
                    tr.op("dve", lambda e: e.tensor_tensor(out=self.mod[:, l, :, s], in0=pt[:, :, s], in1=bm[:, l, :], op=ALU.add),
                          reads=[pk, "bm"], writes=["mod"])
            tr.barrier()

    def make_gs(self, l, which):
        nrm = self.n1 if which == 0 else self.n2
        sc0 = 8 + which * 24
        for s in range(2):
            self.tr.op("dve", lambda e: e.scalar_tensor_tensor(out=self.gs[:, which, :, s], in0=self.mod[:, l, sc0:sc0 + 8, s], scalar=1.0,
                                                             in1=nrm[:, l, :], op0=ALU.add, op1=ALU.mult),
                       reads=["mod", "n1", "n2"], writes=["gs"])

    def adaln(self, es, l, which, xn):
        tr = self.tr
        self.make_gs(l, which)
        sh0 = which * 24
        es = ExitStack()
        sqr = Ring(self, es, f"sq{which}", 2, [128, 512])
        rr = Ring(self, es, f"rr{which}", 2, [128, 512])
        tmpr = Ring(self, es, f"tm{which}", 2, [128, 512])
        for ci, (t0, n) in enumerate(TCH):
            s = 1 if ci == 0 else 0
            pk = ("ps", ci % 2)
            pt = self.ps[ci % 2]
            for c in range(8):
                sq, sk = sqr.next()
                tr.op("act", lambda e: e.activation(out=sq[:, 0:n], in_=self.xT[:, c, t0:t0 + n], func=AF.Square),
                      reads=[("xT", ci)], writes=[sk])
                self.mm(pt[:, 0:n], self.ones32[:], sq[:, 0:n], c == 0, c == 7, reads=[sk, "ones32"], writes=[pk])
            r, rk = rr.next()
            self.rstd_from_ss(pt[:, 0:n], pk, r[:, 0:n], rk, 1.0 / D)
            for c in range(8):
                tm, tk = tmpr.next()
                tr.op("dve", lambda e: e.scalar_tensor_tensor(out=tm[:, 0:n], in0=self.xT[:, c, t0:t0 + n], scalar=self.gs[:, which, c, s:s + 1],
                                                             in1=r[:, 0:n], op0=ALU.mult, op1=ALU.mult),
                      reads=[("xT", ci), "gs", rk], writes=[tk])
                tr.op("act", lambda e: e.activation(out=xn[:, c, t0:t0 + n], in_=tm[:, 0:n], func=AF.Identity,
                                                    bias=self.mod[:, l, sh0 + c, s:s + 1], scale=1.0),
                      reads=[tk, "mod"], writes=[("xn", ci)])
        keep = {k: v for k, v in tr.lastw.items() if isinstance(k, tuple) and k[0] == "xn"}
        tr.barrier()
        tr.lastw.update(keep)
        es.close()

    def phase_adaln_proj(self, l):
        tr = self.tr
        with ExitStack() as es:
            xn = self.sb(es, "xn", [128, 8, T], BF16)
            self.adaln(es, l, 0, xn)
            if "D_x" in self.taps:
                for ci, (t0, n) in enumerate(TCH):
                    self.dma(self.scr["D_xn"].rearrange("(c p) t -> p c t", p=128)[:, :, t0:t0 + n], xn[:, :, t0:t0 + n], reads=[("xn", ci)])
            wr = Ring(self, es, "wfm", 2, [128, 8, 512], BF16)
            st = Ring(self, es, "stg", 3, [128, 512])
            st16 = Ring(self, es, "stg16", 2, [128, 512], BF16)
            wv = self.inp["w_fm"][l].rearrange("(c p) n -> p c n", p=128)
            pi = 0
            for g in range(NFM // 512):
                w, wk = wr.next()
                self.dma(w[:], wv[:, :, g * 512:(g + 1) * 512], writes=[wk], q="gq")
                col = g * 512
                while col < (g + 1) * 512:
                    if col < KR:
                        mc = 128
                    elif col < SQ:
                        mc = 32
                    elif col < PAD:
                        mc = 64
                    elif col < GATE:
                        col += 64
                        continue
                    else:
                        mc = 128
                    lo = col - g * 512
                    for ci, (t0, n) in enumerate(TCH):
                        pk = ("ps", pi % 4)
                        pt = self.ps[pi % 4]
                        pi += 1
                        for c in range(8):
                            self.mm(pt[0:mc, 0:n], w[:, c, lo:lo + mc], xn[:, c, t0:t0 + n], c == 0, c == 7,
                                    reads=[wk, ("xn", ci)], writes=[pk])
                        if col >= GATE:
                            s16, sk = st16.next()
                            tr.op("act", lambda e: e.activation(out=s16[:, 0:n], in_=pt[:, 0:n], func=AF.Sigmoid), reads=[pk], writes=[sk])
                            self.dma(self.scr["D_gate"][col - GATE:col - GATE + 128, t0:t0 + n], s16[:, 0:n], reads=[sk], writes=[("D_gate", ci)])
                        else:
                            s32, sk = st.next()
                            self.copy(self.evac_eng(), s32[0:mc, 0:n], pt[0:mc, 0:n], reads=[pk], writes=[sk])
                            self.dma(self.scr["D_fm"][col:col + mc, t0:t0 + n], s32[0:mc, 0:n], reads=[sk], writes=["D_fm"])
                    col += mc
            wt = self.sb(es, "wtm", [128, 8, NTM], BF16)
            self.dma(wt[:], self.inp["w_tm"][l].rearrange("(c p) n -> p c n", p=128), writes=["wtm"], q="gq")
            stm = Ring(self, es, "stm", 2, [128, NTM])
            for t in range(NT):
                ci = tile_chunk(t)
                pa, pb = self.ps[4 + (t % 2) * 2], self.ps[5 + (t % 2) * 2]
                ka, kb = ("ps", 4 + (t % 2) * 2), ("ps", 5 + (t % 2) * 2)
                for c in range(8):
                    self.mm(pa[:, 0:512], xn[:, c, t * 128:(t + 1) * 128], wt[:, c, 0:512], c == 0, c == 7, reads=["wtm", ("xn", ci)], writes=[ka])
                for c in range(8):
                    self.mm(pb[:, 0:160], xn[:, c, t * 128:(t + 1) * 128], wt[:, c, 512:672], c == 0, c == 7, reads=["wtm", ("xn", ci)], writes=[kb])
                s, sk = stm.next()
                self.copy("act", s[:, 0:512], pa[:, 0:512], reads=[ka], writes=[sk])
                self.copy("dve", s[:, 512:672], pb[:, 0:160], reads=[kb], writes=[sk])
                self.dma(self.scr["D_tm"][t * 128:(t + 1) * 128, :], s[:], reads=[sk], writes=["D_tm"])
            tr.barrier()

    def rms_fm_from_dram(self, es, row0, nchunk, gain, dst, tag):
        tr = self.tr
        ld = Ring(self, es, f"ld{tag}", 2, [128, nchunk, 512])
        sqr = Ring(self, es, f"sqn{tag}", 2, [128, 512])
        rr = Ring(self, es, f"rn{tag}", 2, [128, 512])
        src = self.scr["D_fm"][row0:row0 + nchunk * 128, :].rearrange("(c p) t -> p c t", p=128)
        for ci, (t0, n) in enumerate(TCH):
            x, xk = ld.next()
            self.dma(x[:, :, 0:n], src[:, :, t0:t0 + n], reads=["D_fm"], writes=[xk])
            pk = ("ps", 6 + ci % 2)
            pt = self.ps[6 + ci % 2]
            for c in range(nchunk):
                sq, sk = sqr.next()
                tr.op("act", lambda e: e.activation(out=sq[:, 0:n], in_=x[:, c, 0:n], func=AF.Square), reads=[xk], writes=[sk])
                self.mm(pt[:, 0:n], self.ones32[:], sq[:, 0:n], c == 0, c == nchunk - 1, reads=[sk, "ones32"], writes=[pk])
            r, rk = rr.next()
            self.rstd_from_ss(pt[:, 0:n], pk, r[:, 0:n], rk, 1.0 / (nchunk * 128))
            for c in range(nchunk):
                tr.op("dve", lambda e: e.scalar_tensor_tensor(out=dst[:, c, t0:t0 + n], in0=x[:, c, 0:n], scalar=gain[:, c:c + 1],
                                                             in1=r[:, 0:n], op0=ALU.mult, op1=ALU.mult),
                      reads=[xk, rk, "gain" + tag], writes=[(tag, ci)])

    def rope_from(self, src_ap, src_key, srcp_ap, srcp_key, cos_ap, sin_ap, dst_ap, dst_key, tmpa, tmpb, ka, kb):
        tr = self.tr
        tr.op("dve", lambda e: e.tensor_tensor(out=tmpa, in0=src_ap, in1=cos_ap, op=ALU.mult), reads=[src_key, "rope"], writes=[ka])
        tr.op("pool" if srcp_key[0] != "ps" else "dve", lambda e: e.tensor_tensor(out=tmpb, in0=srcp_ap, in1=sin_ap, op=ALU.mult),
              reads=[srcp_key, "rope"], writes=[kb])
        tr.op("dve", lambda e: e.tensor_tensor(out=dst_ap, in0=tmpa, in1=tmpb, op=ALU.add), reads=[ka, kb], writes=[dst_key])

    def phase_mla(self, l):
        tr = self.tr
        scale = 96.0 ** -0.5
        with ExitStack() as es:
            qg = self.sb(es, "qg", [128, 3]); kg = self.sb(es, "kg", [128, 2])
            self.dma(qg[:], self.inp["qnorm_fm"][:, l, :], writes=["gaincqn"])
            self.dma(kg[:], self.inp["kvnorm_fm"][:, l, :], writes=["gainckvn"])
            cqn = self.sb(es, "cqn", [128, 3, T], BF16)
            ckvn = self.sb(es, "ckvn", [128, 2, T], BF16)
            cosM = self.sb(es, "cosM", [32, T]); sinM = self.sb(es, "sinM", [32, T])
            self.dma(cosM[:], self.inp["ropeM"][0], writes=["rope"])
            self.dma(sinM[:], self.inp["ropeM"][1], writes=["rope"])
            KrT = self.sb(es, "KrT", [128, T], BF16)
            tr.op("pool", lambda e: e.memset(KrT[32:64, :], 0.0), writes=[("KrT", ci) for ci in range(5)])
            tr.op("pool", lambda e: e.memset(KrT[64:128, :], 0.0), writes=[("KrT", ci) for ci in range(5)])
            Vt = self.sb(es, "Vt", [128, NT, 576], BF16)
            wqn = self.sb(es, "wqn", [128, 3, 512], BF16); wqr = self.sb(es, "wqr", [128, 3, 256], BF16)
            wqp = self.sb(es, "wqp", [128, 3, 256], BF16)
            wkn = self.sb(es, "wkn", [128, 2, 512], BF16); wkv = self.sb(es, "wkv", [128, 2, 512], BF16)
            for wt_, nm in ((wqn, "wuq_n"), (wqr, "wuq_r"), (wqp, "wuq_rp"), (wkn, "wukv_k"), (wkv, "wukv_v")):
                self.dma(wt_[:], self.inp[nm][l].rearrange("(c p) n -> p c n", p=128), writes=[nm], q="gq")
            with ExitStack() as es2:
                self.rms_fm_from_dram(es2, CQ, 3, qg, cqn, "cqn")
                self.rms_fm_from_dram(es2, CKV, 2, kg, ckvn, "ckvn")
                kl = Ring(self, es2, "krl", 2, [32, 2, 512])
                ta = Ring(self, es2, "kta", 2, [32, 512]); tb = Ring(self, es2, "ktb", 2, [32, 512])
                for ci, (t0, n) in enumerate(TCH):
                    k2, kk = kl.next()
                    self.dma(k2[:, :, 0:n], self.scr["D_fm"][KR:KR + 64, :].rearrange("(a p) t -> p a t", p=32)[:, :, t0:t0 + n],
                             reads=["D_fm"], writes=[kk])
                    a_, ak = ta.next(); b_, bk = tb.next()
                    self.rope_from(k2[:, 0, 0:n], kk, k2[:, 1, 0:n], kk, cosM[:, t0:t0 + n], sinM[:, t0:t0 + n],
                                   KrT[0:32, t0:t0 + n], ("KrT", ci), a_[:, 0:n], b_[:, 0:n], ak, bk)
                for t in range(NT):
                    tr.op("pool", lambda e: e.memset(Vt[:, t, 512:576], 0.0), writes=[("Vt", t)])
                for t in range(NT):
                    ci = tile_chunk(t)
                    pk = ("ps", 4 + t % 2); pt = self.ps[4 + t % 2]
                    for c in range(2):
                        self.mm(pt[:, :], ckvn[:, c, t * 128:(t + 1) * 128], wkv[:, c, :], c == 0, c == 1, reads=[("ckvn", ci), "wukv_v"], writes=[pk])
                    self.copy(self.evac_eng(), Vt[:, t, 0:512], pt[:, :], reads=[pk], writes=[("Vt", t)])
                tr.barrier()
            if self.stop == "mla_prep":
                return
            with ExitStack() as es2:
                qnr = Ring(self, es2, "QnT", 2, [128, T], BF16); qrr = Ring(self, es2, "QrT", 2, [128, T], BF16)
                knr = Ring(self, es2, "KnT", 2, [128, T], BF16)
                for rg, lo in ((qnr, 64), (qrr, 32), (knr, 64)):
                    for ti, tl in enumerate(rg.tiles):
                        for p0_, p1_ in (((32, 64), (64, 128)) if lo == 32 else ((64, 128),)):
                            tr.op("pool", lambda e: e.memset(tl[p0_:p1_, :], 0.0), writes=[((rg.name, ti), ci) for ci in range(5)])
                ta = Ring(self, es2, "qta", 2, [32, 512]); tb = Ring(self, es2, "qtb", 2, [32, 512])
                pr = Ring(self, es2, "Pexp", 4, [128, 512], BF16)
                rdr = Ring(self, es2, "rden", 2, [64, 512])
                yar = Ring(self, es2, "yah", 2, [64, T], BF16)
                for h in range(8):
                    Qn, qnk = qnr.next(); Qr, qrk = qrr.next(); Kn, knk = knr.next()
                    for ci, (t0, n) in enumerate(TCH):
                        p0, p1, p2, p3 = self.ps[4], self.ps[5], self.ps[6], self.ps[7]
                        for c in range(3):
                            self.mm(p0[0:64, 0:n], wqn[:, c, h * 64:(h + 1) * 64], cqn[:, c, t0:t0 + n], c == 0, c == 2, reads=["wuq_n", ("cqn", ci)], writes=[("ps", 4)])
                        for c in range(3):
                            self.mm(p1[0:32, 0:n], wqr[:, c, h * 32:(h + 1) * 32], cqn[:, c, t0:t0 + n], c == 0, c == 2, reads=["wuq_r", ("cqn", ci)], writes=[("ps", 5)])
                        for c in range(3):
                            self.mm(p2[0:32, 0:n], wqp[:, c, h * 32:(h + 1) * 32], cqn[:, c, t0:t0 + n], c == 0, c == 2, reads=["wuq_rp", ("cqn", ci)], writes=[("ps", 6)])
                        for c in range(2):
                            self.mm(p3[0:64, 0:n], wkn[:, c, h * 64:(h + 1) * 64], ckvn[:, c, t0:t0 + n], c == 0, c == 1, reads=["wukv_k", ("ckvn", ci)], writes=[("ps", 7)])
                        self.copy("act", Qn[0:64, t0:t0 + n], p0[0:64, 0:n], reads=[("ps", 4)], writes=[(qnk, ci)])
                        self.copy("act", Kn[0:64, t0:t0 + n], p3[0:64, 0:n], reads=[("ps", 7)], writes=[(knk, ci)])
                        a_, ak = ta.next(); b_, bk = tb.next()
                        self.rope_from(p1[0:32, 0:n], ("ps", 5), p2[0:32, 0:n], ("ps", 6), cosM[:, t0:t0 + n], sinM[:, t0:t0 + n],
                                       Qr[0:32, t0:t0 + n], (qrk, ci), a_[:, 0:n], b_[:, 0:n], ak, bk)
                    if self.stop == "mla_proj":
                        break
                    ya, yk = yar.next()
                    for ci, (t0, n) in enumerate(TCH):
                        if self.stop == "mla_ctx" and ci > 0:
                            break
                        kts = [0, 1] if ci == 0 else list(range(NT))
                        ob, db = (3, 2) if ci % 2 == 0 else (6, 5)
                        po, pd = self.ps[ob], self.ps[db]
                        def s_stage(i, kt):
                            kci = tile_chunk(kt)
                            bank = (0, 1, 4)[i % 3]
                            sk_ = ("ps", bank); psn = self.ps[bank]
                            self.mm(psn[:, 0:n], Kn[:, kt * 128:(kt + 1) * 128], Qn[:, t0:t0 + n], True, False,
                                    reads=[(knk, kci), (qnk, ci)], writes=[sk_])
                            self.mm(psn[:, 0:n], KrT[:, kt * 128:(kt + 1) * 128], Qr[:, t0:t0 + n], False, True,
                                    reads=[("KrT", kci), (qrk, ci)], writes=[sk_])
                            P, Pk = pr.next()
                            tr.op("act", lambda e: e.activation(out=P[:, 0:n], in_=psn[:, 0:n], func=AF.Exp, scale=scale), reads=[sk_], writes=[Pk])
                            return P, Pk

                        def pv_stage(i, kt, P, Pk):
                            self.mm(po[:, 0:n], Vt[:, kt, h * 64:h * 64 + 128], P[:, 0:n], i == 0, i == len(kts) - 1, reads=[("Vt", kt), Pk], writes=[("ps", ob)])
                            self.mm(pd[:, 0:n], self.ones16[:, :], P[:, 0:n], i == 0, i == len(kts) - 1, reads=["ones16", Pk], writes=[("ps", db)])

                        pend = []
                        for i, kt in enumerate(kts):
                            pend.append((i, kt) + s_stage(i, kt))
                            if len(pend) > 2:
                                pv_stage(*pend.pop(0))
                        while pend:
                            pv_stage(*pend.pop(0))
                        rd, rk = rdr.next()
                        tr.op("dve", lambda e: e.reciprocal(out=rd[:, 0:n], in_=pd[0:64, 0:n]), reads=[("ps", db)], writes=[rk])
                        tr.op("dve", lambda e: e.tensor_tensor(out=ya[:, t0:t0 + n], in0=po[0:64, 0:n], in1=rd[:, 0:n], op=ALU.mult),
                              reads=[("ps", ob), rk], writes=[yk])
                    self.dma(self.scr["D_ya"][h * 64:(h + 1) * 64, :], ya[:], reads=[yk], writes=["D_ya"])
                tr.barrier()

    def phase_swa(self, l):
        tr = self.tr
        with ExitStack() as es:
            QT = self.sb(es, "sQT", [128, 8, T], BF16)
            KT = self.sb(es, "sKT", [128, 2, T], BF16)
            tr.op("pool", lambda e: e.memset(QT[64:128, :, :], 0.0), writes=[("sQK", j) for j in range(8)])
            tr.op("pool", lambda e: e.memset(KT[64:128, :, :], 0.0), writes=[("sQK", 8), ("sQK", 9)])
            Vs = self.sb(es, "sVs", [128, NT, 128], BF16)
            cosS = self.sb(es, "cosS", [64, T]); sinS = self.sb(es, "sinS", [64, T])
            msk = self.sb(es, "smask", [128, 2, 512], BF16)
            snk = self.sb(es, "snk", [64, 8])
            self.dma(cosS[:], self.inp["ropeS"][0], writes=["rope"])
            self.dma(sinS[:], self.inp["ropeS"][1], writes=["rope"])
            self.dma(msk[:], self.inp["swamask"].rearrange("a p n -> p a n"), writes=["smask"], q="gq")
            self.dma(snk[:], self.inp["sink_rep"][:, l, :], writes=["snk"])
            tr.op("act", lambda e: e.activation(out=snk[:], in_=snk[:], func=AF.Exp), reads=["snk"], writes=["snk"])
            self.dma(Vs[:], self.scr["D_tm"][:, SV:SV + 128].rearrange("(t p) c -> p t c", p=128), reads=["D_tm"], writes=["sVs"], q="gq")
            with ExitStack() as es2:
                ld = Ring(self, es2, "sld", 1, [64, T]); ldp = Ring(self, es2, "sldp", 1, [64, T])
                ta = Ring(self, es2, "sta", 1, [64, T]); tb = Ring(self, es2, "stb", 1, [64, T])
                for j in range(10):
                    r0, rp = (SQ + j * 64, SQP + j * 64) if j < 8 else (SK + (j - 8) * 64, SKP + (j - 8) * 64)
                    dst = QT[0:64, j, :] if j < 8 else KT[0:64, j - 8, :]
                    x, xk = ld.next(); xp, xpk = ldp.next()
                    self.dma(x[:], self.scr["D_fm"][r0:r0 + 64, :], reads=["D_fm"], writes=[xk])
                    self.dma(xp[:], self.scr["D_fm"][rp:rp + 64, :], reads=["D_fm"], writes=[xpk])
                    a_, ak = ta.next(); b_, bk = tb.next()
                    self.rope_from(x[:], xk, xp[:], xpk, cosS[:], sinS[:], dst, ("sQK", j), a_[:], b_[:], ak, bk)
                tr.barrier()
            with ExitStack() as es2:
                pr = Ring(self, es2, "sP", 3, [128, 512], BF16)
                dn = Ring(self, es2, "sdn", 2, [64, 512])
                yr = Ring(self, es2, "syb", 2, [64, 4, 128], BF16)
                ybv = self.scr["D_yb"].rearrange("(h d) t -> d h t", d=64)
                it = 0
                for qt in range(NT):
                    if qt < 2:
                        kts = [(0, None), (1, None)]
                    else:
                        kts = [(0, None), (1, None)]
                        if qt > 2:
                            kts.append((qt - 1, 0))
                        kts.append((qt, None))
                        if qt < NT - 1:
                            kts.append((qt + 1, 1))
                    for g in range(2):
                        ob, db = (3, 2) if g == 0 else (6, 5)
                        po, pd = self.ps[ob], self.ps[db]
                        rhs = QT[:, 4 * g:4 * g + 4, qt * 128:(qt + 1) * 128]
                        def s_stage(i, kt, mk):
                            nonlocal it
                            sk_ = ("ps", it % 2); psn = self.ps[it % 2]; it += 1
                            self.mm(psn[:, :].rearrange("p (h q) -> p h q", h=4), KT[:, g, kt * 128:(kt + 1) * 128], rhs, True, True,
                                    reads=[("sQK", 8 + g)] + [("sQK", 4 * g + hh) for hh in range(4)], writes=[sk_])
                            P, Pk = pr.next()
                            tr.op("act", lambda e: e.activation(out=P[:], in_=psn[:, :], func=AF.Exp, scale=0.125), reads=[sk_], writes=[Pk])
                            if mk is not None:
                                tr.op("dve", lambda e: e.tensor_tensor(out=P[:], in0=P[:], in1=msk[:, mk, :], op=ALU.mult), reads=[Pk, "smask"], writes=[Pk])
                            return P, Pk

                        def pv_stage(i, kt, P, Pk):
                            self.mm(po[0:64, :], Vs[:, kt, g * 64:(g + 1) * 64], P[:], i == 0, i == len(kts) - 1, reads=["sVs", Pk], writes=[("ps", ob)])
                            self.mm(pd[:, :], self.ones16[:, :], P[:], i == 0, i == len(kts) - 1, reads=["ones16", Pk], writes=[("ps", db)])

                        prev = None
                        for i, (kt, mk) in enumerate(kts):
                            cur = (i, kt) + s_stage(i, kt, mk)
                            if prev is not None:
                                pv_stage(*prev)
                            prev = cur
                        pv_stage(*prev)
                        d_, dk = dn.next()
                        for hh in range(4):
                            tr.op("dve", lambda e: e.tensor_scalar(out=d_[:, hh * 128:(hh + 1) * 128], in0=pd[0:64, hh * 128:(hh + 1) * 128],
                                                                  scalar1=snk[:, 4 * g + hh:4 * g + hh + 1], scalar2=None, op0=ALU.add),
                                  reads=[("ps", db), "snk"], writes=[dk])
                        tr.op("dve", lambda e: e.reciprocal(out=d_[:], in_=d_[:]), reads=[dk], writes=[dk])
                        y, yk = yr.next()
                        tr.op("dve", lambda e: e.tensor_tensor(out=y[:].rearrange("p h q -> p (h q)"), in0=po[0:64, :], in1=d_[:], op=ALU.mult),
                              reads=[("ps", ob), dk], writes=[yk])
                        self.dma(ybv[:, 4 * g:4 * g + 4, qt * 128:(qt + 1) * 128], y[:], reads=[yk], writes=["D_yb"])
                tr.barrier()

    def phase_gdn_prep(self, l):
        tr = self.tr
        with ExitStack() as es:
            cw = self.sb(es, "gcw", [128, 12, 3])
            bd = self.sb(es, "gbd64", [128, 128])
            self.dma(cw[:], self.inp["gconv_fm"][:, l, :, :], writes=["gcw"])
            self.dma(bd[:], self.inp["bd64"], writes=["gbd64"])
            ld = Ring(self, es, "gld", 2, [128, T]); yr = Ring(self, es, "gy", 2, [128, T])
            sqr = Ring(self, es, "gsq", 2, [128, 512]); rr = Ring(self, es, "grn", 2, [128, 512])
            tmr = Ring(self, es, "gtm", 2, [128, NT, 128])
            xs = {}

            def load(jj):
                x_, xk_ = ld.next()
                self.dma(x_[:], self.scr["D_fm"][GQ + jj * 128:GQ + (jj + 1) * 128, :], reads=["D_fm"], writes=[xk_])
                xs[jj] = (x_, xk_)

            load(0)
            for j in range(12):
                x, xk = xs.pop(j)
                if j + 1 < 12:
                    load(j + 1)
                y, yk = yr.next()
                tr.op("act", lambda e: e.activation(out=y[:], in_=x[:], func=AF.Identity, scale=cw[:, j, 1:2]), reads=[xk, "gcw"], writes=[yk])
                for (o0, o1, i0, i1, k) in ((1, CTX, 0, CTX - 1, 0), (CTX + 1, T, CTX, T - 1, 0), (0, CTX - 1, 1, CTX, 2), (CTX, T - 1, CTX + 1, T, 2)):
                    tr.op("dve", lambda e: e.scalar_tensor_tensor(out=y[:, o0:o1], in0=x[:, i0:i1], scalar=cw[:, j, k:k + 1], in1=y[:, o0:o1],
                                                                 op0=ALU.mult, op1=ALU.add), reads=[xk, yk, "gcw"], writes=[yk])
                tr.op("act", lambda e: e.activation(out=y[:], in_=y[:], func=AF.Silu), reads=[yk], writes=[yk])
                if j < 8:
                    for ci, (t0, n) in enumerate(TCH):
                        sq, sk = sqr.next()
                        tr.op("act", lambda e: e.activation(out=sq[:, 0:n], in_=y[:, t0:t0 + n], func=AF.Square), reads=[yk], writes=[sk])
                        pk = ("ps", ci % 2); pt = self.ps[ci % 2]
                        self.mm(pt[:, 0:n], bd[:], sq[:, 0:n], True, True, reads=[sk, "gbd64"], writes=[pk])
                        r, rk = rr.next()
                        self.rstd_from_ss(pt[:, 0:n], pk, r[:, 0:n], rk, 1.0)
                        tr.op("dve", lambda e: e.scalar_tensor_tensor(out=y[:, t0:t0 + n], in0=y[:, t0:t0 + n], scalar=(0.125 if j < 4 else 1.0),
                                                                     in1=r[:, 0:n], op0=ALU.mult, op1=ALU.mult), reads=[yk, rk], writes=[yk])
                    dst = self.scr["D_gq"] if j < 4 else self.scr["D_gk"]
                    self.dma(dst[(j % 4) * 128:(j % 4 + 1) * 128, :], y[:], reads=[yk], writes=["D_gqk"])
                if j >= 4:
                    tm, tk = tmr.next()
                    for t4 in range(0, NT, 4):
                        nt = min(4, NT - t4)
                        pk = ("ps", 2 + (t4 // 4) % 2); pt = self.ps[2 + (t4 // 4) % 2]
                        for tt in range(nt):
                            t = t4 + tt
                            tr.op("pe", lambda e: e.transpose(pt[:, tt * 128:(tt + 1) * 128], y[:, t * 128:(t + 1) * 128], self.ident[:]),
                                  reads=[yk, "ident"], writes=[pk])
                        self.copy(self.evac_eng(), tm[:, t4:t4 + nt, :], pt[:, 0:nt * 128].rearrange("p (t d) -> p t d", d=128), reads=[pk], writes=[tk])
                    dst = self.scr["D_ktm"] if j < 8 else self.scr["D_vtm"]
                    c = (j % 4) * 2
                    for hh in range(2):
                        self.dma(dst[c + hh].rearrange("(t p) d -> p t d", p=128), tm[:, :, hh * 64:(hh + 1) * 64], reads=[tk], writes=["D_kvtm"])
            tr.barrier()

    def phase_gdn(self, l):
        tr = self.tr
        with ExitStack() as es:
            oacc = self.sb(es, "oacc", [128, NT, 512])
            ab = self.sb(es, "gab", [128, NT, 32])
            g = self.sb(es, "gg", [128, NT, 16]); beta = self.sb(es, "gbeta", [128, NT, 16]); nbeta = self.sb(es, "gnbeta", [128, NT, 16])
            gam = self.sb(es, "ggam", [128, NT, 16]); eg = self.sb(es, "geg", [128, NT, 16]); bg = self.sb(es, "gbg", [128, NT, 16])
            edl = self.sb(es, "gedl", [128, NT, 16]); tmp = self.sb(es, "gtmp", [128, NT, 16])
            alog = self.sb(es, "galog", [128, 16]); dtb = self.sb(es, "gdtb", [128, 16])
            tri = self.sb(es, "gtri", [128, 2, 128]); gm = self.sb(es, "gmask", [128, 4, 128]); sel = self.sb(es, "gsel", [128, 8, 128])
            S = self.sb(es, "gS", [128, 2, 8, 64])
            self.dma(ab[:], self.scr["D_tm"][:, AB:AB + 32].rearrange("(t p) c -> p t c", p=128), reads=["D_tm"], writes=["gab"])
            self.dma(alog[:], self.inp["alog_rep"][:, l, :], writes=["galog"])
            self.dma(dtb[:], self.inp["dtb_rep"][:, l, :], writes=["gdtb"])
            self.dma(tri[:], self.inp["tri"].rearrange("a p n -> p a n"), writes=["gtri"])
            self.dma(gm[:], self.inp["gmask"].rearrange("a p n -> p a n"), writes=["gmaskc"])
            tr.op("pool", lambda e: e.memset(sel[:], 0.0), writes=["gsel"])
            self.dma(sel[0:8, :, :], self.inp["selh"].rearrange("h u n -> u h n"), writes=["gsel"])
            bmk = self.sb(es, "gbmask", [128, 3, 128])
            self.dma(bmk[:], self.inp["bmask"].rearrange("a p n -> p a n"), writes=["gbmask"])
            tr.op("dve", lambda e: e.memset(S[:], 0.0), writes=[(f"S{d}", h) for d in range(2) for h in range(8)])
            tr.op("act", lambda e: e.activation(out=alog[:], in_=alog[:], func=AF.Exp), reads=["galog"], writes=["galog"])
            for t in range(NT):
                tr.op("dve", lambda e: e.tensor_tensor(out=tmp[:, t, :], in0=ab[:, t, 0:16], in1=dtb[:], op=ALU.add), reads=["gab", "gdtb"], writes=["gtmp"])
            tr.op("act", lambda e: e.activation(out=tmp[:], in_=tmp[:], func=AF.Exp), reads=["gtmp"], writes=["gtmp"])
            tr.op("act", lambda e: e.activation(out=tmp[:], in_=tmp[:], func=AF.Ln, bias=self.cst[:, 1:2], scale=1.0), reads=["gtmp", "cst"], writes=["gtmp"])
            for t in range(NT):
                tr.op("dve", lambda e: e.scalar_tensor_tensor(out=g[:, t, :], in0=tmp[:, t, :], scalar=-1.0, in1=alog[:], op0=ALU.mult, op1=ALU.mult),
                      reads=["gtmp", "galog"], writes=["gg"])
            tr.op("act", lambda e: e.activation(out=beta[:], in_=ab[:, :, 16:32], func=AF.Sigmoid), reads=["gab"], writes=["gbeta"])
            tr.op("dve", lambda e: e.tensor_scalar(out=nbeta[:], in0=beta[:], scalar1=-1.0, scalar2=None, op0=ALU.mult), reads=["gbeta"], writes=["gnbeta"])
            pg = self.ps[0][:, 0:NT * 16].rearrange("p (t u) -> p t u", u=16)
            pt_ = self.ps[1][:, 0:NT * 16].rearrange("p (t u) -> p t u", u=16)
            for t in range(NT):
                for d in range(2):
                    self.mm(pg[:, t, d * 8:(d + 1) * 8], tri[:, d, :], g[:, t, d * 8:(d + 1) * 8], True, True, reads=["gtri", "gg"], writes=[("ps", 0)])
                self.mm(pt_[:, t, :], self.ones32[:], g[:, t, :], True, True, reads=["ones32", "gg"], writes=[("ps", 1)])
            self.copy("dve", gam[:], pg, reads=[("ps", 0)], writes=["ggam"])
            tr.op("dve", lambda e: e.tensor_tensor(out=edl[:], in0=pt_, in1=gam[:], op=ALU.subtract), reads=[("ps", 1), "ggam"], writes=["gedl"])
            tr.op("act", lambda e: e.activation(out=edl[:], in_=edl[:], func=AF.Exp), reads=["gedl"], writes=["gedl"])
            tr.op("act", lambda e: e.activation(out=eg[:], in_=gam[:], func=AF.Exp), reads=["ggam"], writes=["geg"])
            tr.op("dve", lambda e: e.tensor_tensor(out=bg[:], in0=beta[:], in1=eg[:], op=ALU.mult), reads=["gbeta", "geg"], writes=["gbg"])
            if self.stop == "gdn_gates":
                tr.barrier()
                return
            es_scan = ExitStack()
            NSLOT = 4
            R1 = lambda nm, shp, n=1: [Ring(self, es_scan, f"g{nm}{sl_}", n, shp) for sl_ in range(NSLOT)]
            qTr = R1("iq", [128, 128]); kTr = R1("ik", [128, 128]); ktmr = R1("iktm", [128, 64]); vtmr = R1("ivtm", [128, 64]); kdr = R1("kd", [128, 64])
            t1r = R1("t1", [128, 128]); t2r = R1("t2", [128, 128]); egr = R1("egb", [64, 128])
            a0r = R1("a0", [128, 256]); pxr_ = R1("px", [128, 256], 2); ptr__ = R1("pt", [128, 128], 2); lr = R1("l", [128, 2, 128])
            ybr_ = R1("yb", [128, 128], 5); qkr = R1("qk", [128, 128]); wtr = R1("wt", [128, 128]); qgr = R1("qg", [128, 128]); vnr = R1("vn", [128, 64])
            gTr = Ring(self, es_scan, "ggT", 3, [128, 128])
            for rg in qTr + kTr + wtr + qgr + [gTr]:
                for ti, tl in enumerate(rg.tiles):
                    tr.op("pool", lambda e: e.memset(tl[:], 0.0), writes=[(rg.name, ti)])
            gqv = self.scr["D_gq"].rearrange("(h d) t -> h d t", d=64)
            gkv = self.scr["D_gk"].rearrange("(h d) t -> h d t", d=64)

            def unit(d, t, h, s_, gT, gTk):
                u = d * 8 + h
                Sk = f"S{d}"
                last = 127 if d == 0 else 0
                sl = slice(t * 128, (t + 1) * 128)
                pA, pAk, pB, pBk = self.ps[2 * s_], ("ps", 2 * s_), self.ps[2 * s_ + 1], ("ps", 2 * s_ + 1)
                qT, qk_ = qTr[s_].next(); kT, kk_ = kTr[s_].next(); ktm, ktk = ktmr[s_].next(); vtm, vtk = vtmr[s_].next()
                self.dma(qT[0:64, :], gqv[h, :, sl], reads=["D_gqk"], writes=[qk_])
                self.dma(kT[0:64, :], gkv[h, :, sl], reads=["D_gqk"], writes=[kk_])
                self.dma(ktm[:], self.scr["D_ktm"][h, sl, :], reads=["D_kvtm"], writes=[ktk])
                self.dma(vtm[:], self.scr["D_vtm"][h, sl, :], reads=["D_kvtm"], writes=[vtk])
                yield
                self.mm(pA[:, 0:128], sel[:, h, :], gT[:], True, True, reads=["gsel", gTk], writes=[pAk])
                self.mm(pB[:, 0:128], kT[:], kT[:], True, True, reads=[kk_], writes=[pBk])
                kd, kdk = kdr[s_].next()
                tr.op("act", lambda e: e.activation(out=kd[:], in_=ktm[:], func=AF.Identity, scale=edl[:, t, u:u + 1]), reads=[ktk, "gedl"], writes=[kdk])
                a0, a0k = a0r[s_].next()
                tr.op("act", lambda e: e.activation(out=a0[:, 128:192], in_=vtm[:], func=AF.Identity, scale=beta[:, t, u:u + 1]),
                      reads=[vtk, "gbeta"], writes=[(a0k, "x")])
                tr.op("act", lambda e: e.activation(out=a0[:, 192:256], in_=ktm[:], func=AF.Identity, scale=bg[:, t, u:u + 1]),
                      reads=[ktk, "gbg"], writes=[(a0k, "x")])
                yield
                t1, t1k = t1r[s_].next(); t2, t2k = t2r[s_].next(); egb, egk = egr[s_].next()
                tr.op("dve", lambda e: e.scalar_tensor_tensor(out=t1[:], in0=pA[:, 0:128], scalar=gam[:, t, u:u + 1], in1=gm[:, 2 * d, :],
                                                             op0=ALU.subtract, op1=ALU.add), reads=[pAk, "ggam", "gmaskc"], writes=[t1k])
                tr.op("dve", lambda e: e.scalar_tensor_tensor(out=t2[:], in0=pA[:, 0:128], scalar=gam[:, t, u:u + 1], in1=gm[:, 2 * d + 1, :],
                                                             op0=ALU.subtract, op1=ALU.add), reads=[pAk, "ggam", "gmaskc"], writes=[t2k])
                tr.op("dve", lambda e: e.tensor_copy(out=egb[:], in_=pA[0:64, 0:128]), reads=[pAk], writes=[egk])
                yield
                tr.op("act", lambda e: e.activation(out=t1[:], in_=t1[:], func=AF.Exp, scale=-1.0), reads=[t1k], writes=[t1k])
                tr.op("act", lambda e: e.activation(out=t2[:], in_=t2[:], func=AF.Exp), reads=[t2k], writes=[t2k])
                tr.op("act", lambda e: e.activation(out=egb[:], in_=egb[:], func=AF.Exp), reads=[egk], writes=[egk])
                self.mm(pA[:, 0:128], kT[:], qT[:], True, True, reads=[kk_, qk_], writes=[pAk])
                yield
                tr.op("dve", lambda e: e.scalar_tensor_tensor(out=a0[:, 0:128], in0=pB[:, 0:128], scalar=nbeta[:, t, u:u + 1], in1=t1[:],
                                                             op0=ALU.mult, op1=ALU.mult), reads=[pBk, "gnbeta", t1k], writes=[(a0k, "p")])
                qk, qkk = qkr[s_].next()
                tr.op("dve", lambda e: e.tensor_tensor(out=qk[:], in0=pA[:, 0:128], in1=t2[:], op=ALU.mult), reads=[pAk, t2k], writes=[qkk])
                qg, qgk = qgr[s_].next()
                tr.op("pool", lambda e: e.tensor_tensor(out=qg[0:64, :], in0=qT[0:64, :], in1=egb[:], op=ALU.mult), reads=[qk_, egk], writes=[qgk])
                yield
                tr.op("pe", lambda e: e.transpose(pB[:, 0:128], a0[:, 0:128], self.ident[:]), reads=[(a0k, "p"), "ident"], writes=[pBk])
                px, pxk = pxr_[s_].next(); pT, pTk = ptr__[s_].next(); lt, ltk = lr[s_].next()
                tr.op("pool", lambda e: e.tensor_tensor(out=pT[:], in0=a0[:, 0:128], in1=bmk[:, 0, :], op=ALU.mult), reads=[(a0k, "p"), "gbmask"], writes=[pTk])
                self.copy("act", px[:, 128:256], self.ident[:], reads=["ident"], writes=[(pxk, "x")])
                yield
                tr.op("dve", lambda e: e.tensor_tensor(out=px[:, 0:128], in0=pB[:, 0:128], in1=bmk[:, 0, :], op=ALU.mult), reads=[pBk, "gbmask"], writes=[(pxk, "p")])
                tr.op("dve", lambda e: e.tensor_tensor(out=lt[:, 0, :], in0=pB[:, 0:128], in1=bmk[:, 1, :], op=ALU.mult), reads=[pBk, "gbmask"], writes=[(ltk, 0)])
                tr.op("dve", lambda e: e.tensor_tensor(out=lt[:, 1, :], in0=pB[:, 0:128], in1=bmk[:, 2, :], op=ALU.mult), reads=[pBk, "gbmask"], writes=[(ltk, 1)])
                yield
                for lev in range(5):
                    if lev < 4:
                        self.mm(pA[:, 0:256], pT[:].bitcast(F32R) if self.use_f32r else pT[:], px[:, 0:256].bitcast(F32R) if self.use_f32r else px[:, 0:256],
                                True, True, reads=[pTk, (pxk, "p"), (pxk, "x")], writes=[pAk])
                        self.mm(pB[:, 0:128], px[:, 0:128], pT[:], True, True, reads=[pTk, (pxk, "p")], writes=[pBk])
                        yield
                        nx, nxk = pxr_[s_].next(); nT, nTk = ptr__[s_].next()
                        self.copy("act", nT[:], pB[:, 0:128], reads=[pBk], writes=[nTk])
                        tr.op("dve", lambda e: e.tensor_tensor(out=nx[:, 128:256], in0=pA[:, 128:256], in1=px[:, 128:256], op=ALU.add),
                              reads=[pAk, (pxk, "x")], writes=[(nxk, "x")])
                        self.copy("dve", nx[:, 0:128], pA[:, 0:128], reads=[pAk], writes=[(nxk, "p")])
                        px, pxk, pT, pTk = nx, nxk, nT, nTk
                        yield
                    else:
                        self.mm(pA[:, 0:128], pT[:], px[:, 128:256], True, True, reads=[pTk, (pxk, "x")], writes=[pAk])
                        yield
                        nx, nxk = pxr_[s_].next()
                        tr.op("dve", lambda e: e.tensor_tensor(out=nx[:, 128:256], in0=pA[:, 0:128], in1=px[:, 128:256], op=ALU.add),
                              reads=[pAk, (pxk, "x")], writes=[(nxk, "x")])
                        px, pxk = nx, nxk
                        yield
                MT, MTk = px[:, 128:256], (pxk, "x")
                cnt = [0]

                def apply(lhsT_ap, lhs_key, rhs_ap, rhs_key, add_ap=None, add_key=None):
                    pb, pbk = (pA, pAk) if cnt[0] % 2 == 0 else (pB, pBk)
                    eng = "act" if cnt[0] % 2 == 0 else "dve"
                    cnt[0] += 1
                    self.mm(pb[:, 0:128], lhsT_ap, rhs_ap, True, True, reads=[lhs_key, rhs_key], writes=[pbk])
                    yield
                    yb, ybk = ybr_[s_].next()
                    if add_ap is None:
                        self.copy(eng, yb[:], pb[:, 0:128], reads=[pbk], writes=[ybk])
                    else:
                        tr.op("dve", lambda e: e.tensor_tensor(out=yb[:], in0=pb[:, 0:128], in1=add_ap, op=ALU.add), reads=[pbk, add_key], writes=[ybk])
                    yield
                    return yb, ybk

                def s64(z_ap, z_key):
                    y1, y1k = yield from apply(MT, MTk, z_ap, z_key)
                    z1, z1k = yield from apply(lt[:, 0, :], (ltk, 0), y1[:], y1k)
                    r_ = yield from apply(MT, MTk, z1[:], z1k, add_ap=y1[:], add_key=y1k)
                    return r_

                y2, y2k = yield from s64(a0[:, 128:256], (a0k, "x"))
                z2, z2k = yield from apply(lt[:, 1, :], (ltk, 1), y2[:], y2k)
                y4, y4k = yield from s64(z2[:], z2k)
                px, pxk = pxr_[s_].next()
                tr.op("pool", lambda e: e.tensor_tensor(out=px[:, 128:256], in0=y2[:], in1=y4[:], op=ALU.add), reads=[y2k, y4k], writes=[(pxk, "x")])
                yield
                tr.op("pe", lambda e: e.transpose(pA[0:64, 0:128], px[:, 192:256], self.ident[:]), reads=[(pxk, "x"), "ident"], writes=[pAk])
                yield
                wt_, wtk = wtr[s_].next()
                self.copy("act", wt_[0:64, :], pA[0:64, 0:128], reads=[pAk], writes=[wtk])
                yield
                self.mm(pB[:, 0:64], wt_[:], S[:, d, h, :], True, True, reads=[wtk, (Sk, h)], writes=[pBk])
                yield
                vn, vnk = vnr[s_].next()
                tr.op("dve", lambda e: e.tensor_tensor(out=vn[:], in0=px[:, 128:192], in1=pB[:, 0:64], op=ALU.subtract),
                      reads=[(pxk, "x"), pBk], writes=[vnk])
                yield
                self.mm(pA[:, 0:64], qg[:], S[:, d, h, :], True, False, reads=[qgk, (Sk, h)], writes=[pAk])
                self.mm(pA[:, 0:64], qk[:], vn[:], False, True, reads=[qkk, vnk], writes=[pAk])
                self.mm(pB[0:64, 0:64], kd[:], vn[:], True, True, reads=[kdk, vnk], writes=[pBk])
                yield
                if d == 0:
                    self.copy("act", oacc[:, t, h * 64:(h + 1) * 64], pA[:, 0:64], reads=[pAk], writes=[("oacc", t, h)])
                else:
                    tr.op("dve", lambda e: e.tensor_tensor(out=oacc[:, t, h * 64:(h + 1) * 64], in0=pA[:, 0:64], in1=oacc[:, t, h * 64:(h + 1) * 64], op=ALU.add),
                          reads=[pAk, ("oacc", t, h)], writes=[("oacc", t, h)])
                tr.op("dve", lambda e: e.scalar_tensor_tensor(out=S[0:64, d, h, :], in0=S[0:64, d, h, :], scalar=egb[:, last:last + 1], in1=pB[0:64, 0:64],
                                                             op0=ALU.mult, op1=ALU.add), reads=[(Sk, h), egk, pBk], writes=[(Sk, h)])

            def unit_stream():
                for d in range(2):
                    order = list(range(NT)) if d == 0 else [1, 0] + list(range(NT - 1, 1, -1))
                    for t in order:
                        gT, gTk = gTr.next()
                        pk = ("ps", 7)
                        self.mm(self.ps[7][0:8, 256:384], g[:, t, d * 8:(d + 1) * 8], tri[:, d, :], True, True, reads=["gg", "gtri"], writes=[pk])
                        self.copy("act", gT[0:8, :], self.ps[7][0:8, 256:384], reads=[pk], writes=[gTk])
                        for h in range(8):
                            yield (d, t, h, gT, gTk)

            stream = unit_stream()
            active = {}
            free_slots = list(range(NSLOT))
            exhausted = False
            while True:
                while free_slots and not exhausted:
                    try:
                        d_, t_, h_, gT_, gTk_ = next(stream)
                    except StopIteration:
                        exhausted = True
                        break
                    s_ = free_slots.pop(0)
                    active[s_] = unit(d_, t_, h_, s_, gT_, gTk_)
                if not active:
                    break
                for s_ in sorted(active):
                    try:
                        next(active[s_])
                    except StopIteration:
                        del active[s_]
                        free_slots.append(s_)
            if (self.stop or "").startswith("gdn_u1"):
                tr.barrier()
                es_scan.close()
                return
            tr.barrier()
            es_scan.close()
            for t in range(NT):
                for h in range(8):
                    tr.lastw[("oacc", t, h)] = None
            tr.lastw = {k: v for k, v in tr.lastw.items() if v is not None}
            if "D_x" in self.taps:
                self.dma(self.scr["D_oc"].rearrange("(t p) c -> p t c", p=128), oacc[:], reads=[("oacc", t, h) for t in range(NT) for h in range(8)])
            gn = self.sb(es, "gn", [128, 64])
            self.dma(gn[:], self.inp["gnorm_rep"][:, l, :], writes=["gn"])
            zr = Ring(self, es, "gz", 2, [128, 512]); sq2 = Ring(self, es, "gsq2", 2, [128, 512]); ssr = Ring(self, es, "gss", 2, [128, 8])
            ycr = Ring(self, es, "gyc", 2, [128, 4, 128], BF16)
            ycv = self.scr["D_yc"].rearrange("(c p) t -> p c t", p=128)
            for t in range(NT):
                z, zk = zr.next()
                self.dma(z[:], self.scr["D_tm"][t * 128:(t + 1) * 128, ZC:ZC + 512], reads=["D_tm"], writes=[zk])
                tr.op("act", lambda e: e.activation(out=z[:], in_=z[:], func=AF.Silu), reads=[zk], writes=[zk])
                ok = [("oacc", t, h) for h in range(8)]
                sq, sk = sq2.next()
                tr.op("pool", lambda e: e.tensor_tensor(out=sq[:], in0=oacc[:, t, :], in1=oacc[:, t, :], op=ALU.mult), reads=ok, writes=[sk])
                ss, ssk = ssr.next()
                tr.op("dve", lambda e: e.tensor_reduce(out=ss[:], in_=sq[:].rearrange("p (h d) -> p h d", d=64), axis=AX.X, op=ALU.add), reads=[sk], writes=[ssk])
                self.rstd_from_ss(ss[:], ssk, ss[:], ssk, 1.0 / 64)
                o3 = oacc[:, t, :].rearrange("p (h d) -> p h d", d=64)
                tr.op("dve", lambda e: e.tensor_tensor(out=sq[:].rearrange("p (h d) -> p h d", d=64), in0=o3, in1=ss[:].unsqueeze(2).to_broadcast([128, 8, 64]), op=ALU.mult),
                      reads=ok + [ssk], writes=[sk])
                tr.op("dve", lambda e: e.tensor_tensor(out=sq[:].rearrange("p (h d) -> p h d", d=64), in0=sq[:].rearrange("p (h d) -> p h d", d=64),
                                                      in1=gn[:].unsqueeze(1).to_broadcast([128, 8, 64]), op=ALU.mult), reads=[sk, "gn"], writes=[sk])
                tr.op("dve", lambda e: e.tensor_tensor(out=sq[:], in0=sq[:], in1=z[:], op=ALU.mult), reads=[sk, zk], writes=[sk])
                pk = ("ps", 4 + t % 2); pt = self.ps[4 + t % 2]
                for c in range(4):
                    tr.op("pe", lambda e: e.transpose(pt[:, c * 128:(c + 1) * 128], sq[:, c * 128:(c + 1) * 128], self.ident[:]), reads=[sk, "ident"], writes=[pk])
                yc, yk = ycr.next()
                self.copy("act", yc[:], pt[:].rearrange("p (c t) -> p c t", c=4), reads=[pk], writes=[yk])
                self.dma(ycv[:, :, t * 128:(t + 1) * 128], yc[:], reads=[yk], writes=["D_yc"])
            tr.barrier()

    def phase_merge(self, l):
        tr = self.tr
        with ExitStack() as es:
            wp = [self.sb(es, f"wp{i}", [128, 4, D], BF16) for i in range(3)]
            wo = self.sb(es, "wo", [128, 8, D], BF16)
            for m in range(8):
                for i, nm in enumerate(("w_pa", "w_pb", "w_pc")):
                    self.dma(wp[i][:, :, m * 128:(m + 1) * 128], self.inp[nm][l].rearrange("(c p) n -> p c n", p=128)[:, :, m * 128:(m + 1) * 128],
                             writes=[(nm, m)], q="gq")
            for m in range(8):
                self.dma(wo[:, :, m * 128:(m + 1) * 128], self.inp["w_out"][l].rearrange("(c p) n -> p c n", p=128)[:, :, m * 128:(m + 1) * 128],
                         writes=[("w_out", m)], q="gq")
            gr = Ring(self, es, "mg", 1, [128, 24, 512], BF16)
            yr = [Ring(self, es, f"my{i}", 2, [128, 4, 512], BF16) for i in range(3)]
            mixr = Ring(self, es, "mix", 2, [128, 512]); tmr = Ring(self, es, "mtm", 2, [128, 512])
            mT = Ring(self, es, "mixT", 1, [128, 8, 512], BF16)
            gv = self.scr["D_gate"].rearrange("(c p) t -> p c t", p=128)
            yv = [self.scr[nm].rearrange("(c p) t -> p c t", p=128) for nm in ("D_ya", "D_yb", "D_yc")]
            pi = 0
            for ci, (t0, n) in enumerate(TCH):
                s = 1 if ci == 0 else 0
                gt, gk = gr.next()
                self.dma(gt[:, :, 0:n], gv[:, :, t0:t0 + n], reads=[("D_gate", ci)], writes=[gk])
                ys = []
                for i in range(3):
                    y, yk = yr[i].next()
                    self.dma(y[:, :, 0:n], yv[i][:, :, t0:t0 + n], reads=["D_ya", "D_yb", "D_yc"], writes=[yk])
                    ys.append((y, yk))
                mt, mtk = mT.next()
                for m in range(8):
                    mix, mk = mixr.next()
                    for i in range(3):
                        pk = ("ps", pi % 4); pt = self.ps[pi % 4]; pi += 1
                        y, yk = ys[i]
                        for c in range(4):
                            self.mm(pt[:, 0:n], wp[i][:, c, m * 128:(m + 1) * 128], y[:, c, 0:n], c == 0, c == 3, reads=[(("w_pa", "w_pb", "w_pc")[i], m), yk], writes=[pk])
                        if i == 0:
                            tr.op("dve", lambda e: e.tensor_tensor(out=mix[:, 0:n], in0=pt[:, 0:n], in1=gt[:, m, 0:n], op=ALU.mult), reads=[pk, gk], writes=[mk])
                        else:
                            tm, tk = tmr.next()
                            tr.op("dve", lambda e: e.tensor_tensor(out=tm[:, 0:n], in0=pt[:, 0:n], in1=gt[:, i * 8 + m, 0:n], op=ALU.mult), reads=[pk, gk], writes=[tk])
                            if i == 1:
                                tr.op("pool", lambda e: e.tensor_tensor(out=mix[:, 0:n], in0=mix[:, 0:n], in1=tm[:, 0:n], op=ALU.add), reads=[mk, tk], writes=[mk])
                            else:
                                tr.op("pool", lambda e: e.tensor_tensor(out=mt[:, m, 0:n], in0=mix[:, 0:n], in1=tm[:, 0:n], op=ALU.add), reads=[mk, tk], writes=[(mtk, m)])
                for m in range(8):
                    pk = ("ps", 4 + m % 4); pt = self.ps[4 + m % 4]
                    for c in range(8):
                        self.mm(pt[:, 0:n], wo[:, c, m * 128:(m + 1) * 128], mt[:, c, 0:n], c == 0, c == 7, reads=[("w_out", m), (mtk, c)], writes=[pk])
                    tr.op("dve", lambda e: e.scalar_tensor_tensor(out=self.xT[:, m, t0:t0 + n], in0=pt[:, 0:n], scalar=self.mod[:, l, 16 + m, s:s + 1],
                                                                 in1=self.xT[:, m, t0:t0 + n], op0=ALU.mult, op1=ALU.add),
                          reads=[pk, "mod", ("xT", ci)], writes=[("xT", ci)])
            tr.barrier()

    def phase_ffn(self, l):
        tr = self.tr
        with ExitStack() as es:
            fcw = self.sb(es, "fcw", [128, 44, 3])
            self.dma(fcw[:], self.inp["fconv_fm"][:, l, :, :], writes=["fcw"])
            with ExitStack() as es2:
                xn = self.sb(es2, "xn2", [128, 8, T], BF16)
                self.adaln(es2, l, 1, xn)
                wr = Ring(self, es2, "wup", 2, [128, 8, 2, 128], BF16)
                hr = [Ring(self, es2, f"fh{i}", 2, [128, T]) for i in range(2)]
                cr = [Ring(self, es2, f"fc{i}", 2, [128, T]) for i in range(2)]
                ar = Ring(self, es2, "fact", 2, [128, T], BF16)
                wv = self.inp["ffn_up"][l].rearrange("(c p) n -> p c n", p=128)
                pi = 0
                for cc in range(22):
                    w, wk = wr.next()
                    self.dma(w[:, :, 0, :], wv[:, :, cc * 128:(cc + 1) * 128], writes=[wk], q="gq")
                    self.dma(w[:, :, 1, :], wv[:, :, 2816 + cc * 128:2816 + (cc + 1) * 128], writes=[wk], q="gq")
                    cs_ = []
                    for i in range(2):
                        hb, hk = hr[i].next()
                        for ci, (t0, n) in enumerate(TCH):
                            pk = ("ps", pi % 4); pt = self.ps[pi % 4]; pi += 1
                            for c in range(8):
                                self.mm(pt[:, 0:n], w[:, c, i, :], xn[:, c, t0:t0 + n], c == 0, c == 7, reads=[wk, ("xn", ci)], writes=[pk])
                            self.copy(self.evac_eng(), hb[:, t0:t0 + n], pt[:, 0:n], reads=[pk], writes=[hk])
                        cb, ck = cr[i].next()
                        j = cc + 22 * i
                        tr.op("act", lambda e: e.activation(out=cb[:], in_=hb[:], func=AF.Identity, scale=fcw[:, j, 1:2]), reads=[hk, "fcw"], writes=[ck])
                        for (o0, o1, i0, i1, k) in ((1, CTX, 0, CTX - 1, 0), (CTX + 1, T, CTX, T - 1, 0), (0, CTX - 1, 1, CTX, 2), (CTX, T - 1, CTX + 1, T, 2)):
                            tr.op("dve", lambda e: e.scalar_tensor_tensor(out=cb[:, o0:o1], in0=hb[:, i0:i1], scalar=fcw[:, j, k:k + 1], in1=cb[:, o0:o1],
                                                                       op0=ALU.mult, op1=ALU.add), reads=[hk, ck, "fcw"], writes=[ck])
                        cs_.append((cb, ck))
                    (ca, cak), (cb, cbk) = cs_
                    tr.op("act", lambda e: e.activation(out=ca[:], in_=ca[:], func=AF.Silu), reads=[cak], writes=[cak])
                    a, ak = ar.next()
                    tr.op("dve", lambda e: e.tensor_tensor(out=a[:], in0=ca[:], in1=cb[:], op=ALU.mult), reads=[cak, cbk], writes=[ak])
                    self.dma(self.scr["D_act"][cc * 128:(cc + 1) * 128, :], a[:], reads=[ak], writes=["D_act"])
                tr.barrier()
            with ExitStack() as es2:
                wd = self.sb(es2, "wdn", [128, 22, D], BF16)
                for m in range(8):
                    self.dma(wd[:, :, m * 128:(m + 1) * 128], self.inp["ffn_down"][l].rearrange("(c p) n -> p c n", p=128)[:, :, m * 128:(m + 1) * 128],
                             writes=[("wdn", m)], q="gq")
                ar = Ring(self, es2, "fa", 2, [128, 22, 512], BF16)
                av = self.scr["D_act"].rearrange("(c p) t -> p c t", p=128)
                for ci, (t0, n) in enumerate(TCH):
                    s = 1 if ci == 0 else 0
                    a, ak = ar.next()
                    self.dma(a[:, :, 0:n], av[:, :, t0:t0 + n], reads=["D_act"], writes=[ak])
                    for m in range(8):
                        pk = ("ps", m % 4); pt = self.ps[m % 4]
                        for c in range(22):
                            self.mm(pt[:, 0:n], wd[:, c, m * 128:(m + 1) * 128], a[:, c, 0:n], c == 0, c == 21, reads=[("wdn", m), ak], writes=[pk])
                        tr.op("dve", lambda e: e.scalar_tensor_tensor(out=self.xT[:, m, t0:t0 + n], in0=pt[:, 0:n], scalar=self.mod[:, l, 40 + m, s:s + 1],
                                                                     in1=self.xT[:, m, t0:t0 + n], op0=ALU.mult, op1=ALU.add),
                              reads=[pk, "mod", ("xT", ci)], writes=[("xT", ci)])
                tr.barrier()

    def phase_final(self):
        tr = self.tr
        with ExitStack() as es:
            nf = self.sb(es, "nf", [128, D])
            self.dma(nf[:], self.inp["normf_rep"], writes=["nf"])
            xr = Ring(self, es, "fx", 2, [128, D]); sqr = Ring(self, es, "fsq", 2, [128, D]); ssr = Ring(self, es, "fss", 2, [128, 1])
            for t in range(2, NT):
                x, xk = xr.next()
                for half in range(2):
                    pk = ("ps", (2 * t + half) % 8); pt = self.ps[(2 * t + half) % 8]
                    for j in range(4):
                        c = half * 4 + j
                        tr.op("pe", lambda e: e.transpose(pt[:, j * 128:(j + 1) * 128], self.xT[:, c, t * 128:(t + 1) * 128], self.ident[:]),
                              reads=["ident"], writes=[pk])
                    self.copy(self.evac_eng(), x[:, half * 512:(half + 1) * 512], pt[:, :], reads=[pk], writes=[xk])
                sq, sk = sqr.next(); ss, ssk = ssr.next()
                tr.op("pool", lambda e: e.tensor_tensor(out=sq[:], in0=x[:], in1=x[:], op=ALU.mult), reads=[xk], writes=[sk])
                tr.op("dve", lambda e: e.tensor_reduce(out=ss[:], in_=sq[:], axis=AX.X, op=ALU.add), reads=[sk], writes=[ssk])
                self.rstd_from_ss(ss[:], ssk, ss[:], ssk, 1.0 / D)
                tr.op("dve", lambda e: e.scalar_tensor_tensor(out=sq[:], in0=x[:], scalar=ss[:, 0:1], in1=nf[:], op0=ALU.mult, op1=ALU.mult),
                      reads=[xk, ssk, "nf"], writes=[sk])
                self.dma(self.out[(t - 2) * 128:(t - 1) * 128, :], sq[:], reads=[sk])


def _rope_tables(d):
    da = d // 2
    half = da // 2
    inv = (10000.0 ** (-np.arange(half, dtype=np.float32) / half)).astype(np.float32)
    t = np.arange(SEQ)
    rows = (t // 64).astype(np.float32)
    cols = (t % 64).astype(np.float32)
    cos = np.ones((d, T), np.float32)
    sin = np.zeros((d, T), np.float32)
    for i in range(d):
        pos = rows if i < da else cols
        ii = i % da
        ang = pos * inv[ii % half]
        cos[i, CTX:] = np.cos(ang)
        sin[i, CTX:] = np.sin(ang) * (-1.0 if ii < half else 1.0)
    return np.stack([cos, sin])


def _partner(d):
    da = d // 2
    half = da // 2
    idx = np.arange(d)
    ii = idx % da
    return np.where(ii < half, idx + half, idx - half)


def _constants():
    j = np.arange(128)[:, None]
    i = np.arange(128)[None, :]
    tri = np.stack([(j <= i), (j >= i)]).astype(np.float32)
    gmask = np.stack([np.where(i < j, 0.0, BIG),
                      np.where(j <= i, 0.0, -BIG),
                      np.where(i > j, 0.0, BIG),
                      np.where(j >= i, 0.0, -BIG)]).astype(np.float32)
    m_prev = (j >= i).astype(np.float32)
    m_next = (j <= i).astype(np.float32)
    swamask = np.stack([np.tile(m_prev, (1, 4)), np.tile(m_next, (1, 4))]).astype(np.float32)
    b32 = (j // 32 == i // 32)
    b64 = (j // 64 == i // 64)
    bmask = np.stack([b32, b64 & ~b32, ~b64]).astype(np.float32)
    selh = np.zeros((8, 8, 128), np.float32)
    for h in range(8):
        selh[h, h, :] = 1.0
    return dict(ident=np.eye(128, dtype=np.float32), tri=tri, gmask=gmask, bmask=bmask, bd64=b64.astype(np.float32), swamask=swamask, selh=selh,
                ropeM=_rope_tables(32), ropeS=_rope_tables(64))


def _fm(v, p=128):
    v = np.asarray(v, np.float32)
    lead = v.shape[:-1]
    n = v.shape[-1] // p
    return np.ascontiguousarray(np.moveaxis(v.reshape(lead + (n, p)), -1, 0))


def prepare_shared(inp, L=DEPTH):
    f = lambda k: np.asarray(inp[k], np.float32)[:L] if k not in ("norm_f",) else np.asarray(inp[k], np.float32)
    w_in = f("w_in")
    p32 = _partner(32)
    p64 = _partner(64)
    sqp = (np.arange(8)[:, None] * 64 + p64[None, :]).reshape(-1)
    skp = (np.arange(2)[:, None] * 64 + p64[None, :]).reshape(-1)
    w_fm = np.concatenate([w_in[:, :, 0:640], w_in[:, :, 640:672], w_in[:, :, 640:672][:, :, p32],
                           w_in[:, :, 672:1184], w_in[:, :, 672:1184][:, :, sqp],
                           w_in[:, :, 1184:1312], w_in[:, :, 1184:1312][:, :, skp],
                           w_in[:, :, 1440:2976], np.zeros((L, D, 64), np.float32), w_in[:, :, 3520:6592]], axis=2)
    assert w_fm.shape[2] == NFM
    w_tm = np.concatenate([w_in[:, :, 1312:1440], w_in[:, :, 2976:3488], w_in[:, :, 3488:3520]], axis=2)
    wuq = f("w_uq").reshape(L, 384, 8, 96)
    wukv = f("w_ukv").reshape(L, 256, 8, 128)
    sh = dict(
        w_mod=f("w_mod"), b_mod_fm=_fm(f("b_mod")),
        norm1_fm=_fm(f("norm1")), norm2_fm=_fm(f("norm2")), normf_rep=np.ascontiguousarray(np.broadcast_to(f("norm_f"), (128, D))),
        w_fm=np.ascontiguousarray(w_fm), w_tm=np.ascontiguousarray(w_tm),
        qnorm_fm=_fm(f("mla_q_norm")), kvnorm_fm=_fm(f("mla_kv_norm")),
        wuq_n=np.ascontiguousarray(wuq[..., 0:64].reshape(L, 384, 512)),
        wuq_r=np.ascontiguousarray(wuq[..., 64:96].reshape(L, 384, 256)),
        wuq_rp=np.ascontiguousarray(wuq[..., 64:96][..., p32].reshape(L, 384, 256)),
        wukv_k=np.ascontiguousarray(wukv[..., 0:64].reshape(L, 256, 512)),
        wukv_v=np.ascontiguousarray(wukv[..., 64:128].reshape(L, 256, 512)),
        sink_rep=np.ascontiguousarray(np.broadcast_to(f("swa_sink"), (64, L, 8))),
        gconv_fm=np.ascontiguousarray(f("gdn_conv").reshape(L, 3, 12, 128).transpose(3, 0, 2, 1)),
        alog_rep=np.ascontiguousarray(np.broadcast_to(f("gdn_a_log").reshape(L, 16), (128, L, 16))),
        dtb_rep=np.ascontiguousarray(np.broadcast_to(f("gdn_dt_bias").reshape(L, 16), (128, L, 16))),
        gnorm_rep=np.ascontiguousarray(np.broadcast_to(f("gdn_norm"), (128, L, 64))),
        w_pa=f("w_branch_a"), w_pb=f("w_branch_b"), w_pc=f("w_branch_c"), w_out=f("w_out"),
        ffn_up=f("ffn_up"), fconv_fm=np.ascontiguousarray(f("ffn_conv").reshape(L, 3, 44, 128).transpose(3, 0, 2, 1)),
        ffn_down=f("ffn_down"),
    )
    sh.update(_constants())
    return sh


def per_core(inp, b):
    x = np.asarray(inp["x"], np.float32)[b]
    ctx = np.asarray(inp["ctx"], np.float32)[b]
    c = np.asarray(inp["c"], np.float32)[b]
    cc = np.asarray(inp["c_ctx"], np.float32)
    return dict(xin=np.ascontiguousarray(np.concatenate([ctx, x], 0)),
                c_fm=np.ascontiguousarray(np.stack([_fm(c), _fm(cc)], axis=-1)))


_CACHE = {}


def kernel(**inputs):
    if "nc" not in _CACHE:
        _CACHE["nc"] = Builder().build()
    nc = _CACHE["nc"]
    shared = prepare_shared(inputs)
    in_maps = []
    for b in range(8):
        m = dict(shared)
        m.update(per_core(inputs, b))
        in_maps.append(m)
    res = run_bass_kernel_spmd(nc, in_maps, core_ids=list(range(8)))
    return np.stack([np.asarray(r["out"], np.float32) for r in res.results], axis=0)
```
